# Optimizing a Trainium2 kernel written in Bass

```python
import math
import jax
import jax.numpy as jnp
from jax import lax
import numpy as np

D_MODEL = 1024
BATCH = 16
SEQ = 4096
DEPTH = 1
DEC_BATCH = 128
DEC_SEQ = 1
PAST_LEN = 8192
PAGE_SIZE = 128

HEAD_DIM = 64
ATTN_PATTERNS = ((128, 1), (512, 4), (2048, 16))
N_PATTERNS = len(ATTN_PATTERNS)
ATTN_HG = 8
WIN_KEYS = 128
ATTN_QKV = N_PATTERNS * 3 * ATTN_HG * HEAD_DIM
ATTN_OUT = ATTN_HG * HEAD_DIM
ROPE_THETA = 10000.0
ATTN_SCALE = HEAD_DIM ** -0.5
NEG_INF = -1e30
SSD_INNER = D_MODEL
SSD_HEADDIM = 64
SSD_HEADS = SSD_INNER // SSD_HEADDIM
SSD_GROUPS = 2
SSD_HPG = SSD_HEADS // SSD_GROUPS
SSD_STATE = 128
SSD_CONV = 4
SSD_CONV_DIM = SSD_INNER + 2 * SSD_GROUPS * SSD_STATE
SSD_CHUNK = 128
MIX_IN = ATTN_QKV + SSD_INNER + SSD_CONV_DIM + SSD_HEADS
MIX_OUT = ATTN_OUT + SSD_INNER
FFN_HIDDEN = -(-8 * D_MODEL // (3 * 256)) * 256
NORM_EPS = 1e-6

kernel_name = 'hymba_ssd_dilated_swa_decoder_step'


def rmsnorm(x, w):
    xf = x.astype(jnp.float32)
    y = xf * lax.rsqrt(jnp.mean(xf * xf, axis=-1, keepdims=True) + NORM_EPS)
    return (y * w.astype(jnp.float32)).astype(x.dtype)


def rope(x, pos):
    half = HEAD_DIM // 2
    inv_freq = ROPE_THETA ** (-jnp.arange(half, dtype=jnp.float32) / half)
    ang = pos.astype(jnp.float32)[:, None] * inv_freq[None, :]
    cos = jnp.cos(ang)[:, None, :]
    sin = jnp.sin(ang)[:, None, :]
    xf = x.astype(jnp.float32)
    x1, x2 = xf[..., :half], xf[..., half:]
    return jnp.concatenate([x1 * cos - x2 * sin, x2 * cos + x1 * sin], axis=-1).astype(x.dtype)


def split_mix(proj):
    lead = proj.shape[:-1]
    o1 = ATTN_QKV
    o2 = o1 + SSD_INNER
    o3 = o2 + SSD_CONV_DIM
    qkv = proj[..., :o1].reshape(lead + (N_PATTERNS, 3, ATTN_HG, HEAD_DIM))
    return qkv, proj[..., o1:o2], proj[..., o2:o3], proj[..., o3:]


def dilated_window_prompt(q, k, v, dil):
    b, s, h, d = q.shape
    L = s // dil
    nb = -(-L // WIN_KEYS)
    Lp = nb * WIN_KEYS

    def to_sub(t):
        t = t.reshape(b, L, dil, h, d).transpose(0, 2, 1, 3, 4)
        t = jnp.pad(t, ((0, 0), (0, 0), (0, Lp - L), (0, 0), (0, 0)))
        return t.reshape(b, dil, nb, WIN_KEYS, h, d)

    def with_prev(t):
        prev = jnp.pad(t, ((0, 0), (0, 0), (1, 0), (0, 0), (0, 0), (0, 0)))[:, :, :-1]
        return jnp.concatenate([prev, t], axis=3)

    qs = to_sub(q)
    kb = with_prev(to_sub(k))
    vb = with_prev(to_sub(v))
    qi = jnp.arange(WIN_KEYS)[:, None]
    ki = jnp.arange(2 * WIN_KEYS)[None, :]
    dist = qi + WIN_KEYS - ki
    blk = jnp.arange(nb)[:, None, None]
    valid = (dist >= 0) & (dist <= WIN_KEYS) & (blk * WIN_KEYS + ki - WIN_KEYS >= 0)
    sc = jnp.einsum('brnqhd,brnkhd->brnhqk', qs, kb, preferred_element_type=jnp.float32) * ATTN_SCALE
    sc = jnp.where(valid[None, None, :, None], sc, NEG_INF)
    m = jnp.max(sc, axis=-1, keepdims=True)
    p = jnp.exp(sc - m)
    den = jnp.sum(p, axis=-1, keepdims=True)
    o = jnp.einsum('brnhqk,brnkhd->brnhqd', p, vb.astype(jnp.float32)) / den
    lse = (m + jnp.log(den))[..., 0]
    o = o.transpose(0, 1, 2, 4, 3, 5).reshape(b, dil, Lp, h, d)[:, :, :L]
    o = o.transpose(0, 2, 1, 3, 4).reshape(b, s, h, d)
    lse = lse.transpose(0, 1, 2, 4, 3).reshape(b, dil, Lp, h)[:, :, :L]
    lse = lse.transpose(0, 2, 1, 3).reshape(b, s, h)
    return o, lse


def dilated_window_sample(q, k_new, v_new, kv_cache, window, dil):
    wb = kv_cache.shape[1]
    t = q.shape[1]
    kv_all = jnp.concatenate([kv_cache.astype(k_new.dtype), jnp.stack([k_new, v_new], axis=2)], axis=1)
    n_keys = window // dil + 1
    idx = wb + jnp.arange(t)[:, None] - dil * jnp.arange(n_keys)[None, :]
    valid = idx >= 0
    sel = kv_all[:, jnp.maximum(idx, 0)]
    sc = jnp.einsum('bthd,btkhd->bthk', q, sel[:, :, :, 0], preferred_element_type=jnp.float32) * ATTN_SCALE
    sc = jnp.where(valid[None, :, None, :], sc, NEG_INF)
    m = jnp.max(sc, axis=-1, keepdims=True)
    p = jnp.exp(sc - m)
    den = jnp.sum(p, axis=-1, keepdims=True)
    o = jnp.einsum('bthk,btkhd->bthd', p, sel[:, :, :, 1].astype(jnp.float32)) / den
    lse = (m + jnp.log(den))[..., 0]
    return o, lse, kv_all[:, -wb:]


def merge_patterns(outs, lses):
    alpha = jax.nn.softmax(jnp.stack(lses, axis=0), axis=0)
    o = sum(alpha[g][..., None] * outs[g] for g in range(N_PATTERNS))
    return o.reshape(o.shape[:-2] + (ATTN_OUT,))


def causal_conv_silu(xpad, conv_w, conv_b, t):
    xf = xpad.astype(jnp.float32)
    w = conv_w.astype(jnp.float32)
    out = conv_b.astype(jnp.float32) + sum(xf[:, j:j + t] * w[j] for j in range(SSD_CONV))
    return jax.nn.silu(out)


def ssd_branch_inputs(xbc_conv, dt_raw, dt_bias, a_log):
    lead = xbc_conv.shape[:-1]
    gn = SSD_GROUPS * SSD_STATE
    xs = xbc_conv[..., :SSD_INNER].reshape(lead + (SSD_HEADS, SSD_HEADDIM))
    Bm = xbc_conv[..., SSD_INNER:SSD_INNER + gn].reshape(lead + (SSD_GROUPS, SSD_STATE))
    Cm = xbc_conv[..., SSD_INNER + gn:].reshape(lead + (SSD_GROUPS, SSD_STATE))
    dt = jax.nn.softplus(dt_raw.astype(jnp.float32) + dt_bias.astype(jnp.float32))
    A = -jnp.exp(a_log.astype(jnp.float32))
    return xs, Bm, Cm, dt, A


def ssd_chunked(xs, dt, A, Bm, Cm):
    b, s = xs.shape[:2]
    nc = s // SSD_CHUNK
    xc = xs.reshape(b, nc, SSD_CHUNK, SSD_GROUPS, SSD_HPG, SSD_HEADDIM)
    dtc = dt.reshape(b, nc, SSD_CHUNK, SSD_GROUPS, SSD_HPG)
    Bc = Bm.reshape(b, nc, SSD_CHUNK, SSD_GROUPS, SSD_STATE)
    Cc = Cm.reshape(b, nc, SSD_CHUNK, SSD_GROUPS, SSD_STATE)
    cs = jnp.cumsum(dtc * A.reshape(SSD_GROUPS, SSD_HPG), axis=2)
    causal = jnp.tril(jnp.ones((SSD_CHUNK, SSD_CHUNK), dtype=bool))[:, :, None, None]
    seg = jnp.where(causal, cs[:, :, :, None] - cs[:, :, None, :], NEG_INF)
    wgt = jnp.einsum('bctgn,bcsgn->bctsg', Cc, Bc)[..., None] * jnp.exp(seg) * dtc[:, :, None]
    y_diag = jnp.einsum('bctsgh,bcsghp->bctghp', wgt, xc)
    decay_end = jnp.exp(cs[:, :, -1:] - cs)
    states = jnp.einsum('bclgn,bclgh,bclghp->bcghpn', Bc, decay_end * dtc, xc)
    chunk_decay = jnp.exp(cs[:, :, -1])

    def step(h, inp):
        st, dec = inp
        return h * dec[..., None, None] + st, h

    h0 = jnp.zeros((b, SSD_GROUPS, SSD_HPG, SSD_HEADDIM, SSD_STATE), jnp.float32)
    h_last, h_in = lax.scan(step, h0, (jnp.moveaxis(states, 1, 0), jnp.moveaxis(chunk_decay, 1, 0)))
    h_in = jnp.moveaxis(h_in, 0, 1)
    y_off = jnp.einsum('bclgn,bcghpn,bclgh->bclghp', Cc, h_in, jnp.exp(cs))
    y = (y_diag + y_off).reshape(b, s, SSD_HEADS, SSD_HEADDIM)
    return y, h_last.reshape(b, SSD_HEADS, SSD_HEADDIM, SSD_STATE)


def ssd_recurrent(xs, dt, A, Bm, Cm, h0):
    Bh = jnp.repeat(Bm, SSD_HPG, axis=2)
    Ch = jnp.repeat(Cm, SSD_HPG, axis=2)

    def step(h, inp):
        xt, dtt, bt, ct = inp
        h = h * jnp.exp(dtt * A)[:, :, None, None] + jnp.einsum('bhp,bhn->bhpn', dtt[:, :, None] * xt, bt)
        return h, jnp.einsum('bhpn,bhn->bhp', h, ct)

    seq_in = (jnp.swapaxes(xs, 0, 1), jnp.swapaxes(dt, 0, 1), jnp.swapaxes(Bh, 0, 1), jnp.swapaxes(Ch, 0, 1))
    h, ys = lax.scan(step, h0.astype(jnp.float32), seq_in)
    return jnp.swapaxes(ys, 0, 1), h


def ssd_gate_out(y, xs, z, d_skip, ssd_norm_w):
    y = y + d_skip.astype(jnp.float32)[:, None] * xs
    y = y.reshape(y.shape[:-2] + (SSD_INNER,)) * jax.nn.silu(z.astype(jnp.float32))
    return rmsnorm(y, ssd_norm_w)


def mixer_prompt(xn, w_in, w_out, conv_w, conv_b, dt_bias, a_log, d_skip, ssd_norm_w):
    b, s, _ = xn.shape
    qkv, z, xbc, dt_raw = split_mix(xn @ w_in)
    pos = jnp.arange(s)
    outs, lses, kv_new = [], [], []
    for g, (window, dil) in enumerate(ATTN_PATTERNS):
        q = rope(qkv[..., g, 0, :, :], pos)
        k = rope(qkv[..., g, 1, :, :], pos)
        v = qkv[..., g, 2, :, :]
        o, lse = dilated_window_prompt(q, k, v, dil)
        outs.append(o)
        lses.append(lse)
        kv_new.append(jnp.stack([k, v], axis=2)[:, s - min(window, s):])
    attn = merge_patterns(outs, lses)
    xbc_pad = jnp.pad(xbc, ((0, 0), (SSD_CONV - 1, 0), (0, 0)))
    conv_new = xbc_pad[:, s:]
    xs, Bm, Cm, dt, A = ssd_branch_inputs(causal_conv_silu(xbc_pad, conv_w, conv_b, s), dt_raw, dt_bias, a_log)
    y, h_new = ssd_chunked(xs, dt, A, Bm, Cm)
    ssm = ssd_gate_out(y, xs, z, d_skip, ssd_norm_w)
    mix = jnp.concatenate([attn.astype(xn.dtype), ssm.astype(xn.dtype)], axis=-1) @ w_out
    return mix, kv_new, conv_new, h_new


def mixer_sample(xn, kv_caches, conv_state, ssm_state, w_in, w_out, conv_w, conv_b, dt_bias, a_log, d_skip, ssd_norm_w):
    b, t, _ = xn.shape
    qkv, z, xbc, dt_raw = split_mix(xn @ w_in)
    pos = PAST_LEN + jnp.arange(t)
    outs, lses, kv_new = [], [], []
    for g, (window, dil) in enumerate(ATTN_PATTERNS):
        q = rope(qkv[..., g, 0, :, :], pos)
        k = rope(qkv[..., g, 1, :, :], pos)
        v = qkv[..., g, 2, :, :]
        o, lse, kv_upd = dilated_window_sample(q, k, v, kv_caches[g], window, dil)
        outs.append(o)
        lses.append(lse)
        kv_new.append(kv_upd)
    attn = merge_patterns(outs, lses)
    xbc_cat = jnp.concatenate([conv_state.astype(xbc.dtype), xbc], axis=1)
    conv_new = xbc_cat[:, -(SSD_CONV - 1):]
    xs, Bm, Cm, dt, A = ssd_branch_inputs(causal_conv_silu(xbc_cat, conv_w, conv_b, t), dt_raw, dt_bias, a_log)
    y, h_new = ssd_recurrent(xs, dt, A, Bm, Cm, ssm_state)
    ssm = ssd_gate_out(y, xs, z, d_skip, ssd_norm_w)
    mix = jnp.concatenate([attn.astype(xn.dtype), ssm.astype(xn.dtype)], axis=-1) @ w_out
    return mix, kv_new, conv_new, h_new


def swiglu(x, w_ffn_in, w_ffn_out):
    hcat = x @ w_ffn_in
    gate, up = hcat[..., :FFN_HIDDEN], hcat[..., FFN_HIDDEN:]
    return (jax.nn.silu(gate) * up) @ w_ffn_out


def setup_inputs(seed: int = 0) -> dict:
    key = jax.random.key(seed)
    ks = jax.random.split(key, 24)
    f32 = jnp.float32

    def nrm(k, shape, scale):
        return jax.random.normal(k, shape, f32) * scale

    def gain(k, n):
        return 1.0 + 0.01 * jax.random.normal(k, (DEPTH, n), f32)

    wb = [min(w, PAST_LEN) for (w, _) in ATTN_PATTERNS]
    dt0 = jnp.exp(jax.random.uniform(ks[13], (DEPTH, SSD_HEADS), f32, math.log(1e-3), math.log(1e-1)))
    return {
        'x_prompt': nrm(ks[0], (BATCH, SEQ, D_MODEL), 1.0),
        'x_sample': nrm(ks[1], (DEC_BATCH, DEC_SEQ, D_MODEL), 1.0),
        'cache_kv_w128': nrm(ks[2], (DEPTH, DEC_BATCH, wb[0], 2, ATTN_HG, HEAD_DIM), 1.0),
        'cache_kv_w512': nrm(ks[3], (DEPTH, DEC_BATCH, wb[1], 2, ATTN_HG, HEAD_DIM), 1.0),
        'cache_kv_w2048': nrm(ks[4], (DEPTH, DEC_BATCH, wb[2], 2, ATTN_HG, HEAD_DIM), 1.0),
        'state_conv': nrm(ks[5], (DEPTH, DEC_BATCH, SSD_CONV - 1, SSD_CONV_DIM), 1.0),
        'state_ssm': nrm(ks[6], (DEPTH, DEC_BATCH, SSD_HEADS, SSD_HEADDIM, SSD_STATE), 0.5),
        'norm_mix_pre': gain(ks[7], D_MODEL),
        'norm_mix_post': gain(ks[8], D_MODEL),
        'norm_ffn_pre': gain(ks[9], D_MODEL),
        'norm_ffn_post': gain(ks[10], D_MODEL),
        'w_in': nrm(ks[11], (DEPTH, D_MODEL, MIX_IN), D_MODEL ** -0.5),
        'w_out': nrm(ks[12], (DEPTH, MIX_OUT, D_MODEL), MIX_OUT ** -0.5),
        'conv_w': nrm(ks[14], (DEPTH, SSD_CONV, SSD_CONV_DIM), SSD_CONV ** -0.5),
        'conv_b': nrm(ks[15], (DEPTH, SSD_CONV_DIM), 0.02),
        'dt_bias': dt0 + jnp.log(-jnp.expm1(-dt0)),
        'a_log': jnp.log(jax.random.uniform(ks[16], (DEPTH, SSD_HEADS), f32, 1.0, 16.0)),
        'd_skip': 1.0 + 0.01 * jax.random.normal(ks[17], (DEPTH, SSD_HEADS), f32),
        'ssd_norm_w': gain(ks[18], SSD_INNER),
        'w_ffn_in': nrm(ks[19], (DEPTH, D_MODEL, 2 * FFN_HIDDEN), D_MODEL ** -0.5),
        'w_ffn_out': nrm(ks[20], (DEPTH, FFN_HIDDEN, D_MODEL), FFN_HIDDEN ** -0.5),
    }


def reference(x_prompt, x_sample, cache_kv_w128, cache_kv_w512, cache_kv_w2048, state_conv, state_ssm,
              norm_mix_pre, norm_mix_post, norm_ffn_pre, norm_ffn_post, w_in, w_out, conv_w, conv_b,
              dt_bias, a_log, d_skip, ssd_norm_w, w_ffn_in, w_ffn_out):
    yp, ys = x_prompt, x_sample
    p_kv = ([], [], [])
    s_kv = ([], [], [])
    p_conv, p_ssm, s_conv, s_ssm = [], [], [], []
    for l in range(DEPTH):
        wl = (w_in[l], w_out[l], conv_w[l], conv_b[l], dt_bias[l], a_log[l], d_skip[l], ssd_norm_w[l])
        mix, kvs, cst, hst = mixer_prompt(rmsnorm(yp, norm_mix_pre[l]), *wl)
        yp = yp + rmsnorm(mix, norm_mix_post[l])
        yp = yp + rmsnorm(swiglu(rmsnorm(yp, norm_ffn_pre[l]), w_ffn_in[l], w_ffn_out[l]), norm_ffn_post[l])
        for g in range(N_PATTERNS):
            p_kv[g].append(kvs[g])
        p_conv.append(cst)
        p_ssm.append(hst)
        caches = (cache_kv_w128[l], cache_kv_w512[l], cache_kv_w2048[l])
        mix, kvs, cst, hst = mixer_sample(rmsnorm(ys, norm_mix_pre[l]), caches, state_conv[l], state_ssm[l], *wl)
        ys = ys + rmsnorm(mix, norm_mix_post[l])
        ys = ys + rmsnorm(swiglu(rmsnorm(ys, norm_ffn_pre[l]), w_ffn_in[l], w_ffn_out[l]), norm_ffn_post[l])
        for g in range(N_PATTERNS):
            s_kv[g].append(kvs[g])
        s_conv.append(cst)
        s_ssm.append(hst)
    p_kv_w128 = jnp.stack(p_kv[0])
    p_kv_w512 = jnp.stack(p_kv[1])
    p_kv_w2048 = jnp.stack(p_kv[2])
    p_conv_new = jnp.stack(p_conv)
    p_ssm_new = jnp.stack(p_ssm)
    s_kv_w128 = jnp.stack(s_kv[0])
    s_kv_w512 = jnp.stack(s_kv[1])
    s_kv_w2048 = jnp.stack(s_kv[2])
    s_conv_new = jnp.stack(s_conv)
    s_ssm_new = jnp.stack(s_ssm)
    return (yp, ys, p_kv_w128, p_kv_w512, p_kv_w2048, p_conv_new, p_ssm_new,
            s_kv_w128, s_kv_w512, s_kv_w2048, s_conv_new, s_ssm_new)
```

```python
import math
import numpy as np
import concourse.bass as bass
import concourse.mybir as mybir
from concourse.bass_utils import run_bass_kernel_spmd

F32 = mybir.dt.float32
BF16 = mybir.dt.bfloat16
ALU = mybir.AluOpType
AF = mybir.ActivationFunctionType

ENGS = ("pe", "act", "dve", "pool", "sp")


class Buf:
    __slots__ = ("t", "name", "last_w", "readers", "dsem", "dcount", "aliases", "excl")

    def __init__(self, t, name):
        self.excl = False
        self.t = t
        self.name = name
        self.last_w = None
        self.readers = []
        self.dsem = None
        self.dcount = 0
        self.aliases = []

    def __getitem__(self, k):
        return self.t[k]


class Op:
    __slots__ = ("eng", "emit", "deps", "is_dma", "buf", "dval", "sig", "idx")

    def __init__(self, eng, emit, is_dma=False):
        self.eng = eng
        self.emit = emit
        self.deps = []
        self.is_dma = is_dma
        self.buf = None
        self.dval = 0
        self.sig = False
        self.idx = 0


class Prog:
    def __init__(self, nc):
        self.nc = nc
        self.ops = []
        self._stack = []
        self.nsb = 0
        self.frozen = False

    def sbuf(self, name, shape, dt):
        g = self.nc.sbuf_tensor(name, list(shape), dt)
        t = g.__enter__()
        self._stack.append(g)
        return Buf(t, name)

    def psum(self, name, shape, dt=F32):
        g = self.nc.psum_tensor(name, list(shape), dt)
        t = g.__enter__()
        self._stack.append(g)
        b = Buf(t, name)
        b.excl = True
        return b

    def view(self, ap, name, parent=None):
        b = Buf(ap, name)
        return b

    def dram(self, name, shape, dt):
        t = self.nc.dram_tensor(name, list(shape), dt, kind="Internal")
        return Buf(t, name)

    @staticmethod
    def alias(a, others):
        for o in others:
            a.aliases.append(o)
            o.aliases.append(a)

    def _add(self, op, reads, writes):
        if self.frozen:
            return op
        deps = []
        for b in reads:
            if b.last_w is not None:
                deps.append(b.last_w)
            if b.excl:
                deps.extend(r for r in b.readers if r.eng != op.eng)
        for b in writes:
            if b.last_w is not None:
                deps.append(b.last_w)
            deps.extend(b.readers)
            for a in b.aliases:
                if a.last_w is not None:
                    deps.append(a.last_w)
                deps.extend(a.readers)
        seen = set()
        for d in deps:
            if d is op or id(d) in seen:
                continue
            seen.add(id(d))
            op.deps.append(d)
        for b in reads:
            if not op.is_dma:
                b.readers = [r for r in b.readers if r.is_dma or r.eng != op.eng]
            b.readers.append(op)
        for b in writes:
            b.last_w = op
            b.readers = []
        self.ops.append(op)
        return op

    def op(self, eng, emit, reads=(), writes=()):
        return self._add(Op(eng, emit), reads, writes)

    def dma(self, eng, out_ap, in_ap, carrier, reads=(), writes=(), **kw):
        def emit(e):
            return e.dma_start(out=out_ap, in_=in_ap, **kw)
        op = Op(eng, emit, is_dma=True)
        op.buf = carrier
        return self._add(op, reads, writes)

    def finalize(self):
        nc = self.nc
        for op in self.ops:
            for d in op.deps:
                if d.is_dma:
                    continue
                if d.eng == op.eng and d.eng == "pe":
                    continue
                d.sig = True
        cnt = {e: 0 for e in ENGS}
        for op in self.ops:
            if op.is_dma:
                b = op.buf
                b.dcount += 16
                op.dval = b.dcount
            elif op.sig:
                cnt[op.eng] += 1
                op.idx = cnt[op.eng]
        esem = {}
        for e in ENGS:
            g = nc.semaphore("es_" + e)
            esem[e] = g.__enter__()
            self._stack.append(g)
        nsem = 0
        for op in self.ops:
            if op.is_dma and op.buf.dsem is None:
                g = nc.semaphore("ds%d" % nsem)
                nsem += 1
                op.buf.dsem = g.__enter__()
                self._stack.append(g)
        by_eng = {e: [] for e in ENGS}
        for op in self.ops:
            by_eng[op.eng].append(op)
        all_dma_bufs = []
        seenb = set()
        for op in self.ops:
            if op.is_dma and id(op.buf) not in seenb:
                seenb.add(id(op.buf))
                all_dma_bufs.append(op.buf)

        def run_engine(ename):
            def body(e):
                known = {x: 0 for x in ENGS}
                knownd = {}
                for op in by_eng[ename]:
                    for d in op.deps:
                        if d.is_dma:
                            k = id(d.buf)
                            if knownd.get(k, 0) < d.dval:
                                e.wait_ge(d.buf.dsem, d.dval)
                                knownd[k] = d.dval
                        else:
                            if d.eng == ename and ename == "pe":
                                continue
                            if known[d.eng] < d.idx:
                                e.wait_ge(esem[d.eng], d.idx)
                                known[d.eng] = d.idx
                    ins = op.emit(e)
                    if op.is_dma:
                        ins.then_inc(op.buf.dsem, 16)
                    elif op.sig:
                        ins.then_inc(esem[ename], 1)
                if ename == "sp":
                    for b in all_dma_bufs:
                        e.wait_ge(b.dsem, b.dcount)
                    for x in ENGS:
                        if x != "sp" and cnt[x] > 0:
                            e.wait_ge(esem[x], cnt[x])
            return body

        with nc.Block() as block:
            block.tensor(run_engine("pe"))
            block.scalar(run_engine("act"))
            block.vector(run_engine("dve"))
            block.gpsimd(run_engine("pool"))
            block.sync(run_engine("sp"))

    def close(self):
        while self._stack:
            g = self._stack.pop()
            g.__exit__(None, None, None)


class Arena:
    def __init__(self, P, name, nbytes):
        self.buf = P.sbuf(name, [128, nbytes // 2], BF16)
        self.items = []
        self.off = 0
        self.nbytes = nbytes

    def phase(self):
        self.off = 0

    def take(self, name, shape, dt):
        n = 1
        for d in shape[1:]:
            n *= d
        nb = n * (4 if dt == F32 else 2)
        nb4 = (nb + 3) // 4 * 4
        assert self.off + nb4 <= self.nbytes, (name, self.off, nb4, self.nbytes)
        ap = self.buf.t[0:shape[0], self.off // 2:(self.off + nb) // 2]
        if dt == F32:
            ap = ap.bitcast(F32)
        if len(shape) == 3:
            ap = ap.rearrange("p (a b) -> p a b", a=shape[1])
        elif len(shape) == 4:
            ap = ap.rearrange("p (a b c) -> p a b c", a=shape[1], b=shape[2])
        v = Buf(ap, name)
        for (lo, hi, o) in self.items:
            if lo < self.off + nb4 and self.off < hi:
                v.aliases.append(o)
                o.aliases.append(v)
        self.items.append((self.off, self.off + nb4, v))
        self.off += nb4
        return v


D = 1024
TT = 512
HD = 64
NH = 8
DILS = (1, 4, 16)
WINS = (128, 512, 2048)
QKV = 4608
O_Z = 4608
O_XBC = 5632
O_DT = 7168
MIX_IN = 7184
FFN_H = 2816
NFC = 22
PAST = 8192
EPS = 1e-6


def host_consts(seq):
    c = {}
    c["c_ident"] = np.eye(128, dtype=np.float32)
    pm = np.zeros((128, 128), np.float32)
    for dp in range(128):
        d = (dp // 64) * 64 + ((dp % 64) + 32) % 64
        pm[d, dp] = 1.0
    c["c_perm"] = pm
    k = np.arange(128)[:, None]
    q = np.arange(128)[None, :]
    bd = (k // 32) == (q // 32)
    masks = np.stack([
        (k >= q), (k <= q), bd, bd & ((k % 32) >= (q % 32)), bd & ((k % 32) <= (q % 32)),
    ]).astype(np.float32)
    c["c_masks"] = masks
    half = HD // 2
    inv_freq = (np.float32(10000.0) ** (-np.arange(half, dtype=np.float32) / np.float32(half))).astype(np.float32)
    pos = np.arange(seq, dtype=np.float32)
    ang = (pos[:, None] * inv_freq[None, :]).astype(np.float32)
    cosv = np.cos(ang).astype(np.float32)
    sinv = np.sin(ang).astype(np.float32)
    p = np.arange(128)
    fidx = p % 32
    sign = np.where((p % 64) < 32, -1.0, 1.0).astype(np.float32)
    rope = np.zeros((3, 2, 128, seq), np.float32)
    nt = seq // TT
    for g in range(3):
        perm = np.zeros(seq, np.int64)
        for T in range(nt):
            for u in range(4):
                w = np.arange(128)
                if g == 0:
                    tau = 128 * u + w
                elif g == 1:
                    tau = 4 * w + u
                else:
                    tau = 16 * (w % 32) + 4 * u + (w // 32)
                perm[T * TT + u * 128 + w] = T * TT + tau
        rope[g, 0] = cosv[perm][:, fidx].T
        rope[g, 1] = (sinv[perm][:, fidx].T) * sign[:, None]
    c["c_rope"] = rope
    sel = np.zeros((65, 64), np.float32)
    sel[64, :] = 1.0
    c["c_sel"] = sel
    angs = (np.float32(PAST) * inv_freq).astype(np.float32)
    col = np.arange(512)
    cs = np.cos(angs).astype(np.float32)[col % 32]
    sn = np.sin(angs).astype(np.float32)[col % 32] * np.where((col % 64) < 32, -1.0, 1.0)
    c["c_srope"] = np.stack([cs, sn]).astype(np.float32)
    dl = np.zeros((16, 16, 128), np.float32)
    for b in range(16):
        dl[b, b, :] = 1.0
    c["c_delta"] = dl.reshape(16, 2048)
    es = np.zeros((128, 16, 16), np.float32)
    for b in range(16):
        es[:, b, b] = 1.0
    c["c_esel"] = es.reshape(128, 256)
    return c


def build(nseq, seq, ns, do_sample=True, debug=None, stop=None, past=PAST):
    nc = bass.Bass("TRN2", target_bir_lowering=False)
    P = Prog(nc)

    def stage(name):
        if stop is not None and name == stop:
            P.frozen = True
    NT = seq // TT
    WB = [min(w, past) for w in WINS]
    PW = [min(w, seq) for w in WINS]

    def din(name, shape):
        return nc.dram_tensor(name, list(shape), F32, kind="ExternalInput")

    def dout(name, shape):
        return nc.dram_tensor(name, list(shape), F32, kind="ExternalOutput")

    x_prompt = din("x_prompt", [nseq, seq, D])
    x_sample = din("x_sample", [ns, D])
    cache = [din("cache%d" % g, [ns, WB[g], 2, 512]) for g in range(3)]
    state_conv = din("state_conv", [ns, 3, 1536])
    state_ssm = din("state_ssm", [ns, 1024, 128])
    norm_mix_pre = din("norm_mix_pre", [1, D])
    norm_mix_post = din("norm_mix_post", [1, D])
    norm_ffn_pre = din("norm_ffn_pre", [1, D])
    norm_ffn_post = din("norm_ffn_post", [1, D])
    w_in = din("w_in", [D, MIX_IN])
    w_out = din("w_out", [1536, D])
    conv_w = din("conv_w", [4, 1536])
    conv_b = din("conv_b", [1, 1536])
    dt_bias = din("dt_bias", [1, 16])
    a_log = din("a_log", [1, 16])
    d_skip = din("d_skip", [1, 16])
    ssd_norm_w = din("ssd_norm_w", [1, D])
    w_ffn_in = din("w_ffn_in", [D, 2 * FFN_H])
    w_ffn_out = din("w_ffn_out", [FFN_H, D])
    c_ident = din("c_ident", [128, 128])
    c_perm = din("c_perm", [128, 128])
    c_masks = din("c_masks", [5, 128, 128])
    c_rope = din("c_rope", [3, 2, 128, seq])
    c_sel = din("c_sel", [65, 64])
    c_srope = din("c_srope", [2, 512])
    c_delta = din("c_delta", [16, 2048])
    c_esel = din("c_esel", [128, 256])

    y_prompt = dout("y_prompt", [nseq, seq, D])
    y_sample = dout("y_sample", [ns, D])
    p_kv = [dout("p_kv%d" % g, [nseq, PW[g], 2, 512]) for g in range(3)]
    p_conv = dout("p_conv", [nseq, 3, 1536])
    p_ssm = dout("p_ssm", [nseq, 1024, 128])
    s_kv = [dout("s_kv%d" % g, [ns, WB[g], 2, 512]) for g in range(3)]
    s_conv = dout("s_conv", [ns, 3, 1536])
    s_ssm = dout("s_ssm", [ns, 1024, 128])
    dbg = {}
    if debug:
        for nm, shp in debug.items():
            dbg[nm] = dout("dbg_" + nm, shp)

    win_b = P.dram("win_b", [D, MIX_IN], BF16)
    wout_b = P.dram("wout_b", [1536, D], BF16)
    wfi_b = P.dram("wfi_b", [D, 2 * FFN_H], BF16)
    wfo_b = P.dram("wfo_b", [FFN_H, D], BF16)
    kscr = [[P.dram("kscr%d_%d" % (g, T), [128, 4, 512], BF16) for T in range(NT)] for g in range(3)]
    vscr = [[P.dram("vscr%d_%d" % (g, T), [128, 4, 8, 65], BF16) for T in range(NT)] for g in range(3)]

    def ACT(out, in_, func, reads, writes, **kw):
        return P.op("act", lambda e: e.activation(out=out, in_=in_, func=func, **kw), reads, writes)

    def TT_(eng, out, in0, in1, op, reads, writes):
        return P.op(eng, lambda e: e.tensor_tensor(out=out, in0=in0, in1=in1, op=op), reads, writes)

    def TS(eng, out, in0, s1, s2, op0, op1, reads, writes):
        if op1 is None:
            return P.op(eng, lambda e: e.tensor_scalar(out=out, in0=in0, scalar1=s1, scalar2=None, op0=op0), reads, writes)
        return P.op(eng, lambda e: e.tensor_scalar(out=out, in0=in0, scalar1=s1, scalar2=s2, op0=op0, op1=op1), reads, writes)

    def STT(eng, out, in0, scalar, in1, op0, op1, reads, writes):
        return P.op(eng, lambda e: e.scalar_tensor_tensor(out=out, in0=in0, scalar=scalar, in1=in1, op0=op0, op1=op1), reads, writes)

    def CP(eng, out, in_, reads, writes):
        if eng == "act":
            return P.op("act", lambda e: e.copy(out=out, in_=in_), reads, writes)
        return P.op(eng, lambda e: e.tensor_copy(out=out, in_=in_), reads, writes)

    def MM(out, lhsT, rhs, start, stop, reads, writes):
        return P.op("pe", lambda e: e.matmul(out, lhsT=lhsT, rhs=rhs, start=start, stop=stop), reads, writes)

    def TR(out, in_, ident, reads, writes):
        return P.op("pe", lambda e: e.transpose(out=out, in_=in_, identity=ident), reads, writes)

    def MSET(eng, ap, val, writes):
        return P.op(eng, lambda e: e.memset(ap, val), (), writes)

    DQ = "pool"
    WQ = "sp"

    banks = [P.psum("bank%d" % i, [128, 512], F32) for i in range(8)]
    held = set()
    lru = list(range(8))

    def palloc():
        for i in lru:
            if i not in held:
                held.add(i)
                lru.remove(i)
                lru.append(i)
                return banks[i]
        raise RuntimeError("out of PSUM banks")

    def pfree(b):
        held.discard(banks.index(b))

    cst = P.sbuf("cst_f", [128, 128], F32)
    ident_f = P.sbuf("ident_f", [128, 128], F32)
    ident_b = P.sbuf("ident_b", [128, 128], BF16)
    perm_b = P.sbuf("perm_b", [128, 128], BF16)
    masks_b = P.sbuf("masks_b", [128, 5, 128], BF16)
    U_f = P.sbuf("U_f", [128, 128], F32)
    ones_f = P.sbuf("ones_f", [128, 128], F32)
    ones_b = P.sbuf("ones_b", [128, 128], BF16)
    sel_f = P.sbuf("sel_f", [65, 64], F32)
    eps_t = P.sbuf("eps_t", [128, 1], F32)
    one_t = P.sbuf("one_t", [128, 1], F32)
    nw_mix = P.sbuf("nw_mix", [128, D], F32)
    nw_ffn = P.sbuf("nw_ffn", [128, D], F32)
    nwp = P.sbuf("nwp", [128, 3, 8], F32)
    cw_t = P.sbuf("cw_t", [128, 12, 4], F32)
    cb_t = P.sbuf("cb_t", [128, 12], F32)
    dtb_t = P.sbuf("dtb_t", [128, 16], F32)
    A_t = P.sbuf("A_t", [128, 16], F32)
    D_t = P.sbuf("D_t", [128, 8], F32)

    P.dma(DQ, ident_f[:], c_ident[:, :], ident_f, writes=[ident_f])
    CP("dve", ident_b[:], ident_f[:], [ident_f], [ident_b])
    P.dma(DQ, cst[:], c_perm[:, :], cst, writes=[cst])
    CP("dve", perm_b[:], cst[:], [cst], [perm_b])
    for i in range(5):
        P.dma(DQ, cst[:], c_masks[i, :, :], cst, writes=[cst])
        CP("dve", masks_b[:, i, :], cst[:], [cst], [masks_b])
    P.dma(DQ, U_f[:], c_masks[1, :, :], U_f, writes=[U_f])
    MSET("dve", ones_f[:], 1.0, [ones_f])
    MSET("dve", ones_b[:], 1.0, [ones_b])
    MSET("dve", eps_t[:], EPS, [eps_t])
    MSET("dve", one_t[:], 1.0, [one_t])
    P.dma(DQ, sel_f[:], c_sel[:, :], sel_f, writes=[sel_f])
    P.dma(DQ, nw_mix[:], norm_mix_post[0:1, :].partition_broadcast(128), nw_mix, writes=[nw_mix])
    P.dma(DQ, nw_ffn[:], norm_ffn_post[0:1, :].partition_broadcast(128), nw_ffn, writes=[nw_ffn])
    for i, src in enumerate((norm_mix_pre, norm_ffn_pre, ssd_norm_w)):
        P.dma(DQ, nwp[:, i, :], src[0, :].rearrange("(k p) -> p k", p=128), nwp, writes=[nwp],
              allow_slow_non_contiguous=True)
    for j_ in range(4):
        P.dma(DQ, cw_t[:, :, j_], conv_w[j_, :].rearrange("(c p) -> p c", p=128), cw_t, writes=[cw_t],
              allow_slow_non_contiguous=True)
    P.dma(DQ, cb_t[:], conv_b[0, :].rearrange("(c p) -> p c", p=128), cb_t, writes=[cb_t],
          allow_slow_non_contiguous=True)
    P.dma(DQ, dtb_t[:], dt_bias[0:1, :].partition_broadcast(128), dtb_t, writes=[dtb_t])
    P.dma(DQ, A_t[:], a_log[0:1, :].partition_broadcast(128), A_t, writes=[A_t])
    ACT(A_t[:], A_t[:], AF.Exp, [A_t], [A_t])
    TS("dve", A_t[:], A_t[:], -1.0, None, ALU.mult, None, [A_t], [A_t])
    dsk2 = d_skip[0, :].rearrange("(c e) -> e c", e=2)
    for e_ in range(2):
        P.dma(DQ, D_t[64 * e_:64 * e_ + 64, :], dsk2[e_:e_ + 1, :].partition_broadcast(64), D_t, writes=[D_t],
              allow_slow_non_contiguous=True)

    stage("consts")
    NRING = 3
    wring = [P.sbuf("wring%d" % i, [128, 8, 512], BF16) for i in range(NRING)]
    wr_i = [0]

    def wnext():
        b = wring[wr_i[0] % NRING]
        wr_i[0] += 1
        return b

    xin = [P.sbuf("xin%d" % i, [128, D], F32) for i in range(2)]
    xnb = [P.sbuf("xnb%d" % i, [128, D], BF16) for i in range(2)]
    junk = P.sbuf("junk", [128, D], BF16)
    hb = [P.sbuf("hb%d" % j, [128, D], F32) for j in range(4)]
    xnT0 = P.sbuf("xnT0", [128, 8, 512], BF16)
    xnTp = P.sbuf("xnTp", [128, 8, 512], BF16)
    qraw = [P.sbuf("qraw%d" % i, [128, 512], BF16) for i in range(2)]
    vcur = [P.sbuf("vcur%d" % i, [128, 4, 8, 65], BF16) for i in range(2)]
    kvst = [P.sbuf("kvst%d" % i, [128, 2, 512], F32) for i in range(1)]
    kvst_i = [0]
    convtail = P.sbuf("convtail", [128, 12, 3], F32)
    yT = P.sbuf("yT", [128, 8, 512], BF16)
    stT = P.sbuf("stT", [128, 16, 64], F32)
    stz = P.sbuf("stz", [128, 16, 128], BF16)
    arA = Arena(P, "arA", 39936)
    acc = arA.take("acc", [128, 8, 512], F32)
    NPT = 6
    PTb = [arA.take("PT%d" % i, [128, 512], BF16) for i in range(NPT)]
    pt_i = [0]
    NSTR = 8
    kstr = [arA.take("kstr%d" % i, [128, 512], BF16) for i in range(NSTR)]
    vstr = [arA.take("vstr%d" % i, [128, 4, 2, 65], BF16) for i in range(NSTR)]
    arA.phase()
    stg = [arA.take("stg%d" % i, [128, 515], F32) for i in range(2)]
    cacc = [arA.take("cacc%d" % i, [128, 512], F32) for i in range(2)]
    xdtz = arA.take("xdtz", [128, 16, 128], BF16)
    xdd = arA.take("xdd", [128, 16, 64], BF16)
    Btok = arA.take("Btok", [128, 2, 128], BF16)
    Rb = arA.take("Rb", [128, 4, 128], F32)
    tmpb = arA.take("tmpb", [128, 4, 128], F32)
    Eb = arA.take("Eb", [128, 4, 128], F32)
    ecs = arA.take("ecs", [128, 4, 128], F32)
    Wb = arA.take("Wb", [128, 16, 128], BF16)
    Cdec = arA.take("Cdec", [128, 16, 128], BF16)
    Gm = arA.take("Gm", [128, 2, 128], F32)
    sqb = [arA.take("sqb%d" % i, [128, 512], BF16) for i in range(2)]
    rstd_b = arA.take("rstd_b", [128, 512], F32)
    ytmp = [arA.take("ytmp%d" % i, [128, 128], F32) for i in range(2)]
    arA.phase()
    fst = [arA.take("fst%d" % i, [128, 4, 512], F32) for i in range(2)]
    arB = Arena(P, "arB", 8192)
    kcur = [arB.take("kcur%d" % i, [128, 4, 512], BF16) for i in range(2)]
    arB.phase()
    m0 = [arB.take("m0_%d" % j, [128, 512], F32) for j in range(4)]
    arC = Arena(P, "arC", 8192)
    qT = arC.take("qT", [128, 4, 512], BF16)
    ropet = arC.take("ropet", [128, 2, 512], F32)
    arC.phase()
    attnT = arC.take("attnT", [64, 8, 512], BF16)
    arD = Arena(P, "arD", 8192)
    rt1 = [arD.take("rt1_%d" % i, [128, 512], F32) for i in range(2)]
    rt2 = [arD.take("rt2_%d" % i, [128, 512], F32) for i in range(2)]
    arD.phase()
    sgt = [arD.take("sgt%d" % i, [128, 512], F32) for i in range(2)]
    arE = Arena(P, "arE", 22528)
    actT = arE.take("actT", [128, NFC, 512], BF16)
    arE.phase()
    sz = arE.take("sz", [128, 8, 512], BF16)
    xc = arE.take("xc", [128, 12, 512], BF16)
    small = {}

    def sm(name, cols=1):
        if name not in small:
            small[name] = P.sbuf("sm_" + name, [128, cols], F32)
        return small[name]

    bst = [P.view(wring[i].t[:, 0:4, :], "bst%d" % i) for i in range(2)]
    for i in range(2):
        Prog.alias(wring[i], [bst[i]])
    prep_i = [0]

    def prep_piece(src, dst, r0, c0, ncol, scale_ap):
        i = prep_i[0]
        prep_i[0] += 1
        f = fst[i % 2]
        b = bst[i % 2]
        fv = f.t.rearrange("p a b -> p (a b)")[:, 0:ncol]
        bv = b.t.rearrange("p a b -> p (a b)")[:, 0:ncol]
        P.dma(WQ if i % 2 == 0 else DQ, fv, src[r0:r0 + 128, c0:c0 + ncol], f, writes=[f])
        eng = ("dve", "pool", "act")[i % 3]
        if scale_ap is None:
            CP(eng, bv, fv, [f], [b])
        elif eng == "act":
            ACT(bv, fv, AF.Copy, [f, nwp], [b], scale=scale_ap)
        else:
            TS(eng, bv, fv, scale_ap, None, ALU.mult, None, [f, nwp], [b])
        P.dma(WQ if i % 2 == 0 else DQ, dst.t[r0:r0 + 128, c0:c0 + ncol], bv, b, reads=[b], writes=[dst])

    for kc in range(8):
        for c0 in range(0, MIX_IN, 2048):
            prep_piece(w_in, win_b, kc * 128, c0, min(2048, MIX_IN - c0), nwp[:, 0, kc:kc + 1])
    for rc in range(12):
        prep_piece(w_out, wout_b, rc * 128, 0, 1024, None if rc < 4 else nwp[:, 2, rc - 4:rc - 3])
    for kc in range(8):
        for c0 in range(0, 2 * FFN_H, 2048):
            prep_piece(w_ffn_in, wfi_b, kc * 128, c0, min(2048, 2 * FFN_H - c0), nwp[:, 1, kc:kc + 1])
    for rc in range(NFC):
        prep_piece(w_ffn_out, wfo_b, rc * 128, 0, 1024, None)

    stage("prep")
    def wblock_in(c0, ncol):
        b = wnext()
        P.dma(WQ, b.t[:, :, 0:ncol], win_b.t[:, c0:c0 + ncol].rearrange("(k p) n -> p k n", p=128), b,
              reads=[win_b], writes=[b])
        return b

    def wblock_fi(c0, ncol):
        b = wnext()
        P.dma(WQ, b.t[:, :, 0:ncol], wfi_b.t[:, c0:c0 + ncol].rearrange("(k p) n -> p k n", p=128), b,
              reads=[wfi_b], writes=[b])
        return b

    def wblock_outA(half):
        b = wnext()
        P.dma(WQ, b.t[0:64, :, :], wout_b.t[0:512, half * 512:half * 512 + 512].rearrange("(h p) n -> p h n", p=64), b,
              reads=[wout_b], writes=[b])
        return b

    def wblock_outS(half):
        b = wnext()
        P.dma(WQ, b.t[:, :, :], wout_b.t[512:1536, half * 512:half * 512 + 512].rearrange("(c p) n -> p c n", p=128), b,
              reads=[wout_b], writes=[b])
        return b

    def wblock_fo(c0, n, half):
        b = wnext()
        P.dma(WQ, b.t[:, 0:n, :], wfo_b.t[c0 * 128:(c0 + n) * 128, half * 512:half * 512 + 512].rearrange("(c p) n -> p c n", p=128), b,
              reads=[wfo_b], writes=[b])
        return b

    nrm_i = [0]

    def rstd_from(ss_ap, ss_buf, out_buf):
        ACT(out_buf[:], ss_ap, AF.Ln, [ss_buf, eps_t], [out_buf], scale=1.0 / D, bias=eps_t[:])
        ACT(out_buf[:], out_buf[:], AF.Exp, [out_buf], [out_buf], scale=-0.5)

    def norm_transpose(src_buf, j, dstT):
        i = nrm_i[0]
        nrm_i[0] += 1
        ss = sm("nss%d" % (i % 2))
        rs = sm("nrs%d" % (i % 2))
        xb = xnb[i % 2]
        ACT(junk[:], src_buf[:], AF.Square, [src_buf], [junk, ss], accum_out=ss[:])
        rstd_from(ss[:], ss, rs)
        TS("dve", xb[:], src_buf[:], rs[:], None, ALU.mult, None, [src_buf, rs], [xb])
        bk = palloc()
        bv = bk.t[:].bitcast(BF16)
        for kc in range(8):
            TR(bv[:, kc * 128:(kc + 1) * 128], xb[:, kc * 128:(kc + 1) * 128], ident_b[:], [xb, ident_b], [bk])
        CP("act", dstT[:, :, j * 128:(j + 1) * 128], bv.rearrange("p (k t) -> p k t", k=8), [bk], [dstT])
        pfree(bk)

    def proj_fm(wb, coff, xT, bank, ncontract=8):
        for kc in range(ncontract):
            MM(bank[:], wb[:, kc, coff:coff + 128], xT[:, kc, :], kc == 0, kc == ncontract - 1, [wb, xT], [bank])

    def proj_tm(wb, ncol, xT, j, bank):
        for kc in range(8):
            MM(bank[:, 0:ncol], xT[:, kc, j * 128:(j + 1) * 128], wb[:, kc, 0:ncol], kc == 0, kc == 7, [wb, xT], [bank])

    rope_i = [0]

    def rope_evac(bank, dest_ap, dest_buf):
        i = rope_i[0]
        rope_i[0] += 1
        qr, t1, t2 = qraw[i % 2], rt1[i % 2], rt2[i % 2]
        stage("r0")
        CP("act", qr[:], bank[:], [bank], [qr])
        stage("r1")
        b2 = palloc()
        MM(b2[:], perm_b[:], qr[:], True, True, [perm_b, qr], [b2])
        stage("r2")
        TT_("dve", t1[:], bank[:], ropet[:, 0, :], ALU.mult, [bank, ropet], [t1])
        stage("r3")
        TT_("dve", t2[:], b2[:], ropet[:, 1, :], ALU.mult, [b2, ropet], [t2])
        pfree(b2)
        stage("r4")
        TT_("pool", dest_ap, t1[:], t2[:], ALU.add, [t1, t2], [dest_buf])

    def acc_view(g, h, rows):
        a = acc.t[rows, h, :]
        if g == 0:
            return a
        if g == 1:
            return a.rearrange("p (w u) -> p u w", u=4)
        return a.rearrange("p (i u r) -> p u r i", u=4, r=4)

    def bank_view(g, bank, rows):
        b = bank.t[rows, :]
        if g == 0:
            return b
        if g == 1:
            return b.rearrange("p (u w) -> p u w", u=4)
        return b.rearrange("p (u r i) -> p u r i", u=4, r=4)

    kv_i = [0]
    str_i = [0]

    def prompt_tile(s, T):
        t0 = T * TT
        for j in range(4):
            P.dma(WQ, hb[j][:], x_prompt[s, t0 + j * 128:t0 + (j + 1) * 128, :], hb[j], writes=[hb[j]])
        for j in range(4):
            xi = xin[j % 2]
            P.dma(DQ, xi[:], x_prompt[s, t0 + j * 128:t0 + (j + 1) * 128, :], xi, writes=[xi])
            norm_transpose(xi, j, xnT0)

        stage("A")
        for g in range(3):
            dil = DILS[g]
            if g == 0:
                xT = xnT0
            else:
                xT = xnTp
                for kc in range(8):
                    if g == 1:
                        src = xnT0.t[:, kc, :].rearrange("p (w u) -> p u w", u=4)
                        dst = xnTp.t[:, kc, :].rearrange("p (u w) -> p u w", u=4)
                    else:
                        src = xnT0.t[:, kc, :].rearrange("p (i u r) -> p u r i", u=4, r=4)
                        dst = xnTp.t[:, kc, :].rearrange("p (u r i) -> p u r i", u=4, r=4)
                    CP("pool", dst, src, [xnT0], [xnTp])
            P.dma(DQ, ropet[:], c_rope[g, :, :, t0:t0 + TT].rearrange("c p n -> p c n"), ropet, writes=[ropet])
            kc_ = kcur[kv_i[0] % 2]
            vc_ = vcur[kv_i[0] % 2]
            kv_i[0] += 1
            cbase = g * 1536
            stage("b0_%d" % g)
            wq = wblock_in(cbase, 512)
            stage("b1_%d" % g)
            for fc in range(4):
                bk = palloc()
                proj_fm(wq, fc * 128, xT, bk)
                rope_evac(bk, qT[:, fc, :], qT)
                pfree(bk)
            stage("b2_%d" % g)
            wk = wblock_in(cbase + 512, 512)
            for fc in range(4):
                bk = palloc()
                proj_fm(wk, fc * 128, xT, bk)
                rope_evac(bk, kc_[:, fc, :], kc_)
                pfree(bk)
            stage("b3_%d" % g)
            wv = wblock_in(cbase + 1024, 512)
            for u in range(4):
                bk = palloc()
                proj_tm(wv, 512, xT, u, bk)
                CP("act", vc_[:, u, :, 0:64], bk.t[:, :].rearrange("p (h d) -> p h d", h=8), [bk], [vc_])
                pfree(bk)
            stage("proj%d" % g)
            nprev = {0: 1, 1: 1, 2: 4}[g]
            if T < NT - 1:
                P.dma(DQ, kscr[g][T].t[:, :, :], kc_[:, :, :], kc_, reads=[kc_], writes=[kscr[g][T]])
                P.dma(DQ, vscr[g][T].t[:, :, :, :], vc_[:, :, :, :], vc_, reads=[vc_], writes=[vscr[g][T]])
            first_out = seq - PW[g]
            units_out = []
            if g == 0:
                if T == NT - 1:
                    units_out = [3]
            elif (T + 1) * TT > first_out:
                units_out = [0, 1, 2, 3]
            for u in units_out:
                st = kvst[0]
                kvst_i[0] += 1
                bk = palloc()
                bv = bk.t[:].bitcast(BF16)
                for hp in range(4):
                    TR(bv[:, hp * 128:(hp + 1) * 128], kc_[:, hp, u * 128:(u + 1) * 128], ident_b[:], [kc_, ident_b], [bk])
                CP("act", st[:, 0, :], bv[:, 0:512], [bk], [st])
                pfree(bk)
                CP("pool", st[:, 1, :].rearrange("p (h d) -> p h d", h=8), vc_[:, u, :, 0:64], [vc_], [st])
                rbase = T * TT - first_out
                if g == 0:
                    P.dma(DQ, p_kv[g][s, 0:128, :, :], st[:, :, :], st, reads=[st])
                elif g == 1:
                    dv = p_kv[g][s, rbase:rbase + TT, :, :].rearrange("(w u) c f -> u w c f", u=4)
                    P.dma(DQ, dv[u], st[:, :, :], st, reads=[st])
                else:
                    dv = p_kv[g][s, rbase:rbase + TT, :, :].rearrange("(i u r) c f -> u r i c f", u=4, r=4)
                    for r in range(4):
                        P.dma(DQ, dv[u, r], st[r * 32:(r + 1) * 32, :, :], st, reads=[st])

            stage("pkv%d" % g)
            def cur_k(hp, u, kb=kc_):
                return kb, kb[:, hp, u * 128:(u + 1) * 128]

            def cur_v(h, u, vb=vc_):
                return vb, vb[:, u, h, 0:65]

            for hp in range(4):
                srcs = []
                deltas = []
                if g == 2:
                    deltas = [d_ for d_ in (4, 3, 2, 1) if T - d_ >= 0]
                elif T >= 1:
                    deltas = [1]
                for d_ in deltas:
                    ks = kstr[str_i[0] % NSTR]
                    vs = vstr[str_i[0] % NSTR]
                    str_i[0] += 1
                    Tp = T - d_
                    if g == 0:
                        P.dma(DQ, ks[:, 384:512], kscr[g][Tp].t[:, hp, 384:512], ks, reads=[kscr[g][Tp]], writes=[ks])
                        P.dma(DQ, vs[:, 3, :, :], vscr[g][Tp].t[:, 3, 2 * hp:2 * hp + 2, :], vs, reads=[vscr[g][Tp]], writes=[vs])
                    else:
                        P.dma(DQ, ks[:, :], kscr[g][Tp].t[:, hp, :], ks, reads=[kscr[g][Tp]], writes=[ks])
                        P.dma(DQ, vs[:, :, :, :], vscr[g][Tp].t[:, :, 2 * hp:2 * hp + 2, :], vs, reads=[vscr[g][Tp]], writes=[vs])

                    def sk(hp_, u, ks=ks):
                        return ks, ks[:, u * 128:(u + 1) * 128]

                    def sv(h, u, vs=vs):
                        return vs, vs[:, u, h % 2, 0:65]
                    if g == 0:
                        pass
                    elif g == 1:
                        srcs.append(dict(units=[0, 1, 2, 3], k=sk, v=sv, mask=0))
                    else:
                        srcs.append(dict(units=[0, 1, 2, 3], k=sk, v=sv, mask={4: 3, 3: 2, 2: 2, 1: 2}[d_]))
                if g == 0:
                    if T >= 1:
                        def ak(hp_, u, ks=ks):
                            if u == 0:
                                return ks, ks[:, 384:512]
                            return cur_k(hp_, u - 1)

                        def av(h, u, vs=vs):
                            if u == 0:
                                return vs, vs[:, 3, h % 2, 0:65]
                            return cur_v(h, u - 1)
                        srcs.append(dict(units=[0, 1, 2, 3], k=ak, v=av, mask=0))
                    else:
                        srcs.append(dict(units=[1, 2, 3], k=lambda hp_, u: cur_k(hp_, u - 1),
                                         v=lambda h, u: cur_v(h, u - 1), mask=0))
                    srcs.append(dict(units=[0, 1, 2, 3], k=cur_k, v=cur_v, mask=1))
                elif g == 1:
                    srcs.append(dict(units=[0, 1, 2, 3], k=cur_k, v=cur_v, mask=1))
                else:
                    srcs.append(dict(units=[0, 1, 2, 3], k=cur_k, v=cur_v, mask=4))

                for hh in range(2):
                    h = 2 * hp + hh
                    pr = slice(64 * hh, 64 * hh + 64)
                    pts = []
                    for src in srcs:
                        sb_ = palloc()
                        for u in src["units"]:
                            kb, kap = src["k"](hp, u)
                            MM(sb_[:, u * 128:(u + 1) * 128], kap[pr, :], qT[pr, hp, u * 128:(u + 1) * 128], True, True,
                               [kb, qT], [sb_])
                        u0 = src["units"][0]
                        pt = PTb[pt_i[0] % NPT]
                        pt_i[0] += 1
                        ACT(pt[:, u0 * 128:512], sb_[:, u0 * 128:512], AF.Exp, [sb_], [pt], scale=0.125)
                        pfree(sb_)
                        nu = 4 - u0
                        mk = masks_b[:, src["mask"], :].unsqueeze(1).to_broadcast([128, nu, 128])
                        ptv = pt[:, u0 * 128:512].rearrange("p (u w) -> p u w", u=nu)
                        TT_("dve" if (pt_i[0] % 2) else "pool", ptv, ptv, mk, ALU.mult, [pt, masks_b], [pt])
                        pts.append(pt)
                    ob = palloc()
                    for u in range(4):
                        contrib = [(src, pt) for src, pt in zip(srcs, pts) if u in src["units"]]
                        for ci, (src, pt) in enumerate(contrib):
                            vb, vap = src["v"](h, u)
                            MM(ob[0:65, u * 128:(u + 1) * 128], vap, pt[:, u * 128:(u + 1) * 128], ci == 0,
                               ci == len(contrib) - 1, [vb, pt], [ob])
                    if g == 0:
                        CP("act", acc[0:65, h, :], ob[0:65, :], [ob], [acc])
                    else:
                        av_ = acc_view(g, h, slice(0, 65))
                        TT_("dve", av_, av_, bank_view(g, ob, slice(0, 65)), ALU.add, [acc, ob], [acc])
                    pfree(ob)

        stage("attn")
        P.op("dve", lambda e: e.reciprocal(out=acc[64:65, :, :], in_=acc[64:65, :, :]), [acc], [acc])
        for h in range(8):
            bk = palloc()
            MM(bk[0:64, :], sel_f[:, :], acc[0:65, h, :], True, True, [sel_f, acc], [bk])
            TT_("dve", attnT[:, h, :], acc[0:64, h, :], bk[0:64, :], ALU.mult, [acc, bk], [attnT])
            pfree(bk)

        stage("merge")
        if T == 0:
            MSET("pool", stT[:], 0.0, [stT])
            MSET("pool", stz[:], 0.0, [stz])
            MSET("pool", convtail[:], 0.0, [convtail])
        MSET("pool", xdtz[:], 0.0, [xdtz])
        for blk in range(2):
            wz = wblock_in(O_Z + blk * 512, 512)
            for fc in range(4):
                bk = palloc()
                proj_fm(wz, fc * 128, xnT0, bk)
                ACT(sz[:, blk * 4 + fc, :], bk[:], AF.Silu, [bk], [sz])
                pfree(bk)
        for blk in range(3):
            wx = wblock_in(O_XBC + blk * 512, 512)
            for fc in range(4):
                c = blk * 4 + fc
                sg_ = stg[c % 2]
                ca = cacc[c % 2]
                bk = palloc()
                proj_fm(wx, fc * 128, xnT0, bk)
                CP("pool", sg_[:, 0:3], convtail[:, c, :], [convtail], [sg_])
                CP("act", sg_[:, 3:515], bk[:], [bk], [sg_])
                pfree(bk)
                CP("pool", convtail[:, c, :], sg_[:, 512:515], [sg_], [convtail])
                TS("dve", ca[:], sg_[:, 0:512], cw_t[:, c, 0:1], None, ALU.mult, None, [sg_, cw_t], [ca])
                for jj in range(1, 4):
                    STT("dve", ca[:], sg_[:, jj:jj + 512], cw_t[:, c, jj:jj + 1], ca[:], ALU.mult, ALU.add,
                        [sg_, cw_t, ca], [ca])
                ACT(xc[:, c, :], ca[:], AF.Silu, [ca, cb_t], [xc], bias=cb_t[:, c:c + 1])
        stage("conv")
        wdt = wblock_in(O_DT, 16)
        for j in range(4):
            jb = slice(j * 128, (j + 1) * 128)
            dt_ = sm("dt", 16)
            a_ = sm("a", 16)
            cs_sb = sm("cs", 16)
            lastcs = sm("lastcs", 16)
            dend = sm("dend", 16)
            cdec = sm("cdec", 16)
            dtd = sm("dtd", 16)
            bk = palloc()
            proj_tm(wdt, 16, xnT0, j, bk)
            TT_("dve", dt_[:], bk[:, 0:16], dtb_t[:], ALU.add, [bk, dtb_t], [dt_])
            pfree(bk)
            ACT(dt_[:], dt_[:], AF.Exp, [dt_], [dt_])
            ACT(dt_[:], dt_[:], AF.Ln, [dt_, one_t], [dt_], bias=one_t[:])
            TT_("dve", a_[:], dt_[:], A_t[:], ALU.mult, [dt_, A_t], [a_])
            bk = palloc()
            MM(bk[:, 0:16], U_f[:], a_[:], True, True, [U_f, a_], [bk])
            CP("dve", cs_sb[:], bk[:, 0:16], [bk], [cs_sb])
            pfree(bk)
            bk = palloc()
            for gg in range(2):
                MM(bk[:, gg * 128:(gg + 1) * 128], xc[:, 8 + gg, jb], xc[:, 10 + gg, jb], True, True, [xc], [bk])
            TT_("dve", Gm[:], bk.t[:, 0:256].rearrange("p (g t) -> p g t", g=2),
                masks_b[:, 1, :].unsqueeze(1).to_broadcast([128, 2, 128]), ALU.mult, [bk, masks_b], [Gm])
            pfree(bk)
            for qd in range(4):
                hs = slice(qd * 4, qd * 4 + 4)
                gg = qd // 2
                TT_("pool", Rb[:], a_[:, hs].unsqueeze(2).to_broadcast([128, 4, 128]),
                    U_f[:].unsqueeze(1).to_broadcast([128, 4, 128]), ALU.mult, [a_, U_f], [Rb])
                bk = palloc()
                MM(bk[:], ones_f[:], Rb[:].rearrange("p h t -> p (h t)"), True, True, [ones_f, Rb], [bk])
                bk3 = bk.t[:, :].rearrange("p (h t) -> p h t", h=4)
                TT_("dve", tmpb[:], bk3, cs_sb[:, hs].unsqueeze(2).to_broadcast([128, 4, 128]), ALU.subtract,
                    [bk, cs_sb], [tmpb])
                ACT(ecs[:], bk3, AF.Exp, [bk], [ecs])
                CP("act", lastcs[:, hs], bk3[:, :, 127], [bk], [lastcs])
                pfree(bk)
                ACT(Eb[:], tmpb[:], AF.Exp, [tmpb], [Eb])
                STT("dve", Wb[:, hs, :], Eb[:], 1e30, Gm[:, gg, :].unsqueeze(1).to_broadcast([128, 4, 128]),
                    ALU.min, ALU.mult, [Eb, Gm], [Wb])
                TT_("pool", Cdec[:, hs, :], ecs[:], xc[:, 10 + gg, jb].unsqueeze(1).to_broadcast([128, 4, 128]), ALU.mult,
                    [ecs, xc], [Cdec])
            TT_("dve", dend[:], lastcs[:], cs_sb[:], ALU.subtract, [lastcs, cs_sb], [dend])
            ACT(dend[:], dend[:], AF.Exp, [dend], [dend])
            ACT(cdec[:], lastcs[:], AF.Exp, [lastcs], [cdec])
            TT_("dve", dtd[:], dt_[:], dend[:], ALU.mult, [dt_, dend], [dtd])
            bk = palloc()
            bv = bk.t[:].bitcast(BF16)
            for c in range(8):
                TR(bv[:, c * 128:(c + 1) * 128], xc[:, c, jb], ident_b[:], [xc, ident_b], [bk])
            xv = bv.rearrange("p (c e d) -> p c e d", c=8, e=2)
            xz = xdtz[:].rearrange("p (c e) f -> p c e f", e=2)
            dtv = dt_[:].rearrange("p (c e) -> p c e", e=2)
            for e_ in range(2):
                TT_("dve", xz[:, :, e_, 64 * e_:64 * e_ + 64], xv[:, :, e_, :],
                    dtv[:, :, e_].unsqueeze(2).to_broadcast([128, 8, 64]), ALU.mult, [bk, dt_], [xdtz])
            TT_("dve", xdd[:], bv[:, 0:1024].rearrange("p (h d) -> p h d", h=16),
                dtd[:].unsqueeze(2).to_broadcast([128, 16, 64]), ALU.mult, [bk, dtd], [xdd])
            pfree(bk)
            bk = palloc()
            bv = bk.t[:].bitcast(BF16)
            for gg in range(2):
                TR(bv[:, gg * 128:(gg + 1) * 128], xc[:, 8 + gg, jb], ident_b[:], [xc, ident_b], [bk])
            CP("act", Btok[:].rearrange("p g n -> p (g n)"), bv[:, 0:256], [bk], [Btok])
            pfree(bk)
            for k2 in range(2):
                bk = palloc()
                for cc in range(4):
                    c = k2 * 4 + cc
                    reg = bk[:, cc * 128:(cc + 1) * 128]
                    MM(reg, xdtz[:, 2 * c, :], Wb[:, 2 * c, :], True, False, [xdtz, Wb], [bk])
                    MM(reg, xdtz[:, 2 * c + 1, :], Wb[:, 2 * c + 1, :], False, False, [xdtz, Wb], [bk])
                    MM(reg, stz[:, 2 * c, :], Cdec[:, 2 * c, :], False, False, [stz, Cdec], [bk])
                    MM(reg, stz[:, 2 * c + 1, :], Cdec[:, 2 * c + 1, :], False, True, [stz, Cdec], [bk])
                for cc in range(4):
                    c = k2 * 4 + cc
                    yt = ytmp[cc % 2]
                    STT("dve", yt[:], xc[:, c, jb], D_t[:, c:c + 1], bk[:, cc * 128:(cc + 1) * 128], ALU.mult, ALU.add,
                        [xc, D_t, bk], [yt])
                    TT_("pool", yT[:, c, jb], yt[:], sz[:, c, jb], ALU.mult, [yt, sz], [yT])
                pfree(bk)
            for gg in range(2):
                bk = palloc()
                MM(bk[:], Btok[:, gg, :], xdd[:, 8 * gg:8 * gg + 8, :].rearrange("p h d -> p (h d)"), True, True,
                   [Btok, xdd], [bk])
                sv_ = stT[:, 8 * gg:8 * gg + 8, :]
                TT_("dve", sv_, sv_, cdec[:, 8 * gg:8 * gg + 8].unsqueeze(2).to_broadcast([128, 8, 64]), ALU.mult,
                    [stT, cdec], [stT])
                TT_("dve", sv_, sv_, bk.t[:, :].rearrange("p (h d) -> p h d", h=8), ALU.add, [stT, bk], [stT])
                pfree(bk)
            sz_ = stz[:].rearrange("p (c e) f -> p c e f", e=2)
            st_ = stT[:].rearrange("p (c e) d -> p c e d", e=2)
            for e_ in range(2):
                CP("pool", sz_[:, :, e_, 64 * e_:64 * e_ + 64], st_[:, :, e_, :], [stT], [stz])
        stage("ssd")
        bk = palloc()
        for c in range(8):
            sq = sqb[c % 2]
            TT_("pool", sq[:], yT[:, c, :], yT[:, c, :], ALU.mult, [yT], [sq])
            MM(bk[:], ones_b[:], sq[:], c == 0, c == 7, [ones_b, sq], [bk])
        ACT(rstd_b[:], bk[:], AF.Ln, [bk, eps_t], [rstd_b], scale=1.0 / D, bias=eps_t[:])
        pfree(bk)
        ACT(rstd_b[:], rstd_b[:], AF.Exp, [rstd_b], [rstd_b], scale=-0.5)
        for c in range(8):
            TT_("dve" if c % 2 else "pool", yT[:, c, :], yT[:, c, :], rstd_b[:], ALU.mult, [yT, rstd_b], [yT])
        if T == NT - 1:
            for j_ in range(3):
                P.dma(DQ, p_conv[s, j_, :].rearrange("(c p) -> p c", p=128), convtail[:, :, j_], convtail,
                      reads=[convtail], allow_slow_non_contiguous=True)
            stf = stT[:].rearrange("p h d -> p (h d)")
            for half in range(2):
                bk = palloc()
                for cc in range(4):
                    c = half * 4 + cc
                    TR(bk[:, cc * 128:(cc + 1) * 128], stf[:, c * 128:(c + 1) * 128], ident_f[:], [stT, ident_f], [bk])
                st = kvst[0]
                kvst_i[0] += 1
                stv = st[:].rearrange("p a b -> p (a b)")[:, 0:512]
                CP("act", stv, bk[:], [bk], [st])
                pfree(bk)
                P.dma(DQ, p_ssm[s, half * 512:(half + 1) * 512, :].rearrange("(c q) n -> q c n", q=128),
                      stv.rearrange("p (c n) -> p c n", c=4), st, reads=[st])

        stage("ssdout")
        out_epilogue_phase(lambda half: (wblock_outA(half), wblock_outS(half)), "mix", nw_mix)

        stage("E")
        for j in range(4):
            norm_transpose(hb[j], j, xnTp)
        stage("F")
        for blk in range(6):
            ncol = 512 if blk < 5 else 256
            wg = wblock_fi(blk * 512, ncol)
            wu = wblock_fi(FFN_H + blk * 512, ncol)
            for fc in range(ncol // 128):
                c = blk * 4 + fc
                gb = palloc()
                proj_fm(wg, fc * 128, xnTp, gb)
                ub = palloc()
                proj_fm(wu, fc * 128, xnTp, ub)
                sg_ = sgt[c % 2]
                ACT(sg_[:], gb[:], AF.Silu, [gb], [sg_])
                pfree(gb)
                TT_("dve", actT[:, c, :], sg_[:], ub[:], ALU.mult, [sg_, ub], [actT])
                pfree(ub)
        stage("G")
        out_epilogue_phase(None, "ffn", nw_ffn)
        for j in range(4):
            P.dma(DQ, y_prompt[s, t0 + j * 128:t0 + (j + 1) * 128, :], hb[j][:], hb[j], reads=[hb[j]])

    def out_epilogue_phase(wfn, kind, nw):
        ssA = [sm("ssA%d" % j) for j in range(4)]
        ssB = [sm("ssB%d" % j) for j in range(4)]
        for half in range(2):
            bks = [palloc() for _ in range(4)]
            if kind == "mix":
                wA, wS = wfn(half)
                for j in range(4):
                    jb = slice(j * 128, (j + 1) * 128)
                    for h in range(8):
                        MM(bks[j][:], attnT[0:64, h, jb], wA[0:64, h, :], h == 0, False, [attnT, wA], [bks[j]])
                    for c in range(8):
                        MM(bks[j][:], yT[:, c, jb], wS[:, c, :], False, c == 7, [yT, wS], [bks[j]])
            else:
                for (c0, n) in ((0, 8), (8, 8), (16, 6)):
                    w = wblock_fo(c0, n, half)
                    for j in range(4):
                        jb = slice(j * 128, (j + 1) * 128)
                        for ci in range(n):
                            c = c0 + ci
                            MM(bks[j][:], actT[:, c, jb], w[:, ci, :], c == 0, c == NFC - 1, [actT, w], [bks[j]])
            for j in range(4):
                bk = bks[j]
                if half == 0:
                    CP("act", m0[j][:], bk[:], [bk], [m0[j]])
                    ACT(junk[:, 0:512], bk[:], AF.Square, [bk], [junk, ssA[j]], accum_out=ssA[j][:])
                    pfree(bk)
                else:
                    ACT(junk[:, 0:512], bk[:], AF.Square, [bk], [junk, ssB[j]], accum_out=ssB[j][:])
                    tot = sm("tot%d" % (j % 2))
                    rs = sm("ers%d" % (j % 2))
                    TT_("dve", tot[:], ssA[j][:], ssB[j][:], ALU.add, [ssA[j], ssB[j]], [tot])
                    rstd_from(tot[:], tot, rs)
                    t1 = rt1[j % 2]
                    t2 = rt2[j % 2]
                    STT("dve", t1[:], m0[j][:], rs[:], nw[:, 0:512], ALU.mult, ALU.mult, [m0[j], rs, nw], [t1])
                    STT("dve", t2[:], bk[:], rs[:], nw[:, 512:1024], ALU.mult, ALU.mult, [bk, rs, nw], [t2])
                    pfree(bk)
                    TT_("pool", hb[j][:, 0:512], hb[j][:, 0:512], t1[:], ALU.add, [hb[j], t1], [hb[j]])
                    TT_("pool", hb[j][:, 512:1024], hb[j][:, 512:1024], t2[:], ALU.add, [hb[j], t2], [hb[j]])

    def sample_phase():
        NS = ns
        R = slice(0, NS)
        qscr = P.dram("qscr", [NS, 3, 512], F32)
        dmy = [Buf(None, "dmy%d" % i) for i in range(4)]

        def sub(parent, ap, name):
            v = Buf(ap, name)
            Prog.alias(parent, [v])
            return v

        def f32view(parent, ncols, name, rows=NS):
            t = parent.t
            ap = t[0:rows] if len(t.shape) == 2 else t[0:rows].rearrange("p a b -> p (a b)")
            if ap.dtype != F32:
                ap = ap.bitcast(F32)
            return sub(parent, ap[:, 0:ncols], name)

        for g in range(3):
            wb = WB[g]
            assert wb == 128 * DILS[g], "sample path assumes a full window in the cache"
            step = 256
            for b in range(NS):
                for r0 in range(0, wb - 1, step):
                    n = min(step, wb - 1 - r0)
                    P.dma("act", s_kv[g][b, r0:r0 + n, :, :], cache[g][b, r0 + 1:r0 + 1 + n, :, :], dmy[g])
        for b0 in range(0, NS, 4):
            P.dma("act", s_conv[b0:b0 + 4, 0:2, :], state_conv[b0:b0 + 4, 1:3, :], dmy[3])

        arA.phase()
        proj_s = arA.take("proj_s", [NS, MIX_IN], F32)
        srope = arA.take("srope", [NS, 2, 512], F32)
        kv_t = [arA.take("kv_t0", [128, 2, 512], F32)]
        prod = arA.take("prod", [128, 512], F32)
        arE.phase()
        kv_t.append(arE.take("kv_t1", [128, 2, 512], F32))
        pvx = [arE.take("pvx%d" % i, [128, 520], F32) for i in range(2)]
        cwrow = [arE.take("cwrow%d" % i, [NS, 1536], F32) for i in range(2)]
        arE.phase()
        ffn_s = arE.take("ffn_s", [NS, 2 * FFN_H], F32)
        arB.phase()
        cacc_s = arB.take("cacc_s", [NS, 1536], F32)
        arC.phase()
        nn = arC.take("nn", [NS, 512], F32)
        attn_f = arC.take("attn_f", [NS, 512], F32)
        arC.phase()
        Bd = arC.take("Bd", [NS, 16, 128], F32)
        arD.phase()
        qb_t = [arD.take("qb_t%d" % i, [128, 512], F32) for i in range(2)]
        tq = arD.take("tq", [NS, 512], F32)
        rq = arD.take("rq", [NS, 512], F32)
        arD.phase()
        Cd = arD.take("Cd", [NS, 16, 128], F32)
        sc_rows = [f32view(xnT0, 1536, "sc0"), f32view(xnTp, 1536, "sc1"), f32view(yT, 1536, "sc2")]
        xs_t = sub(hb[0], hb[0].t[R, :], "xs_t")
        h_s = sub(hb[1], hb[1].t[R, :], "h_s")
        dtx = sub(hb[2], hb[2].t[R, :], "dtx")
        dAx = sub(hb[3], hb[3].t[R, :], "dAx")
        st_s = [sub(xin[i], xin[i].t[:, 0:512].rearrange("p (b n) -> p b n", b=4), "st_s%d" % i) for i in range(2)]
        t1s = f32view(junk, 512, "t1s", rows=128)
        t2s = f32view(xnb[0], 512, "t2s", rows=128)
        hnew = [f32view(xnb[1], 512, "hnew0", rows=128)]
        arS = Arena(P, "arS", 4096)
        hnew.append(arS.take("hnew1", [128, 512], F32))
        t3s = arS.take("t3s", [128, 512], F32)
        xsT = P.sbuf("xsT", [128, 8, NS], BF16)
        catT = P.sbuf("catT", [128, 12, NS], BF16)
        actsT = P.sbuf("actsT", [128, NFC, NS], BF16)
        dtxT = P.sbuf("dtxT", [128, 8, NS], F32)
        dAT = P.sbuf("dAT", [128, 8, NS], F32)
        ySST = P.sbuf("ySST", [128, 8, NS], F32)
        esel_t = P.sbuf("esel_t", [128, 16, 16], F32)
        s8 = P.sbuf("s8", [128, 8], F32)
        p8 = P.sbuf("p8", [128, 8], F32)
        snew = P.sbuf("snew", [NS, 8], F32)
        pnew = P.sbuf("pnew", [NS, 8], F32)
        dnew = P.sbuf("dnew", [NS, 8], F32)
        dts = P.sbuf("dts", [NS, 16], F32)
        dAs = P.sbuf("dAs", [NS, 16], F32)
        drow = P.sbuf("drow", [NS, 16], F32)
        ss16 = P.sbuf("ss16", [NS, 1], F32)
        rs16 = P.sbuf("rs16", [NS, 1], F32)
        catb0 = sub(qraw[0], qraw[0].t[R, :], "catb0")
        catb1 = sub(xnb[1], xnb[1].t[R, :], "catb1")
        nb16 = sub(xnb[0], xnb[0].t[R, :], "nb16")
        Prog.alias(catb1, [hnew[0]])
        Prog.alias(nb16, [t2s])

        P.dma(DQ, esel_t[:].rearrange("p a b -> p (a b)"), c_esel[:, :], esel_t, writes=[esel_t])
        P.dma(DQ, srope[:], c_srope[:, :].rearrange("c (o n) -> o c n", o=1).partition_broadcast(NS), srope, writes=[srope])
        P.dma(DQ, drow[:], d_skip[0:1, :].partition_broadcast(NS), drow, writes=[drow])

        def rms16(src_ap, src_buf, dst_ap, dst_buf, n=D):
            ACT(junk[R, 0:n], src_ap, AF.Square, [src_buf], [junk, ss16], accum_out=ss16[:])
            ACT(rs16[:], ss16[:], AF.Ln, [ss16, eps_t], [rs16], scale=1.0 / n, bias=eps_t[R, :])
            ACT(rs16[:], rs16[:], AF.Exp, [rs16], [rs16], scale=-0.5)
            TS("dve", dst_ap, src_ap, rs16[:], None, ALU.mult, None, [src_buf, rs16], [dst_buf])

        def transpose16(src_ap, src_buf, nchunk, dstT):
            bk = palloc()
            bv = bk.t[:].bitcast(BF16)
            for c in range(nchunk):
                TR(bv[:, c * NS:(c + 1) * NS], src_ap[:, c * 128:(c + 1) * 128], ident_b[R, R], [src_buf, ident_b], [bk])
            CP("act", dstT[:, 0:nchunk, :], bv[:, 0:nchunk * NS].rearrange("p (c t) -> p c t", c=nchunk), [bk], [dstT])
            pfree(bk)

        P.dma(DQ, xs_t[:], x_sample[:, :], xs_t, writes=[xs_t])
        rms16(xs_t[:], xs_t, nb16[:], nb16)
        transpose16(nb16, nb16, 8, xsT)
        for c0 in range(0, MIX_IN, 512):
            ncol = min(512, MIX_IN - c0)
            w = wblock_in(c0, ncol)
            bk = palloc()
            for kc in range(8):
                MM(bk[R, 0:ncol], xsT[:, kc, :], w[:, kc, 0:ncol], kc == 0, kc == 7, [xsT, w], [bk])
            CP("act", proj_s[:, c0:c0 + ncol], bk[R, 0:ncol], [bk], [proj_s])
            pfree(bk)
        for g in range(3):
            for which in range(2):
                c0 = g * 1536 + which * 512
                qv = proj_s[:, c0:c0 + 512]
                q3 = qv.rearrange("p (h e d) -> p h e d", h=8, e=2)
                r3 = rq[:].rearrange("p (h e d) -> p h e d", h=8, e=2)
                CP("pool", r3[:, :, 0, :], q3[:, :, 1, :], [proj_s], [rq])
                CP("pool", r3[:, :, 1, :], q3[:, :, 0, :], [proj_s], [rq])
                TT_("dve", tq[:], qv, srope[:, 0, :], ALU.mult, [proj_s, srope], [tq])
                TT_("dve", rq[:], rq[:], srope[:, 1, :], ALU.mult, [rq, srope], [rq])
                TT_("dve", qv, tq[:], rq[:], ALU.add, [tq, rq], [proj_s])
            P.dma(DQ, qscr.t[:, g, :], proj_s[:, g * 1536:g * 1536 + 512], proj_s, reads=[proj_s], writes=[qscr])
            P.dma(DQ, s_kv[g][:, WB[g] - 1, 0, :], proj_s[:, g * 1536 + 512:g * 1536 + 1024], proj_s, reads=[proj_s])
            P.dma(DQ, s_kv[g][:, WB[g] - 1, 1, :], proj_s[:, g * 1536 + 1024:g * 1536 + 1536], proj_s, reads=[proj_s])
        nbank = palloc()
        dbank = palloc()
        it = 0
        total = 3 * NS
        for g in range(3):
            dil = DILS[g]
            for b in range(NS):
                kv = kv_t[it % 2]
                qb = qb_t[it % 2]
                px = pvx[it % 2]
                P.dma(DQ, kv[:], cache[g][b, :, :, :].rearrange("(i d) c f -> i d c f", d=dil)[:, 0, :, :], kv, writes=[kv])
                P.dma(DQ, qb[:], qscr.t[b:b + 1, g, :].partition_broadcast(128), qb, reads=[qscr], writes=[qb])
                TT_("dve", prod[:], kv[:, 0, :], qb[:], ALU.mult, [kv, qb], [prod])
                P.op("dve", lambda e: e.tensor_reduce(out=s8[:], in_=prod[:].rearrange("p (h d) -> p h d", h=8),
                                                      axis=mybir.AxisListType.X, op=ALU.add), [prod], [s8])
                ACT(px[:, 512:520], s8[:], AF.Exp, [s8], [px], scale=0.125)
                TT_("dve", px[:, 0:512].rearrange("p (h d) -> p h d", h=8), kv[:, 1, :].rearrange("p (h d) -> p h d", h=8),
                    px[:, 512:520].unsqueeze(2).to_broadcast([128, 8, 64]), ALU.mult, [kv, px], [px])
                MM(nbank[R, :], esel_t[:, b, :], px[:, 0:512], it == 0, it == total - 1, [esel_t, px], [nbank])
                MM(dbank[R, 0:8], esel_t[:, b, :], px[:, 512:520], it == 0, it == total - 1, [esel_t, px], [dbank])
                it += 1
        for g in range(3):
            c0 = g * 1536
            TT_("dve", tq[:], proj_s[:, c0:c0 + 512], proj_s[:, c0 + 512:c0 + 1024], ALU.mult, [proj_s], [tq])
            P.op("dve", lambda e: e.tensor_reduce(out=snew[:], in_=tq[:].rearrange("p (h d) -> p h d", h=8),
                                                  axis=mybir.AxisListType.X, op=ALU.add), [tq], [snew])
            ACT(pnew[:], snew[:], AF.Exp, [snew], [pnew], scale=0.125)
            tgt = nn if g == 0 else tq
            TT_("dve", tgt[:].rearrange("p (h d) -> p h d", h=8),
                proj_s[:, c0 + 1024:c0 + 1536].rearrange("p (h d) -> p h d", h=8),
                pnew[:].unsqueeze(2).to_broadcast([NS, 8, 64]), ALU.mult, [proj_s, pnew], [tgt])
            if g == 0:
                CP("dve", dnew[:], pnew[:], [pnew], [dnew])
            else:
                TT_("dve", nn[:], nn[:], tq[:], ALU.add, [nn, tq], [nn])
                TT_("dve", dnew[:], dnew[:], pnew[:], ALU.add, [dnew, pnew], [dnew])
        TT_("dve", nn[:], nn[:], nbank[R, :], ALU.add, [nn, nbank], [nn])
        TT_("dve", dnew[:], dnew[:], dbank[R, 0:8], ALU.add, [dnew, dbank], [dnew])
        pfree(nbank)
        pfree(dbank)
        P.op("dve", lambda e: e.reciprocal(out=dnew[:], in_=dnew[:]), [dnew], [dnew])
        TT_("dve", attn_f[:].rearrange("p (h d) -> p h d", h=8), nn[:].rearrange("p (h d) -> p h d", h=8),
            dnew[:].unsqueeze(2).to_broadcast([NS, 8, 64]), ALU.mult, [nn, dnew], [attn_f])
        CP("dve", catb0[:], attn_f[:], [attn_f], [catb0])

        for j_ in range(3):
            P.dma(DQ, sc_rows[j_][:], state_conv[:, j_, :], sc_rows[j_], writes=[sc_rows[j_]])
        xbc = proj_s[:, O_XBC:O_XBC + 1536]
        P.dma(DQ, s_conv[:, 2, :], xbc, proj_s, reads=[proj_s])
        for j_ in range(4):
            cw = cwrow[j_ % 2]
            P.dma(DQ, cw[:], conv_w[j_:j_ + 1, :].partition_broadcast(NS), cw, writes=[cw])
            src_ap, src_b = (sc_rows[j_][:], sc_rows[j_]) if j_ < 3 else (xbc, proj_s)
            if j_ == 0:
                TT_("dve", cacc_s[:], src_ap, cw[:], ALU.mult, [src_b, cw], [cacc_s])
            else:
                TT_("dve", cw[:], src_ap, cw[:], ALU.mult, [src_b, cw], [cw])
                TT_("dve", cacc_s[:], cacc_s[:], cw[:], ALU.add, [cacc_s, cw], [cacc_s])
        cw = cwrow[0]
        P.dma(DQ, cw[:], conv_b[0:1, :].partition_broadcast(NS), cw, writes=[cw])
        TT_("dve", cacc_s[:], cacc_s[:], cw[:], ALU.add, [cacc_s, cw], [cacc_s])
        ACT(cacc_s[:], cacc_s[:], AF.Silu, [cacc_s], [cacc_s])
        TT_("dve", dts[:], proj_s[:, O_DT:O_DT + 16], dtb_t[R, :], ALU.add, [proj_s, dtb_t], [dts])
        ACT(dts[:], dts[:], AF.Exp, [dts], [dts])
        ACT(dts[:], dts[:], AF.Ln, [dts, one_t], [dts], bias=one_t[R, :])
        TT_("dve", dAs[:], dts[:], A_t[R, :], ALU.mult, [dts, A_t], [dAs])
        ACT(dAs[:], dAs[:], AF.Exp, [dAs], [dAs])
        xs3 = cacc_s[:, 0:1024].rearrange("p (h d) -> p h d", h=16)
        TT_("dve", dtx[:].rearrange("p (h d) -> p h d", h=16), xs3, dts[:].unsqueeze(2).to_broadcast([NS, 16, 64]),
            ALU.mult, [cacc_s, dts], [dtx])
        CP("dve", dAx[:].rearrange("p (h d) -> p h d", h=16), dAs[:].unsqueeze(2).to_broadcast([NS, 16, 64]), [dAs], [dAx])
        for (srcb, dstT) in ((dtx, dtxT), (dAx, dAT)):
            bk = palloc()
            for c in range(8):
                TR(bk[:, c * NS:(c + 1) * NS], srcb[:, c * 128:(c + 1) * 128], ident_f[R, R], [srcb, ident_f], [bk])
            CP("act", dstT[:], bk[:, 0:8 * NS].rearrange("p (c t) -> p c t", c=8), [bk], [dstT])
            pfree(bk)
        idb = ident_f[R, 0:NS].unsqueeze(2).to_broadcast([NS, NS, 128])
        it = 0
        for gg in range(2):
            TT_("dve", Bd[:], cacc_s[:, 1024 + gg * 128:1024 + (gg + 1) * 128].unsqueeze(1).to_broadcast([NS, NS, 128]),
                idb, ALU.mult, [cacc_s, ident_f], [Bd])
            TT_("dve", Cd[:], cacc_s[:, 1280 + gg * 128:1280 + (gg + 1) * 128].unsqueeze(1).to_broadcast([NS, NS, 128]),
                idb, ALU.mult, [cacc_s, ident_f], [Cd])
            for qb_ in range(NS // 4):
                b0 = qb_ * 4
                Bbc = palloc()
                Cbc = palloc()
                MM(Bbc[:], ones_f[R, :], Bd[:, b0:b0 + 4, :].rearrange("p b n -> p (b n)"), True, True, [ones_f, Bd], [Bbc])
                MM(Cbc[:], ones_f[R, :], Cd[:, b0:b0 + 4, :].rearrange("p b n -> p (b n)"), True, True, [ones_f, Cd], [Cbc])
                for cc in range(4):
                    c = gg * 4 + cc
                    st = st_s[it % 2]
                    hn = hnew[it % 2]
                    it += 1
                    P.dma(DQ, st[:], state_ssm[b0:b0 + 4, c * 128:(c + 1) * 128, :].rearrange("b q n -> q b n"), st, writes=[st])
                    TT_("dve", t1s[:].rearrange("p (b n) -> p b n", b=4), st[:],
                        dAT[:, c, b0:b0 + 4].unsqueeze(2).to_broadcast([128, 4, 128]), ALU.mult, [st, dAT], [t1s])
                    TT_("dve", t2s[:].rearrange("p (b n) -> p b n", b=4), Bbc.t[:, :].rearrange("p (b n) -> p b n", b=4),
                        dtxT[:, c, b0:b0 + 4].unsqueeze(2).to_broadcast([128, 4, 128]), ALU.mult, [Bbc, dtxT], [t2s])
                    TT_("pool", hn[:], t1s[:], t2s[:], ALU.add, [t1s, t2s], [hn])
                    P.dma(DQ, s_ssm[b0:b0 + 4, c * 128:(c + 1) * 128, :].rearrange("b q n -> q b n"),
                          hn[:].rearrange("p (b n) -> p b n", b=4), hn, reads=[hn])
                    TT_("dve", t3s[:], hn[:], Cbc[:], ALU.mult, [hn, Cbc], [t3s])
                    P.op("dve", lambda e, c=c, b0=b0: e.tensor_reduce(
                        out=ySST[:, c, b0:b0 + 4], in_=t3s[:].rearrange("p (b n) -> p b n", b=4),
                        axis=mybir.AxisListType.X, op=ALU.add), [t3s], [ySST])
                pfree(Bbc)
                pfree(Cbc)
        ytok = dtx
        for half in range(2):
            bk = palloc()
            for cc in range(4):
                c = half * 4 + cc
                TR(bk[R, cc * 128:(cc + 1) * 128], ySST[:, c, :], ident_f[:, :], [ySST, ident_f], [bk])
            CP("act", ytok[:, half * 512:(half + 1) * 512], bk[R, :], [bk], [ytok])
            pfree(bk)
        TT_("dve", dAx[:].rearrange("p (h d) -> p h d", h=16), xs3, drow[:].unsqueeze(2).to_broadcast([NS, 16, 64]),
            ALU.mult, [cacc_s, drow], [dAx])
        TT_("dve", ytok[:], ytok[:], dAx[:], ALU.add, [ytok, dAx], [ytok])
        ACT(dAx[:], proj_s[:, O_Z:O_Z + 1024], AF.Silu, [proj_s], [dAx])
        TT_("dve", ytok[:], ytok[:], dAx[:], ALU.mult, [ytok, dAx], [ytok])
        rms16(ytok[:], ytok, catb1[:], catb1)
        transpose16(catb0, catb0, 4, catT)
        bk = palloc()
        bv = bk.t[:].bitcast(BF16)
        for c in range(8):
            TR(bv[:, c * NS:(c + 1) * NS], catb1[:, c * 128:(c + 1) * 128], ident_b[R, R], [catb1, ident_b], [bk])
        CP("act", catT[:, 4:12, :], bv[:, 0:8 * NS].rearrange("p (c t) -> p c t", c=8), [bk], [catT])
        pfree(bk)

        def post(bks, nw, res_in, res_out):
            ACT(junk[R, 0:512], bks[0][R, :], AF.Square, [bks[0]], [junk, ss16], accum_out=ss16[:])
            ACT(junk[R, 512:1024], bks[1][R, :], AF.Square, [bks[1]], [junk, rs16], accum_out=rs16[:])
            TT_("dve", ss16[:], ss16[:], rs16[:], ALU.add, [ss16, rs16], [ss16])
            ACT(rs16[:], ss16[:], AF.Ln, [ss16, eps_t], [rs16], scale=1.0 / D, bias=eps_t[R, :])
            ACT(rs16[:], rs16[:], AF.Exp, [rs16], [rs16], scale=-0.5)
            for half in range(2):
                hs = slice(half * 512, (half + 1) * 512)
                STT("dve", tq[:], bks[half][R, :], rs16[:], nw[R, hs], ALU.mult, ALU.mult, [bks[half], rs16, nw], [tq])
                TT_("dve", res_out[:, hs], res_in[:, hs], tq[:], ALU.add, [res_in, tq], [res_out])
                pfree(bks[half])

        bks = [palloc(), palloc()]
        for half in range(2):
            wA = wnext()
            P.dma(WQ, wA.t[:, 0:4, :], wout_b.t[0:512, half * 512:half * 512 + 512].rearrange("(c p) n -> p c n", p=128), wA,
                  reads=[wout_b], writes=[wA])
            wS = wblock_outS(half)
            for c in range(12):
                w_ap = wA[:, c, :] if c < 4 else wS[:, c - 4, :]
                MM(bks[half][R, :], catT[:, c, :], w_ap, c == 0, c == 11, [catT, wA, wS], [bks[half]])
        post(bks, nw_mix, xs_t, h_s)
        rms16(h_s[:], h_s, nb16[:], nb16)
        transpose16(nb16, nb16, 8, xsT)
        for c0 in range(0, 2 * FFN_H, 512):
            w = wblock_fi(c0, 512)
            bk = palloc()
            for kc in range(8):
                MM(bk[R, :], xsT[:, kc, :], w[:, kc, :], kc == 0, kc == 7, [xsT, w], [bk])
            CP("act", ffn_s[:, c0:c0 + 512], bk[R, :], [bk], [ffn_s])
            pfree(bk)
        ACT(ffn_s[:, 0:FFN_H], ffn_s[:, 0:FFN_H], AF.Silu, [ffn_s], [ffn_s])
        TT_("dve", ffn_s[:, 0:FFN_H], ffn_s[:, 0:FFN_H], ffn_s[:, FFN_H:2 * FFN_H], ALU.mult, [ffn_s], [ffn_s])
        actb = sub(hb[2], hb[2].t[R, :].bitcast(BF16), "actb")
        Prog.alias(actb, [dtx])
        actb2 = sub(hb[3], hb[3].t[R, :].bitcast(BF16), "actb2")
        Prog.alias(actb2, [dAx])
        CP("dve", actb[:, 0:2048], ffn_s[:, 0:2048], [ffn_s], [actb])
        CP("dve", actb2[:, 0:768], ffn_s[:, 2048:FFN_H], [ffn_s], [actb2])
        transpose16(actb, actb, 16, actsT)
        bk = palloc()
        bv = bk.t[:].bitcast(BF16)
        for c in range(6):
            TR(bv[:, c * NS:(c + 1) * NS], actb2[:, c * 128:(c + 1) * 128], ident_b[R, R], [actb2, ident_b], [bk])
        CP("act", actsT[:, 16:22, :], bv[:, 0:6 * NS].rearrange("p (c t) -> p c t", c=6), [bk], [actsT])
        pfree(bk)
        bks = [palloc(), palloc()]
        for half in range(2):
            for (c0, n) in ((0, 8), (8, 8), (16, 6)):
                w = wblock_fo(c0, n, half)
                for ci in range(n):
                    c = c0 + ci
                    MM(bks[half][R, :], actsT[:, c, :], w[:, ci, :], c == 0, c == NFC - 1, [actsT, w], [bks[half]])
        post(bks, nw_ffn, h_s, xs_t)
        P.dma(DQ, y_sample[:, :], xs_t[:], xs_t, reads=[xs_t])

    if do_sample:
        sample_phase()

    for v_ in vcur:
        MSET("pool", v_[:, :, :, 64:65], 1.0, [v_])

    for s in range(nseq):
        for T in range(NT):
            prompt_tile(s, T)

    P.finalize()
    P.close()
    return nc


_CACHE = {}
OUT_NAMES = ["y_prompt", "y_sample", "p_kv0", "p_kv1", "p_kv2", "p_conv", "p_ssm",
             "s_kv0", "s_kv1", "s_kv2", "s_conv", "s_ssm"]


def make_in_maps(inp, ncores, nseq, ns, seq, past_override=None):
    f = lambda a: np.ascontiguousarray(np.asarray(a, dtype=np.float32))
    consts = host_consts(seq)
    shared = {
        "norm_mix_pre": f(inp["norm_mix_pre"]), "norm_mix_post": f(inp["norm_mix_post"]),
        "norm_ffn_pre": f(inp["norm_ffn_pre"]), "norm_ffn_post": f(inp["norm_ffn_post"]),
        "w_in": f(inp["w_in"][0]), "w_out": f(inp["w_out"][0]),
        "conv_w": f(inp["conv_w"][0]), "conv_b": f(inp["conv_b"]),
        "dt_bias": f(inp["dt_bias"]), "a_log": f(inp["a_log"]), "d_skip": f(inp["d_skip"]),
        "ssd_norm_w": f(inp["ssd_norm_w"]),
        "w_ffn_in": f(inp["w_ffn_in"][0]), "w_ffn_out": f(inp["w_ffn_out"][0]),
    }
    shared.update(consts)
    xs = np.asarray(inp["x_sample"], dtype=np.float32)
    caches = [np.asarray(inp[k], dtype=np.float32) for k in ("cache_kv_w128", "cache_kv_w512", "cache_kv_w2048")]
    maps = []
    for c in range(ncores):
        m = dict(shared)
        m["x_prompt"] = f(inp["x_prompt"][c * nseq:(c + 1) * nseq])
        m["x_sample"] = f(xs[c * ns:(c + 1) * ns, 0, :])
        for g in range(3):
            cg = caches[g][0, c * ns:(c + 1) * ns]
            if past_override is not None:
                cg = cg[:, :min(WINS[g], past_override)]
            m["cache%d" % g] = f(cg.reshape(ns, cg.shape[1], 2, 512))
        m["state_conv"] = f(np.asarray(inp["state_conv"])[0, c * ns:(c + 1) * ns])
        m["state_ssm"] = f(np.asarray(inp["state_ssm"])[0, c * ns:(c + 1) * ns].reshape(ns, 1024, 128))
        maps.append(m)
    return maps


def assemble(results, ncores, nseq, ns, seq, past=PAST):
    cat = lambda k: np.concatenate([np.asarray(r[k]) for r in results], axis=0)
    pw = [min(w, seq) for w in WINS]
    wb = [min(w, past) for w in WINS]
    B = ncores * nseq
    S = ncores * ns
    outs = [
        cat("y_prompt"),
        cat("y_sample").reshape(S, 1, D),
        cat("p_kv0").reshape(1, B, pw[0], 2, NH, HD),
        cat("p_kv1").reshape(1, B, pw[1], 2, NH, HD),
        cat("p_kv2").reshape(1, B, pw[2], 2, NH, HD),
        cat("p_conv").reshape(1, B, 3, 1536),
        cat("p_ssm").reshape(1, B, 16, 64, 128),
        cat("s_kv0").reshape(1, S, wb[0], 2, NH, HD),
        cat("s_kv1").reshape(1, S, wb[1], 2, NH, HD),
        cat("s_kv2").reshape(1, S, wb[2], 2, NH, HD),
        cat("s_conv").reshape(1, S, 3, 1536),
        cat("s_ssm").reshape(1, S, 16, 64, 128),
    ]
    return tuple(np.ascontiguousarray(o, dtype=np.float32) for o in outs)


def kernel(**inp):
    ncores = 8
    B, seq = inp["x_prompt"].shape[0], inp["x_prompt"].shape[1]
    S = inp["x_sample"].shape[0]
    nseq, ns = B // ncores, S // ncores
    key = (nseq, seq, ns)
    if key not in _CACHE:
        _CACHE[key] = build(nseq, seq, ns)
    nc = _CACHE[key]
    maps = make_in_maps(inp, ncores, nseq, ns, seq)
    res = run_bass_kernel_spmd(nc, maps, core_ids=list(range(ncores)))
    return assemble(res.results, ncores, nseq, ns, seq)
```

```python
import math
import numpy as np
import concourse.bass as bass
import concourse.mybir as mybir
from concourse.bass_utils import run_bass_kernel_spmd

F32 = mybir.dt.float32
BF16 = mybir.dt.bfloat16
ALU = mybir.AluOpType
AF = mybir.ActivationFunctionType

ENGS = ("pe", "act", "dve", "pool", "sp")


class Buf:
    __slots__ = ("t", "name", "last_w", "readers", "dsem", "dcount", "aliases", "excl")

    def __init__(self, t, name):
        self.excl = False
        self.t = t
        self.name = name
        self.last_w = None
        self.readers = []
        self.dsem = None
        self.dcount = 0
        self.aliases = []

    def __getitem__(self, k):
        return self.t[k]


class Op:
    __slots__ = ("eng", "emit", "deps", "is_dma", "buf", "dval", "sig", "idx", "cost", "tbl", "nbytes", "seq",
                 "done", "placed")

    def __init__(self, eng, emit, is_dma=False):
        self.eng = eng
        self.emit = emit
        self.deps = []
        self.is_dma = is_dma
        self.buf = None
        self.dval = 0
        self.sig = False
        self.idx = 0
        self.cost = 100.0
        self.tbl = None
        self.nbytes = 0
        self.seq = 0
        self.done = 0.0
        self.placed = False


class Prog:
    def __init__(self, nc):
        self.nc = nc
        self.ops = []
        self._stack = []
        self.nsb = 0
        self.frozen = False

    def sbuf(self, name, shape, dt):
        g = self.nc.sbuf_tensor(name, list(shape), dt)
        t = g.__enter__()
        self._stack.append(g)
        return Buf(t, name)

    def psum(self, name, shape, dt=F32):
        g = self.nc.psum_tensor(name, list(shape), dt)
        t = g.__enter__()
        self._stack.append(g)
        b = Buf(t, name)
        b.excl = True
        return b

    def view(self, ap, name, parent=None):
        b = Buf(ap, name)
        return b

    def dram(self, name, shape, dt):
        t = self.nc.dram_tensor(name, list(shape), dt, kind="Internal")
        return Buf(t, name)

    @staticmethod
    def alias(a, others):
        for o in others:
            a.aliases.append(o)
            o.aliases.append(a)

    def _add(self, op, reads, writes):
        if self.frozen:
            return op
        deps = []
        for b in reads:
            if b.last_w is not None:
                deps.append(b.last_w)
            if b.excl:
                deps.extend(r for r in b.readers if r.eng != op.eng)
        for b in writes:
            if b.last_w is not None:
                deps.append(b.last_w)
            deps.extend(b.readers)
            for a in b.aliases:
                if a.last_w is not None:
                    deps.append(a.last_w)
                deps.extend(a.readers)
        seen = set()
        for d in deps:
            if d is op or id(d) in seen:
                continue
            seen.add(id(d))
            op.deps.append(d)
        for b in reads:
            b.readers.append(op)
        for b in writes:
            b.last_w = op
            b.readers = []
        op.seq = len(self.ops)
        self.ops.append(op)
        return op

    def op(self, eng, emit, reads=(), writes=(), cost=100.0, tbl=None):
        o = Op(eng, emit)
        o.cost = cost
        o.tbl = tbl
        return self._add(o, reads, writes)

    def dma(self, eng, out_ap, in_ap, carrier, reads=(), writes=(), **kw):
        def emit(e):
            return e.dma_start(out=out_ap, in_=in_ap, **kw)
        op = Op(eng, emit, is_dma=True)
        op.buf = carrier
        n = 1
        for d in out_ap.shape:
            n *= d
        op.nbytes = n * (2 if out_ap.dtype == BF16 else 4)
        op.cost = 60.0
        return self._add(op, reads, writes)

    def schedule(self, window):
        by_eng = {e: [] for e in ENGS}
        for i, op in enumerate(self.ops):
            op.placed = False
            by_eng[op.eng].append(op)
        head = {e: 0 for e in ENGS}
        free = {e: 0.0 for e in ENGS}
        cur_tbl = [None]
        dma_pipe = [0.0]
        order = {e: [] for e in ENGS}
        remaining = len(self.ops)
        LAT = 150.0
        while remaining:
            best = None
            for e in ENGS:
                q = by_eng[e]
                h = head[e]
                while h < len(q) and q[h].placed:
                    h += 1
                head[e] = h
                cnt = 0
                i = h
                W = window[e]
                while i < len(q) and cnt < W:
                    op = q[i]
                    i += 1
                    if op.placed:
                        continue
                    cnt += 1
                    ok = True
                    st = free[e]
                    for d in op.deps:
                        if not d.placed:
                            ok = False
                            break
                        t = d.done + (0.0 if d.eng == e and not d.is_dma else LAT)
                        if t > st:
                            st = t
                    if not ok:
                        continue
                    if e == "act" and op.tbl is not None and cur_tbl[0] is not None and op.tbl != cur_tbl[0]:
                        st += 1300.0
                    key = (st, op.seq)
                    if best is None or key < best[0]:
                        best = (key, e, op)
            (st, _), e, op = best
            op.placed = True
            remaining -= 1
            order[e].append(op)
            if op.is_dma:
                free[e] = st + op.cost
                t0 = max(st + 1500.0, dma_pipe[0])
                dma_pipe[0] = t0 + op.nbytes / 300.0
                op.done = dma_pipe[0] + 500.0
            else:
                free[e] = st + op.cost
                op.done = st + op.cost
                if e == "act" and op.tbl is not None:
                    cur_tbl[0] = op.tbl
        self.ops = []
        for e in ENGS:
            self.ops.extend(order[e])
        self.est_ns = max(free.values())
        return order

    def finalize(self, window=None):
        nc = self.nc
        if window is not None:
            self.schedule(window)
        for op in self.ops:
            for d in op.deps:
                if d.is_dma:
                    continue
                if d.eng == op.eng and d.eng == "pe":
                    continue
                d.sig = True
        cnt = {e: 0 for e in ENGS}
        for op in sorted(self.ops, key=lambda o: o.seq):
            if op.is_dma:
                b = op.buf
                b.dcount += 16
                op.dval = b.dcount
        for op in self.ops:
            if (not op.is_dma) and op.sig:
                cnt[op.eng] += 1
                op.idx = cnt[op.eng]
        esem = {}
        for e in ENGS:
            g = nc.semaphore("es_" + e)
            esem[e] = g.__enter__()
            self._stack.append(g)
        nsem = 0
        for op in self.ops:
            if op.is_dma and op.buf.dsem is None:
                g = nc.semaphore("ds%d" % nsem)
                nsem += 1
                op.buf.dsem = g.__enter__()
                self._stack.append(g)
        by_eng = {e: [] for e in ENGS}
        for op in self.ops:
            by_eng[op.eng].append(op)
        all_dma_bufs = []
        seenb = set()
        for op in self.ops:
            if op.is_dma and id(op.buf) not in seenb:
                seenb.add(id(op.buf))
                all_dma_bufs.append(op.buf)

        def run_engine(ename):
            def body(e):
                known = {x: 0 for x in ENGS}
                knownd = {}
                for op in by_eng[ename]:
                    for d in op.deps:
                        if d.is_dma:
                            k = id(d.buf)
                            if knownd.get(k, 0) < d.dval:
                                e.wait_ge(d.buf.dsem, d.dval)
                                knownd[k] = d.dval
                        else:
                            if d.eng == ename and ename == "pe":
                                continue
                            if known[d.eng] < d.idx:
                                e.wait_ge(esem[d.eng], d.idx)
                                known[d.eng] = d.idx
                    ins = op.emit(e)
                    if op.is_dma:
                        ins.then_inc(op.buf.dsem, 16)
                    elif op.sig:
                        ins.then_inc(esem[ename], 1)
                if ename == "sp":
                    for b in all_dma_bufs:
                        e.wait_ge(b.dsem, b.dcount)
                    for x in ENGS:
                        if x != "sp" and cnt[x] > 0:
                            e.wait_ge(esem[x], cnt[x])
            return body

        with nc.Block() as block:
            block.tensor(run_engine("pe"))
            block.scalar(run_engine("act"))
            block.vector(run_engine("dve"))
            block.gpsimd(run_engine("pool"))
            block.sync(run_engine("sp"))

    def close(self):
        while self._stack:
            g = self._stack.pop()
            g.__exit__(None, None, None)


class Arena:
    def __init__(self, P, name, nbytes):
        self.buf = P.sbuf(name, [128, nbytes // 2], BF16)
        self.items = []
        self.off = 0
        self.nbytes = nbytes

    def phase(self):
        self.off = 0

    def take(self, name, shape, dt):
        n = 1
        for d in shape[1:]:
            n *= d
        nb = n * (4 if dt == F32 else 2)
        nb4 = (nb + 3) // 4 * 4
        assert self.off + nb4 <= self.nbytes, (name, self.off, nb4, self.nbytes)
        ap = self.buf.t[0:shape[0], self.off // 2:(self.off + nb) // 2]
        if dt == F32:
            ap = ap.bitcast(F32)
        if len(shape) == 3:
            ap = ap.rearrange("p (a b) -> p a b", a=shape[1])
        elif len(shape) == 4:
            ap = ap.rearrange("p (a b c) -> p a b c", a=shape[1], b=shape[2])
        v = Buf(ap, name)
        for (lo, hi, o) in self.items:
            if lo < self.off + nb4 and self.off < hi:
                v.aliases.append(o)
                o.aliases.append(v)
        self.items.append((self.off, self.off + nb4, v))
        self.off += nb4
        return v


D = 1024
TT = 512
HD = 64
NH = 8
DILS = (1, 4, 16)
WINS = (128, 512, 2048)
QKV = 4608
O_Z = 4608
O_XBC = 5632
O_DT = 7168
MIX_IN = 7184
FFN_H = 2816
NFC = 22
PAST = 8192
EPS = 1e-6


def host_consts(seq):
    c = {}
    c["c_ident"] = np.eye(128, dtype=np.float32)
    pm = np.zeros((128, 128), np.float32)
    for dp in range(128):
        d = (dp // 64) * 64 + ((dp % 64) + 32) % 64
        pm[d, dp] = 1.0
    c["c_perm"] = pm
    k = np.arange(128)[:, None]
    q = np.arange(128)[None, :]
    bd = (k // 32) == (q // 32)
    masks = np.stack([
        (k >= q), (k <= q), bd, bd & ((k % 32) >= (q % 32)), bd & ((k % 32) <= (q % 32)),
    ]).astype(np.float32)
    c["c_masks"] = masks
    half = HD // 2
    inv_freq = (np.float32(10000.0) ** (-np.arange(half, dtype=np.float32) / np.float32(half))).astype(np.float32)
    pos = np.arange(seq, dtype=np.float32)
    ang = (pos[:, None] * inv_freq[None, :]).astype(np.float32)
    cosv = np.cos(ang).astype(np.float32)
    sinv = np.sin(ang).astype(np.float32)
    p = np.arange(128)
    fidx = p % 32
    sign = np.where((p % 64) < 32, -1.0, 1.0).astype(np.float32)
    rope = np.zeros((3, 2, 128, seq), np.float32)
    nt = seq // TT
    for g in range(3):
        perm = np.zeros(seq, np.int64)
        for T in range(nt):
            for u in range(4):
                w = np.arange(128)
                if g == 0:
                    tau = 128 * u + w
                elif g == 1:
                    tau = 4 * w + u
                else:
                    tau = 16 * (w % 32) + 4 * u + (w // 32)
                perm[T * TT + u * 128 + w] = T * TT + tau
        rope[g, 0] = cosv[perm][:, fidx].T
        rope[g, 1] = (sinv[perm][:, fidx].T) * sign[:, None]
    c["c_rope"] = rope
    sel = np.zeros((65, 64), np.float32)
    sel[64, :] = 1.0
    c["c_sel"] = sel
    angs = (np.float32(PAST) * inv_freq).astype(np.float32)
    col = np.arange(512)
    cs = np.cos(angs).astype(np.float32)[col % 32]
    sn = np.sin(angs).astype(np.float32)[col % 32] * np.where((col % 64) < 32, -1.0, 1.0)
    c["c_srope"] = np.stack([cs, sn]).astype(np.float32)
    dl = np.zeros((16, 16, 128), np.float32)
    for b in range(16):
        dl[b, b, :] = 1.0
    c["c_delta"] = dl.reshape(16, 2048)
    es = np.zeros((128, 16, 16), np.float32)
    for b in range(16):
        es[:, b, b] = 1.0
    c["c_esel"] = es.reshape(128, 256)
    return c


def build(nseq, seq, ns, do_sample=True, debug=None, stop=None, past=PAST,
          sched_window={"pe": 24, "act": 10, "dve": 10, "pool": 10, "sp": 12}):
    nc = bass.Bass("TRN2", target_bir_lowering=False)
    P = Prog(nc)

    def stage(name):
        if stop is not None and name == stop:
            P.frozen = True
    NT = seq // TT
    WB = [min(w, past) for w in WINS]
    PW = [min(w, seq) for w in WINS]

    def din(name, shape):
        return nc.dram_tensor(name, list(shape), F32, kind="ExternalInput")

    def dout(name, shape):
        return nc.dram_tensor(name, list(shape), F32, kind="ExternalOutput")

    x_prompt = din("x_prompt", [nseq, seq, D])
    x_sample = din("x_sample", [ns, D])
    cache = [din("cache%d" % g, [ns, WB[g], 2, 512]) for g in range(3)]
    state_conv = din("state_conv", [ns, 3, 1536])
    state_ssm = din("state_ssm", [ns, 1024, 128])
    norm_mix_pre = din("norm_mix_pre", [1, D])
    norm_mix_post = din("norm_mix_post", [1, D])
    norm_ffn_pre = din("norm_ffn_pre", [1, D])
    norm_ffn_post = din("norm_ffn_post", [1, D])
    w_in = din("w_in", [D, MIX_IN])
    w_out = din("w_out", [1536, D])
    conv_w = din("conv_w", [4, 1536])
    conv_b = din("conv_b", [1, 1536])
    dt_bias = din("dt_bias", [1, 16])
    a_log = din("a_log", [1, 16])
    d_skip = din("d_skip", [1, 16])
    ssd_norm_w = din("ssd_norm_w", [1, D])
    w_ffn_in = din("w_ffn_in", [D, 2 * FFN_H])
    w_ffn_out = din("w_ffn_out", [FFN_H, D])
    c_ident = din("c_ident", [128, 128])
    c_perm = din("c_perm", [128, 128])
    c_masks = din("c_masks", [5, 128, 128])
    c_rope = din("c_rope", [3, 2, 128, seq])
    c_sel = din("c_sel", [65, 64])
    c_srope = din("c_srope", [2, 512])
    c_delta = din("c_delta", [16, 2048])
    c_esel = din("c_esel", [128, 256])

    y_prompt = dout("y_prompt", [nseq, seq, D])
    y_sample = dout("y_sample", [ns, D])
    p_kv = [dout("p_kv%d" % g, [nseq, PW[g], 2, 512]) for g in range(3)]
    p_conv = dout("p_conv", [nseq, 3, 1536])
    p_ssm = dout("p_ssm", [nseq, 1024, 128])
    s_kv = [dout("s_kv%d" % g, [ns, WB[g], 2, 512]) for g in range(3)]
    s_conv = dout("s_conv", [ns, 3, 1536])
    s_ssm = dout("s_ssm", [ns, 1024, 128])
    dbg = {}
    if debug:
        for nm, shp in debug.items():
            dbg[nm] = dout("dbg_" + nm, shp)

    win_b = P.dram("win_b", [D, MIX_IN], BF16)
    wout_b = P.dram("wout_b", [1536, D], BF16)
    wfi_b = P.dram("wfi_b", [D, 2 * FFN_H], BF16)
    wfo_b = P.dram("wfo_b", [FFN_H, D], BF16)
    kscr = [[P.dram("kscr%d_%d" % (g, T), [128, 4, 512], BF16) for T in range(NT)] for g in range(3)]
    vscr = [[P.dram("vscr%d_%d" % (g, T), [128, 4, 8, 65], BF16) for T in range(NT)] for g in range(3)]

    def fsz(ap):
        n = 1
        for d in ap.shape[1:]:
            n *= d
        return n

    def vcost(eng, ap):
        n = fsz(ap)
        if eng == "pool":
            return 100.0 + 2.1 * n
        if eng == "act":
            return 220.0 + 0.85 * n
        return 70.0 + 1.0 * n

    def ACT(out, in_, func, reads, writes, **kw):
        tbl = "silu" if func == AF.Silu else ("exp" if func in (AF.Exp, AF.Ln) else None)
        return P.op("act", lambda e: e.activation(out=out, in_=in_, func=func, **kw), reads, writes,
                    cost=vcost("act", in_), tbl=tbl)

    def TT_(eng, out, in0, in1, op, reads, writes):
        return P.op(eng, lambda e: e.tensor_tensor(out=out, in0=in0, in1=in1, op=op), reads, writes, cost=vcost(eng, out))

    def TS(eng, out, in0, s1, s2, op0, op1, reads, writes):
        if op1 is None:
            return P.op(eng, lambda e: e.tensor_scalar(out=out, in0=in0, scalar1=s1, scalar2=None, op0=op0), reads, writes,
                        cost=vcost(eng, out))
        return P.op(eng, lambda e: e.tensor_scalar(out=out, in0=in0, scalar1=s1, scalar2=s2, op0=op0, op1=op1), reads, writes,
                    cost=vcost(eng, out))

    def STT(eng, out, in0, scalar, in1, op0, op1, reads, writes):
        return P.op(eng, lambda e: e.scalar_tensor_tensor(out=out, in0=in0, scalar=scalar, in1=in1, op0=op0, op1=op1), reads, writes,
                    cost=vcost(eng, out))

    def CP(eng, out, in_, reads, writes):
        if eng == "act":
            return P.op("act", lambda e: e.copy(out=out, in_=in_), reads, writes, cost=vcost("act", out))
        return P.op(eng, lambda e: e.tensor_copy(out=out, in_=in_), reads, writes, cost=vcost(eng, out))

    def MM(out, lhsT, rhs, start, stop, reads, writes):
        c = max(64, fsz(rhs)) * 0.42 * (4.0 if lhsT.dtype == F32 else 1.0) + 8.0
        return P.op("pe", lambda e: e.matmul(out, lhsT=lhsT, rhs=rhs, start=start, stop=stop), reads, writes, cost=c)

    def TR(out, in_, ident, reads, writes):
        c = max(64, fsz(in_)) * 0.42 * (4.0 if in_.dtype == F32 else 1.0) + 8.0
        return P.op("pe", lambda e: e.transpose(out=out, in_=in_, identity=ident), reads, writes, cost=c)

    def MSET(eng, ap, val, writes):
        return P.op(eng, lambda e: e.memset(ap, val), (), writes, cost=vcost(eng, ap) * 0.5)

    DQ = "sp"
    WQ = "sp"

    banks = [P.psum("bank%d" % i, [128, 512], F32) for i in range(8)]
    held = set()
    lru = list(range(8))

    def palloc():
        for i in lru:
            if i not in held:
                held.add(i)
                lru.remove(i)
                lru.append(i)
                return banks[i]
        raise RuntimeError("out of PSUM banks")

    def pfree(b):
        held.discard(banks.index(b))

    cst = P.sbuf("cst_f", [128, 128], F32)
    ident_f = P.sbuf("ident_f", [128, 128], F32)
    ident_b = P.sbuf("ident_b", [128, 128], BF16)
    perm_b = P.sbuf("perm_b", [128, 128], BF16)
    masks_b = P.sbuf("masks_b", [128, 5, 128], BF16)
    U_f = P.sbuf("U_f", [128, 128], F32)
    ones_f = P.sbuf("ones_f", [128, 128], F32)
    ones_b = P.sbuf("ones_b", [128, 128], BF16)
    sel_f = P.sbuf("sel_f", [65, 64], F32)
    eps_t = P.sbuf("eps_t", [128, 1], F32)
    one_t = P.sbuf("one_t", [128, 1], F32)
    nw_mix = P.sbuf("nw_mix", [128, D], F32)
    nw_ffn = P.sbuf("nw_ffn", [128, D], F32)
    nwp = P.sbuf("nwp", [128, 3, 8], F32)
    cw_t = P.sbuf("cw_t", [128, 12, 4], F32)
    cb_t = P.sbuf("cb_t", [128, 12], F32)
    dtb_t = P.sbuf("dtb_t", [128, 16], F32)
    A_t = P.sbuf("A_t", [128, 16], F32)
    D_t = P.sbuf("D_t", [128, 8], F32)

    P.dma(DQ, ident_f[:], c_ident[:, :], ident_f, writes=[ident_f])
    CP("dve", ident_b[:], ident_f[:], [ident_f], [ident_b])
    P.dma(DQ, cst[:], c_perm[:, :], cst, writes=[cst])
    CP("dve", perm_b[:], cst[:], [cst], [perm_b])
    for i in range(5):
        P.dma(DQ, cst[:], c_masks[i, :, :], cst, writes=[cst])
        CP("dve", masks_b[:, i, :], cst[:], [cst], [masks_b])
    P.dma(DQ, U_f[:], c_masks[1, :, :], U_f, writes=[U_f])
    MSET("dve", ones_f[:], 1.0, [ones_f])
    MSET("dve", ones_b[:], 1.0, [ones_b])
    MSET("dve", eps_t[:], EPS, [eps_t])
    MSET("dve", one_t[:], 1.0, [one_t])
    P.dma(DQ, sel_f[:], c_sel[:, :], sel_f, writes=[sel_f])
    P.dma(DQ, nw_mix[:], norm_mix_post[0:1, :].partition_broadcast(128), nw_mix, writes=[nw_mix])
    P.dma(DQ, nw_ffn[:], norm_ffn_post[0:1, :].partition_broadcast(128), nw_ffn, writes=[nw_ffn])
    for i, src in enumerate((norm_mix_pre, norm_ffn_pre, ssd_norm_w)):
        P.dma(DQ, nwp[:, i, :], src[0, :].rearrange("(k p) -> p k", p=128), nwp, writes=[nwp],
              allow_slow_non_contiguous=True)
    for j_ in range(4):
        P.dma(DQ, cw_t[:, :, j_], conv_w[j_, :].rearrange("(c p) -> p c", p=128), cw_t, writes=[cw_t],
              allow_slow_non_contiguous=True)
    P.dma(DQ, cb_t[:], conv_b[0, :].rearrange("(c p) -> p c", p=128), cb_t, writes=[cb_t],
          allow_slow_non_contiguous=True)
    P.dma(DQ, dtb_t[:], dt_bias[0:1, :].partition_broadcast(128), dtb_t, writes=[dtb_t])
    P.dma(DQ, A_t[:], a_log[0:1, :].partition_broadcast(128), A_t, writes=[A_t])
    ACT(A_t[:], A_t[:], AF.Exp, [A_t], [A_t])
    TS("dve", A_t[:], A_t[:], -1.0, None, ALU.mult, None, [A_t], [A_t])
    dsk2 = d_skip[0, :].rearrange("(c e) -> e c", e=2)
    for e_ in range(2):
        P.dma(DQ, D_t[64 * e_:64 * e_ + 64, :], dsk2[e_:e_ + 1, :].partition_broadcast(64), D_t, writes=[D_t],
              allow_slow_non_contiguous=True)

    stage("consts")
    NRING = 3
    wring = [P.sbuf("wring%d" % i, [128, 8, 512], BF16) for i in range(NRING)]
    wr_i = [0]

    def wnext():
        b = wring[wr_i[0] % NRING]
        wr_i[0] += 1
        return b

    xin = [P.sbuf("xin%d" % i, [128, D], F32) for i in range(2)]
    xnb = [P.sbuf("xnb%d" % i, [128, D], BF16) for i in range(2)]
    junk = P.sbuf("junk", [128, D], BF16)
    hb = [P.sbuf("hb%d" % j, [128, D], F32) for j in range(4)]
    xnT0 = P.sbuf("xnT0", [128, 8, 512], BF16)
    xnTp = P.sbuf("xnTp", [128, 8, 512], BF16)
    qraw = [P.sbuf("qraw%d" % i, [128, 512], BF16) for i in range(2)]
    vcur = [P.sbuf("vcur%d" % i, [128, 4, 8, 65], BF16) for i in range(2)]
    kvst = [P.sbuf("kvst%d" % i, [128, 2, 512], F32) for i in range(1)]
    kvst_i = [0]
    convtail = P.sbuf("convtail", [128, 12, 3], F32)
    yT = P.sbuf("yT", [128, 8, 512], BF16)
    stT = P.sbuf("stT", [128, 16, 64], F32)
    stz = P.sbuf("stz", [128, 16, 128], BF16)
    arA = Arena(P, "arA", 39936)
    acc = arA.take("acc", [128, 8, 512], F32)
    NPT = 6
    PTb = [arA.take("PT%d" % i, [128, 512], BF16) for i in range(NPT)]
    pt_i = [0]
    NSTR = 8
    kstr = [arA.take("kstr%d" % i, [128, 512], BF16) for i in range(NSTR)]
    vstr = [arA.take("vstr%d" % i, [128, 4, 2, 65], BF16) for i in range(NSTR)]
    arA.phase()
    stg = [arA.take("stg%d" % i, [128, 515], F32) for i in range(2)]
    cacc = [arA.take("cacc%d" % i, [128, 512], F32) for i in range(2)]
    xdtz = arA.take("xdtz", [128, 16, 128], BF16)
    xdd = arA.take("xdd", [128, 16, 64], BF16)
    Btok = arA.take("Btok", [128, 2, 128], BF16)
    Rb = arA.take("Rb", [128, 4, 128], F32)
    tmpb = arA.take("tmpb", [128, 4, 128], F32)
    Eb = arA.take("Eb", [128, 4, 128], F32)
    ecs = arA.take("ecs", [128, 4, 128], F32)
    Wb = arA.take("Wb", [128, 16, 128], BF16)
    Cdec = arA.take("Cdec", [128, 16, 128], BF16)
    Gm = arA.take("Gm", [128, 2, 128], F32)
    sqb = [arA.take("sqb%d" % i, [128, 512], BF16) for i in range(2)]
    rstd_b = arA.take("rstd_b", [128, 512], F32)
    ytmp = [arA.take("ytmp%d" % i, [128, 128], F32) for i in range(2)]
    arA.phase()
    fst = [arA.take("fst%d" % i, [128, 4, 512], F32) for i in range(4)]
    arB = Arena(P, "arB", 8192)
    kcur = [arB.take("kcur%d" % i, [128, 4, 512], BF16) for i in range(2)]
    arB.phase()
    m0 = [arB.take("m0_%d" % j, [128, 512], F32) for j in range(4)]
    arC = Arena(P, "arC", 8192)
    qT = arC.take("qT", [128, 4, 512], BF16)
    ropet = arC.take("ropet", [128, 2, 512], F32)
    arC.phase()
    attnT = arC.take("attnT", [64, 8, 512], BF16)
    arD = Arena(P, "arD", 8192)
    rt1 = [arD.take("rt1_%d" % i, [128, 512], F32) for i in range(2)]
    rt2 = [arD.take("rt2_%d" % i, [128, 512], F32) for i in range(2)]
    arD.phase()
    sgt = [arD.take("sgt%d" % i, [128, 512], F32) for i in range(2)]
    arE = Arena(P, "arE", 22528)
    actT = arE.take("actT", [128, NFC, 512], BF16)
    arE.phase()
    sz = arE.take("sz", [128, 8, 512], BF16)
    xc = arE.take("xc", [128, 12, 512], BF16)
    small = {}

    def sm(name, cols=1):
        if name not in small:
            small[name] = P.sbuf("sm_" + name, [128, cols], F32)
        return small[name]

    bst = [P.view(wring[i // 2].t[:, 4 * (i % 2):4 * (i % 2) + 4, :], "bst%d" % i) for i in range(6)]
    for i in range(6):
        Prog.alias(wring[i // 2], [bst[i]])
    prep_i = [0]

    def prep_piece(src, dst, r0, c0, ncol, scale_ap):
        i = prep_i[0]
        prep_i[0] += 1
        f = fst[i % 4]
        b = bst[i % 6]
        fv = f.t.rearrange("p a b -> p (a b)")[:, 0:ncol]
        bv = b.t.rearrange("p a b -> p (a b)")[:, 0:ncol]
        P.dma(WQ, fv, src[r0:r0 + 128, c0:c0 + ncol], f, writes=[f])
        eng = ("dve", "act")[i % 2]
        if scale_ap is None:
            CP(eng, bv, fv, [f], [b])
        elif eng == "act":
            ACT(bv, fv, AF.Copy, [f, nwp], [b], scale=scale_ap)
        else:
            TS(eng, bv, fv, scale_ap, None, ALU.mult, None, [f, nwp], [b])
        P.dma(WQ, dst.t[r0:r0 + 128, c0:c0 + ncol], bv, b, reads=[b], writes=[dst])

    for kc in range(8):
        for c0 in range(0, MIX_IN, 2048):
            prep_piece(w_in, win_b, kc * 128, c0, min(2048, MIX_IN - c0), nwp[:, 0, kc:kc + 1])
    for rc in range(12):
        prep_piece(w_out, wout_b, rc * 128, 0, 1024, None if rc < 4 else nwp[:, 2, rc - 4:rc - 3])
    for kc in range(8):
        for c0 in range(0, 2 * FFN_H, 2048):
            prep_piece(w_ffn_in, wfi_b, kc * 128, c0, min(2048, 2 * FFN_H - c0), nwp[:, 1, kc:kc + 1])
    for rc in range(NFC):
        prep_piece(w_ffn_out, wfo_b, rc * 128, 0, 1024, None)

    stage("prep")
    def wblock_in(c0, ncol):
        b = wnext()
        P.dma(WQ, b.t[:, :, 0:ncol], win_b.t[:, c0:c0 + ncol].rearrange("(k p) n -> p k n", p=128), b,
              reads=[win_b], writes=[b])
        return b

    def wblock_fi(c0, ncol):
        b = wnext()
        P.dma(WQ, b.t[:, :, 0:ncol], wfi_b.t[:, c0:c0 + ncol].rearrange("(k p) n -> p k n", p=128), b,
              reads=[wfi_b], writes=[b])
        return b

    def wblock_outA(half):
        b = wnext()
        P.dma(WQ, b.t[0:64, :, :], wout_b.t[0:512, half * 512:half * 512 + 512].rearrange("(h p) n -> p h n", p=64), b,
              reads=[wout_b], writes=[b])
        return b

    def wblock_outS(half):
        b = wnext()
        P.dma(WQ, b.t[:, :, :], wout_b.t[512:1536, half * 512:half * 512 + 512].rearrange("(c p) n -> p c n", p=128), b,
              reads=[wout_b], writes=[b])
        return b

    def wblock_fo(c0, n, half):
        b = wnext()
        P.dma(WQ, b.t[:, 0:n, :], wfo_b.t[c0 * 128:(c0 + n) * 128, half * 512:half * 512 + 512].rearrange("(c p) n -> p c n", p=128), b,
              reads=[wfo_b], writes=[b])
        return b

    nrm_i = [0]

    def rstd_from(ss_ap, ss_buf, out_buf):
        ACT(out_buf[:], ss_ap, AF.Ln, [ss_buf, eps_t], [out_buf], scale=1.0 / D, bias=eps_t[:])
        ACT(out_buf[:], out_buf[:], AF.Exp, [out_buf], [out_buf], scale=-0.5)

    def norm_transpose(src_buf, j, dstT):
        i = nrm_i[0]
        nrm_i[0] += 1
        ss = sm("nss%d" % (i % 2))
        rs = sm("nrs%d" % (i % 2))
        xb = xnb[i % 2]
        ACT(junk[:], src_buf[:], AF.Square, [src_buf], [junk, ss], accum_out=ss[:])
        rstd_from(ss[:], ss, rs)
        TS("dve", xb[:], src_buf[:], rs[:], None, ALU.mult, None, [src_buf, rs], [xb])
        bk = palloc()
        bv = bk.t[:].bitcast(BF16)
        for kc in range(8):
            TR(bv[:, kc * 128:(kc + 1) * 128], xb[:, kc * 128:(kc + 1) * 128], ident_b[:], [xb, ident_b], [bk])
        CP("act", dstT[:, :, j * 128:(j + 1) * 128], bv.rearrange("p (k t) -> p k t", k=8), [bk], [dstT])
        pfree(bk)

    def proj_fm(wb, coff, xT, bank, ncontract=8):
        for kc in range(ncontract):
            MM(bank[:], wb[:, kc, coff:coff + 128], xT[:, kc, :], kc == 0, kc == ncontract - 1, [wb, xT], [bank])

    def proj_tm(wb, ncol, xT, j, bank):
        for kc in range(8):
            MM(bank[:, 0:ncol], xT[:, kc, j * 128:(j + 1) * 128], wb[:, kc, 0:ncol], kc == 0, kc == 7, [wb, xT], [bank])

    rope_i = [0]

    def rope_evac(bank, dest_ap, dest_buf):
        i = rope_i[0]
        rope_i[0] += 1
        qr, t1, t2 = qraw[i % 2], rt1[i % 2], rt2[i % 2]
        stage("r0")
        CP("act", qr[:], bank[:], [bank], [qr])
        stage("r1")
        b2 = palloc()
        MM(b2[:], perm_b[:], qr[:], True, True, [perm_b, qr], [b2])
        stage("r2")
        TT_("dve", t1[:], bank[:], ropet[:, 0, :], ALU.mult, [bank, ropet], [t1])
        stage("r3")
        TT_("dve", t2[:], b2[:], ropet[:, 1, :], ALU.mult, [b2, ropet], [t2])
        pfree(b2)
        stage("r4")
        TT_("pool", dest_ap, t1[:], t2[:], ALU.add, [t1, t2], [dest_buf])

    def acc_view(g, h, rows):
        a = acc.t[rows, h, :]
        if g == 0:
            return a
        if g == 1:
            return a.rearrange("p (w u) -> p u w", u=4)
        return a.rearrange("p (i u r) -> p u r i", u=4, r=4)

    def bank_view(g, bank, rows):
        b = bank.t[rows, :]
        if g == 0:
            return b
        if g == 1:
            return b.rearrange("p (u w) -> p u w", u=4)
        return b.rearrange("p (u r i) -> p u r i", u=4, r=4)

    kv_i = [0]
    str_i = [0]

    def prompt_tile(s, T):
        t0 = T * TT
        for j in range(4):
            P.dma(WQ, hb[j][:], x_prompt[s, t0 + j * 128:t0 + (j + 1) * 128, :], hb[j], writes=[hb[j]])
        for j in range(4):
            xi = xin[j % 2]
            P.dma(DQ, xi[:], x_prompt[s, t0 + j * 128:t0 + (j + 1) * 128, :], xi, writes=[xi])
            norm_transpose(xi, j, xnT0)

        stage("A")
        for g in range(3):
            dil = DILS[g]
            if g == 0:
                xT = xnT0
            else:
                xT = xnTp
                for kc in range(8):
                    if g == 1:
                        src = xnT0.t[:, kc, :].rearrange("p (w u) -> p u w", u=4)
                        dst = xnTp.t[:, kc, :].rearrange("p (u w) -> p u w", u=4)
                    else:
                        src = xnT0.t[:, kc, :].rearrange("p (i u r) -> p u r i", u=4, r=4)
                        dst = xnTp.t[:, kc, :].rearrange("p (u r i) -> p u r i", u=4, r=4)
                    CP("pool", dst, src, [xnT0], [xnTp])
            P.dma(DQ, ropet[:], c_rope[g, :, :, t0:t0 + TT].rearrange("c p n -> p c n"), ropet, writes=[ropet])
            kc_ = kcur[kv_i[0] % 2]
            vc_ = vcur[kv_i[0] % 2]
            kv_i[0] += 1
            cbase = g * 1536
            stage("b0_%d" % g)
            wq = wblock_in(cbase, 512)
            stage("b1_%d" % g)
            for fc in range(4):
                bk = palloc()
                proj_fm(wq, fc * 128, xT, bk)
                rope_evac(bk, qT[:, fc, :], qT)
                pfree(bk)
            stage("b2_%d" % g)
            wk = wblock_in(cbase + 512, 512)
            for fc in range(4):
                bk = palloc()
                proj_fm(wk, fc * 128, xT, bk)
                rope_evac(bk, kc_[:, fc, :], kc_)
                pfree(bk)
            stage("b3_%d" % g)
            wv = wblock_in(cbase + 1024, 512)
            for u in range(4):
                bk = palloc()
                proj_tm(wv, 512, xT, u, bk)
                CP("act", vc_[:, u, :, 0:64], bk.t[:, :].rearrange("p (h d) -> p h d", h=8), [bk], [vc_])
                pfree(bk)
            stage("proj%d" % g)
            nprev = {0: 1, 1: 1, 2: 4}[g]
            if T < NT - 1:
                P.dma(DQ, kscr[g][T].t[:, :, :], kc_[:, :, :], kc_, reads=[kc_], writes=[kscr[g][T]])
                P.dma(DQ, vscr[g][T].t[:, :, :, :], vc_[:, :, :, :], vc_, reads=[vc_], writes=[vscr[g][T]])
            first_out = seq - PW[g]
            units_out = []
            if g == 0:
                if T == NT - 1:
                    units_out = [3]
            elif (T + 1) * TT > first_out:
                units_out = [0, 1, 2, 3]
            for u in units_out:
                st = kvst[0]
                kvst_i[0] += 1
                bk = palloc()
                bv = bk.t[:].bitcast(BF16)
                for hp in range(4):
                    TR(bv[:, hp * 128:(hp + 1) * 128], kc_[:, hp, u * 128:(u + 1) * 128], ident_b[:], [kc_, ident_b], [bk])
                CP("act", st[:, 0, :], bv[:, 0:512], [bk], [st])
                pfree(bk)
                CP("pool", st[:, 1, :].rearrange("p (h d) -> p h d", h=8), vc_[:, u, :, 0:64], [vc_], [st])
                rbase = T * TT - first_out
                if g == 0:
                    P.dma(DQ, p_kv[g][s, 0:128, :, :], st[:, :, :], st, reads=[st])
                elif g == 1:
                    dv = p_kv[g][s, rbase:rbase + TT, :, :].rearrange("(w u) c f -> u w c f", u=4)
                    P.dma(DQ, dv[u], st[:, :, :], st, reads=[st])
                else:
                    dv = p_kv[g][s, rbase:rbase + TT, :, :].rearrange("(i u r) c f -> u r i c f", u=4, r=4)
                    for r in range(4):
                        P.dma(DQ, dv[u, r], st[r * 32:(r + 1) * 32, :, :], st, reads=[st])

            stage("pkv%d" % g)
            def cur_k(hp, u, kb=kc_):
                return kb, kb[:, hp, u * 128:(u + 1) * 128]

            def cur_v(h, u, vb=vc_):
                return vb, vb[:, u, h, 0:65]

            for hp in range(4):
                srcs = []
                deltas = []
                if g == 2:
                    deltas = [d_ for d_ in (4, 3, 2, 1) if T - d_ >= 0]
                elif T >= 1:
                    deltas = [1]
                for d_ in deltas:
                    ks = kstr[str_i[0] % NSTR]
                    vs = vstr[str_i[0] % NSTR]
                    str_i[0] += 1
                    Tp = T - d_
                    if g == 0:
                        P.dma(DQ, ks[:, 384:512], kscr[g][Tp].t[:, hp, 384:512], ks, reads=[kscr[g][Tp]], writes=[ks])
                        P.dma(DQ, vs[:, 3, :, :], vscr[g][Tp].t[:, 3, 2 * hp:2 * hp + 2, :], vs, reads=[vscr[g][Tp]], writes=[vs])
                    else:
                        P.dma(DQ, ks[:, :], kscr[g][Tp].t[:, hp, :], ks, reads=[kscr[g][Tp]], writes=[ks])
                        P.dma(DQ, vs[:, :, :, :], vscr[g][Tp].t[:, :, 2 * hp:2 * hp + 2, :], vs, reads=[vscr[g][Tp]], writes=[vs])

                    def sk(hp_, u, ks=ks):
                        return ks, ks[:, u * 128:(u + 1) * 128]

                    def sv(h, u, vs=vs):
                        return vs, vs[:, u, h % 2, 0:65]
                    if g == 0:
                        pass
                    elif g == 1:
                        srcs.append(dict(units=[0, 1, 2, 3], k=sk, v=sv, mask=0))
                    else:
                        srcs.append(dict(units=[0, 1, 2, 3], k=sk, v=sv, mask={4: 3, 3: 2, 2: 2, 1: 2}[d_]))
                if g == 0:
                    if T >= 1:
                        def ak(hp_, u, ks=ks):
                            if u == 0:
                                return ks, ks[:, 384:512]
                            return cur_k(hp_, u - 1)

                        def av(h, u, vs=vs):
                            if u == 0:
                                return vs, vs[:, 3, h % 2, 0:65]
                            return cur_v(h, u - 1)
                        srcs.append(dict(units=[0, 1, 2, 3], k=ak, v=av, mask=0))
                    else:
                        srcs.append(dict(units=[1, 2, 3], k=lambda hp_, u: cur_k(hp_, u - 1),
                                         v=lambda h, u: cur_v(h, u - 1), mask=0))
                    srcs.append(dict(units=[0, 1, 2, 3], k=cur_k, v=cur_v, mask=1))
                elif g == 1:
                    srcs.append(dict(units=[0, 1, 2, 3], k=cur_k, v=cur_v, mask=1))
                else:
                    srcs.append(dict(units=[0, 1, 2, 3], k=cur_k, v=cur_v, mask=4))

                for hh in range(2):
                    h = 2 * hp + hh
                    pr = slice(64 * hh, 64 * hh + 64)
                    pts = []
                    for src in srcs:
                        sb_ = palloc()
                        for u in src["units"]:
                            kb, kap = src["k"](hp, u)
                            MM(sb_[:, u * 128:(u + 1) * 128], kap[pr, :], qT[pr, hp, u * 128:(u + 1) * 128], True, True,
                               [kb, qT], [sb_])
                        u0 = src["units"][0]
                        pt = PTb[pt_i[0] % NPT]
                        pt_i[0] += 1
                        ACT(pt[:, u0 * 128:512], sb_[:, u0 * 128:512], AF.Exp, [sb_], [pt], scale=0.125)
                        pfree(sb_)
                        nu = 4 - u0
                        mk = masks_b[:, src["mask"], :].unsqueeze(1).to_broadcast([128, nu, 128])
                        ptv = pt[:, u0 * 128:512].rearrange("p (u w) -> p u w", u=nu)
                        TT_("dve" if (pt_i[0] % 2) else "pool", ptv, ptv, mk, ALU.mult, [pt, masks_b], [pt])
                        pts.append(pt)
                    ob = palloc()
                    for u in range(4):
                        contrib = [(src, pt) for src, pt in zip(srcs, pts) if u in src["units"]]
                        for ci, (src, pt) in enumerate(contrib):
                            vb, vap = src["v"](h, u)
                            MM(ob[0:65, u * 128:(u + 1) * 128], vap, pt[:, u * 128:(u + 1) * 128], ci == 0,
                               ci == len(contrib) - 1, [vb, pt], [ob])
                    if g == 0:
                        CP("act", acc[0:65, h, :], ob[0:65, :], [ob], [acc])
                    else:
                        av_ = acc_view(g, h, slice(0, 65))
                        TT_("dve", av_, av_, bank_view(g, ob, slice(0, 65)), ALU.add, [acc, ob], [acc])
                    pfree(ob)

        stage("attn")
        P.op("dve", lambda e: e.reciprocal(out=acc[64:65, :, :], in_=acc[64:65, :, :]), [acc], [acc], cost=4400.0)
        for h in range(8):
            bk = palloc()
            MM(bk[0:64, :], sel_f[:, :], acc[0:65, h, :], True, True, [sel_f, acc], [bk])
            TT_("dve", attnT[:, h, :], acc[0:64, h, :], bk[0:64, :], ALU.mult, [acc, bk], [attnT])
            pfree(bk)

        stage("merge")
        if T == 0:
            MSET("pool", stT[:], 0.0, [stT])
            MSET("pool", stz[:], 0.0, [stz])
            MSET("pool", convtail[:], 0.0, [convtail])
        MSET("pool", xdtz[:], 0.0, [xdtz])
        for blk in range(2):
            wz = wblock_in(O_Z + blk * 512, 512)
            for fc in range(4):
                bk = palloc()
                proj_fm(wz, fc * 128, xnT0, bk)
                ACT(sz[:, blk * 4 + fc, :], bk[:], AF.Silu, [bk], [sz])
                pfree(bk)
        for blk in range(3):
            wx = wblock_in(O_XBC + blk * 512, 512)
            for fc in range(4):
                c = blk * 4 + fc
                sg_ = stg[c % 2]
                ca = cacc[c % 2]
                bk = palloc()
                proj_fm(wx, fc * 128, xnT0, bk)
                CP("pool", sg_[:, 0:3], convtail[:, c, :], [convtail], [sg_])
                CP("act", sg_[:, 3:515], bk[:], [bk], [sg_])
                pfree(bk)
                CP("pool", convtail[:, c, :], sg_[:, 512:515], [sg_], [convtail])
                TS("dve", ca[:], sg_[:, 0:512], cw_t[:, c, 0:1], None, ALU.mult, None, [sg_, cw_t], [ca])
                for jj in range(1, 4):
                    STT("dve", ca[:], sg_[:, jj:jj + 512], cw_t[:, c, jj:jj + 1], ca[:], ALU.mult, ALU.add,
                        [sg_, cw_t, ca], [ca])
                ACT(xc[:, c, :], ca[:], AF.Silu, [ca, cb_t], [xc], bias=cb_t[:, c:c + 1])
        stage("conv")
        wdt = wblock_in(O_DT, 16)
        for j in range(4):
            jb = slice(j * 128, (j + 1) * 128)
            dt_ = sm("dt", 16)
            a_ = sm("a", 16)
            cs_sb = sm("cs", 16)
            lastcs = sm("lastcs", 16)
            dend = sm("dend", 16)
            cdec = sm("cdec", 16)
            dtd = sm("dtd", 16)
            bk = palloc()
            proj_tm(wdt, 16, xnT0, j, bk)
            TT_("dve", dt_[:], bk[:, 0:16], dtb_t[:], ALU.add, [bk, dtb_t], [dt_])
            pfree(bk)
            ACT(dt_[:], dt_[:], AF.Exp, [dt_], [dt_])
            ACT(dt_[:], dt_[:], AF.Ln, [dt_, one_t], [dt_], bias=one_t[:])
            TT_("dve", a_[:], dt_[:], A_t[:], ALU.mult, [dt_, A_t], [a_])
            bk = palloc()
            MM(bk[:, 0:16], U_f[:], a_[:], True, True, [U_f, a_], [bk])
            CP("dve", cs_sb[:], bk[:, 0:16], [bk], [cs_sb])
            pfree(bk)
            bk = palloc()
            for gg in range(2):
                MM(bk[:, gg * 128:(gg + 1) * 128], xc[:, 8 + gg, jb], xc[:, 10 + gg, jb], True, True, [xc], [bk])
            TT_("dve", Gm[:], bk.t[:, 0:256].rearrange("p (g t) -> p g t", g=2),
                masks_b[:, 1, :].unsqueeze(1).to_broadcast([128, 2, 128]), ALU.mult, [bk, masks_b], [Gm])
            pfree(bk)
            for qd in range(4):
                hs = slice(qd * 4, qd * 4 + 4)
                gg = qd // 2
                TT_("pool", Rb[:], a_[:, hs].unsqueeze(2).to_broadcast([128, 4, 128]),
                    U_f[:].unsqueeze(1).to_broadcast([128, 4, 128]), ALU.mult, [a_, U_f], [Rb])
                bk = palloc()
                MM(bk[:], ones_f[:], Rb[:].rearrange("p h t -> p (h t)"), True, True, [ones_f, Rb], [bk])
                bk3 = bk.t[:, :].rearrange("p (h t) -> p h t", h=4)
                TT_("dve", tmpb[:], bk3, cs_sb[:, hs].unsqueeze(2).to_broadcast([128, 4, 128]), ALU.subtract,
                    [bk, cs_sb], [tmpb])
                ACT(ecs[:], bk3, AF.Exp, [bk], [ecs])
                CP("act", lastcs[:, hs], bk3[:, :, 127], [bk], [lastcs])
                pfree(bk)
                ACT(Eb[:], tmpb[:], AF.Exp, [tmpb], [Eb])
                STT("dve", Wb[:, hs, :], Eb[:], 1e30, Gm[:, gg, :].unsqueeze(1).to_broadcast([128, 4, 128]),
                    ALU.min, ALU.mult, [Eb, Gm], [Wb])
                TT_("pool", Cdec[:, hs, :], ecs[:], xc[:, 10 + gg, jb].unsqueeze(1).to_broadcast([128, 4, 128]), ALU.mult,
                    [ecs, xc], [Cdec])
            TT_("dve", dend[:], lastcs[:], cs_sb[:], ALU.subtract, [lastcs, cs_sb], [dend])
            ACT(dend[:], dend[:], AF.Exp, [dend], [dend])
            ACT(cdec[:], lastcs[:], AF.Exp, [lastcs], [cdec])
            TT_("dve", dtd[:], dt_[:], dend[:], ALU.mult, [dt_, dend], [dtd])
            bk = palloc()
            bv = bk.t[:].bitcast(BF16)
            for c in range(8):
                TR(bv[:, c * 128:(c + 1) * 128], xc[:, c, jb], ident_b[:], [xc, ident_b], [bk])
            xv = bv.rearrange("p (c e d) -> p c e d", c=8, e=2)
            xz = xdtz[:].rearrange("p (c e) f -> p c e f", e=2)
            dtv = dt_[:].rearrange("p (c e) -> p c e", e=2)
            for e_ in range(2):
                TT_("dve", xz[:, :, e_, 64 * e_:64 * e_ + 64], xv[:, :, e_, :],
                    dtv[:, :, e_].unsqueeze(2).to_broadcast([128, 8, 64]), ALU.mult, [bk, dt_], [xdtz])
            TT_("dve", xdd[:], bv[:, 0:1024].rearrange("p (h d) -> p h d", h=16),
                dtd[:].unsqueeze(2).to_broadcast([128, 16, 64]), ALU.mult, [bk, dtd], [xdd])
            pfree(bk)
            bk = palloc()
            bv = bk.t[:].bitcast(BF16)
            for gg in range(2):
                TR(bv[:, gg * 128:(gg + 1) * 128], xc[:, 8 + gg, jb], ident_b[:], [xc, ident_b], [bk])
            CP("act", Btok[:].rearrange("p g n -> p (g n)"), bv[:, 0:256], [bk], [Btok])
            pfree(bk)
            for k2 in range(2):
                bk = palloc()
                for cc in range(4):
                    c = k2 * 4 + cc
                    reg = bk[:, cc * 128:(cc + 1) * 128]
                    MM(reg, xdtz[:, 2 * c, :], Wb[:, 2 * c, :], True, False, [xdtz, Wb], [bk])
                    MM(reg, xdtz[:, 2 * c + 1, :], Wb[:, 2 * c + 1, :], False, False, [xdtz, Wb], [bk])
                    MM(reg, stz[:, 2 * c, :], Cdec[:, 2 * c, :], False, False, [stz, Cdec], [bk])
                    MM(reg, stz[:, 2 * c + 1, :], Cdec[:, 2 * c + 1, :], False, True, [stz, Cdec], [bk])
                for cc in range(4):
                    c = k2 * 4 + cc
                    yt = ytmp[cc % 2]
                    STT("dve", yt[:], xc[:, c, jb], D_t[:, c:c + 1], bk[:, cc * 128:(cc + 1) * 128], ALU.mult, ALU.add,
                        [xc, D_t, bk], [yt])
                    TT_("pool", yT[:, c, jb], yt[:], sz[:, c, jb], ALU.mult, [yt, sz], [yT])
                pfree(bk)
            for gg in range(2):
                bk = palloc()
                MM(bk[:], Btok[:, gg, :], xdd[:, 8 * gg:8 * gg + 8, :].rearrange("p h d -> p (h d)"), True, True,
                   [Btok, xdd], [bk])
                sv_ = stT[:, 8 * gg:8 * gg + 8, :]
                TT_("dve", sv_, sv_, cdec[:, 8 * gg:8 * gg + 8].unsqueeze(2).to_broadcast([128, 8, 64]), ALU.mult,
                    [stT, cdec], [stT])
                TT_("dve", sv_, sv_, bk.t[:, :].rearrange("p (h d) -> p h d", h=8), ALU.add, [stT, bk], [stT])
                pfree(bk)
            sz_ = stz[:].rearrange("p (c e) f -> p c e f", e=2)
            st_ = stT[:].rearrange("p (c e) d -> p c e d", e=2)
            for e_ in range(2):
                CP("pool", sz_[:, :, e_, 64 * e_:64 * e_ + 64], st_[:, :, e_, :], [stT], [stz])
        stage("ssd")
        bk = palloc()
        for c in range(8):
            sq = sqb[c % 2]
            TT_("pool", sq[:], yT[:, c, :], yT[:, c, :], ALU.mult, [yT], [sq])
            MM(bk[:], ones_b[:], sq[:], c == 0, c == 7, [ones_b, sq], [bk])
        ACT(rstd_b[:], bk[:], AF.Ln, [bk, eps_t], [rstd_b], scale=1.0 / D, bias=eps_t[:])
        pfree(bk)
        ACT(rstd_b[:], rstd_b[:], AF.Exp, [rstd_b], [rstd_b], scale=-0.5)
        for c in range(8):
            TT_("dve" if c % 2 else "pool", yT[:, c, :], yT[:, c, :], rstd_b[:], ALU.mult, [yT, rstd_b], [yT])
        if T == NT - 1:
            for j_ in range(3):
                P.dma(DQ, p_conv[s, j_, :].rearrange("(c p) -> p c", p=128), convtail[:, :, j_], convtail,
                      reads=[convtail], allow_slow_non_contiguous=True)
            stf = stT[:].rearrange("p h d -> p (h d)")
            for half in range(2):
                bk = palloc()
                for cc in range(4):
                    c = half * 4 + cc
                    TR(bk[:, cc * 128:(cc + 1) * 128], stf[:, c * 128:(c + 1) * 128], ident_f[:], [stT, ident_f], [bk])
                st = kvst[0]
                kvst_i[0] += 1
                stv = st[:].rearrange("p a b -> p (a b)")[:, 0:512]
                CP("act", stv, bk[:], [bk], [st])
                pfree(bk)
                P.dma(DQ, p_ssm[s, half * 512:(half + 1) * 512, :].rearrange("(c q) n -> q c n", q=128),
                      stv.rearrange("p (c n) -> p c n", c=4), st, reads=[st])

        stage("ssdout")
        out_epilogue_phase(lambda half: (wblock_outA(half), wblock_outS(half)), "mix", nw_mix)

        stage("E")
        for j in range(4):
            norm_transpose(hb[j], j, xnTp)
        stage("F")
        for blk in range(6):
            ncol = 512 if blk < 5 else 256
            wg = wblock_fi(blk * 512, ncol)
            wu = wblock_fi(FFN_H + blk * 512, ncol)
            for fc in range(ncol // 128):
                c = blk * 4 + fc
                gb = palloc()
                proj_fm(wg, fc * 128, xnTp, gb)
                ub = palloc()
                proj_fm(wu, fc * 128, xnTp, ub)
                sg_ = sgt[c % 2]
                ACT(sg_[:], gb[:], AF.Silu, [gb], [sg_])
                pfree(gb)
                TT_("dve", actT[:, c, :], sg_[:], ub[:], ALU.mult, [sg_, ub], [actT])
                pfree(ub)
        stage("G")
        out_epilogue_phase(None, "ffn", nw_ffn)
        for j in range(4):
            P.dma(DQ, y_prompt[s, t0 + j * 128:t0 + (j + 1) * 128, :], hb[j][:], hb[j], reads=[hb[j]])

    def out_epilogue_phase(wfn, kind, nw):
        ssA = [sm("ssA%d" % j) for j in range(4)]
        ssB = [sm("ssB%d" % j) for j in range(4)]
        for half in range(2):
            bks = [palloc() for _ in range(4)]
            if kind == "mix":
                wA, wS = wfn(half)
                for j in range(4):
                    jb = slice(j * 128, (j + 1) * 128)
                    for h in range(8):
                        MM(bks[j][:], attnT[0:64, h, jb], wA[0:64, h, :], h == 0, False, [attnT, wA], [bks[j]])
                    for c in range(8):
                        MM(bks[j][:], yT[:, c, jb], wS[:, c, :], False, c == 7, [yT, wS], [bks[j]])
            else:
                for (c0, n) in ((0, 8), (8, 8), (16, 6)):
                    w = wblock_fo(c0, n, half)
                    for j in range(4):
                        jb = slice(j * 128, (j + 1) * 128)
                        for ci in range(n):
                            c = c0 + ci
                            MM(bks[j][:], actT[:, c, jb], w[:, ci, :], c == 0, c == NFC - 1, [actT, w], [bks[j]])
            for j in range(4):
                bk = bks[j]
                if half == 0:
                    CP("act", m0[j][:], bk[:], [bk], [m0[j]])
                    ACT(junk[:, 0:512], bk[:], AF.Square, [bk], [junk, ssA[j]], accum_out=ssA[j][:])
                    pfree(bk)
                else:
                    ACT(junk[:, 0:512], bk[:], AF.Square, [bk], [junk, ssB[j]], accum_out=ssB[j][:])
                    tot = sm("tot%d" % (j % 2))
                    rs = sm("ers%d" % (j % 2))
                    TT_("dve", tot[:], ssA[j][:], ssB[j][:], ALU.add, [ssA[j], ssB[j]], [tot])
                    rstd_from(tot[:], tot, rs)
                    t1 = rt1[j % 2]
                    t2 = rt2[j % 2]
                    STT("dve", t1[:], m0[j][:], rs[:], nw[:, 0:512], ALU.mult, ALU.mult, [m0[j], rs, nw], [t1])
                    STT("dve", t2[:], bk[:], rs[:], nw[:, 512:1024], ALU.mult, ALU.mult, [bk, rs, nw], [t2])
                    pfree(bk)
                    TT_("pool", hb[j][:, 0:512], hb[j][:, 0:512], t1[:], ALU.add, [hb[j], t1], [hb[j]])
                    TT_("pool", hb[j][:, 512:1024], hb[j][:, 512:1024], t2[:], ALU.add, [hb[j], t2], [hb[j]])

    bulk = []

    def bulk_drain(k):
        for _ in range(min(k, len(bulk))):
            o, i_, carrier = bulk.pop(0)
            P.dma("act", o, i_, carrier)

    def sample_phase():
        NS = ns
        R = slice(0, NS)
        qscr = P.dram("qscr", [NS, 3, 512], F32)
        dmy = [Buf(None, "dmy%d" % i) for i in range(4)]

        def sub(parent, ap, name):
            v = Buf(ap, name)
            Prog.alias(parent, [v])
            return v

        def f32view(parent, ncols, name, rows=NS):
            t = parent.t
            ap = t[0:rows] if len(t.shape) == 2 else t[0:rows].rearrange("p a b -> p (a b)")
            if ap.dtype != F32:
                ap = ap.bitcast(F32)
            return sub(parent, ap[:, 0:ncols], name)

        for g in range(3):
            wb = WB[g]
            assert wb == 128 * DILS[g], "sample path assumes a full window in the cache"
            step = 512
            for b in range(NS):
                for r0 in range(0, wb - 1, step):
                    n = min(step, wb - 1 - r0)
                    bulk.append((s_kv[g][b, r0:r0 + n, :, :], cache[g][b, r0 + 1:r0 + 1 + n, :, :], dmy[g]))
        for b0 in range(0, NS, 4):
            P.dma("act", s_conv[b0:b0 + 4, 0:2, :], state_conv[b0:b0 + 4, 1:3, :], dmy[3])

        arA.phase()
        proj_s = arA.take("proj_s", [NS, MIX_IN], F32)
        srope = arA.take("srope", [NS, 2, 512], F32)
        kv_t = [arA.take("kv_t0", [128, 2, 512], F32)]
        prod = arA.take("prod", [128, 512], F32)
        arE.phase()
        kv_t.append(arE.take("kv_t1", [128, 2, 512], F32))
        pvx = [arE.take("pvx%d" % i, [128, 520], F32) for i in range(2)]
        cwrow = [arE.take("cwrow%d" % i, [NS, 1536], F32) for i in range(2)]
        arE.phase()
        ffn_s = arE.take("ffn_s", [NS, 2 * FFN_H], F32)
        arB.phase()
        cacc_s = arB.take("cacc_s", [NS, 1536], F32)
        arC.phase()
        nn = arC.take("nn", [NS, 512], F32)
        attn_f = arC.take("attn_f", [NS, 512], F32)
        arC.phase()
        Bd = arC.take("Bd", [NS, 16, 128], F32)
        arD.phase()
        qb_t = [arD.take("qb_t%d" % i, [128, 512], F32) for i in range(2)]
        tq = arD.take("tq", [NS, 512], F32)
        rq = arD.take("rq", [NS, 512], F32)
        arD.phase()
        Cd = arD.take("Cd", [NS, 16, 128], F32)
        sc_rows = [f32view(xnT0, 1536, "sc0"), f32view(xnTp, 1536, "sc1"), f32view(yT, 1536, "sc2")]
        xs_t = sub(hb[0], hb[0].t[R, :], "xs_t")
        h_s = sub(hb[1], hb[1].t[R, :], "h_s")
        dtx = sub(hb[2], hb[2].t[R, :], "dtx")
        dAx = sub(hb[3], hb[3].t[R, :], "dAx")
        st_s = [sub(xin[i], xin[i].t[:, 0:512].rearrange("p (b n) -> p b n", b=4), "st_s%d" % i) for i in range(2)]
        t1s = f32view(junk, 512, "t1s", rows=128)
        t2s = f32view(xnb[0], 512, "t2s", rows=128)
        hnew = [f32view(xnb[1], 512, "hnew0", rows=128)]
        arS = Arena(P, "arS", 4096)
        hnew.append(arS.take("hnew1", [128, 512], F32))
        t3s = arS.take("t3s", [128, 512], F32)
        xsT = P.sbuf("xsT", [128, 8, NS], BF16)
        catT = P.sbuf("catT", [128, 12, NS], BF16)
        actsT = P.sbuf("actsT", [128, NFC, NS], BF16)
        dtxT = P.sbuf("dtxT", [128, 8, NS], F32)
        dAT = P.sbuf("dAT", [128, 8, NS], F32)
        ySST = P.sbuf("ySST", [128, 8, NS], F32)
        esel_t = P.sbuf("esel_t", [128, 16, 16], F32)
        s8 = P.sbuf("s8", [128, 8], F32)
        p8 = P.sbuf("p8", [128, 8], F32)
        snew = P.sbuf("snew", [NS, 8], F32)
        pnew = P.sbuf("pnew", [NS, 8], F32)
        dnew = P.sbuf("dnew", [NS, 8], F32)
        dts = P.sbuf("dts", [NS, 16], F32)
        dAs = P.sbuf("dAs", [NS, 16], F32)
        drow = P.sbuf("drow", [NS, 16], F32)
        ss16 = P.sbuf("ss16", [NS, 1], F32)
        rs16 = P.sbuf("rs16", [NS, 1], F32)
        catb0 = sub(qraw[0], qraw[0].t[R, :], "catb0")
        catb1 = sub(xnb[1], xnb[1].t[R, :], "catb1")
        nb16 = sub(xnb[0], xnb[0].t[R, :], "nb16")
        Prog.alias(catb1, [hnew[0]])
        Prog.alias(nb16, [t2s])

        P.dma(DQ, esel_t[:].rearrange("p a b -> p (a b)"), c_esel[:, :], esel_t, writes=[esel_t])
        P.dma(DQ, srope[:], c_srope[:, :].rearrange("c (o n) -> o c n", o=1).partition_broadcast(NS), srope, writes=[srope])
        P.dma(DQ, drow[:], d_skip[0:1, :].partition_broadcast(NS), drow, writes=[drow])

        def rms16(src_ap, src_buf, dst_ap, dst_buf, n=D):
            ACT(junk[R, 0:n], src_ap, AF.Square, [src_buf], [junk, ss16], accum_out=ss16[:])
            ACT(rs16[:], ss16[:], AF.Ln, [ss16, eps_t], [rs16], scale=1.0 / n, bias=eps_t[R, :])
            ACT(rs16[:], rs16[:], AF.Exp, [rs16], [rs16], scale=-0.5)
            TS("dve", dst_ap, src_ap, rs16[:], None, ALU.mult, None, [src_buf, rs16], [dst_buf])

        def transpose16(src_ap, src_buf, nchunk, dstT):
            bk = palloc()
            bv = bk.t[:].bitcast(BF16)
            for c in range(nchunk):
                TR(bv[:, c * NS:(c + 1) * NS], src_ap[:, c * 128:(c + 1) * 128], ident_b[R, R], [src_buf, ident_b], [bk])
            CP("act", dstT[:, 0:nchunk, :], bv[:, 0:nchunk * NS].rearrange("p (c t) -> p c t", c=nchunk), [bk], [dstT])
            pfree(bk)

        P.dma(DQ, xs_t[:], x_sample[:, :], xs_t, writes=[xs_t])
        rms16(xs_t[:], xs_t, nb16[:], nb16)
        transpose16(nb16, nb16, 8, xsT)
        for c0 in range(0, MIX_IN, 512):
            ncol = min(512, MIX_IN - c0)
            w = wblock_in(c0, ncol)
            bk = palloc()
            for kc in range(8):
                MM(bk[R, 0:ncol], xsT[:, kc, :], w[:, kc, 0:ncol], kc == 0, kc == 7, [xsT, w], [bk])
            CP("act", proj_s[:, c0:c0 + ncol], bk[R, 0:ncol], [bk], [proj_s])
            pfree(bk)
        for g in range(3):
            for which in range(2):
                c0 = g * 1536 + which * 512
                qv = proj_s[:, c0:c0 + 512]
                q3 = qv.rearrange("p (h e d) -> p h e d", h=8, e=2)
                r3 = rq[:].rearrange("p (h e d) -> p h e d", h=8, e=2)
                CP("pool", r3[:, :, 0, :], q3[:, :, 1, :], [proj_s], [rq])
                CP("pool", r3[:, :, 1, :], q3[:, :, 0, :], [proj_s], [rq])
                TT_("dve", tq[:], qv, srope[:, 0, :], ALU.mult, [proj_s, srope], [tq])
                TT_("dve", rq[:], rq[:], srope[:, 1, :], ALU.mult, [rq, srope], [rq])
                TT_("dve", qv, tq[:], rq[:], ALU.add, [tq, rq], [proj_s])
            P.dma(DQ, qscr.t[:, g, :], proj_s[:, g * 1536:g * 1536 + 512], proj_s, reads=[proj_s], writes=[qscr])
            P.dma(DQ, s_kv[g][:, WB[g] - 1, 0, :], proj_s[:, g * 1536 + 512:g * 1536 + 1024], proj_s, reads=[proj_s])
            P.dma(DQ, s_kv[g][:, WB[g] - 1, 1, :], proj_s[:, g * 1536 + 1024:g * 1536 + 1536], proj_s, reads=[proj_s])
        nbank = palloc()
        dbank = palloc()
        it = 0
        total = 3 * NS
        for g in range(3):
            dil = DILS[g]
            for b in range(NS):
                kv = kv_t[it % 2]
                qb = qb_t[it % 2]
                px = pvx[it % 2]
                P.dma(DQ, kv[:], cache[g][b, :, :, :].rearrange("(i d) c f -> i d c f", d=dil)[:, 0, :, :], kv, writes=[kv])
                P.dma(DQ, qb[:], qscr.t[b:b + 1, g, :].partition_broadcast(128), qb, reads=[qscr], writes=[qb])
                TT_("dve", prod[:], kv[:, 0, :], qb[:], ALU.mult, [kv, qb], [prod])
                P.op("dve", lambda e: e.tensor_reduce(out=s8[:], in_=prod[:].rearrange("p (h d) -> p h d", h=8),
                                                      axis=mybir.AxisListType.X, op=ALU.add), [prod], [s8], cost=600.0)
                ACT(px[:, 512:520], s8[:], AF.Exp, [s8], [px], scale=0.125)
                TT_("dve", px[:, 0:512].rearrange("p (h d) -> p h d", h=8), kv[:, 1, :].rearrange("p (h d) -> p h d", h=8),
                    px[:, 512:520].unsqueeze(2).to_broadcast([128, 8, 64]), ALU.mult, [kv, px], [px])
                MM(nbank[R, :], esel_t[:, b, :], px[:, 0:512], it == 0, it == total - 1, [esel_t, px], [nbank])
                MM(dbank[R, 0:8], esel_t[:, b, :], px[:, 512:520], it == 0, it == total - 1, [esel_t, px], [dbank])
                it += 1
        for g in range(3):
            c0 = g * 1536
            TT_("dve", tq[:], proj_s[:, c0:c0 + 512], proj_s[:, c0 + 512:c0 + 1024], ALU.mult, [proj_s], [tq])
            P.op("dve", lambda e: e.tensor_reduce(out=snew[:], in_=tq[:].rearrange("p (h d) -> p h d", h=8),
                                                  axis=mybir.AxisListType.X, op=ALU.add), [tq], [snew])
            ACT(pnew[:], snew[:], AF.Exp, [snew], [pnew], scale=0.125)
            tgt = nn if g == 0 else tq
            TT_("dve", tgt[:].rearrange("p (h d) -> p h d", h=8),
                proj_s[:, c0 + 1024:c0 + 1536].rearrange("p (h d) -> p h d", h=8),
                pnew[:].unsqueeze(2).to_broadcast([NS, 8, 64]), ALU.mult, [proj_s, pnew], [tgt])
            if g == 0:
                CP("dve", dnew[:], pnew[:], [pnew], [dnew])
            else:
                TT_("dve", nn[:], nn[:], tq[:], ALU.add, [nn, tq], [nn])
                TT_("dve", dnew[:], dnew[:], pnew[:], ALU.add, [dnew, pnew], [dnew])
        TT_("dve", nn[:], nn[:], nbank[R, :], ALU.add, [nn, nbank], [nn])
        TT_("dve", dnew[:], dnew[:], dbank[R, 0:8], ALU.add, [dnew, dbank], [dnew])
        pfree(nbank)
        pfree(dbank)
        P.op("dve", lambda e: e.reciprocal(out=dnew[:], in_=dnew[:]), [dnew], [dnew])
        TT_("dve", attn_f[:].rearrange("p (h d) -> p h d", h=8), nn[:].rearrange("p (h d) -> p h d", h=8),
            dnew[:].unsqueeze(2).to_broadcast([NS, 8, 64]), ALU.mult, [nn, dnew], [attn_f])
        CP("dve", catb0[:], attn_f[:], [attn_f], [catb0])

        for j_ in range(3):
            P.dma(DQ, sc_rows[j_][:], state_conv[:, j_, :], sc_rows[j_], writes=[sc_rows[j_]])
        xbc = proj_s[:, O_XBC:O_XBC + 1536]
        P.dma(DQ, s_conv[:, 2, :], xbc, proj_s, reads=[proj_s])
        for j_ in range(4):
            cw = cwrow[j_ % 2]
            P.dma(DQ, cw[:], conv_w[j_:j_ + 1, :].partition_broadcast(NS), cw, writes=[cw])
            src_ap, src_b = (sc_rows[j_][:], sc_rows[j_]) if j_ < 3 else (xbc, proj_s)
            if j_ == 0:
                TT_("dve", cacc_s[:], src_ap, cw[:], ALU.mult, [src_b, cw], [cacc_s])
            else:
                TT_("dve", cw[:], src_ap, cw[:], ALU.mult, [src_b, cw], [cw])
                TT_("dve", cacc_s[:], cacc_s[:], cw[:], ALU.add, [cacc_s, cw], [cacc_s])
        cw = cwrow[0]
        P.dma(DQ, cw[:], conv_b[0:1, :].partition_broadcast(NS), cw, writes=[cw])
        TT_("dve", cacc_s[:], cacc_s[:], cw[:], ALU.add, [cacc_s, cw], [cacc_s])
        ACT(cacc_s[:], cacc_s[:], AF.Silu, [cacc_s], [cacc_s])
        TT_("dve", dts[:], proj_s[:, O_DT:O_DT + 16], dtb_t[R, :], ALU.add, [proj_s, dtb_t], [dts])
        ACT(dts[:], dts[:], AF.Exp, [dts], [dts])
        ACT(dts[:], dts[:], AF.Ln, [dts, one_t], [dts], bias=one_t[R, :])
        TT_("dve", dAs[:], dts[:], A_t[R, :], ALU.mult, [dts, A_t], [dAs])
        ACT(dAs[:], dAs[:], AF.Exp, [dAs], [dAs])
        xs3 = cacc_s[:, 0:1024].rearrange("p (h d) -> p h d", h=16)
        TT_("dve", dtx[:].rearrange("p (h d) -> p h d", h=16), xs3, dts[:].unsqueeze(2).to_broadcast([NS, 16, 64]),
            ALU.mult, [cacc_s, dts], [dtx])
        CP("dve", dAx[:].rearrange("p (h d) -> p h d", h=16), dAs[:].unsqueeze(2).to_broadcast([NS, 16, 64]), [dAs], [dAx])
        for (srcb, dstT) in ((dtx, dtxT), (dAx, dAT)):
            bk = palloc()
            for c in range(8):
                TR(bk[:, c * NS:(c + 1) * NS], srcb[:, c * 128:(c + 1) * 128], ident_f[R, R], [srcb, ident_f], [bk])
            CP("act", dstT[:], bk[:, 0:8 * NS].rearrange("p (c t) -> p c t", c=8), [bk], [dstT])
            pfree(bk)
        idb = ident_f[R, 0:NS].unsqueeze(2).to_broadcast([NS, NS, 128])
        it = 0
        for gg in range(2):
            TT_("dve", Bd[:], cacc_s[:, 1024 + gg * 128:1024 + (gg + 1) * 128].unsqueeze(1).to_broadcast([NS, NS, 128]),
                idb, ALU.mult, [cacc_s, ident_f], [Bd])
            TT_("dve", Cd[:], cacc_s[:, 1280 + gg * 128:1280 + (gg + 1) * 128].unsqueeze(1).to_broadcast([NS, NS, 128]),
                idb, ALU.mult, [cacc_s, ident_f], [Cd])
            for qb_ in range(NS // 4):
                b0 = qb_ * 4
                Bbc = palloc()
                Cbc = palloc()
                MM(Bbc[:], ones_f[R, :], Bd[:, b0:b0 + 4, :].rearrange("p b n -> p (b n)"), True, True, [ones_f, Bd], [Bbc])
                MM(Cbc[:], ones_f[R, :], Cd[:, b0:b0 + 4, :].rearrange("p b n -> p (b n)"), True, True, [ones_f, Cd], [Cbc])
                for cc in range(4):
                    c = gg * 4 + cc
                    st = st_s[it % 2]
                    hn = hnew[it % 2]
                    it += 1
                    P.dma(DQ, st[:], state_ssm[b0:b0 + 4, c * 128:(c + 1) * 128, :].rearrange("b q n -> q b n"), st, writes=[st])
                    TT_("dve", t1s[:].rearrange("p (b n) -> p b n", b=4), st[:],
                        dAT[:, c, b0:b0 + 4].unsqueeze(2).to_broadcast([128, 4, 128]), ALU.mult, [st, dAT], [t1s])
                    TT_("dve", t2s[:].rearrange("p (b n) -> p b n", b=4), Bbc.t[:, :].rearrange("p (b n) -> p b n", b=4),
                        dtxT[:, c, b0:b0 + 4].unsqueeze(2).to_broadcast([128, 4, 128]), ALU.mult, [Bbc, dtxT], [t2s])
                    TT_("pool", hn[:], t1s[:], t2s[:], ALU.add, [t1s, t2s], [hn])
                    P.dma(DQ, s_ssm[b0:b0 + 4, c * 128:(c + 1) * 128, :].rearrange("b q n -> q b n"),
                          hn[:].rearrange("p (b n) -> p b n", b=4), hn, reads=[hn])
                    TT_("dve", t3s[:], hn[:], Cbc[:], ALU.mult, [hn, Cbc], [t3s])
                    P.op("dve", lambda e, c=c, b0=b0: e.tensor_reduce(
                        out=ySST[:, c, b0:b0 + 4], in_=t3s[:].rearrange("p (b n) -> p b n", b=4),
                        axis=mybir.AxisListType.X, op=ALU.add), [t3s], [ySST], cost=600.0)
                pfree(Bbc)
                pfree(Cbc)
        ytok = dtx
        for half in range(2):
            bk = palloc()
            for cc in range(4):
                c = half * 4 + cc
                TR(bk[R, cc * 128:(cc + 1) * 128], ySST[:, c, :], ident_f[:, :], [ySST, ident_f], [bk])
            CP("act", ytok[:, half * 512:(half + 1) * 512], bk[R, :], [bk], [ytok])
            pfree(bk)
        TT_("dve", dAx[:].rearrange("p (h d) -> p h d", h=16), xs3, drow[:].unsqueeze(2).to_broadcast([NS, 16, 64]),
            ALU.mult, [cacc_s, drow], [dAx])
        TT_("dve", ytok[:], ytok[:], dAx[:], ALU.add, [ytok, dAx], [ytok])
        ACT(dAx[:], proj_s[:, O_Z:O_Z + 1024], AF.Silu, [proj_s], [dAx])
        TT_("dve", ytok[:], ytok[:], dAx[:], ALU.mult, [ytok, dAx], [ytok])
        rms16(ytok[:], ytok, catb1[:], catb1)
        transpose16(catb0, catb0, 4, catT)
        bk = palloc()
        bv = bk.t[:].bitcast(BF16)
        for c in range(8):
            TR(bv[:, c * NS:(c + 1) * NS], catb1[:, c * 128:(c + 1) * 128], ident_b[R, R], [catb1, ident_b], [bk])
        CP("act", catT[:, 4:12, :], bv[:, 0:8 * NS].rearrange("p (c t) -> p c t", c=8), [bk], [catT])
        pfree(bk)

        def post(bks, nw, res_in, res_out):
            ACT(junk[R, 0:512], bks[0][R, :], AF.Square, [bks[0]], [junk, ss16], accum_out=ss16[:])
            ACT(junk[R, 512:1024], bks[1][R, :], AF.Square, [bks[1]], [junk, rs16], accum_out=rs16[:])
            TT_("dve", ss16[:], ss16[:], rs16[:], ALU.add, [ss16, rs16], [ss16])
            ACT(rs16[:], ss16[:], AF.Ln, [ss16, eps_t], [rs16], scale=1.0 / D, bias=eps_t[R, :])
            ACT(rs16[:], rs16[:], AF.Exp, [rs16], [rs16], scale=-0.5)
            for half in range(2):
                hs = slice(half * 512, (half + 1) * 512)
                STT("dve", tq[:], bks[half][R, :], rs16[:], nw[R, hs], ALU.mult, ALU.mult, [bks[half], rs16, nw], [tq])
                TT_("dve", res_out[:, hs], res_in[:, hs], tq[:], ALU.add, [res_in, tq], [res_out])
                pfree(bks[half])

        bks = [palloc(), palloc()]
        for half in range(2):
            wA = wnext()
            P.dma(WQ, wA.t[:, 0:4, :], wout_b.t[0:512, half * 512:half * 512 + 512].rearrange("(c p) n -> p c n", p=128), wA,
                  reads=[wout_b], writes=[wA])
            wS = wblock_outS(half)
            for c in range(12):
                w_ap = wA[:, c, :] if c < 4 else wS[:, c - 4, :]
                MM(bks[half][R, :], catT[:, c, :], w_ap, c == 0, c == 11, [catT, wA, wS], [bks[half]])
        post(bks, nw_mix, xs_t, h_s)
        rms16(h_s[:], h_s, nb16[:], nb16)
        transpose16(nb16, nb16, 8, xsT)
        for c0 in range(0, 2 * FFN_H, 512):
            w = wblock_fi(c0, 512)
            bk = palloc()
            for kc in range(8):
                MM(bk[R, :], xsT[:, kc, :], w[:, kc, :], kc == 0, kc == 7, [xsT, w], [bk])
            CP("act", ffn_s[:, c0:c0 + 512], bk[R, :], [bk], [ffn_s])
            pfree(bk)
        ACT(ffn_s[:, 0:FFN_H], ffn_s[:, 0:FFN_H], AF.Silu, [ffn_s], [ffn_s])
        TT_("dve", ffn_s[:, 0:FFN_H], ffn_s[:, 0:FFN_H], ffn_s[:, FFN_H:2 * FFN_H], ALU.mult, [ffn_s], [ffn_s])
        actb = sub(hb[2], hb[2].t[R, :].bitcast(BF16), "actb")
        Prog.alias(actb, [dtx])
        actb2 = sub(hb[3], hb[3].t[R, :].bitcast(BF16), "actb2")
        Prog.alias(actb2, [dAx])
        CP("dve", actb[:, 0:2048], ffn_s[:, 0:2048], [ffn_s], [actb])
        CP("dve", actb2[:, 0:768], ffn_s[:, 2048:FFN_H], [ffn_s], [actb2])
        transpose16(actb, actb, 16, actsT)
        bk = palloc()
        bv = bk.t[:].bitcast(BF16)
        for c in range(6):
            TR(bv[:, c * NS:(c + 1) * NS], actb2[:, c * 128:(c + 1) * 128], ident_b[R, R], [actb2, ident_b], [bk])
        CP("act", actsT[:, 16:22, :], bv[:, 0:6 * NS].rearrange("p (c t) -> p c t", c=6), [bk], [actsT])
        pfree(bk)
        bks = [palloc(), palloc()]
        for half in range(2):
            for (c0, n) in ((0, 8), (8, 8), (16, 6)):
                w = wblock_fo(c0, n, half)
                for ci in range(n):
                    c = c0 + ci
                    MM(bks[half][R, :], actsT[:, c, :], w[:, ci, :], c == 0, c == NFC - 1, [actsT, w], [bks[half]])
        post(bks, nw_ffn, h_s, xs_t)
        P.dma(DQ, y_sample[:, :], xs_t[:], xs_t, reads=[xs_t])

    if do_sample:
        sample_phase()

    for v_ in vcur:
        MSET("pool", v_[:, :, :, 64:65], 1.0, [v_])

    ntiles = nseq * NT
    per_tile = (len(bulk) + ntiles - 1) // ntiles
    for s in range(nseq):
        for T in range(NT):
            bulk_drain(per_tile)
            prompt_tile(s, T)
    bulk_drain(len(bulk))

    P.finalize(window=sched_window)
    P.close()
    return nc


_CACHE = {}
OUT_NAMES = ["y_prompt", "y_sample", "p_kv0", "p_kv1", "p_kv2", "p_conv", "p_ssm",
             "s_kv0", "s_kv1", "s_kv2", "s_conv", "s_ssm"]


def make_in_maps(inp, ncores, nseq, ns, seq, past_override=None):
    f = lambda a: np.ascontiguousarray(np.asarray(a, dtype=np.float32))
    consts = host_consts(seq)
    shared = {
        "norm_mix_pre": f(inp["norm_mix_pre"]), "norm_mix_post": f(inp["norm_mix_post"]),
        "norm_ffn_pre": f(inp["norm_ffn_pre"]), "norm_ffn_post": f(inp["norm_ffn_post"]),
        "w_in": f(inp["w_in"][0]), "w_out": f(inp["w_out"][0]),
        "conv_w": f(inp["conv_w"][0]), "conv_b": f(inp["conv_b"]),
        "dt_bias": f(inp["dt_bias"]), "a_log": f(inp["a_log"]), "d_skip": f(inp["d_skip"]),
        "ssd_norm_w": f(inp["ssd_norm_w"]),
        "w_ffn_in": f(inp["w_ffn_in"][0]), "w_ffn_out": f(inp["w_ffn_out"][0]),
    }
    shared.update(consts)
    xs = np.asarray(inp["x_sample"], dtype=np.float32)
    caches = [np.asarray(inp[k], dtype=np.float32) for k in ("cache_kv_w128", "cache_kv_w512", "cache_kv_w2048")]
    maps = []
    for c in range(ncores):
        m = dict(shared)
        m["x_prompt"] = f(inp["x_prompt"][c * nseq:(c + 1) * nseq])
        m["x_sample"] = f(xs[c * ns:(c + 1) * ns, 0, :])
        for g in range(3):
            cg = caches[g][0, c * ns:(c + 1) * ns]
            if past_override is not None:
                cg = cg[:, :min(WINS[g], past_override)]
            m["cache%d" % g] = f(cg.reshape(ns, cg.shape[1], 2, 512))
        m["state_conv"] = f(np.asarray(inp["state_conv"])[0, c * ns:(c + 1) * ns])
        m["state_ssm"] = f(np.asarray(inp["state_ssm"])[0, c * ns:(c + 1) * ns].reshape(ns, 1024, 128))
        maps.append(m)
    return maps


def assemble(results, ncores, nseq, ns, seq, past=PAST):
    cat = lambda k: np.concatenate([np.asarray(r[k]) for r in results], axis=0)
    pw = [min(w, seq) for w in WINS]
    wb = [min(w, past) for w in WINS]
    B = ncores * nseq
    S = ncores * ns
    outs = [
        cat("y_prompt"),
        cat("y_sample").reshape(S, 1, D),
        cat("p_kv0").reshape(1, B, pw[0], 2, NH, HD),
        cat("p_kv1").reshape(1, B, pw[1], 2, NH, HD),
        cat("p_kv2").reshape(1, B, pw[2], 2, NH, HD),
        cat("p_conv").reshape(1, B, 3, 1536),
        cat("p_ssm").reshape(1, B, 16, 64, 128),
        cat("s_kv0").reshape(1, S, wb[0], 2, NH, HD),
        cat("s_kv1").reshape(1, S, wb[1], 2, NH, HD),
        cat("s_kv2").reshape(1, S, wb[2], 2, NH, HD),
        cat("s_conv").reshape(1, S, 3, 1536),
        cat("s_ssm").reshape(1, S, 16, 64, 128),
    ]
    return tuple(np.ascontiguousarray(o, dtype=np.float32) for o in outs)


def kernel(**inp):
    ncores = 8
    B, seq = inp["x_prompt"].shape[0], inp["x_prompt"].shape[1]
    S = inp["x_sample"].shape[0]
    nseq, ns = B // ncores, S // ncores
    key = (nseq, seq, ns)
    if key not in _CACHE:
        _CACHE[key] = build(nseq, seq, ns)
    nc = _CACHE[key]
    maps = make_in_maps(inp, ncores, nseq, ns, seq)
    res = run_bass_kernel_spmd(nc, maps, core_ids=list(range(ncores)))
    return assemble(res.results, ncores, nseq, ns, seq)
```

```python
import math
import numpy as np
import concourse.bass as bass
import concourse.mybir as mybir
from concourse.bass_utils import run_bass_kernel_spmd

F32 = mybir.dt.float32
BF16 = mybir.dt.bfloat16
ALU = mybir.AluOpType
AF = mybir.ActivationFunctionType

ENGS = ("pe", "act", "dve", "pool", "sp")


class Buf:
    __slots__ = ("t", "name", "last_w", "readers", "dsem", "dcount", "aliases", "excl")

    def __init__(self, t, name):
        self.excl = False
        self.t = t
        self.name = name
        self.last_w = None
        self.readers = []
        self.dsem = None
        self.dcount = 0
        self.aliases = []

    def __getitem__(self, k):
        return self.t[k]


class Op:
    __slots__ = ("eng", "emit", "deps", "is_dma", "buf", "dval", "sig", "idx", "cost", "tbl", "nbytes", "seq",
                 "done", "placed")

    def __init__(self, eng, emit, is_dma=False):
        self.eng = eng
        self.emit = emit
        self.deps = []
        self.is_dma = is_dma
        self.buf = None
        self.dval = 0
        self.sig = False
        self.idx = 0
        self.cost = 100.0
        self.tbl = None
        self.nbytes = 0
        self.seq = 0
        self.done = 0.0
        self.placed = False


class Prog:
    def __init__(self, nc):
        self.nc = nc
        self.ops = []
        self._stack = []
        self.nsb = 0
        self.frozen = False

    def sbuf(self, name, shape, dt):
        g = self.nc.sbuf_tensor(name, list(shape), dt)
        t = g.__enter__()
        self._stack.append(g)
        return Buf(t, name)

    def psum(self, name, shape, dt=F32):
        g = self.nc.psum_tensor(name, list(shape), dt)
        t = g.__enter__()
        self._stack.append(g)
        b = Buf(t, name)
        b.excl = True
        return b

    def view(self, ap, name, parent=None):
        b = Buf(ap, name)
        return b

    def dram(self, name, shape, dt):
        t = self.nc.dram_tensor(name, list(shape), dt, kind="Internal")
        return Buf(t, name)

    @staticmethod
    def alias(a, others):
        for o in others:
            a.aliases.append(o)
            o.aliases.append(a)

    def _add(self, op, reads, writes):
        if self.frozen:
            return op
        deps = []
        for b in reads:
            if b.last_w is not None:
                deps.append(b.last_w)
            if b.excl:
                deps.extend(r for r in b.readers if r.eng != op.eng)
        for b in writes:
            if b.last_w is not None:
                deps.append(b.last_w)
            deps.extend(b.readers)
            for a in b.aliases:
                if a.last_w is not None:
                    deps.append(a.last_w)
                deps.extend(a.readers)
        seen = set()
        for d in deps:
            if d is op or id(d) in seen:
                continue
            seen.add(id(d))
            op.deps.append(d)
        for b in reads:
            b.readers.append(op)
        for b in writes:
            b.last_w = op
            b.readers = []
        op.seq = len(self.ops)
        self.ops.append(op)
        return op

    def op(self, eng, emit, reads=(), writes=(), cost=100.0, tbl=None):
        o = Op(eng, emit)
        o.cost = cost
        o.tbl = tbl
        return self._add(o, reads, writes)

    def dma(self, eng, out_ap, in_ap, carrier, reads=(), writes=(), **kw):
        def emit(e):
            return e.dma_start(out=out_ap, in_=in_ap, **kw)
        op = Op(eng, emit, is_dma=True)
        op.buf = carrier
        n = 1
        for d in out_ap.shape:
            n *= d
        op.nbytes = n * (2 if out_ap.dtype == BF16 else 4)
        op.cost = 60.0
        return self._add(op, reads, writes)

    def schedule(self, window):
        by_eng = {e: [] for e in ENGS}
        for i, op in enumerate(self.ops):
            op.placed = False
            by_eng[op.eng].append(op)
        head = {e: 0 for e in ENGS}
        free = {e: 0.0 for e in ENGS}
        cur_tbl = [None]
        dma_pipe = [0.0]
        order = {e: [] for e in ENGS}
        remaining = len(self.ops)
        LAT = 150.0
        while remaining:
            best = None
            for e in ENGS:
                q = by_eng[e]
                h = head[e]
                while h < len(q) and q[h].placed:
                    h += 1
                head[e] = h
                cnt = 0
                i = h
                W = window[e]
                while i < len(q) and cnt < W:
                    op = q[i]
                    i += 1
                    if op.placed:
                        continue
                    cnt += 1
                    ok = True
                    st = free[e]
                    for d in op.deps:
                        if not d.placed:
                            ok = False
                            break
                        t = d.done + ((0.0 if e == "pe" else 120.0) if d.eng == e and not d.is_dma else LAT)
                        if t > st:
                            st = t
                    if not ok:
                        continue
                    if e == "act" and op.tbl is not None and cur_tbl[0] is not None and op.tbl != cur_tbl[0]:
                        st += 1300.0
                    key = (st, op.seq)
                    if best is None or key < best[0]:
                        best = (key, e, op)
            (st, _), e, op = best
            op.placed = True
            remaining -= 1
            order[e].append(op)
            if op.is_dma:
                free[e] = st + op.cost
                t0 = max(st + 1500.0, dma_pipe[0])
                dma_pipe[0] = t0 + op.nbytes / 300.0
                op.done = dma_pipe[0] + 500.0
            else:
                free[e] = st + op.cost
                op.done = st + op.cost
                if e == "act" and op.tbl is not None:
                    cur_tbl[0] = op.tbl
        self.ops = []
        for e in ENGS:
            self.ops.extend(order[e])
        self.est_ns = max(free.values())
        return order

    def finalize(self, window=None):
        nc = self.nc
        if window is not None:
            self.schedule(window)
        for op in self.ops:
            for d in op.deps:
                if d.is_dma:
                    continue
                if d.eng == op.eng and d.eng == "pe":
                    continue
                d.sig = True
        cnt = {e: 0 for e in ENGS}
        for op in sorted(self.ops, key=lambda o: o.seq):
            if op.is_dma:
                b = op.buf
                b.dcount += 16
                op.dval = b.dcount
        for op in self.ops:
            if (not op.is_dma) and op.sig:
                cnt[op.eng] += 1
                op.idx = cnt[op.eng]
        esem = {}
        for e in ENGS:
            g = nc.semaphore("es_" + e)
            esem[e] = g.__enter__()
            self._stack.append(g)
        nsem = 0
        for op in self.ops:
            if op.is_dma and op.buf.dsem is None:
                g = nc.semaphore("ds%d" % nsem)
                nsem += 1
                op.buf.dsem = g.__enter__()
                self._stack.append(g)
        by_eng = {e: [] for e in ENGS}
        for op in self.ops:
            by_eng[op.eng].append(op)
        all_dma_bufs = []
        seenb = set()
        for op in self.ops:
            if op.is_dma and id(op.buf) not in seenb:
                seenb.add(id(op.buf))
                all_dma_bufs.append(op.buf)

        def run_engine(ename):
            def body(e):
                known = {x: 0 for x in ENGS}
                knownd = {}
                for op in by_eng[ename]:
                    for d in op.deps:
                        if d.is_dma:
                            k = id(d.buf)
                            if knownd.get(k, 0) < d.dval:
                                e.wait_ge(d.buf.dsem, d.dval)
                                knownd[k] = d.dval
                        else:
                            if d.eng == ename and ename == "pe":
                                continue
                            if known[d.eng] < d.idx:
                                e.wait_ge(esem[d.eng], d.idx)
                                known[d.eng] = d.idx
                    ins = op.emit(e)
                    if op.is_dma:
                        ins.then_inc(op.buf.dsem, 16)
                    elif op.sig:
                        ins.then_inc(esem[ename], 1)
                if ename == "sp":
                    for b in all_dma_bufs:
                        e.wait_ge(b.dsem, b.dcount)
                    for x in ENGS:
                        if x != "sp" and cnt[x] > 0:
                            e.wait_ge(esem[x], cnt[x])
            return body

        with nc.Block() as block:
            block.tensor(run_engine("pe"))
            block.scalar(run_engine("act"))
            block.vector(run_engine("dve"))
            block.gpsimd(run_engine("pool"))
            block.sync(run_engine("sp"))

    def close(self):
        while self._stack:
            g = self._stack.pop()
            g.__exit__(None, None, None)


class Arena:
    def __init__(self, P, name, nbytes):
        self.buf = P.sbuf(name, [128, nbytes // 2], BF16)
        self.items = []
        self.off = 0
        self.nbytes = nbytes

    def phase(self):
        self.off = 0

    def take(self, name, shape, dt):
        n = 1
        for d in shape[1:]:
            n *= d
        nb = n * (4 if dt == F32 else 2)
        nb4 = (nb + 3) // 4 * 4
        assert self.off + nb4 <= self.nbytes, (name, self.off, nb4, self.nbytes)
        ap = self.buf.t[0:shape[0], self.off // 2:(self.off + nb) // 2]
        if dt == F32:
            ap = ap.bitcast(F32)
        if len(shape) == 3:
            ap = ap.rearrange("p (a b) -> p a b", a=shape[1])
        elif len(shape) == 4:
            ap = ap.rearrange("p (a b c) -> p a b c", a=shape[1], b=shape[2])
        v = Buf(ap, name)
        for (lo, hi, o) in self.items:
            if lo < self.off + nb4 and self.off < hi:
                v.aliases.append(o)
                o.aliases.append(v)
        self.items.append((self.off, self.off + nb4, v))
        self.off += nb4
        return v


D = 1024
TT = 512
HD = 64
NH = 8
DILS = (1, 4, 16)
WINS = (128, 512, 2048)
QKV = 4608
O_Z = 4608
O_XBC = 5632
O_DT = 7168
MIX_IN = 7184
FFN_H = 2816
NFC = 22
PAST = 8192
EPS = 1e-6


def host_consts(seq):
    c = {}
    c["c_ident"] = np.eye(128, dtype=np.float32)
    pm = np.zeros((128, 128), np.float32)
    for dp in range(128):
        d = (dp // 64) * 64 + ((dp % 64) + 32) % 64
        pm[d, dp] = 1.0
    c["c_perm"] = pm
    k = np.arange(128)[:, None]
    q = np.arange(128)[None, :]
    bd = (k // 32) == (q // 32)
    masks = np.stack([
        (k >= q), (k <= q), bd, bd & ((k % 32) >= (q % 32)), bd & ((k % 32) <= (q % 32)),
    ]).astype(np.float32)
    c["c_masks"] = masks
    half = HD // 2
    inv_freq = (np.float32(10000.0) ** (-np.arange(half, dtype=np.float32) / np.float32(half))).astype(np.float32)
    pos = np.arange(seq, dtype=np.float32)
    ang = (pos[:, None] * inv_freq[None, :]).astype(np.float32)
    cosv = np.cos(ang).astype(np.float32)
    sinv = np.sin(ang).astype(np.float32)
    p = np.arange(128)
    fidx = p % 32
    sign = np.where((p % 64) < 32, -1.0, 1.0).astype(np.float32)
    rope = np.zeros((3, 2, 128, seq), np.float32)
    nt = seq // TT
    for g in range(3):
        perm = np.zeros(seq, np.int64)
        for T in range(nt):
            for u in range(4):
                w = np.arange(128)
                if g == 0:
                    tau = 128 * u + w
                elif g == 1:
                    tau = 4 * w + u
                else:
                    tau = 16 * (w % 32) + 4 * u + (w // 32)
                perm[T * TT + u * 128 + w] = T * TT + tau
        rope[g, 0] = cosv[perm][:, fidx].T
        rope[g, 1] = (sinv[perm][:, fidx].T) * sign[:, None]
    c["c_rope"] = rope
    sel = np.zeros((65, 64), np.float32)
    sel[64, :] = 1.0
    c["c_sel"] = sel
    angs = (np.float32(PAST) * inv_freq).astype(np.float32)
    col = np.arange(512)
    cs = np.cos(angs).astype(np.float32)[col % 32]
    sn = np.sin(angs).astype(np.float32)[col % 32] * np.where((col % 64) < 32, -1.0, 1.0)
    c["c_srope"] = np.stack([cs, sn]).astype(np.float32)
    dl = np.zeros((16, 16, 128), np.float32)
    for b in range(16):
        dl[b, b, :] = 1.0
    c["c_delta"] = dl.reshape(16, 2048)
    es = np.zeros((128, 16, 16), np.float32)
    for b in range(16):
        es[:, b, b] = 1.0
    c["c_esel"] = es.reshape(128, 256)
    return c


def build(nseq, seq, ns, do_sample=True, debug=None, stop=None, past=PAST,
          sched_window={"pe": 40, "act": 20, "dve": 20, "pool": 16, "sp": 48}):
    nc = bass.Bass("TRN2", target_bir_lowering=False)
    P = Prog(nc)

    def stage(name):
        if stop is not None and name == stop:
            P.frozen = True
    NT = seq // TT
    WB = [min(w, past) for w in WINS]
    PW = [min(w, seq) for w in WINS]

    def din(name, shape):
        return nc.dram_tensor(name, list(shape), F32, kind="ExternalInput")

    def dout(name, shape):
        return nc.dram_tensor(name, list(shape), F32, kind="ExternalOutput")

    x_prompt = din("x_prompt", [nseq, seq, D])
    x_sample = din("x_sample", [ns, D])
    cache = [din("cache%d" % g, [ns, WB[g], 2, 512]) for g in range(3)]
    state_conv = din("state_conv", [ns, 3, 1536])
    state_ssm = din("state_ssm", [ns, 1024, 128])
    norm_mix_pre = din("norm_mix_pre", [1, D])
    norm_mix_post = din("norm_mix_post", [1, D])
    norm_ffn_pre = din("norm_ffn_pre", [1, D])
    norm_ffn_post = din("norm_ffn_post", [1, D])
    w_in = din("w_in", [D, MIX_IN])
    w_out = din("w_out", [1536, D])
    conv_w = din("conv_w", [4, 1536])
    conv_b = din("conv_b", [1, 1536])
    dt_bias = din("dt_bias", [1, 16])
    a_log = din("a_log", [1, 16])
    d_skip = din("d_skip", [1, 16])
    ssd_norm_w = din("ssd_norm_w", [1, D])
    w_ffn_in = din("w_ffn_in", [D, 2 * FFN_H])
    w_ffn_out = din("w_ffn_out", [FFN_H, D])
    c_ident = din("c_ident", [128, 128])
    c_perm = din("c_perm", [128, 128])
    c_masks = din("c_masks", [5, 128, 128])
    c_rope = din("c_rope", [3, 2, 128, seq])
    c_sel = din("c_sel", [65, 64])
    c_srope = din("c_srope", [2, 512])
    c_delta = din("c_delta", [16, 2048])
    c_esel = din("c_esel", [128, 256])

    y_prompt = dout("y_prompt", [nseq, seq, D])
    y_sample = dout("y_sample", [ns, D])
    p_kv = [dout("p_kv%d" % g, [nseq, PW[g], 2, 512]) for g in range(3)]
    p_conv = dout("p_conv", [nseq, 3, 1536])
    p_ssm = dout("p_ssm", [nseq, 1024, 128])
    s_kv = [dout("s_kv%d" % g, [ns, WB[g], 2, 512]) for g in range(3)]
    s_conv = dout("s_conv", [ns, 3, 1536])
    s_ssm = dout("s_ssm", [ns, 1024, 128])
    dbg = {}
    if debug:
        for nm, shp in debug.items():
            dbg[nm] = dout("dbg_" + nm, shp)

    win_b = P.dram("win_b", [D, MIX_IN], BF16)
    wout_b = P.dram("wout_b", [1536, D], BF16)
    wfi_b = P.dram("wfi_b", [D, 2 * FFN_H], BF16)
    wfo_b = P.dram("wfo_b", [FFN_H, D], BF16)
    kscr = [[P.dram("kscr%d_%d" % (g, T), [128, 4, 512], BF16) for T in range(NT)] for g in range(3)]
    vscr = [[P.dram("vscr%d_%d" % (g, T), [128, 4, 8, 65], BF16) for T in range(NT)] for g in range(3)]

    def fsz(ap):
        n = 1
        for d in ap.shape[1:]:
            n *= d
        return n

    def vcost(eng, ap):
        n = fsz(ap)
        if eng == "pool":
            return 100.0 + 2.1 * n
        if eng == "act":
            return 220.0 + 0.85 * n
        return 70.0 + 1.0 * n

    def ACT(out, in_, func, reads, writes, **kw):
        tbl = "silu" if func == AF.Silu else ("exp" if func in (AF.Exp, AF.Ln) else None)
        return P.op("act", lambda e: e.activation(out=out, in_=in_, func=func, **kw), reads, writes,
                    cost=vcost("act", in_), tbl=tbl)

    def TT_(eng, out, in0, in1, op, reads, writes):
        return P.op(eng, lambda e: e.tensor_tensor(out=out, in0=in0, in1=in1, op=op), reads, writes, cost=vcost(eng, out))

    def TS(eng, out, in0, s1, s2, op0, op1, reads, writes):
        if op1 is None:
            return P.op(eng, lambda e: e.tensor_scalar(out=out, in0=in0, scalar1=s1, scalar2=None, op0=op0), reads, writes,
                        cost=vcost(eng, out))
        return P.op(eng, lambda e: e.tensor_scalar(out=out, in0=in0, scalar1=s1, scalar2=s2, op0=op0, op1=op1), reads, writes,
                    cost=vcost(eng, out))

    def STT(eng, out, in0, scalar, in1, op0, op1, reads, writes):
        return P.op(eng, lambda e: e.scalar_tensor_tensor(out=out, in0=in0, scalar=scalar, in1=in1, op0=op0, op1=op1), reads, writes,
                    cost=vcost(eng, out))

    def CP(eng, out, in_, reads, writes):
        if eng == "act":
            return P.op("act", lambda e: e.copy(out=out, in_=in_), reads, writes, cost=vcost("act", out))
        return P.op(eng, lambda e: e.tensor_copy(out=out, in_=in_), reads, writes, cost=vcost(eng, out))

    def MM(out, lhsT, rhs, start, stop, reads, writes):
        c = max(64, fsz(rhs)) * 0.42 * (4.0 if lhsT.dtype == F32 else 1.0) + 8.0
        return P.op("pe", lambda e: e.matmul(out, lhsT=lhsT, rhs=rhs, start=start, stop=stop), reads, writes, cost=c)

    def TR(out, in_, ident, reads, writes):
        c = max(64, fsz(in_)) * 0.42 * (4.0 if in_.dtype == F32 else 1.0) + 8.0
        return P.op("pe", lambda e: e.transpose(out=out, in_=in_, identity=ident), reads, writes, cost=c)

    def MSET(eng, ap, val, writes):
        return P.op(eng, lambda e: e.memset(ap, val), (), writes, cost=vcost(eng, ap) * 0.5)

    DQ = "sp"
    WQ = "sp"

    banks = [P.psum("bank%d" % i, [128, 512], F32) for i in range(8)]
    held = set()
    lru = list(range(8))

    def palloc():
        for i in lru:
            if i not in held:
                held.add(i)
                lru.remove(i)
                lru.append(i)
                return banks[i]
        raise RuntimeError("out of PSUM banks")

    def pfree(b):
        held.discard(banks.index(b))

    cst = P.sbuf("cst_f", [128, 128], F32)
    ident_f = P.sbuf("ident_f", [128, 128], F32)
    ident_b = P.sbuf("ident_b", [128, 128], BF16)
    perm_b = P.sbuf("perm_b", [128, 128], BF16)
    masks_b = P.sbuf("masks_b", [128, 5, 128], BF16)
    U_f = P.sbuf("U_f", [128, 128], F32)
    ones_f = P.sbuf("ones_f", [128, 128], F32)
    ones_b = P.sbuf("ones_b", [128, 128], BF16)
    sel_f = P.sbuf("sel_f", [65, 64], F32)
    eps_t = P.sbuf("eps_t", [128, 1], F32)
    one_t = P.sbuf("one_t", [128, 1], F32)
    nw_mix = P.sbuf("nw_mix", [128, D], F32)
    nw_ffn = P.sbuf("nw_ffn", [128, D], F32)
    nwp = P.sbuf("nwp", [128, 3, 8], F32)
    cw_t = P.sbuf("cw_t", [128, 12, 4], F32)
    cb_t = P.sbuf("cb_t", [128, 12], F32)
    dtb_t = P.sbuf("dtb_t", [128, 16], F32)
    A_t = P.sbuf("A_t", [128, 16], F32)
    D_t = P.sbuf("D_t", [128, 8], F32)

    P.dma(DQ, ident_f[:], c_ident[:, :], ident_f, writes=[ident_f])
    CP("dve", ident_b[:], ident_f[:], [ident_f], [ident_b])
    P.dma(DQ, cst[:], c_perm[:, :], cst, writes=[cst])
    CP("dve", perm_b[:], cst[:], [cst], [perm_b])
    for i in range(5):
        P.dma(DQ, cst[:], c_masks[i, :, :], cst, writes=[cst])
        CP("dve", masks_b[:, i, :], cst[:], [cst], [masks_b])
    P.dma(DQ, U_f[:], c_masks[1, :, :], U_f, writes=[U_f])
    MSET("dve", ones_f[:], 1.0, [ones_f])
    MSET("dve", ones_b[:], 1.0, [ones_b])
    MSET("dve", eps_t[:], EPS, [eps_t])
    MSET("dve", one_t[:], 1.0, [one_t])
    P.dma(DQ, sel_f[:], c_sel[:, :], sel_f, writes=[sel_f])
    P.dma(DQ, nw_mix[:], norm_mix_post[0:1, :].partition_broadcast(128), nw_mix, writes=[nw_mix])
    P.dma(DQ, nw_ffn[:], norm_ffn_post[0:1, :].partition_broadcast(128), nw_ffn, writes=[nw_ffn])
    for i, src in enumerate((norm_mix_pre, norm_ffn_pre, ssd_norm_w)):
        P.dma(DQ, nwp[:, i, :], src[0, :].rearrange("(k p) -> p k", p=128), nwp, writes=[nwp],
              allow_slow_non_contiguous=True)
    for j_ in range(4):
        P.dma(DQ, cw_t[:, :, j_], conv_w[j_, :].rearrange("(c p) -> p c", p=128), cw_t, writes=[cw_t],
              allow_slow_non_contiguous=True)
    P.dma(DQ, cb_t[:], conv_b[0, :].rearrange("(c p) -> p c", p=128), cb_t, writes=[cb_t],
          allow_slow_non_contiguous=True)
    P.dma(DQ, dtb_t[:], dt_bias[0:1, :].partition_broadcast(128), dtb_t, writes=[dtb_t])
    P.dma(DQ, A_t[:], a_log[0:1, :].partition_broadcast(128), A_t, writes=[A_t])
    ACT(A_t[:], A_t[:], AF.Exp, [A_t], [A_t])
    TS("dve", A_t[:], A_t[:], -1.0, None, ALU.mult, None, [A_t], [A_t])
    dsk2 = d_skip[0, :].rearrange("(c e) -> e c", e=2)
    for e_ in range(2):
        P.dma(DQ, D_t[64 * e_:64 * e_ + 64, :], dsk2[e_:e_ + 1, :].partition_broadcast(64), D_t, writes=[D_t],
              allow_slow_non_contiguous=True)

    stage("consts")
    NRING = 3
    wring = [P.sbuf("wring%d" % i, [128, 8, 512], BF16) for i in range(NRING)]
    wr_i = [0]

    def wnext():
        b = wring[wr_i[0] % NRING]
        wr_i[0] += 1
        return b

    xin = [P.sbuf("xin%d" % i, [128, D], F32) for i in range(2)]
    xnb = [P.sbuf("xnb%d" % i, [128, D], BF16) for i in range(2)]
    junk = P.sbuf("junk", [128, D], BF16)
    hb = [P.sbuf("hb%d" % j, [128, D], F32) for j in range(4)]
    xnT0 = P.sbuf("xnT0", [128, 8, 512], BF16)
    xnTp = P.sbuf("xnTp", [128, 8, 512], BF16)
    qraw = [P.sbuf("qraw%d" % i, [128, 512], BF16) for i in range(2)]
    vcur = [P.sbuf("vcur%d" % i, [128, 4, 8, 65], BF16) for i in range(2)]
    kvst = [P.sbuf("kvst%d" % i, [128, 2, 512], F32) for i in range(1)]
    kvst_i = [0]
    convtail = P.sbuf("convtail", [128, 12, 3], F32)
    yT = P.sbuf("yT", [128, 8, 512], BF16)
    stT = P.sbuf("stT", [128, 16, 64], F32)
    stz = P.sbuf("stz", [128, 16, 128], BF16)
    arA = Arena(P, "arA", 39936)
    acc = arA.take("acc", [128, 8, 512], F32)
    NPT = 6
    PTb = [arA.take("PT%d" % i, [128, 512], BF16) for i in range(NPT)]
    pt_i = [0]
    NSTR = 8
    kstr = [arA.take("kstr%d" % i, [128, 512], BF16) for i in range(NSTR)]
    vstr = [arA.take("vstr%d" % i, [128, 4, 2, 65], BF16) for i in range(NSTR)]
    arA.phase()
    stg = [arA.take("stg%d" % i, [128, 515], F32) for i in range(2)]
    cacc = [arA.take("cacc%d" % i, [128, 512], F32) for i in range(2)]
    xdtz = arA.take("xdtz", [128, 16, 128], BF16)
    xdd = arA.take("xdd", [128, 16, 64], BF16)
    Btok = arA.take("Btok", [128, 2, 128], BF16)
    Rb = arA.take("Rb", [128, 4, 128], F32)
    tmpb = arA.take("tmpb", [128, 4, 128], F32)
    Eb = arA.take("Eb", [128, 4, 128], F32)
    ecs = arA.take("ecs", [128, 4, 128], F32)
    Wb = arA.take("Wb", [128, 16, 128], BF16)
    Cdec = arA.take("Cdec", [128, 16, 128], BF16)
    Gm = arA.take("Gm", [128, 2, 128], F32)
    sqb = [arA.take("sqb%d" % i, [128, 512], BF16) for i in range(2)]
    rstd_b = arA.take("rstd_b", [128, 512], F32)
    ytmp = [arA.take("ytmp%d" % i, [128, 128], F32) for i in range(2)]
    arA.phase()
    fst = [arA.take("fst%d" % i, [128, 4, 512], F32) for i in range(4)]
    arB = Arena(P, "arB", 8192)
    kcur = [arB.take("kcur%d" % i, [128, 4, 512], BF16) for i in range(2)]
    arB.phase()
    m0 = [arB.take("m0_%d" % j, [128, 512], F32) for j in range(4)]
    arC = Arena(P, "arC", 8192)
    qT = arC.take("qT", [128, 4, 512], BF16)
    ropet = arC.take("ropet", [128, 2, 512], F32)
    arC.phase()
    attnT = arC.take("attnT", [64, 8, 512], BF16)
    arD = Arena(P, "arD", 8192)
    rt1 = [arD.take("rt1_%d" % i, [128, 512], F32) for i in range(2)]
    rt2 = [arD.take("rt2_%d" % i, [128, 512], F32) for i in range(2)]
    arD.phase()
    sgt = [arD.take("sgt%d" % i, [128, 512], F32) for i in range(2)]
    arE = Arena(P, "arE", 22528)
    actT = arE.take("actT", [128, NFC, 512], BF16)
    arE.phase()
    sz = arE.take("sz", [128, 8, 512], BF16)
    xc = arE.take("xc", [128, 12, 512], BF16)
    small = {}

    def sm(name, cols=1):
        if name not in small:
            small[name] = P.sbuf("sm_" + name, [128, cols], F32)
        return small[name]

    bst = [P.view(wring[i // 2].t[:, 4 * (i % 2):4 * (i % 2) + 4, :], "bst%d" % i) for i in range(6)]
    for i in range(6):
        Prog.alias(wring[i // 2], [bst[i]])
    prep_i = [0]

    def prep_piece(src, dst, r0, c0, ncol, scale_ap):
        i = prep_i[0]
        prep_i[0] += 1
        f = fst[i % 4]
        b = bst[i % 6]
        fv = f.t.rearrange("p a b -> p (a b)")[:, 0:ncol]
        bv = b.t.rearrange("p a b -> p (a b)")[:, 0:ncol]
        P.dma(WQ, fv, src[r0:r0 + 128, c0:c0 + ncol], f, writes=[f])
        eng = ("dve", "act")[i % 2]
        if scale_ap is None:
            CP(eng, bv, fv, [f], [b])
        elif eng == "act":
            ACT(bv, fv, AF.Copy, [f, nwp], [b], scale=scale_ap)
        else:
            TS(eng, bv, fv, scale_ap, None, ALU.mult, None, [f, nwp], [b])
        P.dma(WQ, dst.t[r0:r0 + 128, c0:c0 + ncol], bv, b, reads=[b], writes=[dst])

    for kc in range(8):
        for c0 in range(0, MIX_IN, 2048):
            prep_piece(w_in, win_b, kc * 128, c0, min(2048, MIX_IN - c0), nwp[:, 0, kc:kc + 1])
    for rc in range(12):
        prep_piece(w_out, wout_b, rc * 128, 0, 1024, None if rc < 4 else nwp[:, 2, rc - 4:rc - 3])
    for kc in range(8):
        for c0 in range(0, 2 * FFN_H, 2048):
            prep_piece(w_ffn_in, wfi_b, kc * 128, c0, min(2048, 2 * FFN_H - c0), nwp[:, 1, kc:kc + 1])
    for rc in range(NFC):
        prep_piece(w_ffn_out, wfo_b, rc * 128, 0, 1024, None)

    stage("prep")
    def wblock_in(c0, ncol):
        b = wnext()
        P.dma(WQ, b.t[:, :, 0:ncol], win_b.t[:, c0:c0 + ncol].rearrange("(k p) n -> p k n", p=128), b,
              reads=[win_b], writes=[b])
        return b

    def wblock_fi(c0, ncol):
        b = wnext()
        P.dma(WQ, b.t[:, :, 0:ncol], wfi_b.t[:, c0:c0 + ncol].rearrange("(k p) n -> p k n", p=128), b,
              reads=[wfi_b], writes=[b])
        return b

    def wblock_outA(half):
        b = wnext()
        P.dma(WQ, b.t[0:64, :, :], wout_b.t[0:512, half * 512:half * 512 + 512].rearrange("(h p) n -> p h n", p=64), b,
              reads=[wout_b], writes=[b])
        return b

    def wblock_outS(half):
        b = wnext()
        P.dma(WQ, b.t[:, :, :], wout_b.t[512:1536, half * 512:half * 512 + 512].rearrange("(c p) n -> p c n", p=128), b,
              reads=[wout_b], writes=[b])
        return b

    def wblock_fo(c0, n, half):
        b = wnext()
        P.dma(WQ, b.t[:, 0:n, :], wfo_b.t[c0 * 128:(c0 + n) * 128, half * 512:half * 512 + 512].rearrange("(c p) n -> p c n", p=128), b,
              reads=[wfo_b], writes=[b])
        return b

    nrm_i = [0]

    def rstd_from(ss_ap, ss_buf, out_buf):
        ACT(out_buf[:], ss_ap, AF.Ln, [ss_buf, eps_t], [out_buf], scale=1.0 / D, bias=eps_t[:])
        ACT(out_buf[:], out_buf[:], AF.Exp, [out_buf], [out_buf], scale=-0.5)

    def norm_transpose(src_buf, j, dstT):
        i = nrm_i[0]
        nrm_i[0] += 1
        ss = sm("nss%d" % (i % 2))
        rs = sm("nrs%d" % (i % 2))
        xb = xnb[i % 2]
        ACT(junk[:], src_buf[:], AF.Square, [src_buf], [junk, ss], accum_out=ss[:])
        rstd_from(ss[:], ss, rs)
        TS("dve", xb[:], src_buf[:], rs[:], None, ALU.mult, None, [src_buf, rs], [xb])
        bk = palloc()
        bv = bk.t[:].bitcast(BF16)
        for kc in range(8):
            TR(bv[:, kc * 128:(kc + 1) * 128], xb[:, kc * 128:(kc + 1) * 128], ident_b[:], [xb, ident_b], [bk])
        CP("act", dstT[:, :, j * 128:(j + 1) * 128], bv.rearrange("p (k t) -> p k t", k=8), [bk], [dstT])
        pfree(bk)

    def proj_fm(wb, coff, xT, bank, ncontract=8):
        for kc in range(ncontract):
            MM(bank[:], wb[:, kc, coff:coff + 128], xT[:, kc, :], kc == 0, kc == ncontract - 1, [wb, xT], [bank])

    def proj_tm(wb, ncol, xT, j, bank):
        for kc in range(8):
            MM(bank[:, 0:ncol], xT[:, kc, j * 128:(j + 1) * 128], wb[:, kc, 0:ncol], kc == 0, kc == 7, [wb, xT], [bank])

    rope_i = [0]

    def rope_evac(bank, dest_ap, dest_buf):
        i = rope_i[0]
        rope_i[0] += 1
        qr, t1, t2 = qraw[i % 2], rt1[i % 2], rt2[i % 2]
        stage("r0")
        CP("act", qr[:], bank[:], [bank], [qr])
        stage("r1")
        b2 = palloc()
        MM(b2[:], perm_b[:], qr[:], True, True, [perm_b, qr], [b2])
        stage("r2")
        TT_("dve", t1[:], bank[:], ropet[:, 0, :], ALU.mult, [bank, ropet], [t1])
        stage("r3")
        TT_("dve", t2[:], b2[:], ropet[:, 1, :], ALU.mult, [b2, ropet], [t2])
        pfree(b2)
        stage("r4")
        TT_("pool", dest_ap, t1[:], t2[:], ALU.add, [t1, t2], [dest_buf])

    def acc_view(g, h, rows):
        a = acc.t[rows, h, :]
        if g == 0:
            return a
        if g == 1:
            return a.rearrange("p (w u) -> p u w", u=4)
        return a.rearrange("p (i u r) -> p u r i", u=4, r=4)

    def bank_view(g, bank, rows):
        b = bank.t[rows, :]
        if g == 0:
            return b
        if g == 1:
            return b.rearrange("p (u w) -> p u w", u=4)
        return b.rearrange("p (u r i) -> p u r i", u=4, r=4)

    kv_i = [0]
    str_i = [0]

    quota = [0]

    def bd():
        if quota[0] > 0:
            quota[0] -= 1
            bulk_drain(1)

    def phase_A(s, T):
        t0 = T * TT
        for j in range(4):
            xi = xin[j % 2]
            P.dma(DQ, xi[:], x_prompt[s, t0 + j * 128:t0 + (j + 1) * 128, :], xi, writes=[xi])
            norm_transpose(xi, j, xnT0)

    def prompt_tile(s, T, nxt):
        t0 = T * TT
        for j in range(4):
            P.dma(WQ, hb[j][:], x_prompt[s, t0 + j * 128:t0 + (j + 1) * 128, :], hb[j], writes=[hb[j]])
        stage("A")
        for g in range(3):
            dil = DILS[g]
            if g == 0:
                xT = xnT0
            else:
                xT = xnTp
                for kc in range(8):
                    if g == 1:
                        src = xnT0.t[:, kc, :].rearrange("p (w u) -> p u w", u=4)
                        dst = xnTp.t[:, kc, :].rearrange("p (u w) -> p u w", u=4)
                    else:
                        src = xnT0.t[:, kc, :].rearrange("p (i u r) -> p u r i", u=4, r=4)
                        dst = xnTp.t[:, kc, :].rearrange("p (u r i) -> p u r i", u=4, r=4)
                    CP("pool", dst, src, [xnT0], [xnTp])
            P.dma(DQ, ropet[:], c_rope[g, :, :, t0:t0 + TT].rearrange("c p n -> p c n"), ropet, writes=[ropet])
            kc_ = kcur[kv_i[0] % 2]
            vc_ = vcur[kv_i[0] % 2]
            kv_i[0] += 1
            cbase = g * 1536
            stage("b0_%d" % g)
            wq = wblock_in(cbase, 512)
            stage("b1_%d" % g)
            for fc in range(4):
                bk = palloc()
                proj_fm(wq, fc * 128, xT, bk)
                rope_evac(bk, qT[:, fc, :], qT)
                pfree(bk)
            stage("b2_%d" % g)
            wk = wblock_in(cbase + 512, 512)
            for fc in range(4):
                bk = palloc()
                proj_fm(wk, fc * 128, xT, bk)
                rope_evac(bk, kc_[:, fc, :], kc_)
                pfree(bk)
            stage("b3_%d" % g)
            wv = wblock_in(cbase + 1024, 512)
            for u in range(4):
                bk = palloc()
                proj_tm(wv, 512, xT, u, bk)
                CP("act", vc_[:, u, :, 0:64], bk.t[:, :].rearrange("p (h d) -> p h d", h=8), [bk], [vc_])
                pfree(bk)
            stage("proj%d" % g)
            nprev = {0: 1, 1: 1, 2: 4}[g]
            if T < NT - 1:
                P.dma(DQ, kscr[g][T].t[:, :, :], kc_[:, :, :], kc_, reads=[kc_], writes=[kscr[g][T]])
                P.dma(DQ, vscr[g][T].t[:, :, :, :], vc_[:, :, :, :], vc_, reads=[vc_], writes=[vscr[g][T]])
            first_out = seq - PW[g]
            units_out = []
            if g == 0:
                if T == NT - 1:
                    units_out = [3]
            elif (T + 1) * TT > first_out:
                units_out = [0, 1, 2, 3]
            for u in units_out:
                st = kvst[0]
                kvst_i[0] += 1
                bk = palloc()
                bv = bk.t[:].bitcast(BF16)
                for hp in range(4):
                    TR(bv[:, hp * 128:(hp + 1) * 128], kc_[:, hp, u * 128:(u + 1) * 128], ident_b[:], [kc_, ident_b], [bk])
                CP("act", st[:, 0, :], bv[:, 0:512], [bk], [st])
                pfree(bk)
                CP("pool", st[:, 1, :].rearrange("p (h d) -> p h d", h=8), vc_[:, u, :, 0:64], [vc_], [st])
                rbase = T * TT - first_out
                if g == 0:
                    P.dma(DQ, p_kv[g][s, 0:128, :, :], st[:, :, :], st, reads=[st])
                elif g == 1:
                    dv = p_kv[g][s, rbase:rbase + TT, :, :].rearrange("(w u) c f -> u w c f", u=4)
                    P.dma(DQ, dv[u], st[:, :, :], st, reads=[st])
                else:
                    dv = p_kv[g][s, rbase:rbase + TT, :, :].rearrange("(i u r) c f -> u r i c f", u=4, r=4)
                    for r in range(4):
                        P.dma(DQ, dv[u, r], st[r * 32:(r + 1) * 32, :, :], st, reads=[st])

            stage("pkv%d" % g)
            def cur_k(hp, u, kb=kc_):
                return kb, kb[:, hp, u * 128:(u + 1) * 128]

            def cur_v(h, u, vb=vc_):
                return vb, vb[:, u, h, 0:65]

            for hp in range(4):
                srcs = []
                deltas = []
                if g == 2:
                    deltas = [d_ for d_ in (4, 3, 2, 1) if T - d_ >= 0]
                elif T >= 1:
                    deltas = [1]
                for d_ in deltas:
                    ks = kstr[str_i[0] % NSTR]
                    vs = vstr[str_i[0] % NSTR]
                    str_i[0] += 1
                    Tp = T - d_
                    if g == 0:
                        P.dma(DQ, ks[:, 384:512], kscr[g][Tp].t[:, hp, 384:512], ks, reads=[kscr[g][Tp]], writes=[ks])
                        P.dma(DQ, vs[:, 3, :, :], vscr[g][Tp].t[:, 3, 2 * hp:2 * hp + 2, :], vs, reads=[vscr[g][Tp]], writes=[vs])
                    else:
                        P.dma(DQ, ks[:, :], kscr[g][Tp].t[:, hp, :], ks, reads=[kscr[g][Tp]], writes=[ks])
                        P.dma(DQ, vs[:, :, :, :], vscr[g][Tp].t[:, :, 2 * hp:2 * hp + 2, :], vs, reads=[vscr[g][Tp]], writes=[vs])

                    def sk(hp_, u, ks=ks):
                        return ks, ks[:, u * 128:(u + 1) * 128]

                    def sv(h, u, vs=vs):
                        return vs, vs[:, u, h % 2, 0:65]
                    if g == 0:
                        pass
                    elif g == 1:
                        srcs.append(dict(units=[0, 1, 2, 3], k=sk, v=sv, mask=0))
                    else:
                        srcs.append(dict(units=[0, 1, 2, 3], k=sk, v=sv, mask={4: 3, 3: 2, 2: 2, 1: 2}[d_]))
                if g == 0:
                    if T >= 1:
                        def ak(hp_, u, ks=ks):
                            if u == 0:
                                return ks, ks[:, 384:512]
                            return cur_k(hp_, u - 1)

                        def av(h, u, vs=vs):
                            if u == 0:
                                return vs, vs[:, 3, h % 2, 0:65]
                            return cur_v(h, u - 1)
                        srcs.append(dict(units=[0, 1, 2, 3], k=ak, v=av, mask=0))
                    else:
                        srcs.append(dict(units=[1, 2, 3], k=lambda hp_, u: cur_k(hp_, u - 1),
                                         v=lambda h, u: cur_v(h, u - 1), mask=0))
                    srcs.append(dict(units=[0, 1, 2, 3], k=cur_k, v=cur_v, mask=1))
                elif g == 1:
                    srcs.append(dict(units=[0, 1, 2, 3], k=cur_k, v=cur_v, mask=1))
                else:
                    srcs.append(dict(units=[0, 1, 2, 3], k=cur_k, v=cur_v, mask=4))

                for hh in range(2):
                    h = 2 * hp + hh
                    pr = slice(64 * hh, 64 * hh + 64)
                    pts = []
                    for src in srcs:
                        sb_ = palloc()
                        for u in src["units"]:
                            kb, kap = src["k"](hp, u)
                            MM(sb_[:, u * 128:(u + 1) * 128], kap[pr, :], qT[pr, hp, u * 128:(u + 1) * 128], True, True,
                               [kb, qT], [sb_])
                        u0 = src["units"][0]
                        pt = PTb[pt_i[0] % NPT]
                        pt_i[0] += 1
                        ACT(pt[:, u0 * 128:512], sb_[:, u0 * 128:512], AF.Exp, [sb_], [pt], scale=0.125)
                        pfree(sb_)
                        nu = 4 - u0
                        mk = masks_b[:, src["mask"], :].unsqueeze(1).to_broadcast([128, nu, 128])
                        ptv = pt[:, u0 * 128:512].rearrange("p (u w) -> p u w", u=nu)
                        TT_("dve" if (pt_i[0] % 2) else "pool", ptv, ptv, mk, ALU.mult, [pt, masks_b], [pt])
                        pts.append(pt)
                    ob = palloc()
                    for u in range(4):
                        contrib = [(src, pt) for src, pt in zip(srcs, pts) if u in src["units"]]
                        for ci, (src, pt) in enumerate(contrib):
                            vb, vap = src["v"](h, u)
                            MM(ob[0:65, u * 128:(u + 1) * 128], vap, pt[:, u * 128:(u + 1) * 128], ci == 0,
                               ci == len(contrib) - 1, [vb, pt], [ob])
                    if g == 0:
                        CP("act", acc[0:65, h, :], ob[0:65, :], [ob], [acc])
                    else:
                        av_ = acc_view(g, h, slice(0, 65))
                        TT_("dve", av_, av_, bank_view(g, ob, slice(0, 65)), ALU.add, [acc, ob], [acc])
                    pfree(ob)
            bd()

        stage("attn")
        P.op("dve", lambda e: e.reciprocal(out=acc[64:65, :, :], in_=acc[64:65, :, :]), [acc], [acc], cost=4400.0)
        for h in range(8):
            bk = palloc()
            MM(bk[0:64, :], sel_f[:, :], acc[0:65, h, :], True, True, [sel_f, acc], [bk])
            TT_("dve", attnT[:, h, :], acc[0:64, h, :], bk[0:64, :], ALU.mult, [acc, bk], [attnT])
            pfree(bk)

        stage("merge")
        if T == 0:
            MSET("pool", stT[:], 0.0, [stT])
            MSET("pool", stz[:], 0.0, [stz])
            MSET("pool", convtail[:], 0.0, [convtail])
        MSET("pool", xdtz[:], 0.0, [xdtz])
        for blk in range(2):
            wz = wblock_in(O_Z + blk * 512, 512)
            for fc in range(4):
                bk = palloc()
                proj_fm(wz, fc * 128, xnT0, bk)
                ACT(sz[:, blk * 4 + fc, :], bk[:], AF.Silu, [bk], [sz])
                pfree(bk)
        for blk in range(3):
            wx = wblock_in(O_XBC + blk * 512, 512)
            for fc in range(4):
                c = blk * 4 + fc
                sg_ = stg[c % 2]
                ca = cacc[c % 2]
                bk = palloc()
                proj_fm(wx, fc * 128, xnT0, bk)
                CP("pool", sg_[:, 0:3], convtail[:, c, :], [convtail], [sg_])
                CP("act", sg_[:, 3:515], bk[:], [bk], [sg_])
                pfree(bk)
                CP("pool", convtail[:, c, :], sg_[:, 512:515], [sg_], [convtail])
                TS("dve", ca[:], sg_[:, 0:512], cw_t[:, c, 0:1], None, ALU.mult, None, [sg_, cw_t], [ca])
                for jj in range(1, 4):
                    STT("dve", ca[:], sg_[:, jj:jj + 512], cw_t[:, c, jj:jj + 1], ca[:], ALU.mult, ALU.add,
                        [sg_, cw_t, ca], [ca])
                ACT(xc[:, c, :], ca[:], AF.Silu, [ca, cb_t], [xc], bias=cb_t[:, c:c + 1])
        stage("conv")
        bd()
        wdt = wblock_in(O_DT, 16)
        for j in range(4):
            bd()
            jb = slice(j * 128, (j + 1) * 128)
            dt_ = sm("dt", 16)
            a_ = sm("a", 16)
            cs_sb = sm("cs", 16)
            lastcs = sm("lastcs", 16)
            dend = sm("dend", 16)
            cdec = sm("cdec", 16)
            dtd = sm("dtd", 16)
            bk = palloc()
            proj_tm(wdt, 16, xnT0, j, bk)
            TT_("dve", dt_[:], bk[:, 0:16], dtb_t[:], ALU.add, [bk, dtb_t], [dt_])
            pfree(bk)
            ACT(dt_[:], dt_[:], AF.Exp, [dt_], [dt_])
            ACT(dt_[:], dt_[:], AF.Ln, [dt_, one_t], [dt_], bias=one_t[:])
            TT_("dve", a_[:], dt_[:], A_t[:], ALU.mult, [dt_, A_t], [a_])
            bk = palloc()
            MM(bk[:, 0:16], U_f[:], a_[:], True, True, [U_f, a_], [bk])
            CP("dve", cs_sb[:], bk[:, 0:16], [bk], [cs_sb])
            pfree(bk)
            bk = palloc()
            for gg in range(2):
                MM(bk[:, gg * 128:(gg + 1) * 128], xc[:, 8 + gg, jb], xc[:, 10 + gg, jb], True, True, [xc], [bk])
            TT_("dve", Gm[:], bk.t[:, 0:256].rearrange("p (g t) -> p g t", g=2),
                masks_b[:, 1, :].unsqueeze(1).to_broadcast([128, 2, 128]), ALU.mult, [bk, masks_b], [Gm])
            pfree(bk)
            for qd in range(4):
                hs = slice(qd * 4, qd * 4 + 4)
                gg = qd // 2
                TT_("pool", Rb[:], a_[:, hs].unsqueeze(2).to_broadcast([128, 4, 128]),
                    U_f[:].unsqueeze(1).to_broadcast([128, 4, 128]), ALU.mult, [a_, U_f], [Rb])
                bk = palloc()
                MM(bk[:], ones_f[:], Rb[:].rearrange("p h t -> p (h t)"), True, True, [ones_f, Rb], [bk])
                bk3 = bk.t[:, :].rearrange("p (h t) -> p h t", h=4)
                TT_("dve", tmpb[:], bk3, cs_sb[:, hs].unsqueeze(2).to_broadcast([128, 4, 128]), ALU.subtract,
                    [bk, cs_sb], [tmpb])
                ACT(ecs[:], bk3, AF.Exp, [bk], [ecs])
                CP("act", lastcs[:, hs], bk3[:, :, 127], [bk], [lastcs])
                pfree(bk)
                ACT(Eb[:], tmpb[:], AF.Exp, [tmpb], [Eb])
                STT("dve", Wb[:, hs, :], Eb[:], 1e30, Gm[:, gg, :].unsqueeze(1).to_broadcast([128, 4, 128]),
                    ALU.min, ALU.mult, [Eb, Gm], [Wb])
                TT_("pool", Cdec[:, hs, :], ecs[:], xc[:, 10 + gg, jb].unsqueeze(1).to_broadcast([128, 4, 128]), ALU.mult,
                    [ecs, xc], [Cdec])
            TT_("dve", dend[:], lastcs[:], cs_sb[:], ALU.subtract, [lastcs, cs_sb], [dend])
            ACT(dend[:], dend[:], AF.Exp, [dend], [dend])
            ACT(cdec[:], lastcs[:], AF.Exp, [lastcs], [cdec])
            TT_("dve", dtd[:], dt_[:], dend[:], ALU.mult, [dt_, dend], [dtd])
            bk = palloc()
            bv = bk.t[:].bitcast(BF16)
            for c in range(8):
                TR(bv[:, c * 128:(c + 1) * 128], xc[:, c, jb], ident_b[:], [xc, ident_b], [bk])
            xv = bv.rearrange("p (c e d) -> p c e d", c=8, e=2)
            xz = xdtz[:].rearrange("p (c e) f -> p c e f", e=2)
            dtv = dt_[:].rearrange("p (c e) -> p c e", e=2)
            for e_ in range(2):
                TT_("dve", xz[:, :, e_, 64 * e_:64 * e_ + 64], xv[:, :, e_, :],
                    dtv[:, :, e_].unsqueeze(2).to_broadcast([128, 8, 64]), ALU.mult, [bk, dt_], [xdtz])
            TT_("dve", xdd[:], bv[:, 0:1024].rearrange("p (h d) -> p h d", h=16),
                dtd[:].unsqueeze(2).to_broadcast([128, 16, 64]), ALU.mult, [bk, dtd], [xdd])
            pfree(bk)
            bk = palloc()
            bv = bk.t[:].bitcast(BF16)
            for gg in range(2):
                TR(bv[:, gg * 128:(gg + 1) * 128], xc[:, 8 + gg, jb], ident_b[:], [xc, ident_b], [bk])
            CP("act", Btok[:].rearrange("p g n -> p (g n)"), bv[:, 0:256], [bk], [Btok])
            pfree(bk)
            for k2 in range(2):
                bk = palloc()
                for cc in range(4):
                    c = k2 * 4 + cc
                    reg = bk[:, cc * 128:(cc + 1) * 128]
                    MM(reg, xdtz[:, 2 * c, :], Wb[:, 2 * c, :], True, False, [xdtz, Wb], [bk])
                    MM(reg, xdtz[:, 2 * c + 1, :], Wb[:, 2 * c + 1, :], False, False, [xdtz, Wb], [bk])
                    MM(reg, stz[:, 2 * c, :], Cdec[:, 2 * c, :], False, False, [stz, Cdec], [bk])
                    MM(reg, stz[:, 2 * c + 1, :], Cdec[:, 2 * c + 1, :], False, True, [stz, Cdec], [bk])
                for cc in range(4):
                    c = k2 * 4 + cc
                    yt = ytmp[cc % 2]
                    STT("dve", yt[:], xc[:, c, jb], D_t[:, c:c + 1], bk[:, cc * 128:(cc + 1) * 128], ALU.mult, ALU.add,
                        [xc, D_t, bk], [yt])
                    TT_("pool", yT[:, c, jb], yt[:], sz[:, c, jb], ALU.mult, [yt, sz], [yT])
                pfree(bk)
            for gg in range(2):
                bk = palloc()
                MM(bk[:], Btok[:, gg, :], xdd[:, 8 * gg:8 * gg + 8, :].rearrange("p h d -> p (h d)"), True, True,
                   [Btok, xdd], [bk])
                sv_ = stT[:, 8 * gg:8 * gg + 8, :]
                TT_("dve", sv_, sv_, cdec[:, 8 * gg:8 * gg + 8].unsqueeze(2).to_broadcast([128, 8, 64]), ALU.mult,
                    [stT, cdec], [stT])
                TT_("dve", sv_, sv_, bk.t[:, :].rearrange("p (h d) -> p h d", h=8), ALU.add, [stT, bk], [stT])
                pfree(bk)
            sz_ = stz[:].rearrange("p (c e) f -> p c e f", e=2)
            st_ = stT[:].rearrange("p (c e) d -> p c e d", e=2)
            for e_ in range(2):
                CP("pool", sz_[:, :, e_, 64 * e_:64 * e_ + 64], st_[:, :, e_, :], [stT], [stz])
        stage("ssd")
        bk = palloc()
        for c in range(8):
            sq = sqb[c % 2]
            TT_("pool", sq[:], yT[:, c, :], yT[:, c, :], ALU.mult, [yT], [sq])
            MM(bk[:], ones_b[:], sq[:], c == 0, c == 7, [ones_b, sq], [bk])
        ACT(rstd_b[:], bk[:], AF.Ln, [bk, eps_t], [rstd_b], scale=1.0 / D, bias=eps_t[:])
        pfree(bk)
        ACT(rstd_b[:], rstd_b[:], AF.Exp, [rstd_b], [rstd_b], scale=-0.5)
        for c in range(8):
            TT_("dve" if c % 2 else "pool", yT[:, c, :], yT[:, c, :], rstd_b[:], ALU.mult, [yT, rstd_b], [yT])
        if T == NT - 1:
            for j_ in range(3):
                P.dma(DQ, p_conv[s, j_, :].rearrange("(c p) -> p c", p=128), convtail[:, :, j_], convtail,
                      reads=[convtail], allow_slow_non_contiguous=True)
            stf = stT[:].rearrange("p h d -> p (h d)")
            for half in range(2):
                bk = palloc()
                for cc in range(4):
                    c = half * 4 + cc
                    TR(bk[:, cc * 128:(cc + 1) * 128], stf[:, c * 128:(c + 1) * 128], ident_f[:], [stT, ident_f], [bk])
                st = kvst[0]
                kvst_i[0] += 1
                stv = st[:].rearrange("p a b -> p (a b)")[:, 0:512]
                CP("act", stv, bk[:], [bk], [st])
                pfree(bk)
                P.dma(DQ, p_ssm[s, half * 512:(half + 1) * 512, :].rearrange("(c q) n -> q c n", q=128),
                      stv.rearrange("p (c n) -> p c n", c=4), st, reads=[st])

        stage("ssdout")
        out_epilogue_phase(lambda half: (wblock_outA(half), wblock_outS(half)), "mix", nw_mix)

        stage("E")
        bd()
        for j in range(4):
            norm_transpose(hb[j], j, xnTp)
        stage("F")
        if nxt is not None:
            phase_A(*nxt)
        bd()
        for blk in range(6):
            ncol = 512 if blk < 5 else 256
            wg = wblock_fi(blk * 512, ncol)
            wu = wblock_fi(FFN_H + blk * 512, ncol)
            for fc in range(ncol // 128):
                c = blk * 4 + fc
                gb = palloc()
                proj_fm(wg, fc * 128, xnTp, gb)
                ub = palloc()
                proj_fm(wu, fc * 128, xnTp, ub)
                sg_ = sgt[c % 2]
                ACT(sg_[:], gb[:], AF.Silu, [gb], [sg_])
                pfree(gb)
                TT_("dve", actT[:, c, :], sg_[:], ub[:], ALU.mult, [sg_, ub], [actT])
                pfree(ub)
        stage("G")
        out_epilogue_phase(None, "ffn", nw_ffn)
        for j in range(4):
            P.dma(DQ, y_prompt[s, t0 + j * 128:t0 + (j + 1) * 128, :], hb[j][:], hb[j], reads=[hb[j]])

    def out_epilogue_phase(wfn, kind, nw):
        ssA = [sm("ssA%d" % j) for j in range(4)]
        ssB = [sm("ssB%d" % j) for j in range(4)]
        for half in range(2):
            bks = [palloc() for _ in range(4)]
            if kind == "mix":
                wA, wS = wfn(half)
                for j in range(4):
                    jb = slice(j * 128, (j + 1) * 128)
                    for h in range(8):
                        MM(bks[j][:], attnT[0:64, h, jb], wA[0:64, h, :], h == 0, False, [attnT, wA], [bks[j]])
                    for c in range(8):
                        MM(bks[j][:], yT[:, c, jb], wS[:, c, :], False, c == 7, [yT, wS], [bks[j]])
            else:
                for (c0, n) in ((0, 8), (8, 8), (16, 6)):
                    w = wblock_fo(c0, n, half)
                    for j in range(4):
                        jb = slice(j * 128, (j + 1) * 128)
                        for ci in range(n):
                            c = c0 + ci
                            MM(bks[j][:], actT[:, c, jb], w[:, ci, :], c == 0, c == NFC - 1, [actT, w], [bks[j]])
            for j in range(4):
                bk = bks[j]
                if half == 0:
                    CP("act", m0[j][:], bk[:], [bk], [m0[j]])
                    ACT(junk[:, 0:512], bk[:], AF.Square, [bk], [junk, ssA[j]], accum_out=ssA[j][:])
                    pfree(bk)
                else:
                    ACT(junk[:, 0:512], bk[:], AF.Square, [bk], [junk, ssB[j]], accum_out=ssB[j][:])
                    tot = sm("tot%d" % (j % 2))
                    rs = sm("ers%d" % (j % 2))
                    TT_("dve", tot[:], ssA[j][:], ssB[j][:], ALU.add, [ssA[j], ssB[j]], [tot])
                    rstd_from(tot[:], tot, rs)
                    t1 = rt1[j % 2]
                    t2 = rt2[j % 2]
                    STT("dve", t1[:], m0[j][:], rs[:], nw[:, 0:512], ALU.mult, ALU.mult, [m0[j], rs, nw], [t1])
                    STT("dve", t2[:], bk[:], rs[:], nw[:, 512:1024], ALU.mult, ALU.mult, [bk, rs, nw], [t2])
                    pfree(bk)
                    TT_("pool", hb[j][:, 0:512], hb[j][:, 0:512], t1[:], ALU.add, [hb[j], t1], [hb[j]])
                    TT_("pool", hb[j][:, 512:1024], hb[j][:, 512:1024], t2[:], ALU.add, [hb[j], t2], [hb[j]])

    bulk = []

    def bulk_drain(k):
        for _ in range(min(k, len(bulk))):
            o, i_, carrier = bulk.pop(0)
            P.dma("act", o, i_, carrier)

    def sample_phase():
        NS = ns
        R = slice(0, NS)
        qscr = P.dram("qscr", [NS, 3, 512], F32)
        dmy = [Buf(None, "dmy%d" % i) for i in range(4)]

        def sub(parent, ap, name):
            v = Buf(ap, name)
            Prog.alias(parent, [v])
            return v

        def f32view(parent, ncols, name, rows=NS):
            t = parent.t
            ap = t[0:rows] if len(t.shape) == 2 else t[0:rows].rearrange("p a b -> p (a b)")
            if ap.dtype != F32:
                ap = ap.bitcast(F32)
            return sub(parent, ap[:, 0:ncols], name)

        for g in range(3):
            wb = WB[g]
            assert wb == 128 * DILS[g], "sample path assumes a full window in the cache"
            step = 256
            for b in range(NS):
                for r0 in range(0, wb - 1, step):
                    n = min(step, wb - 1 - r0)
                    bulk.append((s_kv[g][b, r0:r0 + n, :, :], cache[g][b, r0 + 1:r0 + 1 + n, :, :], dmy[g]))
        for b0 in range(0, NS, 4):
            P.dma("act", s_conv[b0:b0 + 4, 0:2, :], state_conv[b0:b0 + 4, 1:3, :], dmy[3])

        arA.phase()
        proj_s = arA.take("proj_s", [NS, MIX_IN], F32)
        srope = arA.take("srope", [NS, 2, 512], F32)
        kv_t = [arA.take("kv_t0", [128, 2, 512], F32)]
        prod = arA.take("prod", [128, 512], F32)
        arE.phase()
        kv_t.append(arE.take("kv_t1", [128, 2, 512], F32))
        pvx = [arE.take("pvx%d" % i, [128, 520], F32) for i in range(2)]
        cwrow = [arE.take("cwrow%d" % i, [NS, 1536], F32) for i in range(2)]
        arE.phase()
        ffn_s = arE.take("ffn_s", [NS, 2 * FFN_H], F32)
        arB.phase()
        cacc_s = arB.take("cacc_s", [NS, 1536], F32)
        arC.phase()
        nn = arC.take("nn", [NS, 512], F32)
        attn_f = arC.take("attn_f", [NS, 512], F32)
        arC.phase()
        Bd = arC.take("Bd", [NS, 16, 128], F32)
        arD.phase()
        qb_t = [arD.take("qb_t%d" % i, [128, 512], F32) for i in range(2)]
        tq = arD.take("tq", [NS, 512], F32)
        rq = arD.take("rq", [NS, 512], F32)
        arD.phase()
        Cd = arD.take("Cd", [NS, 16, 128], F32)
        sc_rows = [f32view(xnT0, 1536, "sc0"), f32view(xnTp, 1536, "sc1"), f32view(yT, 1536, "sc2")]
        xs_t = sub(hb[0], hb[0].t[R, :], "xs_t")
        h_s = sub(hb[1], hb[1].t[R, :], "h_s")
        dtx = sub(hb[2], hb[2].t[R, :], "dtx")
        dAx = sub(hb[3], hb[3].t[R, :], "dAx")
        st_s = [sub(xin[i], xin[i].t[:, 0:512].rearrange("p (b n) -> p b n", b=4), "st_s%d" % i) for i in range(2)]
        t1s = f32view(junk, 512, "t1s", rows=128)
        t2s = f32view(xnb[0], 512, "t2s", rows=128)
        hnew = [f32view(xnb[1], 512, "hnew0", rows=128)]
        arS = Arena(P, "arS", 4096)
        hnew.append(arS.take("hnew1", [128, 512], F32))
        t3s = arS.take("t3s", [128, 512], F32)
        xsT = P.sbuf("xsT", [128, 8, NS], BF16)
        catT = P.sbuf("catT", [128, 12, NS], BF16)
        actsT = P.sbuf("actsT", [128, NFC, NS], BF16)
        dtxT = P.sbuf("dtxT", [128, 8, NS], F32)
        dAT = P.sbuf("dAT", [128, 8, NS], F32)
        ySST = P.sbuf("ySST", [128, 8, NS], F32)
        esel_t = P.sbuf("esel_t", [128, 16, 16], F32)
        s8 = P.sbuf("s8", [128, 8], F32)
        p8 = P.sbuf("p8", [128, 8], F32)
        snew = P.sbuf("snew", [NS, 8], F32)
        pnew = P.sbuf("pnew", [NS, 8], F32)
        dnew = P.sbuf("dnew", [NS, 8], F32)
        dts = P.sbuf("dts", [NS, 16], F32)
        dAs = P.sbuf("dAs", [NS, 16], F32)
        drow = P.sbuf("drow", [NS, 16], F32)
        ss16 = P.sbuf("ss16", [NS, 1], F32)
        rs16 = P.sbuf("rs16", [NS, 1], F32)
        catb0 = sub(qraw[0], qraw[0].t[R, :], "catb0")
        catb1 = sub(xnb[1], xnb[1].t[R, :], "catb1")
        nb16 = sub(xnb[0], xnb[0].t[R, :], "nb16")
        Prog.alias(catb1, [hnew[0]])
        Prog.alias(nb16, [t2s])

        P.dma(DQ, esel_t[:].rearrange("p a b -> p (a b)"), c_esel[:, :], esel_t, writes=[esel_t])
        P.dma(DQ, srope[:], c_srope[:, :].rearrange("c (o n) -> o c n", o=1).partition_broadcast(NS), srope, writes=[srope])
        P.dma(DQ, drow[:], d_skip[0:1, :].partition_broadcast(NS), drow, writes=[drow])

        def rms16(src_ap, src_buf, dst_ap, dst_buf, n=D):
            ACT(junk[R, 0:n], src_ap, AF.Square, [src_buf], [junk, ss16], accum_out=ss16[:])
            ACT(rs16[:], ss16[:], AF.Ln, [ss16, eps_t], [rs16], scale=1.0 / n, bias=eps_t[R, :])
            ACT(rs16[:], rs16[:], AF.Exp, [rs16], [rs16], scale=-0.5)
            TS("dve", dst_ap, src_ap, rs16[:], None, ALU.mult, None, [src_buf, rs16], [dst_buf])

        def transpose16(src_ap, src_buf, nchunk, dstT):
            bk = palloc()
            bv = bk.t[:].bitcast(BF16)
            for c in range(nchunk):
                TR(bv[:, c * NS:(c + 1) * NS], src_ap[:, c * 128:(c + 1) * 128], ident_b[R, R], [src_buf, ident_b], [bk])
            CP("act", dstT[:, 0:nchunk, :], bv[:, 0:nchunk * NS].rearrange("p (c t) -> p c t", c=nchunk), [bk], [dstT])
            pfree(bk)

        P.dma(DQ, xs_t[:], x_sample[:, :], xs_t, writes=[xs_t])
        rms16(xs_t[:], xs_t, nb16[:], nb16)
        transpose16(nb16, nb16, 8, xsT)
        for c0 in range(0, MIX_IN, 512):
            ncol = min(512, MIX_IN - c0)
            w = wblock_in(c0, ncol)
            bk = palloc()
            for kc in range(8):
                MM(bk[R, 0:ncol], xsT[:, kc, :], w[:, kc, 0:ncol], kc == 0, kc == 7, [xsT, w], [bk])
            CP("act", proj_s[:, c0:c0 + ncol], bk[R, 0:ncol], [bk], [proj_s])
            pfree(bk)
        for g in range(3):
            for which in range(2):
                c0 = g * 1536 + which * 512
                qv = proj_s[:, c0:c0 + 512]
                q3 = qv.rearrange("p (h e d) -> p h e d", h=8, e=2)
                r3 = rq[:].rearrange("p (h e d) -> p h e d", h=8, e=2)
                CP("pool", r3[:, :, 0, :], q3[:, :, 1, :], [proj_s], [rq])
                CP("pool", r3[:, :, 1, :], q3[:, :, 0, :], [proj_s], [rq])
                TT_("dve", tq[:], qv, srope[:, 0, :], ALU.mult, [proj_s, srope], [tq])
                TT_("dve", rq[:], rq[:], srope[:, 1, :], ALU.mult, [rq, srope], [rq])
                TT_("dve", qv, tq[:], rq[:], ALU.add, [tq, rq], [proj_s])
            P.dma(DQ, qscr.t[:, g, :], proj_s[:, g * 1536:g * 1536 + 512], proj_s, reads=[proj_s], writes=[qscr])
            P.dma(DQ, s_kv[g][:, WB[g] - 1, 0, :], proj_s[:, g * 1536 + 512:g * 1536 + 1024], proj_s, reads=[proj_s])
            P.dma(DQ, s_kv[g][:, WB[g] - 1, 1, :], proj_s[:, g * 1536 + 1024:g * 1536 + 1536], proj_s, reads=[proj_s])
        nbank = palloc()
        dbank = palloc()
        it = 0
        total = 3 * NS
        for g in range(3):
            dil = DILS[g]
            for b in range(NS):
                kv = kv_t[it % 2]
                qb = qb_t[it % 2]
                px = pvx[it % 2]
                P.dma(DQ, kv[:], cache[g][b, :, :, :].rearrange("(i d) c f -> i d c f", d=dil)[:, 0, :, :], kv, writes=[kv])
                P.dma(DQ, qb[:], qscr.t[b:b + 1, g, :].partition_broadcast(128), qb, reads=[qscr], writes=[qb])
                TT_("dve", prod[:], kv[:, 0, :], qb[:], ALU.mult, [kv, qb], [prod])
                P.op("dve", lambda e: e.tensor_reduce(out=s8[:], in_=prod[:].rearrange("p (h d) -> p h d", h=8),
                                                      axis=mybir.AxisListType.X, op=ALU.add), [prod], [s8], cost=600.0)
                ACT(px[:, 512:520], s8[:], AF.Exp, [s8], [px], scale=0.125)
                TT_("dve", px[:, 0:512].rearrange("p (h d) -> p h d", h=8), kv[:, 1, :].rearrange("p (h d) -> p h d", h=8),
                    px[:, 512:520].unsqueeze(2).to_broadcast([128, 8, 64]), ALU.mult, [kv, px], [px])
                MM(nbank[R, :], esel_t[:, b, :], px[:, 0:512], it == 0, it == total - 1, [esel_t, px], [nbank])
                MM(dbank[R, 0:8], esel_t[:, b, :], px[:, 512:520], it == 0, it == total - 1, [esel_t, px], [dbank])
                it += 1
        for g in range(3):
            c0 = g * 1536
            TT_("dve", tq[:], proj_s[:, c0:c0 + 512], proj_s[:, c0 + 512:c0 + 1024], ALU.mult, [proj_s], [tq])
            P.op("dve", lambda e: e.tensor_reduce(out=snew[:], in_=tq[:].rearrange("p (h d) -> p h d", h=8),
                                                  axis=mybir.AxisListType.X, op=ALU.add), [tq], [snew])
            ACT(pnew[:], snew[:], AF.Exp, [snew], [pnew], scale=0.125)
            tgt = nn if g == 0 else tq
            TT_("dve", tgt[:].rearrange("p (h d) -> p h d", h=8),
                proj_s[:, c0 + 1024:c0 + 1536].rearrange("p (h d) -> p h d", h=8),
                pnew[:].unsqueeze(2).to_broadcast([NS, 8, 64]), ALU.mult, [proj_s, pnew], [tgt])
            if g == 0:
                CP("dve", dnew[:], pnew[:], [pnew], [dnew])
            else:
                TT_("dve", nn[:], nn[:], tq[:], ALU.add, [nn, tq], [nn])
                TT_("dve", dnew[:], dnew[:], pnew[:], ALU.add, [dnew, pnew], [dnew])
        TT_("dve", nn[:], nn[:], nbank[R, :], ALU.add, [nn, nbank], [nn])
        TT_("dve", dnew[:], dnew[:], dbank[R, 0:8], ALU.add, [dnew, dbank], [dnew])
        pfree(nbank)
        pfree(dbank)
        P.op("dve", lambda e: e.reciprocal(out=dnew[:], in_=dnew[:]), [dnew], [dnew])
        TT_("dve", attn_f[:].rearrange("p (h d) -> p h d", h=8), nn[:].rearrange("p (h d) -> p h d", h=8),
            dnew[:].unsqueeze(2).to_broadcast([NS, 8, 64]), ALU.mult, [nn, dnew], [attn_f])
        CP("dve", catb0[:], attn_f[:], [attn_f], [catb0])

        for j_ in range(3):
            P.dma(DQ, sc_rows[j_][:], state_conv[:, j_, :], sc_rows[j_], writes=[sc_rows[j_]])
        xbc = proj_s[:, O_XBC:O_XBC + 1536]
        P.dma(DQ, s_conv[:, 2, :], xbc, proj_s, reads=[proj_s])
        for j_ in range(4):
            cw = cwrow[j_ % 2]
            P.dma(DQ, cw[:], conv_w[j_:j_ + 1, :].partition_broadcast(NS), cw, writes=[cw])
            src_ap, src_b = (sc_rows[j_][:], sc_rows[j_]) if j_ < 3 else (xbc, proj_s)
            if j_ == 0:
                TT_("dve", cacc_s[:], src_ap, cw[:], ALU.mult, [src_b, cw], [cacc_s])
            else:
                TT_("dve", cw[:], src_ap, cw[:], ALU.mult, [src_b, cw], [cw])
                TT_("dve", cacc_s[:], cacc_s[:], cw[:], ALU.add, [cacc_s, cw], [cacc_s])
        cw = cwrow[0]
        P.dma(DQ, cw[:], conv_b[0:1, :].partition_broadcast(NS), cw, writes=[cw])
        TT_("dve", cacc_s[:], cacc_s[:], cw[:], ALU.add, [cacc_s, cw], [cacc_s])
        ACT(cacc_s[:], cacc_s[:], AF.Silu, [cacc_s], [cacc_s])
        TT_("dve", dts[:], proj_s[:, O_DT:O_DT + 16], dtb_t[R, :], ALU.add, [proj_s, dtb_t], [dts])
        ACT(dts[:], dts[:], AF.Exp, [dts], [dts])
        ACT(dts[:], dts[:], AF.Ln, [dts, one_t], [dts], bias=one_t[R, :])
        TT_("dve", dAs[:], dts[:], A_t[R, :], ALU.mult, [dts, A_t], [dAs])
        ACT(dAs[:], dAs[:], AF.Exp, [dAs], [dAs])
        xs3 = cacc_s[:, 0:1024].rearrange("p (h d) -> p h d", h=16)
        TT_("dve", dtx[:].rearrange("p (h d) -> p h d", h=16), xs3, dts[:].unsqueeze(2).to_broadcast([NS, 16, 64]),
            ALU.mult, [cacc_s, dts], [dtx])
        CP("dve", dAx[:].rearrange("p (h d) -> p h d", h=16), dAs[:].unsqueeze(2).to_broadcast([NS, 16, 64]), [dAs], [dAx])
        for (srcb, dstT) in ((dtx, dtxT), (dAx, dAT)):
            bk = palloc()
            for c in range(8):
                TR(bk[:, c * NS:(c + 1) * NS], srcb[:, c * 128:(c + 1) * 128], ident_f[R, R], [srcb, ident_f], [bk])
            CP("act", dstT[:], bk[:, 0:8 * NS].rearrange("p (c t) -> p c t", c=8), [bk], [dstT])
            pfree(bk)
        idb = ident_f[R, 0:NS].unsqueeze(2).to_broadcast([NS, NS, 128])
        it = 0
        for gg in range(2):
            TT_("dve", Bd[:], cacc_s[:, 1024 + gg * 128:1024 + (gg + 1) * 128].unsqueeze(1).to_broadcast([NS, NS, 128]),
                idb, ALU.mult, [cacc_s, ident_f], [Bd])
            TT_("dve", Cd[:], cacc_s[:, 1280 + gg * 128:1280 + (gg + 1) * 128].unsqueeze(1).to_broadcast([NS, NS, 128]),
                idb, ALU.mult, [cacc_s, ident_f], [Cd])
            for qb_ in range(NS // 4):
                b0 = qb_ * 4
                Bbc = palloc()
                Cbc = palloc()
                MM(Bbc[:], ones_f[R, :], Bd[:, b0:b0 + 4, :].rearrange("p b n -> p (b n)"), True, True, [ones_f, Bd], [Bbc])
                MM(Cbc[:], ones_f[R, :], Cd[:, b0:b0 + 4, :].rearrange("p b n -> p (b n)"), True, True, [ones_f, Cd], [Cbc])
                for cc in range(4):
                    c = gg * 4 + cc
                    st = st_s[it % 2]
                    hn = hnew[it % 2]
                    it += 1
                    P.dma(DQ, st[:], state_ssm[b0:b0 + 4, c * 128:(c + 1) * 128, :].rearrange("b q n -> q b n"), st, writes=[st])
                    TT_("dve", t1s[:].rearrange("p (b n) -> p b n", b=4), st[:],
                        dAT[:, c, b0:b0 + 4].unsqueeze(2).to_broadcast([128, 4, 128]), ALU.mult, [st, dAT], [t1s])
                    TT_("dve", t2s[:].rearrange("p (b n) -> p b n", b=4), Bbc.t[:, :].rearrange("p (b n) -> p b n", b=4),
                        dtxT[:, c, b0:b0 + 4].unsqueeze(2).to_broadcast([128, 4, 128]), ALU.mult, [Bbc, dtxT], [t2s])
                    TT_("pool", hn[:], t1s[:], t2s[:], ALU.add, [t1s, t2s], [hn])
                    P.dma(DQ, s_ssm[b0:b0 + 4, c * 128:(c + 1) * 128, :].rearrange("b q n -> q b n"),
                          hn[:].rearrange("p (b n) -> p b n", b=4), hn, reads=[hn])
                    TT_("dve", t3s[:], hn[:], Cbc[:], ALU.mult, [hn, Cbc], [t3s])
                    P.op("dve", lambda e, c=c, b0=b0: e.tensor_reduce(
                        out=ySST[:, c, b0:b0 + 4], in_=t3s[:].rearrange("p (b n) -> p b n", b=4),
                        axis=mybir.AxisListType.X, op=ALU.add), [t3s], [ySST], cost=600.0)
                pfree(Bbc)
                pfree(Cbc)
        ytok = dtx
        for half in range(2):
            bk = palloc()
            for cc in range(4):
                c = half * 4 + cc
                TR(bk[R, cc * 128:(cc + 1) * 128], ySST[:, c, :], ident_f[:, :], [ySST, ident_f], [bk])
            CP("act", ytok[:, half * 512:(half + 1) * 512], bk[R, :], [bk], [ytok])
            pfree(bk)
        TT_("dve", dAx[:].rearrange("p (h d) -> p h d", h=16), xs3, drow[:].unsqueeze(2).to_broadcast([NS, 16, 64]),
            ALU.mult, [cacc_s, drow], [dAx])
        TT_("dve", ytok[:], ytok[:], dAx[:], ALU.add, [ytok, dAx], [ytok])
        ACT(dAx[:], proj_s[:, O_Z:O_Z + 1024], AF.Silu, [proj_s], [dAx])
        TT_("dve", ytok[:], ytok[:], dAx[:], ALU.mult, [ytok, dAx], [ytok])
        rms16(ytok[:], ytok, catb1[:], catb1)
        transpose16(catb0, catb0, 4, catT)
        bk = palloc()
        bv = bk.t[:].bitcast(BF16)
        for c in range(8):
            TR(bv[:, c * NS:(c + 1) * NS], catb1[:, c * 128:(c + 1) * 128], ident_b[R, R], [catb1, ident_b], [bk])
        CP("act", catT[:, 4:12, :], bv[:, 0:8 * NS].rearrange("p (c t) -> p c t", c=8), [bk], [catT])
        pfree(bk)

        def post(bks, nw, res_in, res_out):
            ACT(junk[R, 0:512], bks[0][R, :], AF.Square, [bks[0]], [junk, ss16], accum_out=ss16[:])
            ACT(junk[R, 512:1024], bks[1][R, :], AF.Square, [bks[1]], [junk, rs16], accum_out=rs16[:])
            TT_("dve", ss16[:], ss16[:], rs16[:], ALU.add, [ss16, rs16], [ss16])
            ACT(rs16[:], ss16[:], AF.Ln, [ss16, eps_t], [rs16], scale=1.0 / D, bias=eps_t[R, :])
            ACT(rs16[:], rs16[:], AF.Exp, [rs16], [rs16], scale=-0.5)
            for half in range(2):
                hs = slice(half * 512, (half + 1) * 512)
                STT("dve", tq[:], bks[half][R, :], rs16[:], nw[R, hs], ALU.mult, ALU.mult, [bks[half], rs16, nw], [tq])
                TT_("dve", res_out[:, hs], res_in[:, hs], tq[:], ALU.add, [res_in, tq], [res_out])
                pfree(bks[half])

        bks = [palloc(), palloc()]
        for half in range(2):
            wA = wnext()
            P.dma(WQ, wA.t[:, 0:4, :], wout_b.t[0:512, half * 512:half * 512 + 512].rearrange("(c p) n -> p c n", p=128), wA,
                  reads=[wout_b], writes=[wA])
            wS = wblock_outS(half)
            for c in range(12):
                w_ap = wA[:, c, :] if c < 4 else wS[:, c - 4, :]
                MM(bks[half][R, :], catT[:, c, :], w_ap, c == 0, c == 11, [catT, wA, wS], [bks[half]])
        post(bks, nw_mix, xs_t, h_s)
        rms16(h_s[:], h_s, nb16[:], nb16)
        transpose16(nb16, nb16, 8, xsT)
        for c0 in range(0, 2 * FFN_H, 512):
            w = wblock_fi(c0, 512)
            bk = palloc()
            for kc in range(8):
                MM(bk[R, :], xsT[:, kc, :], w[:, kc, :], kc == 0, kc == 7, [xsT, w], [bk])
            CP("act", ffn_s[:, c0:c0 + 512], bk[R, :], [bk], [ffn_s])
            pfree(bk)
        ACT(ffn_s[:, 0:FFN_H], ffn_s[:, 0:FFN_H], AF.Silu, [ffn_s], [ffn_s])
        TT_("dve", ffn_s[:, 0:FFN_H], ffn_s[:, 0:FFN_H], ffn_s[:, FFN_H:2 * FFN_H], ALU.mult, [ffn_s], [ffn_s])
        actb = sub(hb[2], hb[2].t[R, :].bitcast(BF16), "actb")
        Prog.alias(actb, [dtx])
        actb2 = sub(hb[3], hb[3].t[R, :].bitcast(BF16), "actb2")
        Prog.alias(actb2, [dAx])
        CP("dve", actb[:, 0:2048], ffn_s[:, 0:2048], [ffn_s], [actb])
        CP("dve", actb2[:, 0:768], ffn_s[:, 2048:FFN_H], [ffn_s], [actb2])
        transpose16(actb, actb, 16, actsT)
        bk = palloc()
        bv = bk.t[:].bitcast(BF16)
        for c in range(6):
            TR(bv[:, c * NS:(c + 1) * NS], actb2[:, c * 128:(c + 1) * 128], ident_b[R, R], [actb2, ident_b], [bk])
        CP("act", actsT[:, 16:22, :], bv[:, 0:6 * NS].rearrange("p (c t) -> p c t", c=6), [bk], [actsT])
        pfree(bk)
        bks = [palloc(), palloc()]
        for half in range(2):
            for (c0, n) in ((0, 8), (8, 8), (16, 6)):
                w = wblock_fo(c0, n, half)
                for ci in range(n):
                    c = c0 + ci
                    MM(bks[half][R, :], actsT[:, c, :], w[:, ci, :], c == 0, c == NFC - 1, [actsT, w], [bks[half]])
        post(bks, nw_ffn, h_s, xs_t)
        P.dma(DQ, y_sample[:, :], xs_t[:], xs_t, reads=[xs_t])

    if do_sample:
        sample_phase()

    for v_ in vcur:
        MSET("pool", v_[:, :, :, 64:65], 1.0, [v_])

    ntiles = nseq * NT
    per_tile = (len(bulk) + ntiles - 1) // ntiles
    tiles = [(s, T) for s in range(nseq) for T in range(NT)]
    phase_A(*tiles[0])
    for i, (s, T) in enumerate(tiles):
        quota[0] = per_tile
        prompt_tile(s, T, tiles[i + 1] if i + 1 < len(tiles) else None)
        bulk_drain(quota[0])
    bulk_drain(len(bulk))

    P.finalize(window=sched_window)
    P.close()
    return nc


_CACHE = {}
OUT_NAMES = ["y_prompt", "y_sample", "p_kv0", "p_kv1", "p_kv2", "p_conv", "p_ssm",
             "s_kv0", "s_kv1", "s_kv2", "s_conv", "s_ssm"]


def make_in_maps(inp, ncores, nseq, ns, seq, past_override=None):
    f = lambda a: np.ascontiguousarray(np.asarray(a, dtype=np.float32))
    consts = host_consts(seq)
    shared = {
        "norm_mix_pre": f(inp["norm_mix_pre"]), "norm_mix_post": f(inp["norm_mix_post"]),
        "norm_ffn_pre": f(inp["norm_ffn_pre"]), "norm_ffn_post": f(inp["norm_ffn_post"]),
        "w_in": f(inp["w_in"][0]), "w_out": f(inp["w_out"][0]),
        "conv_w": f(inp["conv_w"][0]), "conv_b": f(inp["conv_b"]),
        "dt_bias": f(inp["dt_bias"]), "a_log": f(inp["a_log"]), "d_skip": f(inp["d_skip"]),
        "ssd_norm_w": f(inp["ssd_norm_w"]),
        "w_ffn_in": f(inp["w_ffn_in"][0]), "w_ffn_out": f(inp["w_ffn_out"][0]),
    }
    shared.update(consts)
    xs = np.asarray(inp["x_sample"], dtype=np.float32)
    caches = [np.asarray(inp[k], dtype=np.float32) for k in ("cache_kv_w128", "cache_kv_w512", "cache_kv_w2048")]
    maps = []
    for c in range(ncores):
        m = dict(shared)
        m["x_prompt"] = f(inp["x_prompt"][c * nseq:(c + 1) * nseq])
        m["x_sample"] = f(xs[c * ns:(c + 1) * ns, 0, :])
        for g in range(3):
            cg = caches[g][0, c * ns:(c + 1) * ns]
            if past_override is not None:
                cg = cg[:, :min(WINS[g], past_override)]
            m["cache%d" % g] = f(cg.reshape(ns, cg.shape[1], 2, 512))
        m["state_conv"] = f(np.asarray(inp["state_conv"])[0, c * ns:(c + 1) * ns])
        m["state_ssm"] = f(np.asarray(inp["state_ssm"])[0, c * ns:(c + 1) * ns].reshape(ns, 1024, 128))
        maps.append(m)
    return maps


def assemble(results, ncores, nseq, ns, seq, past=PAST):
    cat = lambda k: np.concatenate([np.asarray(r[k]) for r in results], axis=0)
    pw = [min(w, seq) for w in WINS]
    wb = [min(w, past) for w in WINS]
    B = ncores * nseq
    S = ncores * ns
    outs = [
        cat("y_prompt"),
        cat("y_sample").reshape(S, 1, D),
        cat("p_kv0").reshape(1, B, pw[0], 2, NH, HD),
        cat("p_kv1").reshape(1, B, pw[1], 2, NH, HD),
        cat("p_kv2").reshape(1, B, pw[2], 2, NH, HD),
        cat("p_conv").reshape(1, B, 3, 1536),
        cat("p_ssm").reshape(1, B, 16, 64, 128),
        cat("s_kv0").reshape(1, S, wb[0], 2, NH, HD),
        cat("s_kv1").reshape(1, S, wb[1], 2, NH, HD),
        cat("s_kv2").reshape(1, S, wb[2], 2, NH, HD),
        cat("s_conv").reshape(1, S, 3, 1536),
        cat("s_ssm").reshape(1, S, 16, 64, 128),
    ]
    return tuple(np.ascontiguousarray(o, dtype=np.float32) for o in outs)


def kernel(**inp):
    ncores = 8
    B, seq = inp["x_prompt"].shape[0], inp["x_prompt"].shape[1]
    S = inp["x_sample"].shape[0]
    nseq, ns = B // ncores, S // ncores
    key = (nseq, seq, ns)
    if key not in _CACHE:
        _CACHE[key] = build(nseq, seq, ns)
    nc = _CACHE[key]
    maps = make_in_maps(inp, ncores, nseq, ns, seq)
    res = run_bass_kernel_spmd(nc, maps, core_ids=list(range(ncores)))
    return assemble(res.results, ncores, nseq, ns, seq)
```

```python
import math
import numpy as np
import concourse.bass as bass
import concourse.mybir as mybir
from concourse.bass_utils import run_bass_kernel_spmd

F32 = mybir.dt.float32
BF16 = mybir.dt.bfloat16
ALU = mybir.AluOpType
AF = mybir.ActivationFunctionType

ENGS = ("pe", "act", "dve", "pool", "sp")


class Buf:
    __slots__ = ("t", "name", "last_w", "readers", "dsem", "dcount", "aliases", "excl")

    def __init__(self, t, name):
        self.excl = False
        self.t = t
        self.name = name
        self.last_w = None
        self.readers = []
        self.dsem = None
        self.dcount = 0
        self.aliases = []

    def __getitem__(self, k):
        return self.t[k]


class Op:
    __slots__ = ("eng", "emit", "deps", "is_dma", "buf", "dval", "sig", "idx", "cost", "tbl", "nbytes", "seq",
                 "done", "placed")

    def __init__(self, eng, emit, is_dma=False):
        self.eng = eng
        self.emit = emit
        self.deps = []
        self.is_dma = is_dma
        self.buf = None
        self.dval = 0
        self.sig = False
        self.idx = 0
        self.cost = 100.0
        self.tbl = None
        self.nbytes = 0
        self.seq = 0
        self.done = 0.0
        self.placed = False


class Prog:
    def __init__(self, nc):
        self.nc = nc
        self.ops = []
        self._stack = []
        self.nsb = 0
        self.frozen = False

    def sbuf(self, name, shape, dt):
        g = self.nc.sbuf_tensor(name, list(shape), dt)
        t = g.__enter__()
        self._stack.append(g)
        return Buf(t, name)

    def psum(self, name, shape, dt=F32):
        g = self.nc.psum_tensor(name, list(shape), dt)
        t = g.__enter__()
        self._stack.append(g)
        b = Buf(t, name)
        b.excl = True
        return b

    def view(self, ap, name, parent=None):
        b = Buf(ap, name)
        return b

    def dram(self, name, shape, dt):
        t = self.nc.dram_tensor(name, list(shape), dt, kind="Internal")
        return Buf(t, name)

    @staticmethod
    def alias(a, others):
        for o in others:
            a.aliases.append(o)
            o.aliases.append(a)

    def _add(self, op, reads, writes):
        if self.frozen:
            return op
        deps = []
        for b in reads:
            if b.last_w is not None:
                deps.append(b.last_w)
            if b.excl:
                deps.extend(r for r in b.readers if r.eng != op.eng)
        for b in writes:
            if b.last_w is not None:
                deps.append(b.last_w)
            deps.extend(b.readers)
            for a in b.aliases:
                if a.last_w is not None:
                    deps.append(a.last_w)
                deps.extend(a.readers)
        seen = set()
        for d in deps:
            if d is op or id(d) in seen:
                continue
            seen.add(id(d))
            op.deps.append(d)
        for b in reads:
            b.readers.append(op)
        for b in writes:
            b.last_w = op
            b.readers = []
        op.seq = len(self.ops)
        self.ops.append(op)
        return op

    def op(self, eng, emit, reads=(), writes=(), cost=100.0, tbl=None):
        o = Op(eng, emit)
        o.cost = cost
        o.tbl = tbl
        return self._add(o, reads, writes)

    def dma(self, eng, out_ap, in_ap, carrier, reads=(), writes=(), **kw):
        def emit(e):
            return e.dma_start(out=out_ap, in_=in_ap, **kw)
        op = Op(eng, emit, is_dma=True)
        op.buf = carrier
        n = 1
        for d in out_ap.shape:
            n *= d
        op.nbytes = n * (2 if out_ap.dtype == BF16 else 4)
        op.cost = 60.0
        return self._add(op, reads, writes)

    def schedule(self, window):
        by_eng = {e: [] for e in ENGS}
        for i, op in enumerate(self.ops):
            op.placed = False
            by_eng[op.eng].append(op)
        head = {e: 0 for e in ENGS}
        free = {e: 0.0 for e in ENGS}
        cur_tbl = [None]
        dma_pipe = [0.0]
        order = {e: [] for e in ENGS}
        remaining = len(self.ops)
        LAT = 500.0
        while remaining:
            best = None
            for e in ENGS:
                q = by_eng[e]
                h = head[e]
                while h < len(q) and q[h].placed:
                    h += 1
                head[e] = h
                cnt = 0
                i = h
                W = window[e]
                while i < len(q) and cnt < W:
                    op = q[i]
                    i += 1
                    if op.placed:
                        continue
                    cnt += 1
                    ok = True
                    st = free[e]
                    for d in op.deps:
                        if not d.placed:
                            ok = False
                            break
                        t = d.done + ((0.0 if e == "pe" else 120.0) if d.eng == e and not d.is_dma else LAT)
                        if t > st:
                            st = t
                    if not ok:
                        continue
                    if e == "act" and op.tbl is not None and cur_tbl[0] is not None and op.tbl != cur_tbl[0]:
                        st += 1300.0
                    key = (st, op.seq)
                    if best is None or key < best[0]:
                        best = (key, e, op)
            (st, _), e, op = best
            op.placed = True
            remaining -= 1
            order[e].append(op)
            if op.is_dma:
                free[e] = st + op.cost
                t0 = max(st + 1500.0, dma_pipe[0])
                dma_pipe[0] = t0 + op.nbytes / 300.0
                op.done = dma_pipe[0] + 500.0
            else:
                free[e] = st + op.cost
                op.done = st + op.cost
                if e == "act" and op.tbl is not None:
                    cur_tbl[0] = op.tbl
        self.ops = []
        for e in ENGS:
            self.ops.extend(order[e])
        self.est_ns = max(free.values())
        return order

    def finalize(self, window=None):
        nc = self.nc
        if window is not None:
            self.schedule(window)
        for op in self.ops:
            for d in op.deps:
                if d.is_dma:
                    continue
                if d.eng == op.eng and d.eng == "pe":
                    continue
                d.sig = True
        cnt = {e: 0 for e in ENGS}
        for op in sorted(self.ops, key=lambda o: o.seq):
            if op.is_dma:
                b = op.buf
                b.dcount += 16
                op.dval = b.dcount
        for op in self.ops:
            if (not op.is_dma) and op.sig:
                cnt[op.eng] += 1
                op.idx = cnt[op.eng]
        esem = {}
        for e in ENGS:
            g = nc.semaphore("es_" + e)
            esem[e] = g.__enter__()
            self._stack.append(g)
        nsem = 0
        for op in self.ops:
            if op.is_dma and op.buf.dsem is None:
                g = nc.semaphore("ds%d" % nsem)
                nsem += 1
                op.buf.dsem = g.__enter__()
                self._stack.append(g)
        by_eng = {e: [] for e in ENGS}
        for op in self.ops:
            by_eng[op.eng].append(op)
        all_dma_bufs = []
        seenb = set()
        for op in self.ops:
            if op.is_dma and id(op.buf) not in seenb:
                seenb.add(id(op.buf))
                all_dma_bufs.append(op.buf)

        def run_engine(ename):
            def body(e):
                known = {x: 0 for x in ENGS}
                knownd = {}
                for op in by_eng[ename]:
                    for d in op.deps:
                        if d.is_dma:
                            k = id(d.buf)
                            if knownd.get(k, 0) < d.dval:
                                e.wait_ge(d.buf.dsem, d.dval)
                                knownd[k] = d.dval
                        else:
                            if d.eng == ename and ename == "pe":
                                continue
                            if known[d.eng] < d.idx:
                                e.wait_ge(esem[d.eng], d.idx)
                                known[d.eng] = d.idx
                    ins = op.emit(e)
                    if op.is_dma:
                        ins.then_inc(op.buf.dsem, 16)
                    elif op.sig:
                        ins.then_inc(esem[ename], 1)
                if ename == "sp":
                    for b in all_dma_bufs:
                        e.wait_ge(b.dsem, b.dcount)
                    for x in ENGS:
                        if x != "sp" and cnt[x] > 0:
                            e.wait_ge(esem[x], cnt[x])
            return body

        with nc.Block() as block:
            block.tensor(run_engine("pe"))
            block.scalar(run_engine("act"))
            block.vector(run_engine("dve"))
            block.gpsimd(run_engine("pool"))
            block.sync(run_engine("sp"))

    def close(self):
        while self._stack:
            g = self._stack.pop()
            g.__exit__(None, None, None)


class Arena:
    def __init__(self, P, name, nbytes):
        self.buf = P.sbuf(name, [128, nbytes // 2], BF16)
        self.items = []
        self.off = 0
        self.nbytes = nbytes

    def phase(self):
        self.off = 0

    def take(self, name, shape, dt):
        n = 1
        for d in shape[1:]:
            n *= d
        nb = n * (4 if dt == F32 else 2)
        nb4 = (nb + 3) // 4 * 4
        assert self.off + nb4 <= self.nbytes, (name, self.off, nb4, self.nbytes)
        ap = self.buf.t[0:shape[0], self.off // 2:(self.off + nb) // 2]
        if dt == F32:
            ap = ap.bitcast(F32)
        if len(shape) == 3:
            ap = ap.rearrange("p (a b) -> p a b", a=shape[1])
        elif len(shape) == 4:
            ap = ap.rearrange("p (a b c) -> p a b c", a=shape[1], b=shape[2])
        v = Buf(ap, name)
        for (lo, hi, o) in self.items:
            if lo < self.off + nb4 and self.off < hi:
                v.aliases.append(o)
                o.aliases.append(v)
        self.items.append((self.off, self.off + nb4, v))
        self.off += nb4
        return v


D = 1024
TT = 512
HD = 64
NH = 8
DILS = (1, 4, 16)
WINS = (128, 512, 2048)
QKV = 4608
O_Z = 4608
O_XBC = 5632
O_DT = 7168
MIX_IN = 7184
FFN_H = 2816
NFC = 22
PAST = 8192
EPS = 1e-6


def host_consts(seq):
    c = {}
    c["c_ident"] = np.eye(128, dtype=np.float32)
    pm = np.zeros((128, 128), np.float32)
    for dp in range(128):
        d = (dp // 64) * 64 + ((dp % 64) + 32) % 64
        pm[d, dp] = 1.0
    c["c_perm"] = pm
    k = np.arange(128)[:, None]
    q = np.arange(128)[None, :]
    bd = (k // 32) == (q // 32)
    masks = np.stack([
        (k >= q), (k <= q), bd, bd & ((k % 32) >= (q % 32)), bd & ((k % 32) <= (q % 32)),
    ]).astype(np.float32)
    c["c_masks"] = masks
    half = HD // 2
    inv_freq = (np.float32(10000.0) ** (-np.arange(half, dtype=np.float32) / np.float32(half))).astype(np.float32)
    pos = np.arange(seq, dtype=np.float32)
    ang = (pos[:, None] * inv_freq[None, :]).astype(np.float32)
    cosv = np.cos(ang).astype(np.float32)
    sinv = np.sin(ang).astype(np.float32)
    p = np.arange(128)
    fidx = p % 32
    sign = np.where((p % 64) < 32, -1.0, 1.0).astype(np.float32)
    rope = np.zeros((3, 2, 128, seq), np.float32)
    nt = seq // TT
    for g in range(3):
        perm = np.zeros(seq, np.int64)
        for T in range(nt):
            for u in range(4):
                w = np.arange(128)
                if g == 0:
                    tau = 128 * u + w
                elif g == 1:
                    tau = 4 * w + u
                else:
                    tau = 16 * (w % 32) + 4 * u + (w // 32)
                perm[T * TT + u * 128 + w] = T * TT + tau
        rope[g, 0] = cosv[perm][:, fidx].T
        rope[g, 1] = (sinv[perm][:, fidx].T) * sign[:, None]
    c["c_rope"] = rope
    sel = np.zeros((65, 64), np.float32)
    sel[64, :] = 1.0
    c["c_sel"] = sel
    angs = (np.float32(PAST) * inv_freq).astype(np.float32)
    col = np.arange(512)
    cs = np.cos(angs).astype(np.float32)[col % 32]
    sn = np.sin(angs).astype(np.float32)[col % 32] * np.where((col % 64) < 32, -1.0, 1.0)
    c["c_srope"] = np.stack([cs, sn]).astype(np.float32)
    dl = np.zeros((16, 16, 128), np.float32)
    for b in range(16):
        dl[b, b, :] = 1.0
    c["c_delta"] = dl.reshape(16, 2048)
    es = np.zeros((128, 16, 16), np.float32)
    for b in range(16):
        es[:, b, b] = 1.0
    c["c_esel"] = es.reshape(128, 256)
    return c


def build(nseq, seq, ns, do_sample=True, debug=None, stop=None, past=PAST,
          sched_window={"pe": 64, "act": 32, "dve": 32, "pool": 24, "sp": 64}):
    nc = bass.Bass("TRN2", target_bir_lowering=False)
    P = Prog(nc)

    def stage(name):
        if stop is not None and name == stop:
            P.frozen = True
    NT = seq // TT
    WB = [min(w, past) for w in WINS]
    PW = [min(w, seq) for w in WINS]

    def din(name, shape):
        return nc.dram_tensor(name, list(shape), F32, kind="ExternalInput")

    def dout(name, shape):
        return nc.dram_tensor(name, list(shape), F32, kind="ExternalOutput")

    x_prompt = din("x_prompt", [nseq, seq, D])
    x_sample = din("x_sample", [ns, D])
    cache = [din("cache%d" % g, [ns, WB[g], 2, 512]) for g in range(3)]
    state_conv = din("state_conv", [ns, 3, 1536])
    state_ssm = din("state_ssm", [ns, 1024, 128])
    norm_mix_pre = din("norm_mix_pre", [1, D])
    norm_mix_post = din("norm_mix_post", [1, D])
    norm_ffn_pre = din("norm_ffn_pre", [1, D])
    norm_ffn_post = din("norm_ffn_post", [1, D])
    w_in = din("w_in", [D, MIX_IN])
    w_out = din("w_out", [1536, D])
    conv_w = din("conv_w", [4, 1536])
    conv_b = din("conv_b", [1, 1536])
    dt_bias = din("dt_bias", [1, 16])
    a_log = din("a_log", [1, 16])
    d_skip = din("d_skip", [1, 16])
    ssd_norm_w = din("ssd_norm_w", [1, D])
    w_ffn_in = din("w_ffn_in", [D, 2 * FFN_H])
    w_ffn_out = din("w_ffn_out", [FFN_H, D])
    c_ident = din("c_ident", [128, 128])
    c_perm = din("c_perm", [128, 128])
    c_masks = din("c_masks", [5, 128, 128])
    c_rope = din("c_rope", [3, 2, 128, seq])
    c_sel = din("c_sel", [65, 64])
    c_srope = din("c_srope", [2, 512])
    c_delta = din("c_delta", [16, 2048])
    c_esel = din("c_esel", [128, 256])

    y_prompt = dout("y_prompt", [nseq, seq, D])
    y_sample = dout("y_sample", [ns, D])
    p_kv = [dout("p_kv%d" % g, [nseq, PW[g], 2, 512]) for g in range(3)]
    p_conv = dout("p_conv", [nseq, 3, 1536])
    p_ssm = dout("p_ssm", [nseq, 1024, 128])
    s_kv = [dout("s_kv%d" % g, [ns, WB[g], 2, 512]) for g in range(3)]
    s_conv = dout("s_conv", [ns, 3, 1536])
    s_ssm = dout("s_ssm", [ns, 1024, 128])
    dbg = {}
    if debug:
        for nm, shp in debug.items():
            dbg[nm] = dout("dbg_" + nm, shp)

    win_b = P.dram("win_b", [D, MIX_IN], BF16)
    wout_b = P.dram("wout_b", [1536, D], BF16)
    wfi_b = P.dram("wfi_b", [D, 2 * FFN_H], BF16)
    wfo_b = P.dram("wfo_b", [FFN_H, D], BF16)
    kscr = [[P.dram("kscr%d_%d" % (g, T), [128, 4, 512], BF16) for T in range(NT)] for g in range(3)]
    vscr = [[P.dram("vscr%d_%d" % (g, T), [128, 4, 8, 65], BF16) for T in range(NT)] for g in range(3)]

    def fsz(ap):
        n = 1
        for d in ap.shape[1:]:
            n *= d
        return n

    def vcost(eng, ap):
        n = fsz(ap)
        if eng == "pool":
            return 100.0 + 2.1 * n
        if eng == "act":
            return 220.0 + 0.85 * n
        return 70.0 + 1.0 * n

    def ACT(out, in_, func, reads, writes, **kw):
        tbl = "silu" if func == AF.Silu else ("exp" if func in (AF.Exp, AF.Ln) else None)
        return P.op("act", lambda e: e.activation(out=out, in_=in_, func=func, **kw), reads, writes,
                    cost=vcost("act", in_), tbl=tbl)

    def TT_(eng, out, in0, in1, op, reads, writes):
        return P.op(eng, lambda e: e.tensor_tensor(out=out, in0=in0, in1=in1, op=op), reads, writes, cost=vcost(eng, out))

    def TS(eng, out, in0, s1, s2, op0, op1, reads, writes):
        if op1 is None:
            return P.op(eng, lambda e: e.tensor_scalar(out=out, in0=in0, scalar1=s1, scalar2=None, op0=op0), reads, writes,
                        cost=vcost(eng, out))
        return P.op(eng, lambda e: e.tensor_scalar(out=out, in0=in0, scalar1=s1, scalar2=s2, op0=op0, op1=op1), reads, writes,
                    cost=vcost(eng, out))

    def STT(eng, out, in0, scalar, in1, op0, op1, reads, writes):
        return P.op(eng, lambda e: e.scalar_tensor_tensor(out=out, in0=in0, scalar=scalar, in1=in1, op0=op0, op1=op1), reads, writes,
                    cost=vcost(eng, out))

    def CP(eng, out, in_, reads, writes):
        if eng == "act":
            return P.op("act", lambda e: e.copy(out=out, in_=in_), reads, writes, cost=vcost("act", out))
        return P.op(eng, lambda e: e.tensor_copy(out=out, in_=in_), reads, writes, cost=vcost(eng, out))

    def MM(out, lhsT, rhs, start, stop, reads, writes):
        c = max(64, fsz(rhs)) * 0.42 * (4.0 if lhsT.dtype == F32 else 1.0) + 8.0
        return P.op("pe", lambda e: e.matmul(out, lhsT=lhsT, rhs=rhs, start=start, stop=stop), reads, writes, cost=c)

    def TR(out, in_, ident, reads, writes):
        c = max(64, fsz(in_)) * 0.42 * (4.0 if in_.dtype == F32 else 1.0) + 8.0
        return P.op("pe", lambda e: e.transpose(out=out, in_=in_, identity=ident), reads, writes, cost=c)

    def MSET(eng, ap, val, writes):
        return P.op(eng, lambda e: e.memset(ap, val), (), writes, cost=vcost(eng, ap) * 0.5)

    DQ = "sp"
    WQ = "sp"

    banks = [P.psum("bank%d" % i, [128, 512], F32) for i in range(8)]
    held = set()
    lru = list(range(8))

    def palloc():
        for i in lru:
            if i not in held:
                held.add(i)
                lru.remove(i)
                lru.append(i)
                return banks[i]
        raise RuntimeError("out of PSUM banks")

    def pfree(b):
        held.discard(banks.index(b))

    cst = P.sbuf("cst_f", [128, 128], F32)
    ident_f = P.sbuf("ident_f", [128, 128], F32)
    ident_b = P.sbuf("ident_b", [128, 128], BF16)
    perm_b = P.sbuf("perm_b", [128, 128], BF16)
    masks_b = P.sbuf("masks_b", [128, 1, 128], BF16)
    U_f = P.sbuf("U_f", [128, 128], F32)
    ones_f = P.sbuf("ones_f", [128, 128], F32)
    ones_b = P.sbuf("ones_b", [128, 128], BF16)
    sel_f = P.sbuf("sel_f", [65, 64], F32)
    eps_t = P.sbuf("eps_t", [128, 1], F32)
    one_t = P.sbuf("one_t", [128, 1], F32)
    nw_mix = P.sbuf("nw_mix", [128, D], F32)
    nw_ffn = P.sbuf("nw_ffn", [128, D], F32)
    nwp = P.sbuf("nwp", [128, 3, 8], F32)
    cw_t = P.sbuf("cw_t", [128, 12, 4], F32)
    cb_t = P.sbuf("cb_t", [128, 12], F32)
    dtb_t = P.sbuf("dtb_t", [128, 16], F32)
    A_t = P.sbuf("A_t", [128, 16], F32)
    D_t = P.sbuf("D_t", [128, 8], F32)

    P.dma(DQ, ident_f[:], c_ident[:, :], ident_f, writes=[ident_f])
    CP("dve", ident_b[:], ident_f[:], [ident_f], [ident_b])
    P.dma(DQ, cst[:], c_perm[:, :], cst, writes=[cst])
    CP("dve", perm_b[:], cst[:], [cst], [perm_b])
    negm_b = P.sbuf("negm_b", [128, 5, 128], BF16)
    for i in range(5):
        P.dma(DQ, cst[:], c_masks[i, :, :], cst, writes=[cst])
        if i == 1:
            CP("dve", masks_b[:, 0, :], cst[:], [cst], [masks_b])
        TS("dve", negm_b[:, i, :], cst[:], -1.0, 30000.0, ALU.add, ALU.mult, [cst], [negm_b])
    P.dma(DQ, U_f[:], c_masks[1, :, :], U_f, writes=[U_f])
    MSET("dve", ones_f[:], 1.0, [ones_f])
    MSET("dve", ones_b[:], 1.0, [ones_b])
    MSET("dve", eps_t[:], EPS, [eps_t])
    MSET("dve", one_t[:], 1.0, [one_t])
    P.dma(DQ, sel_f[:], c_sel[:, :], sel_f, writes=[sel_f])
    P.dma(DQ, nw_mix[:], norm_mix_post[0:1, :].partition_broadcast(128), nw_mix, writes=[nw_mix])
    P.dma(DQ, nw_ffn[:], norm_ffn_post[0:1, :].partition_broadcast(128), nw_ffn, writes=[nw_ffn])
    for i, src in enumerate((norm_mix_pre, norm_ffn_pre, ssd_norm_w)):
        P.dma(DQ, nwp[:, i, :], src[0, :].rearrange("(k p) -> p k", p=128), nwp, writes=[nwp],
              allow_slow_non_contiguous=True)
    for j_ in range(4):
        P.dma(DQ, cw_t[:, :, j_], conv_w[j_, :].rearrange("(c p) -> p c", p=128), cw_t, writes=[cw_t],
              allow_slow_non_contiguous=True)
    P.dma(DQ, cb_t[:], conv_b[0, :].rearrange("(c p) -> p c", p=128), cb_t, writes=[cb_t],
          allow_slow_non_contiguous=True)
    P.dma(DQ, dtb_t[:], dt_bias[0:1, :].partition_broadcast(128), dtb_t, writes=[dtb_t])
    P.dma(DQ, A_t[:], a_log[0:1, :].partition_broadcast(128), A_t, writes=[A_t])
    ACT(A_t[:], A_t[:], AF.Exp, [A_t], [A_t])
    TS("dve", A_t[:], A_t[:], -1.0, None, ALU.mult, None, [A_t], [A_t])
    dsk2 = d_skip[0, :].rearrange("(c e) -> e c", e=2)
    for e_ in range(2):
        P.dma(DQ, D_t[64 * e_:64 * e_ + 64, :], dsk2[e_:e_ + 1, :].partition_broadcast(64), D_t, writes=[D_t],
              allow_slow_non_contiguous=True)

    stage("consts")
    NRING = 3
    wring = [P.sbuf("wring%d" % i, [128, 8, 512], BF16) for i in range(NRING)]
    wr_i = [0]

    def wnext():
        b = wring[wr_i[0] % NRING]
        wr_i[0] += 1
        return b

    xin = [P.sbuf("xin%d" % i, [128, D], F32) for i in range(2)]
    xnb = [P.sbuf("xnb%d" % i, [128, D], BF16) for i in range(2)]
    hb = [P.sbuf("hb%d" % j, [128, D], F32) for j in range(4)]
    xnT0 = P.sbuf("xnT0", [128, 8, 512], BF16)
    xnTp = P.sbuf("xnTp", [128, 8, 512], BF16)
    qraw = [P.sbuf("qraw%d" % i, [128, 512], BF16) for i in range(2)]
    vcur = [P.sbuf("vcur%d" % i, [128, 4, 8, 65], BF16) for i in range(2)]
    arK = Arena(P, "arK", 4096)
    kvst = [arK.take("kvst0", [128, 2, 512], F32)]
    arK.phase()
    junk = arK.take("junk", [128, D], BF16)
    kvst_i = [0]
    convtail = P.sbuf("convtail", [128, 12, 3], F32)
    yT = P.sbuf("yT", [128, 8, 512], BF16)
    stT = P.sbuf("stT", [128, 16, 64], F32)
    stz = P.sbuf("stz", [128, 16, 128], BF16)
    arA = Arena(P, "arA", 39936)
    acc = arA.take("acc", [128, 8, 512], F32)
    NPT = 6
    PTb = [arA.take("PT%d" % i, [128, 512], BF16) for i in range(NPT)]
    pt_i = [0]
    NSTR = 8
    kstr = [arA.take("kstr%d" % i, [128, 512], BF16) for i in range(NSTR)]
    vstr = [arA.take("vstr%d" % i, [128, 4, 2, 65], BF16) for i in range(NSTR)]
    arA.phase()
    stg = [arA.take("stg%d" % i, [128, 515], F32) for i in range(2)]
    cacc = [arA.take("cacc%d" % i, [128, 512], F32) for i in range(2)]
    xdtz = arA.take("xdtz", [128, 16, 128], BF16)
    xdd = arA.take("xdd", [128, 16, 64], BF16)
    Btok = arA.take("Btok", [128, 2, 128], BF16)
    Rb = arA.take("Rb", [128, 4, 128], F32)
    Eb = arA.take("Eb", [128, 4, 128], F32)
    ecs = arA.take("ecs", [128, 4, 128], F32)
    Wb = arA.take("Wb", [128, 16, 128], BF16)
    Cdec = arA.take("Cdec", [128, 16, 128], BF16)
    Gm = arA.take("Gm", [128, 2, 128], F32)
    sqb = [arA.take("sqb%d" % i, [128, 512], BF16) for i in range(2)]
    rstd_b = arA.take("rstd_b", [128, 512], F32)
    ytmp = [arA.take("ytmp%d" % i, [128, 128], F32) for i in range(2)]
    arA.phase()
    Rb = [Rb, arA.take("Rb1", [128, 4, 128], F32)]
    Eb = [Eb, arA.take("Eb1", [128, 4, 128], F32)]
    ecs = [ecs, arA.take("ecs1", [128, 4, 128], F32)]
    Gm = [Gm, arA.take("Gm1", [128, 2, 128], F32)]
    arA.phase()
    fst = [arA.take("fst%d" % i, [128, 4, 512], F32) for i in range(4)]
    arB = Arena(P, "arB", 8192)
    kcur = [arB.take("kcur%d" % i, [128, 4, 512], BF16) for i in range(2)]
    arB.phase()
    m0 = [arB.take("m0_%d" % j, [128, 512], F32) for j in range(4)]
    arC = Arena(P, "arC", 8192)
    qT = arC.take("qT", [128, 4, 512], BF16)
    ropet = arC.take("ropet", [128, 2, 512], F32)
    arC.phase()
    attnT = arC.take("attnT", [64, 8, 512], BF16)
    arD = Arena(P, "arD", 8192)
    rt1 = [arD.take("rt1_%d" % i, [128, 512], F32) for i in range(2)]
    rt2 = [arD.take("rt2_%d" % i, [128, 512], F32) for i in range(2)]
    arD.phase()
    sgt = [arD.take("sgt%d" % i, [128, 512], F32) for i in range(2)]
    arE = Arena(P, "arE", 22528)
    actT = arE.take("actT", [128, NFC, 512], BF16)
    arE.phase()
    sz = arE.take("sz", [128, 8, 512], BF16)
    xc = arE.take("xc", [128, 12, 512], BF16)
    small = {}

    def sm(name, cols=1):
        if name not in small:
            small[name] = P.sbuf("sm_" + name, [128, cols], F32)
        return small[name]

    bst = [P.view(wring[i // 2].t[:, 4 * (i % 2):4 * (i % 2) + 4, :], "bst%d" % i) for i in range(6)]
    for i in range(6):
        Prog.alias(wring[i // 2], [bst[i]])
    prep_i = [0]

    def prep_piece(src, dst, r0, c0, ncol, scale_ap):
        i = prep_i[0]
        prep_i[0] += 1
        f = fst[i % 4]
        b = bst[i % 6]
        fv = f.t.rearrange("p a b -> p (a b)")[:, 0:ncol]
        bv = b.t.rearrange("p a b -> p (a b)")[:, 0:ncol]
        P.dma(WQ, fv, src[r0:r0 + 128, c0:c0 + ncol], f, writes=[f])
        eng = ("dve", "act")[i % 2]
        if scale_ap is None:
            CP(eng, bv, fv, [f], [b])
        elif eng == "act":
            ACT(bv, fv, AF.Copy, [f, nwp], [b], scale=scale_ap)
        else:
            TS(eng, bv, fv, scale_ap, None, ALU.mult, None, [f, nwp], [b])
        P.dma(WQ, dst.t[r0:r0 + 128, c0:c0 + ncol], bv, b, reads=[b], writes=[dst])

    for kc in range(8):
        for c0 in range(0, MIX_IN, 2048):
            prep_piece(w_in, win_b, kc * 128, c0, min(2048, MIX_IN - c0), nwp[:, 0, kc:kc + 1])
    for rc in range(12):
        prep_piece(w_out, wout_b, rc * 128, 0, 1024, None if rc < 4 else nwp[:, 2, rc - 4:rc - 3])
    for kc in range(8):
        for c0 in range(0, 2 * FFN_H, 2048):
            prep_piece(w_ffn_in, wfi_b, kc * 128, c0, min(2048, 2 * FFN_H - c0), nwp[:, 1, kc:kc + 1])
    for rc in range(NFC):
        prep_piece(w_ffn_out, wfo_b, rc * 128, 0, 1024, None)

    stage("prep")
    def wblock_in(c0, ncol):
        b = wnext()
        P.dma(WQ, b.t[:, :, 0:ncol], win_b.t[:, c0:c0 + ncol].rearrange("(k p) n -> p k n", p=128), b,
              reads=[win_b], writes=[b])
        return b

    def wblock_fi(c0, ncol):
        b = wnext()
        P.dma(WQ, b.t[:, :, 0:ncol], wfi_b.t[:, c0:c0 + ncol].rearrange("(k p) n -> p k n", p=128), b,
              reads=[wfi_b], writes=[b])
        return b

    def wblock_outA(half):
        b = wnext()
        P.dma(WQ, b.t[0:64, :, :], wout_b.t[0:512, half * 512:half * 512 + 512].rearrange("(h p) n -> p h n", p=64), b,
              reads=[wout_b], writes=[b])
        return b

    def wblock_outS(half):
        b = wnext()
        P.dma(WQ, b.t[:, :, :], wout_b.t[512:1536, half * 512:half * 512 + 512].rearrange("(c p) n -> p c n", p=128), b,
              reads=[wout_b], writes=[b])
        return b

    def wblock_fo(c0, n, half):
        b = wnext()
        P.dma(WQ, b.t[:, 0:n, :], wfo_b.t[c0 * 128:(c0 + n) * 128, half * 512:half * 512 + 512].rearrange("(c p) n -> p c n", p=128), b,
              reads=[wfo_b], writes=[b])
        return b

    nrm_i = [0]

    def rstd_from(ss_ap, ss_buf, out_buf):
        ACT(out_buf[:], ss_ap, AF.Ln, [ss_buf, eps_t], [out_buf], scale=1.0 / D, bias=eps_t[:])
        ACT(out_buf[:], out_buf[:], AF.Exp, [out_buf], [out_buf], scale=-0.5)

    def norm_transpose(src_buf, j, dstT):
        i = nrm_i[0]
        nrm_i[0] += 1
        ss = sm("nss%d" % (i % 2))
        rs = sm("nrs%d" % (i % 2))
        xb = xnb[i % 2]
        ACT(junk[:], src_buf[:], AF.Square, [src_buf], [junk, ss], accum_out=ss[:])
        rstd_from(ss[:], ss, rs)
        ACT(xb[:], src_buf[:], AF.Copy, [src_buf, rs], [xb], scale=rs[:])
        bk = palloc()
        bv = bk.t[:].bitcast(BF16)
        for kc in range(8):
            TR(bv[:, kc * 128:(kc + 1) * 128], xb[:, kc * 128:(kc + 1) * 128], ident_b[:], [xb, ident_b], [bk])
        CP("act", dstT[:, :, j * 128:(j + 1) * 128], bv.rearrange("p (k t) -> p k t", k=8), [bk], [dstT])
        pfree(bk)

    def proj_fm(wb, coff, xT, bank, ncontract=8):
        for kc in range(ncontract):
            MM(bank[:], wb[:, kc, coff:coff + 128], xT[:, kc, :], kc == 0, kc == ncontract - 1, [wb, xT], [bank])

    def proj_tm(wb, ncol, xT, j, bank):
        for kc in range(8):
            MM(bank[:, 0:ncol], xT[:, kc, j * 128:(j + 1) * 128], wb[:, kc, 0:ncol], kc == 0, kc == 7, [wb, xT], [bank])

    rope_i = [0]

    def rope_evac(bank, dest_ap, dest_buf):
        i = rope_i[0]
        rope_i[0] += 1
        qr, t1, t2 = qraw[i % 2], rt1[i % 2], rt2[i % 2]
        stage("r0")
        CP("act", qr[:], bank[:], [bank], [qr])
        stage("r1")
        b2 = palloc()
        MM(b2[:], perm_b[:], qr[:], True, True, [perm_b, qr], [b2])
        stage("r2")
        TT_("dve", t1[:], bank[:], ropet[:, 0, :], ALU.mult, [bank, ropet], [t1])
        stage("r3")
        TT_("dve", t2[:], b2[:], ropet[:, 1, :], ALU.mult, [b2, ropet], [t2])
        pfree(b2)
        stage("r4")
        TT_("pool", dest_ap, t1[:], t2[:], ALU.add, [t1, t2], [dest_buf])

    def acc_view(g, h, rows):
        a = acc.t[rows, h, :]
        if g == 0:
            return a
        if g == 1:
            return a.rearrange("p (w u) -> p u w", u=4)
        return a.rearrange("p (i u r) -> p u r i", u=4, r=4)

    def bank_view(g, bank, rows):
        b = bank.t[rows, :]
        if g == 0:
            return b
        if g == 1:
            return b.rearrange("p (u w) -> p u w", u=4)
        return b.rearrange("p (u r i) -> p u r i", u=4, r=4)

    kv_i = [0]
    str_i = [0]

    quota = [0]

    def bd():
        if quota[0] > 0:
            quota[0] -= 1
            bulk_drain(1)

    def phase_A(s, T):
        t0 = T * TT
        for j in range(4):
            xi = xin[j % 2]
            P.dma(DQ, xi[:], x_prompt[s, t0 + j * 128:t0 + (j + 1) * 128, :], xi, writes=[xi])
            norm_transpose(xi, j, xnT0)

    def prompt_tile(s, T, nxt):
        t0 = T * TT
        for j in range(4):
            P.dma(WQ, hb[j][:], x_prompt[s, t0 + j * 128:t0 + (j + 1) * 128, :], hb[j], writes=[hb[j]])
        stage("A")
        for g in range(3):
            dil = DILS[g]
            if g == 0:
                xT = xnT0
            else:
                xT = xnTp
                for kc in range(8):
                    if g == 1:
                        src = xnT0.t[:, kc, :].rearrange("p (w u) -> p u w", u=4)
                        dst = xnTp.t[:, kc, :].rearrange("p (u w) -> p u w", u=4)
                    else:
                        src = xnT0.t[:, kc, :].rearrange("p (i u r) -> p u r i", u=4, r=4)
                        dst = xnTp.t[:, kc, :].rearrange("p (u r i) -> p u r i", u=4, r=4)
                    CP("act" if kc % 2 else "pool", dst, src, [xnT0], [xnTp])
            P.dma(DQ, ropet[:], c_rope[g, :, :, t0:t0 + TT].rearrange("c p n -> p c n"), ropet, writes=[ropet])
            kc_ = kcur[kv_i[0] % 2]
            vc_ = vcur[kv_i[0] % 2]
            kv_i[0] += 1
            cbase = g * 1536
            stage("b0_%d" % g)
            wq = wblock_in(cbase, 512)
            stage("b1_%d" % g)
            for fc in range(4):
                bk = palloc()
                proj_fm(wq, fc * 128, xT, bk)
                rope_evac(bk, qT[:, fc, :], qT)
                pfree(bk)
            stage("b2_%d" % g)
            wk = wblock_in(cbase + 512, 512)
            for fc in range(4):
                bk = palloc()
                proj_fm(wk, fc * 128, xT, bk)
                rope_evac(bk, kc_[:, fc, :], kc_)
                pfree(bk)
            stage("b3_%d" % g)
            wv = wblock_in(cbase + 1024, 512)
            for u in range(4):
                bk = palloc()
                proj_tm(wv, 512, xT, u, bk)
                CP("act", vc_[:, u, :, 0:64], bk.t[:, :].rearrange("p (h d) -> p h d", h=8), [bk], [vc_])
                pfree(bk)
            stage("proj%d" % g)
            nprev = {0: 1, 1: 1, 2: 4}[g]
            if T < NT - 1:
                P.dma(DQ, kscr[g][T].t[:, :, :], kc_[:, :, :], kc_, reads=[kc_], writes=[kscr[g][T]])
                P.dma(DQ, vscr[g][T].t[:, :, :, :], vc_[:, :, :, :], vc_, reads=[vc_], writes=[vscr[g][T]])
            first_out = seq - PW[g]
            units_out = []
            if g == 0:
                if T == NT - 1:
                    units_out = [3]
            elif (T + 1) * TT > first_out:
                units_out = [0, 1, 2, 3]
            for u in units_out:
                st = kvst[0]
                kvst_i[0] += 1
                bk = palloc()
                bv = bk.t[:].bitcast(BF16)
                for hp in range(4):
                    TR(bv[:, hp * 128:(hp + 1) * 128], kc_[:, hp, u * 128:(u + 1) * 128], ident_b[:], [kc_, ident_b], [bk])
                CP("act", st[:, 0, :], bv[:, 0:512], [bk], [st])
                pfree(bk)
                CP("pool", st[:, 1, :].rearrange("p (h d) -> p h d", h=8), vc_[:, u, :, 0:64], [vc_], [st])
                rbase = T * TT - first_out
                if g == 0:
                    P.dma(DQ, p_kv[g][s, 0:128, :, :], st[:, :, :], st, reads=[st])
                elif g == 1:
                    dv = p_kv[g][s, rbase:rbase + TT, :, :].rearrange("(w u) c f -> u w c f", u=4)
                    P.dma(DQ, dv[u], st[:, :, :], st, reads=[st])
                else:
                    dv = p_kv[g][s, rbase:rbase + TT, :, :].rearrange("(i u r) c f -> u r i c f", u=4, r=4)
                    for r in range(4):
                        P.dma(DQ, dv[u, r], st[r * 32:(r + 1) * 32, :, :], st, reads=[st])

            stage("pkv%d" % g)
            def cur_k(hp, u, kb=kc_):
                return kb, kb[:, hp, u * 128:(u + 1) * 128]

            def cur_v(h, u, vb=vc_):
                return vb, vb[:, u, h, 0:65]

            for hp in range(4):
                srcs = []
                deltas = []
                if g == 2:
                    deltas = [d_ for d_ in (4, 3, 2, 1) if T - d_ >= 0]
                elif T >= 1:
                    deltas = [1]
                for d_ in deltas:
                    ks = kstr[str_i[0] % NSTR]
                    vs = vstr[str_i[0] % NSTR]
                    str_i[0] += 1
                    Tp = T - d_
                    if g == 0:
                        P.dma(DQ, ks[:, 384:512], kscr[g][Tp].t[:, hp, 384:512], ks, reads=[kscr[g][Tp]], writes=[ks])
                        P.dma(DQ, vs[:, 3, :, :], vscr[g][Tp].t[:, 3, 2 * hp:2 * hp + 2, :], vs, reads=[vscr[g][Tp]], writes=[vs])
                    else:
                        P.dma(DQ, ks[:, :], kscr[g][Tp].t[:, hp, :], ks, reads=[kscr[g][Tp]], writes=[ks])
                        P.dma(DQ, vs[:, :, :, :], vscr[g][Tp].t[:, :, 2 * hp:2 * hp + 2, :], vs, reads=[vscr[g][Tp]], writes=[vs])

                    def sk(hp_, u, ks=ks):
                        return ks, ks[:, u * 128:(u + 1) * 128]

                    def sv(h, u, vs=vs):
                        return vs, vs[:, u, h % 2, 0:65]
                    if g == 0:
                        pass
                    elif g == 1:
                        srcs.append(dict(units=[0, 1, 2, 3], k=sk, v=sv, mask=0))
                    else:
                        srcs.append(dict(units=[0, 1, 2, 3], k=sk, v=sv, mask={4: 3, 3: 2, 2: 2, 1: 2}[d_]))
                if g == 0:
                    if T >= 1:
                        def ak(hp_, u, ks=ks):
                            if u == 0:
                                return ks, ks[:, 384:512]
                            return cur_k(hp_, u - 1)

                        def av(h, u, vs=vs):
                            if u == 0:
                                return vs, vs[:, 3, h % 2, 0:65]
                            return cur_v(h, u - 1)
                        srcs.append(dict(units=[0, 1, 2, 3], k=ak, v=av, mask=0))
                    else:
                        srcs.append(dict(units=[1, 2, 3], k=lambda hp_, u: cur_k(hp_, u - 1),
                                         v=lambda h, u: cur_v(h, u - 1), mask=0))
                    srcs.append(dict(units=[0, 1, 2, 3], k=cur_k, v=cur_v, mask=1))
                elif g == 1:
                    srcs.append(dict(units=[0, 1, 2, 3], k=cur_k, v=cur_v, mask=1))
                else:
                    srcs.append(dict(units=[0, 1, 2, 3], k=cur_k, v=cur_v, mask=4))

                for hh in range(2):
                    h = 2 * hp + hh
                    pr = slice(64 * hh, 64 * hh + 64)
                    pts = []
                    for src in srcs:
                        sb_ = palloc()
                        u0 = src["units"][0]
                        nu = 4 - u0
                        MM(sb_[:, u0 * 128:512].rearrange("p (u w) -> p u w", u=nu), ident_b[:, :],
                           negm_b[:, src["mask"], :].unsqueeze(1).to_broadcast([128, nu, 128]), True, False,
                           [ident_b, negm_b], [sb_])
                        for u in src["units"]:
                            kb, kap = src["k"](hp, u)
                            MM(sb_[:, u * 128:(u + 1) * 128], kap[pr, :], qT[pr, hp, u * 128:(u + 1) * 128], False, u == 3,
                               [kb, qT], [sb_])
                        pt = PTb[pt_i[0] % NPT]
                        pt_i[0] += 1
                        ACT(pt[:, u0 * 128:512], sb_[:, u0 * 128:512], AF.Exp, [sb_], [pt], scale=0.125)
                        pfree(sb_)
                        pts.append(pt)
                    ob = palloc()
                    for u in range(4):
                        contrib = [(src, pt) for src, pt in zip(srcs, pts) if u in src["units"]]
                        for ci, (src, pt) in enumerate(contrib):
                            vb, vap = src["v"](h, u)
                            MM(ob[0:65, u * 128:(u + 1) * 128], vap, pt[:, u * 128:(u + 1) * 128], ci == 0,
                               ci == len(contrib) - 1, [vb, pt], [ob])
                    if g == 0:
                        CP("act", acc[0:65, h, :], ob[0:65, :], [ob], [acc])
                    else:
                        av_ = acc_view(g, h, slice(0, 65))
                        TT_("dve", av_, av_, bank_view(g, ob, slice(0, 65)), ALU.add, [acc, ob], [acc])
                    pfree(ob)
            bd()

        stage("attn")
        P.op("dve", lambda e: e.reciprocal(out=acc[64:65, :, :], in_=acc[64:65, :, :]), [acc], [acc], cost=4400.0)
        for h in range(8):
            bk = palloc()
            MM(bk[0:64, :], sel_f[:, :], acc[0:65, h, :], True, True, [sel_f, acc], [bk])
            TT_("dve", attnT[:, h, :], acc[0:64, h, :], bk[0:64, :], ALU.mult, [acc, bk], [attnT])
            pfree(bk)

        stage("merge")
        if T == 0:
            MSET("pool", stT[:], 0.0, [stT])
            MSET("pool", stz[:], 0.0, [stz])
            MSET("pool", convtail[:], 0.0, [convtail])
        MSET("pool", xdtz[:], 0.0, [xdtz])
        for blk in range(2):
            wz = wblock_in(O_Z + blk * 512, 512)
            for fc in range(4):
                bk = palloc()
                proj_fm(wz, fc * 128, xnT0, bk)
                ACT(sz[:, blk * 4 + fc, :], bk[:], AF.Silu, [bk], [sz])
                pfree(bk)
        for blk in range(3):
            wx = wblock_in(O_XBC + blk * 512, 512)
            for fc in range(4):
                c = blk * 4 + fc
                sg_ = stg[c % 2]
                ca = cacc[c % 2]
                bk = palloc()
                proj_fm(wx, fc * 128, xnT0, bk)
                CP("pool", sg_[:, 0:3], convtail[:, c, :], [convtail], [sg_])
                CP("act", sg_[:, 3:515], bk[:], [bk], [sg_])
                pfree(bk)
                CP("pool", convtail[:, c, :], sg_[:, 512:515], [sg_], [convtail])
                TS("dve", ca[:], sg_[:, 0:512], cw_t[:, c, 0:1], None, ALU.mult, None, [sg_, cw_t], [ca])
                for jj in range(1, 4):
                    STT("dve", ca[:], sg_[:, jj:jj + 512], cw_t[:, c, jj:jj + 1], ca[:], ALU.mult, ALU.add,
                        [sg_, cw_t, ca], [ca])
                ACT(xc[:, c, :], ca[:], AF.Silu, [ca, cb_t], [xc], bias=cb_t[:, c:c + 1])
        stage("conv")
        bd()
        wdt = wblock_in(O_DT, 16)
        for j in range(4):
            bd()
            jb = slice(j * 128, (j + 1) * 128)
            pj = str(j % 2)
            dt_ = sm("dt" + pj, 16)
            a_ = sm("a" + pj, 16)
            cs_sb = sm("cs" + pj, 16)
            ncs = sm("ncs" + pj, 16)
            lastcs = sm("lastcs" + pj, 16)
            dend = sm("dend" + pj, 16)
            cdec = sm("cdec" + pj, 16)
            dtd = sm("dtd" + pj, 16)
            Gm_ = Gm[j % 2]
            bk = palloc()
            proj_tm(wdt, 16, xnT0, j, bk)
            TT_("dve", dt_[:], bk[:, 0:16], dtb_t[:], ALU.add, [bk, dtb_t], [dt_])
            pfree(bk)
            ACT(dt_[:], dt_[:], AF.Exp, [dt_], [dt_])
            ACT(dt_[:], dt_[:], AF.Ln, [dt_, one_t], [dt_], bias=one_t[:])
            TT_("dve", a_[:], dt_[:], A_t[:], ALU.mult, [dt_, A_t], [a_])
            bk = palloc()
            MM(bk[:, 0:16], U_f[:], a_[:], True, True, [U_f, a_], [bk])
            CP("dve", cs_sb[:], bk[:, 0:16], [bk], [cs_sb])
            pfree(bk)
            TS("dve", ncs[:], cs_sb[:], -1.0, None, ALU.mult, None, [cs_sb], [ncs])
            bk = palloc()
            for gg in range(2):
                MM(bk[:, gg * 128:(gg + 1) * 128], xc[:, 8 + gg, jb], xc[:, 10 + gg, jb], True, True, [xc], [bk])
            TT_("dve", Gm_[:], bk.t[:, 0:256].rearrange("p (g t) -> p g t", g=2),
                masks_b[:, 0, :].unsqueeze(1).to_broadcast([128, 2, 128]), ALU.mult, [bk, masks_b], [Gm_])
            pfree(bk)
            for qd in range(4):
                hs = slice(qd * 4, qd * 4 + 4)
                gg = qd // 2
                Rb_, Eb_, ecs_ = Rb[qd % 2], Eb[qd % 2], ecs[qd % 2]
                TT_("pool", Rb_[:], a_[:, hs].unsqueeze(2).to_broadcast([128, 4, 128]),
                    U_f[:].unsqueeze(1).to_broadcast([128, 4, 128]), ALU.mult, [a_, U_f], [Rb_])
                bk = palloc()
                MM(bk[:], ones_f[:], Rb_[:].rearrange("p h t -> p (h t)"), True, True, [ones_f, Rb_], [bk])
                bk3 = bk.t[:, :].rearrange("p (h t) -> p h t", h=4)
                for hh in range(4):
                    ACT(Eb_[:, hh, :], bk3[:, hh, :], AF.Exp, [bk, ncs], [Eb_], bias=ncs[:, qd * 4 + hh:qd * 4 + hh + 1])
                ACT(ecs_[:], bk3, AF.Exp, [bk], [ecs_])
                CP("act", lastcs[:, hs], bk3[:, :, 127], [bk], [lastcs])
                pfree(bk)
                STT("dve", Wb[:, hs, :], Eb_[:], 1e30, Gm_[:, gg, :].unsqueeze(1).to_broadcast([128, 4, 128]),
                    ALU.min, ALU.mult, [Eb_, Gm_], [Wb])
                TT_("pool", Cdec[:, hs, :], ecs_[:], xc[:, 10 + gg, jb].unsqueeze(1).to_broadcast([128, 4, 128]), ALU.mult,
                    [ecs_, xc], [Cdec])
            TT_("dve", dend[:], lastcs[:], cs_sb[:], ALU.subtract, [lastcs, cs_sb], [dend])
            ACT(dend[:], dend[:], AF.Exp, [dend], [dend])
            ACT(cdec[:], lastcs[:], AF.Exp, [lastcs], [cdec])
            TT_("dve", dtd[:], dt_[:], dend[:], ALU.mult, [dt_, dend], [dtd])
            bk = palloc()
            bv = bk.t[:].bitcast(BF16)
            for c in range(8):
                TR(bv[:, c * 128:(c + 1) * 128], xc[:, c, jb], ident_b[:], [xc, ident_b], [bk])
            xv = bv.rearrange("p (c e d) -> p c e d", c=8, e=2)
            xz = xdtz[:].rearrange("p (c e) f -> p c e f", e=2)
            dtv = dt_[:].rearrange("p (c e) -> p c e", e=2)
            for e_ in range(2):
                TT_("dve", xz[:, :, e_, 64 * e_:64 * e_ + 64], xv[:, :, e_, :],
                    dtv[:, :, e_].unsqueeze(2).to_broadcast([128, 8, 64]), ALU.mult, [bk, dt_], [xdtz])
            TT_("dve", xdd[:], bv[:, 0:1024].rearrange("p (h d) -> p h d", h=16),
                dtd[:].unsqueeze(2).to_broadcast([128, 16, 64]), ALU.mult, [bk, dtd], [xdd])
            pfree(bk)
            bk = palloc()
            bv = bk.t[:].bitcast(BF16)
            for gg in range(2):
                TR(bv[:, gg * 128:(gg + 1) * 128], xc[:, 8 + gg, jb], ident_b[:], [xc, ident_b], [bk])
            CP("act", Btok[:].rearrange("p g n -> p (g n)"), bv[:, 0:256], [bk], [Btok])
            pfree(bk)
            for k2 in range(2):
                bk = palloc()
                for cc in range(4):
                    c = k2 * 4 + cc
                    reg = bk[:, cc * 128:(cc + 1) * 128]
                    MM(reg, xdtz[:, 2 * c, :], Wb[:, 2 * c, :], True, False, [xdtz, Wb], [bk])
                    MM(reg, xdtz[:, 2 * c + 1, :], Wb[:, 2 * c + 1, :], False, False, [xdtz, Wb], [bk])
                    MM(reg, stz[:, 2 * c, :], Cdec[:, 2 * c, :], False, False, [stz, Cdec], [bk])
                    MM(reg, stz[:, 2 * c + 1, :], Cdec[:, 2 * c + 1, :], False, True, [stz, Cdec], [bk])
                for cc in range(4):
                    c = k2 * 4 + cc
                    yt = ytmp[cc % 2]
                    STT("dve", yt[:], xc[:, c, jb], D_t[:, c:c + 1], bk[:, cc * 128:(cc + 1) * 128], ALU.mult, ALU.add,
                        [xc, D_t, bk], [yt])
                    TT_("pool", yT[:, c, jb], yt[:], sz[:, c, jb], ALU.mult, [yt, sz], [yT])
                pfree(bk)
            for gg in range(2):
                bk = palloc()
                MM(bk[:], Btok[:, gg, :], xdd[:, 8 * gg:8 * gg + 8, :].rearrange("p h d -> p (h d)"), True, True,
                   [Btok, xdd], [bk])
                sv_ = stT[:, 8 * gg:8 * gg + 8, :]
                TT_("dve", sv_, sv_, cdec[:, 8 * gg:8 * gg + 8].unsqueeze(2).to_broadcast([128, 8, 64]), ALU.mult,
                    [stT, cdec], [stT])
                TT_("dve", sv_, sv_, bk.t[:, :].rearrange("p (h d) -> p h d", h=8), ALU.add, [stT, bk], [stT])
                pfree(bk)
            sz_ = stz[:].rearrange("p (c e) f -> p c e f", e=2)
            st_ = stT[:].rearrange("p (c e) d -> p c e d", e=2)
            for e_ in range(2):
                CP("pool", sz_[:, :, e_, 64 * e_:64 * e_ + 64], st_[:, :, e_, :], [stT], [stz])
        stage("ssd")
        bk = palloc()
        for c in range(8):
            sq = sqb[c % 2]
            ACT(sq[:], yT[:, c, :], AF.Square, [yT], [sq])
            MM(bk[:], ones_b[:], sq[:], c == 0, c == 7, [ones_b, sq], [bk])
        ACT(rstd_b[:], bk[:], AF.Ln, [bk, eps_t], [rstd_b], scale=1.0 / D, bias=eps_t[:])
        pfree(bk)
        ACT(rstd_b[:], rstd_b[:], AF.Exp, [rstd_b], [rstd_b], scale=-0.5)
        for c in range(8):
            TT_("dve" if c % 2 else "pool", yT[:, c, :], yT[:, c, :], rstd_b[:], ALU.mult, [yT, rstd_b], [yT])
        if T == NT - 1:
            for j_ in range(3):
                P.dma(DQ, p_conv[s, j_, :].rearrange("(c p) -> p c", p=128), convtail[:, :, j_], convtail,
                      reads=[convtail], allow_slow_non_contiguous=True)
            stf = stT[:].rearrange("p h d -> p (h d)")
            for half in range(2):
                bk = palloc()
                for cc in range(4):
                    c = half * 4 + cc
                    TR(bk[:, cc * 128:(cc + 1) * 128], stf[:, c * 128:(c + 1) * 128], ident_f[:], [stT, ident_f], [bk])
                st = kvst[0]
                kvst_i[0] += 1
                stv = st[:].rearrange("p a b -> p (a b)")[:, 0:512]
                CP("act", stv, bk[:], [bk], [st])
                pfree(bk)
                P.dma(DQ, p_ssm[s, half * 512:(half + 1) * 512, :].rearrange("(c q) n -> q c n", q=128),
                      stv.rearrange("p (c n) -> p c n", c=4), st, reads=[st])

        stage("ssdout")
        out_epilogue_phase(lambda half: (wblock_outA(half), wblock_outS(half)), "mix", nw_mix)

        stage("E")
        bd()
        for j in range(4):
            norm_transpose(hb[j], j, xnTp)
        stage("F")
        if nxt is not None:
            phase_A(*nxt)
        bd()
        for blk in range(6):
            ncol = 512 if blk < 5 else 256
            wg = wblock_fi(blk * 512, ncol)
            wu = wblock_fi(FFN_H + blk * 512, ncol)
            for fc in range(ncol // 128):
                c = blk * 4 + fc
                gb = palloc()
                proj_fm(wg, fc * 128, xnTp, gb)
                ub = palloc()
                proj_fm(wu, fc * 128, xnTp, ub)
                sg_ = sgt[c % 2]
                ACT(sg_[:], gb[:], AF.Silu, [gb], [sg_])
                pfree(gb)
                TT_("dve", actT[:, c, :], sg_[:], ub[:], ALU.mult, [sg_, ub], [actT])
                pfree(ub)
        stage("G")
        out_epilogue_phase(None, "ffn", nw_ffn)
        for j in range(4):
            P.dma(DQ, y_prompt[s, t0 + j * 128:t0 + (j + 1) * 128, :], hb[j][:], hb[j], reads=[hb[j]])

    def out_epilogue_phase(wfn, kind, nw):
        ssA = [sm("ssA%d" % j) for j in range(4)]
        ssB = [sm("ssB%d" % j) for j in range(4)]
        for half in range(2):
            bks = [palloc() for _ in range(4)]
            if kind == "mix":
                wA, wS = wfn(half)
                for j in range(4):
                    jb = slice(j * 128, (j + 1) * 128)
                    for h in range(8):
                        MM(bks[j][:], attnT[0:64, h, jb], wA[0:64, h, :], h == 0, False, [attnT, wA], [bks[j]])
                    for c in range(8):
                        MM(bks[j][:], yT[:, c, jb], wS[:, c, :], False, c == 7, [yT, wS], [bks[j]])
            else:
                for (c0, n) in ((0, 8), (8, 8), (16, 6)):
                    w = wblock_fo(c0, n, half)
                    for j in range(4):
                        jb = slice(j * 128, (j + 1) * 128)
                        for ci in range(n):
                            c = c0 + ci
                            MM(bks[j][:], actT[:, c, jb], w[:, ci, :], c == 0, c == NFC - 1, [actT, w], [bks[j]])
            for j in range(4):
                bk = bks[j]
                if half == 0:
                    CP("act", m0[j][:], bk[:], [bk], [m0[j]])
                    ACT(junk[:, 0:512], bk[:], AF.Square, [bk], [junk, ssA[j]], accum_out=ssA[j][:])
                    pfree(bk)
                else:
                    ACT(junk[:, 0:512], bk[:], AF.Square, [bk], [junk, ssB[j]], accum_out=ssB[j][:])
                    tot = sm("tot%d" % (j % 2))
                    rs = sm("ers%d" % (j % 2))
                    TT_("dve", tot[:], ssA[j][:], ssB[j][:], ALU.add, [ssA[j], ssB[j]], [tot])
                    rstd_from(tot[:], tot, rs)
                    t1 = rt1[j % 2]
                    t2 = rt2[j % 2]
                    STT("dve", t1[:], m0[j][:], rs[:], nw[:, 0:512], ALU.mult, ALU.mult, [m0[j], rs, nw], [t1])
                    STT("dve", t2[:], bk[:], rs[:], nw[:, 512:1024], ALU.mult, ALU.mult, [bk, rs, nw], [t2])
                    pfree(bk)
                    TT_("pool", hb[j][:, 0:512], hb[j][:, 0:512], t1[:], ALU.add, [hb[j], t1], [hb[j]])
                    TT_("pool", hb[j][:, 512:1024], hb[j][:, 512:1024], t2[:], ALU.add, [hb[j], t2], [hb[j]])

    bulk = []

    def bulk_drain(k):
        for _ in range(min(k, len(bulk))):
            o, i_, carrier = bulk.pop(0)
            P.dma("act", o, i_, carrier)

    def sample_phase():
        NS = ns
        R = slice(0, NS)
        qscr = P.dram("qscr", [NS, 3, 512], F32)
        dmy = [Buf(None, "dmy%d" % i) for i in range(4)]

        def sub(parent, ap, name):
            v = Buf(ap, name)
            Prog.alias(parent, [v])
            return v

        def f32view(parent, ncols, name, rows=NS):
            t = parent.t
            ap = t[0:rows] if len(t.shape) == 2 else t[0:rows].rearrange("p a b -> p (a b)")
            if ap.dtype != F32:
                ap = ap.bitcast(F32)
            return sub(parent, ap[:, 0:ncols], name)

        for g in range(3):
            wb = WB[g]
            assert wb == 128 * DILS[g], "sample path assumes a full window in the cache"
            step = 256
            for b in range(NS):
                for r0 in range(0, wb - 1, step):
                    n = min(step, wb - 1 - r0)
                    bulk.append((s_kv[g][b, r0:r0 + n, :, :], cache[g][b, r0 + 1:r0 + 1 + n, :, :], dmy[g]))
        for b0 in range(0, NS, 4):
            P.dma("act", s_conv[b0:b0 + 4, 0:2, :], state_conv[b0:b0 + 4, 1:3, :], dmy[3])

        arA.phase()
        proj_s = arA.take("proj_s", [NS, MIX_IN], F32)
        srope = arA.take("srope", [NS, 2, 512], F32)
        kv_t = [arA.take("kv_t0", [128, 2, 512], F32)]
        prod = arA.take("prod", [128, 512], F32)
        arE.phase()
        kv_t.append(arE.take("kv_t1", [128, 2, 512], F32))
        pvx = [arE.take("pvx%d" % i, [128, 520], F32) for i in range(2)]
        cwrow = [arE.take("cwrow%d" % i, [NS, 1536], F32) for i in range(2)]
        arE.phase()
        ffn_s = arE.take("ffn_s", [NS, 2 * FFN_H], F32)
        arB.phase()
        cacc_s = arB.take("cacc_s", [NS, 1536], F32)
        arC.phase()
        nn = arC.take("nn", [NS, 512], F32)
        attn_f = arC.take("attn_f", [NS, 512], F32)
        arC.phase()
        Bd = arC.take("Bd", [NS, 16, 128], F32)
        arD.phase()
        qb_t = [arD.take("qb_t%d" % i, [128, 512], F32) for i in range(2)]
        tq = arD.take("tq", [NS, 512], F32)
        rq = arD.take("rq", [NS, 512], F32)
        arD.phase()
        Cd = arD.take("Cd", [NS, 16, 128], F32)
        sc_rows = [f32view(xnT0, 1536, "sc0"), f32view(xnTp, 1536, "sc1"), f32view(yT, 1536, "sc2")]
        xs_t = sub(hb[0], hb[0].t[R, :], "xs_t")
        h_s = sub(hb[1], hb[1].t[R, :], "h_s")
        dtx = sub(hb[2], hb[2].t[R, :], "dtx")
        dAx = sub(hb[3], hb[3].t[R, :], "dAx")
        st_s = [sub(xin[i], xin[i].t[:, 0:512].rearrange("p (b n) -> p b n", b=4), "st_s%d" % i) for i in range(2)]
        t1s = f32view(junk, 512, "t1s", rows=128)
        t2s = f32view(xnb[0], 512, "t2s", rows=128)
        hnew = [f32view(xnb[1], 512, "hnew0", rows=128)]
        arS = Arena(P, "arS", 4096)
        hnew.append(arS.take("hnew1", [128, 512], F32))
        t3s = arS.take("t3s", [128, 512], F32)
        xsT = P.sbuf("xsT", [128, 8, NS], BF16)
        catT = P.sbuf("catT", [128, 12, NS], BF16)
        actsT = P.sbuf("actsT", [128, NFC, NS], BF16)
        dtxT = P.sbuf("dtxT", [128, 8, NS], F32)
        dAT = P.sbuf("dAT", [128, 8, NS], F32)
        ySST = P.sbuf("ySST", [128, 8, NS], F32)
        esel_t = P.sbuf("esel_t", [128, 16, 16], F32)
        s8 = P.sbuf("s8", [128, 8], F32)
        p8 = P.sbuf("p8", [128, 8], F32)
        snew = P.sbuf("snew", [NS, 8], F32)
        pnew = P.sbuf("pnew", [NS, 8], F32)
        dnew = P.sbuf("dnew", [NS, 8], F32)
        dts = P.sbuf("dts", [NS, 16], F32)
        dAs = P.sbuf("dAs", [NS, 16], F32)
        drow = P.sbuf("drow", [NS, 16], F32)
        ss16 = P.sbuf("ss16", [NS, 1], F32)
        rs16 = P.sbuf("rs16", [NS, 1], F32)
        catb0 = sub(qraw[0], qraw[0].t[R, :], "catb0")
        catb1 = sub(xnb[1], xnb[1].t[R, :], "catb1")
        nb16 = sub(xnb[0], xnb[0].t[R, :], "nb16")
        Prog.alias(catb1, [hnew[0]])
        Prog.alias(nb16, [t2s])

        P.dma(DQ, esel_t[:].rearrange("p a b -> p (a b)"), c_esel[:, :], esel_t, writes=[esel_t])
        P.dma(DQ, srope[:], c_srope[:, :].rearrange("c (o n) -> o c n", o=1).partition_broadcast(NS), srope, writes=[srope])
        P.dma(DQ, drow[:], d_skip[0:1, :].partition_broadcast(NS), drow, writes=[drow])

        def rms16(src_ap, src_buf, dst_ap, dst_buf, n=D):
            ACT(junk[R, 0:n], src_ap, AF.Square, [src_buf], [junk, ss16], accum_out=ss16[:])
            ACT(rs16[:], ss16[:], AF.Ln, [ss16, eps_t], [rs16], scale=1.0 / n, bias=eps_t[R, :])
            ACT(rs16[:], rs16[:], AF.Exp, [rs16], [rs16], scale=-0.5)
            TS("dve", dst_ap, src_ap, rs16[:], None, ALU.mult, None, [src_buf, rs16], [dst_buf])

        def transpose16(src_ap, src_buf, nchunk, dstT):
            bk = palloc()
            bv = bk.t[:].bitcast(BF16)
            for c in range(nchunk):
                TR(bv[:, c * NS:(c + 1) * NS], src_ap[:, c * 128:(c + 1) * 128], ident_b[R, R], [src_buf, ident_b], [bk])
            CP("act", dstT[:, 0:nchunk, :], bv[:, 0:nchunk * NS].rearrange("p (c t) -> p c t", c=nchunk), [bk], [dstT])
            pfree(bk)

        P.dma(DQ, xs_t[:], x_sample[:, :], xs_t, writes=[xs_t])
        rms16(xs_t[:], xs_t, nb16[:], nb16)
        transpose16(nb16, nb16, 8, xsT)
        for c0 in range(0, MIX_IN, 512):
            ncol = min(512, MIX_IN - c0)
            w = wblock_in(c0, ncol)
            bk = palloc()
            for kc in range(8):
                MM(bk[R, 0:ncol], xsT[:, kc, :], w[:, kc, 0:ncol], kc == 0, kc == 7, [xsT, w], [bk])
            CP("act", proj_s[:, c0:c0 + ncol], bk[R, 0:ncol], [bk], [proj_s])
            pfree(bk)
        for g in range(3):
            for which in range(2):
                c0 = g * 1536 + which * 512
                qv = proj_s[:, c0:c0 + 512]
                q3 = qv.rearrange("p (h e d) -> p h e d", h=8, e=2)
                r3 = rq[:].rearrange("p (h e d) -> p h e d", h=8, e=2)
                CP("pool", r3[:, :, 0, :], q3[:, :, 1, :], [proj_s], [rq])
                CP("pool", r3[:, :, 1, :], q3[:, :, 0, :], [proj_s], [rq])
                TT_("dve", tq[:], qv, srope[:, 0, :], ALU.mult, [proj_s, srope], [tq])
                TT_("dve", rq[:], rq[:], srope[:, 1, :], ALU.mult, [rq, srope], [rq])
                TT_("dve", qv, tq[:], rq[:], ALU.add, [tq, rq], [proj_s])
            P.dma(DQ, qscr.t[:, g, :], proj_s[:, g * 1536:g * 1536 + 512], proj_s, reads=[proj_s], writes=[qscr])
            P.dma(DQ, s_kv[g][:, WB[g] - 1, 0, :], proj_s[:, g * 1536 + 512:g * 1536 + 1024], proj_s, reads=[proj_s])
            P.dma(DQ, s_kv[g][:, WB[g] - 1, 1, :], proj_s[:, g * 1536 + 1024:g * 1536 + 1536], proj_s, reads=[proj_s])
        nbank = palloc()
        dbank = palloc()
        it = 0
        total = 3 * NS
        for g in range(3):
            dil = DILS[g]
            for b in range(NS):
                kv = kv_t[it % 2]
                qb = qb_t[it % 2]
                px = pvx[it % 2]
                P.dma(DQ, kv[:], cache[g][b, :, :, :].rearrange("(i d) c f -> i d c f", d=dil)[:, 0, :, :], kv, writes=[kv])
                P.dma(DQ, qb[:], qscr.t[b:b + 1, g, :].partition_broadcast(128), qb, reads=[qscr], writes=[qb])
                TT_("dve", prod[:], kv[:, 0, :], qb[:], ALU.mult, [kv, qb], [prod])
                P.op("dve", lambda e: e.tensor_reduce(out=s8[:], in_=prod[:].rearrange("p (h d) -> p h d", h=8),
                                                      axis=mybir.AxisListType.X, op=ALU.add), [prod], [s8], cost=600.0)
                ACT(px[:, 512:520], s8[:], AF.Exp, [s8], [px], scale=0.125)
                TT_("dve", px[:, 0:512].rearrange("p (h d) -> p h d", h=8), kv[:, 1, :].rearrange("p (h d) -> p h d", h=8),
                    px[:, 512:520].unsqueeze(2).to_broadcast([128, 8, 64]), ALU.mult, [kv, px], [px])
                MM(nbank[R, :], esel_t[:, b, :], px[:, 0:512], it == 0, it == total - 1, [esel_t, px], [nbank])
                MM(dbank[R, 0:8], esel_t[:, b, :], px[:, 512:520], it == 0, it == total - 1, [esel_t, px], [dbank])
                it += 1
        for g in range(3):
            c0 = g * 1536
            TT_("dve", tq[:], proj_s[:, c0:c0 + 512], proj_s[:, c0 + 512:c0 + 1024], ALU.mult, [proj_s], [tq])
            P.op("dve", lambda e: e.tensor_reduce(out=snew[:], in_=tq[:].rearrange("p (h d) -> p h d", h=8),
                                                  axis=mybir.AxisListType.X, op=ALU.add), [tq], [snew])
            ACT(pnew[:], snew[:], AF.Exp, [snew], [pnew], scale=0.125)
            tgt = nn if g == 0 else tq
            TT_("dve", tgt[:].rearrange("p (h d) -> p h d", h=8),
                proj_s[:, c0 + 1024:c0 + 1536].rearrange("p (h d) -> p h d", h=8),
                pnew[:].unsqueeze(2).to_broadcast([NS, 8, 64]), ALU.mult, [proj_s, pnew], [tgt])
            if g == 0:
                CP("dve", dnew[:], pnew[:], [pnew], [dnew])
            else:
                TT_("dve", nn[:], nn[:], tq[:], ALU.add, [nn, tq], [nn])
                TT_("dve", dnew[:], dnew[:], pnew[:], ALU.add, [dnew, pnew], [dnew])
        TT_("dve", nn[:], nn[:], nbank[R, :], ALU.add, [nn, nbank], [nn])
        TT_("dve", dnew[:], dnew[:], dbank[R, 0:8], ALU.add, [dnew, dbank], [dnew])
        pfree(nbank)
        pfree(dbank)
        P.op("dve", lambda e: e.reciprocal(out=dnew[:], in_=dnew[:]), [dnew], [dnew])
        TT_("dve", attn_f[:].rearrange("p (h d) -> p h d", h=8), nn[:].rearrange("p (h d) -> p h d", h=8),
            dnew[:].unsqueeze(2).to_broadcast([NS, 8, 64]), ALU.mult, [nn, dnew], [attn_f])
        CP("dve", catb0[:], attn_f[:], [attn_f], [catb0])

        for j_ in range(3):
            P.dma(DQ, sc_rows[j_][:], state_conv[:, j_, :], sc_rows[j_], writes=[sc_rows[j_]])
        xbc = proj_s[:, O_XBC:O_XBC + 1536]
        P.dma(DQ, s_conv[:, 2, :], xbc, proj_s, reads=[proj_s])
        for j_ in range(4):
            cw = cwrow[j_ % 2]
            P.dma(DQ, cw[:], conv_w[j_:j_ + 1, :].partition_broadcast(NS), cw, writes=[cw])
            src_ap, src_b = (sc_rows[j_][:], sc_rows[j_]) if j_ < 3 else (xbc, proj_s)
            if j_ == 0:
                TT_("dve", cacc_s[:], src_ap, cw[:], ALU.mult, [src_b, cw], [cacc_s])
            else:
                TT_("dve", cw[:], src_ap, cw[:], ALU.mult, [src_b, cw], [cw])
                TT_("dve", cacc_s[:], cacc_s[:], cw[:], ALU.add, [cacc_s, cw], [cacc_s])
        cw = cwrow[0]
        P.dma(DQ, cw[:], conv_b[0:1, :].partition_broadcast(NS), cw, writes=[cw])
        TT_("dve", cacc_s[:], cacc_s[:], cw[:], ALU.add, [cacc_s, cw], [cacc_s])
        ACT(cacc_s[:], cacc_s[:], AF.Silu, [cacc_s], [cacc_s])
        TT_("dve", dts[:], proj_s[:, O_DT:O_DT + 16], dtb_t[R, :], ALU.add, [proj_s, dtb_t], [dts])
        ACT(dts[:], dts[:], AF.Exp, [dts], [dts])
        ACT(dts[:], dts[:], AF.Ln, [dts, one_t], [dts], bias=one_t[R, :])
        TT_("dve", dAs[:], dts[:], A_t[R, :], ALU.mult, [dts, A_t], [dAs])
        ACT(dAs[:], dAs[:], AF.Exp, [dAs], [dAs])
        xs3 = cacc_s[:, 0:1024].rearrange("p (h d) -> p h d", h=16)
        TT_("dve", dtx[:].rearrange("p (h d) -> p h d", h=16), xs3, dts[:].unsqueeze(2).to_broadcast([NS, 16, 64]),
            ALU.mult, [cacc_s, dts], [dtx])
        CP("dve", dAx[:].rearrange("p (h d) -> p h d", h=16), dAs[:].unsqueeze(2).to_broadcast([NS, 16, 64]), [dAs], [dAx])
        for (srcb, dstT) in ((dtx, dtxT), (dAx, dAT)):
            bk = palloc()
            for c in range(8):
                TR(bk[:, c * NS:(c + 1) * NS], srcb[:, c * 128:(c + 1) * 128], ident_f[R, R], [srcb, ident_f], [bk])
            CP("act", dstT[:], bk[:, 0:8 * NS].rearrange("p (c t) -> p c t", c=8), [bk], [dstT])
            pfree(bk)
        idb = ident_f[R, 0:NS].unsqueeze(2).to_broadcast([NS, NS, 128])
        it = 0
        for gg in range(2):
            TT_("dve", Bd[:], cacc_s[:, 1024 + gg * 128:1024 + (gg + 1) * 128].unsqueeze(1).to_broadcast([NS, NS, 128]),
                idb, ALU.mult, [cacc_s, ident_f], [Bd])
            TT_("dve", Cd[:], cacc_s[:, 1280 + gg * 128:1280 + (gg + 1) * 128].unsqueeze(1).to_broadcast([NS, NS, 128]),
                idb, ALU.mult, [cacc_s, ident_f], [Cd])
            for qb_ in range(NS // 4):
                b0 = qb_ * 4
                Bbc = palloc()
                Cbc = palloc()
                MM(Bbc[:], ones_f[R, :], Bd[:, b0:b0 + 4, :].rearrange("p b n -> p (b n)"), True, True, [ones_f, Bd], [Bbc])
                MM(Cbc[:], ones_f[R, :], Cd[:, b0:b0 + 4, :].rearrange("p b n -> p (b n)"), True, True, [ones_f, Cd], [Cbc])
                for cc in range(4):
                    c = gg * 4 + cc
                    st = st_s[it % 2]
                    hn = hnew[it % 2]
                    it += 1
                    P.dma(DQ, st[:], state_ssm[b0:b0 + 4, c * 128:(c + 1) * 128, :].rearrange("b q n -> q b n"), st, writes=[st])
                    TT_("dve", t1s[:].rearrange("p (b n) -> p b n", b=4), st[:],
                        dAT[:, c, b0:b0 + 4].unsqueeze(2).to_broadcast([128, 4, 128]), ALU.mult, [st, dAT], [t1s])
                    TT_("dve", t2s[:].rearrange("p (b n) -> p b n", b=4), Bbc.t[:, :].rearrange("p (b n) -> p b n", b=4),
                        dtxT[:, c, b0:b0 + 4].unsqueeze(2).to_broadcast([128, 4, 128]), ALU.mult, [Bbc, dtxT], [t2s])
                    TT_("pool", hn[:], t1s[:], t2s[:], ALU.add, [t1s, t2s], [hn])
                    P.dma(DQ, s_ssm[b0:b0 + 4, c * 128:(c + 1) * 128, :].rearrange("b q n -> q b n"),
                          hn[:].rearrange("p (b n) -> p b n", b=4), hn, reads=[hn])
                    TT_("dve", t3s[:], hn[:], Cbc[:], ALU.mult, [hn, Cbc], [t3s])
                    P.op("dve", lambda e, c=c, b0=b0: e.tensor_reduce(
                        out=ySST[:, c, b0:b0 + 4], in_=t3s[:].rearrange("p (b n) -> p b n", b=4),
                        axis=mybir.AxisListType.X, op=ALU.add), [t3s], [ySST], cost=600.0)
                pfree(Bbc)
                pfree(Cbc)
        ytok = dtx
        for half in range(2):
            bk = palloc()
            for cc in range(4):
                c = half * 4 + cc
                TR(bk[R, cc * 128:(cc + 1) * 128], ySST[:, c, :], ident_f[:, :], [ySST, ident_f], [bk])
            CP("act", ytok[:, half * 512:(half + 1) * 512], bk[R, :], [bk], [ytok])
            pfree(bk)
        TT_("dve", dAx[:].rearrange("p (h d) -> p h d", h=16), xs3, drow[:].unsqueeze(2).to_broadcast([NS, 16, 64]),
            ALU.mult, [cacc_s, drow], [dAx])
        TT_("dve", ytok[:], ytok[:], dAx[:], ALU.add, [ytok, dAx], [ytok])
        ACT(dAx[:], proj_s[:, O_Z:O_Z + 1024], AF.Silu, [proj_s], [dAx])
        TT_("dve", ytok[:], ytok[:], dAx[:], ALU.mult, [ytok, dAx], [ytok])
        rms16(ytok[:], ytok, catb1[:], catb1)
        transpose16(catb0, catb0, 4, catT)
        bk = palloc()
        bv = bk.t[:].bitcast(BF16)
        for c in range(8):
            TR(bv[:, c * NS:(c + 1) * NS], catb1[:, c * 128:(c + 1) * 128], ident_b[R, R], [catb1, ident_b], [bk])
        CP("act", catT[:, 4:12, :], bv[:, 0:8 * NS].rearrange("p (c t) -> p c t", c=8), [bk], [catT])
        pfree(bk)

        def post(bks, nw, res_in, res_out):
            ACT(junk[R, 0:512], bks[0][R, :], AF.Square, [bks[0]], [junk, ss16], accum_out=ss16[:])
            ACT(junk[R, 512:1024], bks[1][R, :], AF.Square, [bks[1]], [junk, rs16], accum_out=rs16[:])
            TT_("dve", ss16[:], ss16[:], rs16[:], ALU.add, [ss16, rs16], [ss16])
            ACT(rs16[:], ss16[:], AF.Ln, [ss16, eps_t], [rs16], scale=1.0 / D, bias=eps_t[R, :])
            ACT(rs16[:], rs16[:], AF.Exp, [rs16], [rs16], scale=-0.5)
            for half in range(2):
                hs = slice(half * 512, (half + 1) * 512)
                STT("dve", tq[:], bks[half][R, :], rs16[:], nw[R, hs], ALU.mult, ALU.mult, [bks[half], rs16, nw], [tq])
                TT_("dve", res_out[:, hs], res_in[:, hs], tq[:], ALU.add, [res_in, tq], [res_out])
                pfree(bks[half])

        bks = [palloc(), palloc()]
        for half in range(2):
            wA = wnext()
            P.dma(WQ, wA.t[:, 0:4, :], wout_b.t[0:512, half * 512:half * 512 + 512].rearrange("(c p) n -> p c n", p=128), wA,
                  reads=[wout_b], writes=[wA])
            wS = wblock_outS(half)
            for c in range(12):
                w_ap = wA[:, c, :] if c < 4 else wS[:, c - 4, :]
                MM(bks[half][R, :], catT[:, c, :], w_ap, c == 0, c == 11, [catT, wA, wS], [bks[half]])
        post(bks, nw_mix, xs_t, h_s)
        rms16(h_s[:], h_s, nb16[:], nb16)
        transpose16(nb16, nb16, 8, xsT)
        for c0 in range(0, 2 * FFN_H, 512):
            w = wblock_fi(c0, 512)
            bk = palloc()
            for kc in range(8):
                MM(bk[R, :], xsT[:, kc, :], w[:, kc, :], kc == 0, kc == 7, [xsT, w], [bk])
            CP("act", ffn_s[:, c0:c0 + 512], bk[R, :], [bk], [ffn_s])
            pfree(bk)
        ACT(ffn_s[:, 0:FFN_H], ffn_s[:, 0:FFN_H], AF.Silu, [ffn_s], [ffn_s])
        TT_("dve", ffn_s[:, 0:FFN_H], ffn_s[:, 0:FFN_H], ffn_s[:, FFN_H:2 * FFN_H], ALU.mult, [ffn_s], [ffn_s])
        actb = sub(hb[2], hb[2].t[R, :].bitcast(BF16), "actb")
        Prog.alias(actb, [dtx])
        actb2 = sub(hb[3], hb[3].t[R, :].bitcast(BF16), "actb2")
        Prog.alias(actb2, [dAx])
        CP("dve", actb[:, 0:2048], ffn_s[:, 0:2048], [ffn_s], [actb])
        CP("dve", actb2[:, 0:768], ffn_s[:, 2048:FFN_H], [ffn_s], [actb2])
        transpose16(actb, actb, 16, actsT)
        bk = palloc()
        bv = bk.t[:].bitcast(BF16)
        for c in range(6):
            TR(bv[:, c * NS:(c + 1) * NS], actb2[:, c * 128:(c + 1) * 128], ident_b[R, R], [actb2, ident_b], [bk])
        CP("act", actsT[:, 16:22, :], bv[:, 0:6 * NS].rearrange("p (c t) -> p c t", c=6), [bk], [actsT])
        pfree(bk)
        bks = [palloc(), palloc()]
        for half in range(2):
            for (c0, n) in ((0, 8), (8, 8), (16, 6)):
                w = wblock_fo(c0, n, half)
                for ci in range(n):
                    c = c0 + ci
                    MM(bks[half][R, :], actsT[:, c, :], w[:, ci, :], c == 0, c == NFC - 1, [actsT, w], [bks[half]])
        post(bks, nw_ffn, h_s, xs_t)
        P.dma(DQ, y_sample[:, :], xs_t[:], xs_t, reads=[xs_t])

    if do_sample:
        sample_phase()

    for v_ in vcur:
        MSET("pool", v_[:, :, :, 64:65], 1.0, [v_])

    ntiles = nseq * NT
    per_tile = (len(bulk) + ntiles - 1) // ntiles
    tiles = [(s, T) for s in range(nseq) for T in range(NT)]
    phase_A(*tiles[0])
    for i, (s, T) in enumerate(tiles):
        quota[0] = per_tile
        prompt_tile(s, T, tiles[i + 1] if i + 1 < len(tiles) else None)
        bulk_drain(quota[0])
    bulk_drain(len(bulk))

    P.finalize(window=sched_window)
    P.close()
    return nc


_CACHE = {}
OUT_NAMES = ["y_prompt", "y_sample", "p_kv0", "p_kv1", "p_kv2", "p_conv", "p_ssm",
             "s_kv0", "s_kv1", "s_kv2", "s_conv", "s_ssm"]


def make_in_maps(inp, ncores, nseq, ns, seq, past_override=None):
    f = lambda a: np.ascontiguousarray(np.asarray(a, dtype=np.float32))
    consts = host_consts(seq)
    shared = {
        "norm_mix_pre": f(inp["norm_mix_pre"]), "norm_mix_post": f(inp["norm_mix_post"]),
        "norm_ffn_pre": f(inp["norm_ffn_pre"]), "norm_ffn_post": f(inp["norm_ffn_post"]),
        "w_in": f(inp["w_in"][0]), "w_out": f(inp["w_out"][0]),
        "conv_w": f(inp["conv_w"][0]), "conv_b": f(inp["conv_b"]),
        "dt_bias": f(inp["dt_bias"]), "a_log": f(inp["a_log"]), "d_skip": f(inp["d_skip"]),
        "ssd_norm_w": f(inp["ssd_norm_w"]),
        "w_ffn_in": f(inp["w_ffn_in"][0]), "w_ffn_out": f(inp["w_ffn_out"][0]),
    }
    shared.update(consts)
    xs = np.asarray(inp["x_sample"], dtype=np.float32)
    caches = [np.asarray(inp[k], dtype=np.float32) for k in ("cache_kv_w128", "cache_kv_w512", "cache_kv_w2048")]
    maps = []
    for c in range(ncores):
        m = dict(shared)
        m["x_prompt"] = f(inp["x_prompt"][c * nseq:(c + 1) * nseq])
        m["x_sample"] = f(xs[c * ns:(c + 1) * ns, 0, :])
        for g in range(3):
            cg = caches[g][0, c * ns:(c + 1) * ns]
            if past_override is not None:
                cg = cg[:, :min(WINS[g], past_override)]
            m["cache%d" % g] = f(cg.reshape(ns, cg.shape[1], 2, 512))
        m["state_conv"] = f(np.asarray(inp["state_conv"])[0, c * ns:(c + 1) * ns])
        m["state_ssm"] = f(np.asarray(inp["state_ssm"])[0, c * ns:(c + 1) * ns].reshape(ns, 1024, 128))
        maps.append(m)
    return maps


def assemble(results, ncores, nseq, ns, seq, past=PAST):
    cat = lambda k: np.concatenate([np.asarray(r[k]) for r in results], axis=0)
    pw = [min(w, seq) for w in WINS]
    wb = [min(w, past) for w in WINS]
    B = ncores * nseq
    S = ncores * ns
    outs = [
        cat("y_prompt"),
        cat("y_sample").reshape(S, 1, D),
        cat("p_kv0").reshape(1, B, pw[0], 2, NH, HD),
        cat("p_kv1").reshape(1, B, pw[1], 2, NH, HD),
        cat("p_kv2").reshape(1, B, pw[2], 2, NH, HD),
        cat("p_conv").reshape(1, B, 3, 1536),
        cat("p_ssm").reshape(1, B, 16, 64, 128),
        cat("s_kv0").reshape(1, S, wb[0], 2, NH, HD),
        cat("s_kv1").reshape(1, S, wb[1], 2, NH, HD),
        cat("s_kv2").reshape(1, S, wb[2], 2, NH, HD),
        cat("s_conv").reshape(1, S, 3, 1536),
        cat("s_ssm").reshape(1, S, 16, 64, 128),
    ]
    return tuple(np.ascontiguousarray(o, dtype=np.float32) for o in outs)


def kernel(**inp):
    ncores = 8
    B, seq = inp["x_prompt"].shape[0], inp["x_prompt"].shape[1]
    S = inp["x_sample"].shape[0]
    nseq, ns = B // ncores, S // ncores
    key = (nseq, seq, ns)
    if key not in _CACHE:
        _CACHE[key] = build(nseq, seq, ns)
    nc = _CACHE[key]
    maps = make_in_maps(inp, ncores, nseq, ns, seq)
    res = run_bass_kernel_spmd(nc, maps, core_ids=list(range(ncores)))
    return assemble(res.results, ncores, nseq, ns, seq)
```

```python
import math
import numpy as np
import concourse.bass as bass
import concourse.mybir as mybir
from concourse.bass_utils import run_bass_kernel_spmd

F32 = mybir.dt.float32
BF16 = mybir.dt.bfloat16
ALU = mybir.AluOpType
AF = mybir.ActivationFunctionType

ENGS = ("pe", "act", "dve", "pool", "sp")


class Buf:
    __slots__ = ("t", "name", "last_w", "readers", "dsem", "dcount", "aliases", "excl")

    def __init__(self, t, name):
        self.excl = False
        self.t = t
        self.name = name
        self.last_w = None
        self.readers = []
        self.dsem = None
        self.dcount = 0
        self.aliases = []

    def __getitem__(self, k):
        return self.t[k]


class Op:
    __slots__ = ("eng", "emit", "deps", "is_dma", "buf", "dval", "sig", "idx", "cost", "tbl", "nbytes", "seq",
                 "done", "placed")

    def __init__(self, eng, emit, is_dma=False):
        self.eng = eng
        self.emit = emit
        self.deps = []
        self.is_dma = is_dma
        self.buf = None
        self.dval = 0
        self.sig = False
        self.idx = 0
        self.cost = 100.0
        self.tbl = None
        self.nbytes = 0
        self.seq = 0
        self.done = 0.0
        self.placed = False


class Prog:
    def __init__(self, nc):
        self.nc = nc
        self.ops = []
        self._stack = []
        self.nsb = 0
        self.frozen = False

    def sbuf(self, name, shape, dt):
        g = self.nc.sbuf_tensor(name, list(shape), dt)
        t = g.__enter__()
        self._stack.append(g)
        return Buf(t, name)

    def psum(self, name, shape, dt=F32):
        g = self.nc.psum_tensor(name, list(shape), dt)
        t = g.__enter__()
        self._stack.append(g)
        b = Buf(t, name)
        b.excl = True
        return b

    def view(self, ap, name, parent=None):
        b = Buf(ap, name)
        return b

    def dram(self, name, shape, dt):
        t = self.nc.dram_tensor(name, list(shape), dt, kind="Internal")
        return Buf(t, name)

    @staticmethod
    def alias(a, others):
        for o in others:
            a.aliases.append(o)
            o.aliases.append(a)

    def _add(self, op, reads, writes):
        if self.frozen:
            return op
        deps = []
        for b in reads:
            if b.last_w is not None:
                deps.append(b.last_w)
            if b.excl:
                deps.extend(r for r in b.readers if r.eng != op.eng)
        for b in writes:
            if b.last_w is not None:
                deps.append(b.last_w)
            deps.extend(b.readers)
            for a in b.aliases:
                if a.last_w is not None:
                    deps.append(a.last_w)
                deps.extend(a.readers)
        seen = set()
        for d in deps:
            if d is op or id(d) in seen:
                continue
            seen.add(id(d))
            op.deps.append(d)
        for b in reads:
            b.readers.append(op)
        for b in writes:
            b.last_w = op
            b.readers = []
        op.seq = len(self.ops)
        self.ops.append(op)
        return op

    def op(self, eng, emit, reads=(), writes=(), cost=100.0, tbl=None):
        o = Op(eng, emit)
        o.cost = cost
        o.tbl = tbl
        return self._add(o, reads, writes)

    def dma(self, eng, out_ap, in_ap, carrier, reads=(), writes=(), **kw):
        def emit(e):
            return e.dma_start(out=out_ap, in_=in_ap, **kw)
        op = Op(eng, emit, is_dma=True)
        op.buf = carrier
        n = 1
        for d in out_ap.shape:
            n *= d
        op.nbytes = n * (2 if out_ap.dtype == BF16 else 4)
        op.cost = 60.0
        return self._add(op, reads, writes)

    def schedule(self, window):
        by_eng = {e: [] for e in ENGS}
        for i, op in enumerate(self.ops):
            op.placed = False
            by_eng[op.eng].append(op)
        head = {e: 0 for e in ENGS}
        free = {e: 0.0 for e in ENGS}
        cur_tbl = [None]
        dma_pipe = [0.0]
        order = {e: [] for e in ENGS}
        remaining = len(self.ops)
        LAT = 500.0
        while remaining:
            best = None
            for e in ENGS:
                q = by_eng[e]
                h = head[e]
                while h < len(q) and q[h].placed:
                    h += 1
                head[e] = h
                cnt = 0
                i = h
                W = window[e]
                while i < len(q) and cnt < W:
                    op = q[i]
                    i += 1
                    if op.placed:
                        continue
                    cnt += 1
                    ok = True
                    st = free[e]
                    for d in op.deps:
                        if not d.placed:
                            ok = False
                            break
                        t = d.done + ((0.0 if e == "pe" else 120.0) if d.eng == e and not d.is_dma else LAT)
                        if t > st:
                            st = t
                    if not ok:
                        continue
                    if e == "act" and op.tbl is not None and cur_tbl[0] is not None and op.tbl != cur_tbl[0]:
                        st += 1300.0
                    key = (st, op.seq)
                    if best is None or key < best[0]:
                        best = (key, e, op)
            (st, _), e, op = best
            op.placed = True
            remaining -= 1
            order[e].append(op)
            if op.is_dma:
                free[e] = st + op.cost
                t0 = max(st + 1500.0, dma_pipe[0])
                dma_pipe[0] = t0 + op.nbytes / 300.0
                op.done = dma_pipe[0] + 500.0
            else:
                free[e] = st + op.cost
                op.done = st + op.cost
                if e == "act" and op.tbl is not None:
                    cur_tbl[0] = op.tbl
        self.ops = []
        for e in ENGS:
            self.ops.extend(order[e])
        self.est_ns = max(free.values())
        return order

    def finalize(self, window=None):
        nc = self.nc
        if window is not None:
            self.schedule(window)
        for op in self.ops:
            for d in op.deps:
                if d.is_dma:
                    continue
                if d.eng == op.eng and d.eng == "pe":
                    continue
                d.sig = True
        cnt = {e: 0 for e in ENGS}
        for op in sorted(self.ops, key=lambda o: o.seq):
            if op.is_dma:
                b = op.buf
                b.dcount += 16
                op.dval = b.dcount
        for op in self.ops:
            if (not op.is_dma) and op.sig:
                cnt[op.eng] += 1
                op.idx = cnt[op.eng]
        esem = {}
        for e in ENGS:
            g = nc.semaphore("es_" + e)
            esem[e] = g.__enter__()
            self._stack.append(g)
        nsem = 0
        for op in self.ops:
            if op.is_dma and op.buf.dsem is None:
                g = nc.semaphore("ds%d" % nsem)
                nsem += 1
                op.buf.dsem = g.__enter__()
                self._stack.append(g)
        by_eng = {e: [] for e in ENGS}
        for op in self.ops:
            by_eng[op.eng].append(op)
        all_dma_bufs = []
        seenb = set()
        for op in self.ops:
            if op.is_dma and id(op.buf) not in seenb:
                seenb.add(id(op.buf))
                all_dma_bufs.append(op.buf)

        def run_engine(ename):
            def body(e):
                known = {x: 0 for x in ENGS}
                knownd = {}
                for op in by_eng[ename]:
                    for d in op.deps:
                        if d.is_dma:
                            k = id(d.buf)
                            if knownd.get(k, 0) < d.dval:
                                e.wait_ge(d.buf.dsem, d.dval)
                                knownd[k] = d.dval
                        else:
                            if d.eng == ename and ename == "pe":
                                continue
                            if known[d.eng] < d.idx:
                                e.wait_ge(esem[d.eng], d.idx)
                                known[d.eng] = d.idx
                    ins = op.emit(e)
                    if op.is_dma:
                        ins.then_inc(op.buf.dsem, 16)
                    elif op.sig:
                        ins.then_inc(esem[ename], 1)
                if ename == "sp":
                    for b in all_dma_bufs:
                        e.wait_ge(b.dsem, b.dcount)
                    for x in ENGS:
                        if x != "sp" and cnt[x] > 0:
                            e.wait_ge(esem[x], cnt[x])
            return body

        with nc.Block() as block:
            block.tensor(run_engine("pe"))
            block.scalar(run_engine("act"))
            block.vector(run_engine("dve"))
            block.gpsimd(run_engine("pool"))
            block.sync(run_engine("sp"))

    def close(self):
        while self._stack:
            g = self._stack.pop()
            g.__exit__(None, None, None)


class Arena:
    def __init__(self, P, name, nbytes):
        self.buf = P.sbuf(name, [128, nbytes // 2], BF16)
        self.items = []
        self.off = 0
        self.nbytes = nbytes

    def phase(self):
        self.off = 0

    def take(self, name, shape, dt):
        n = 1
        for d in shape[1:]:
            n *= d
        nb = n * (4 if dt == F32 else 2)
        nb4 = (nb + 3) // 4 * 4
        assert self.off + nb4 <= self.nbytes, (name, self.off, nb4, self.nbytes)
        ap = self.buf.t[0:shape[0], self.off // 2:(self.off + nb) // 2]
        if dt == F32:
            ap = ap.bitcast(F32)
        if len(shape) == 3:
            ap = ap.rearrange("p (a b) -> p a b", a=shape[1])
        elif len(shape) == 4:
            ap = ap.rearrange("p (a b c) -> p a b c", a=shape[1], b=shape[2])
        v = Buf(ap, name)
        for (lo, hi, o) in self.items:
            if lo < self.off + nb4 and self.off < hi:
                v.aliases.append(o)
                o.aliases.append(v)
        self.items.append((self.off, self.off + nb4, v))
        self.off += nb4
        return v


D = 1024
TT = 512
HD = 64
NH = 8
DILS = (1, 4, 16)
WINS = (128, 512, 2048)
QKV = 4608
O_Z = 4608
O_XBC = 5632
O_DT = 7168
MIX_IN = 7184
FFN_H = 2816
NFC = 22
PAST = 8192
EPS = 1e-6


def host_consts(seq):
    c = {}
    c["c_ident"] = np.eye(128, dtype=np.float32)
    pm = np.zeros((128, 128), np.float32)
    for dp in range(128):
        d = (dp // 64) * 64 + ((dp % 64) + 32) % 64
        pm[d, dp] = 1.0
    c["c_perm"] = pm
    k = np.arange(128)[:, None]
    q = np.arange(128)[None, :]
    bd = (k // 32) == (q // 32)
    masks = np.stack([
        (k >= q), (k <= q), bd, bd & ((k % 32) >= (q % 32)), bd & ((k % 32) <= (q % 32)),
    ]).astype(np.float32)
    c["c_masks"] = masks
    half = HD // 2
    inv_freq = (np.float32(10000.0) ** (-np.arange(half, dtype=np.float32) / np.float32(half))).astype(np.float32)
    pos = np.arange(seq, dtype=np.float32)
    ang = (pos[:, None] * inv_freq[None, :]).astype(np.float32)
    cosv = np.cos(ang).astype(np.float32)
    sinv = np.sin(ang).astype(np.float32)
    p = np.arange(128)
    fidx = p % 32
    sign = np.where((p % 64) < 32, -1.0, 1.0).astype(np.float32)
    rope = np.zeros((3, 2, 128, seq), np.float32)
    nt = seq // TT
    for g in range(3):
        perm = np.zeros(seq, np.int64)
        for T in range(nt):
            for u in range(4):
                w = np.arange(128)
                if g == 0:
                    tau = 128 * u + w
                elif g == 1:
                    tau = 4 * w + u
                else:
                    tau = 16 * (w % 32) + 4 * u + (w // 32)
                perm[T * TT + u * 128 + w] = T * TT + tau
        rope[g, 0] = cosv[perm][:, fidx].T
        rope[g, 1] = (sinv[perm][:, fidx].T) * sign[:, None]
    c["c_rope"] = rope
    sel = np.zeros((65, 64), np.float32)
    sel[64, :] = 1.0
    c["c_sel"] = sel
    angs = (np.float32(PAST) * inv_freq).astype(np.float32)
    col = np.arange(512)
    cs = np.cos(angs).astype(np.float32)[col % 32]
    sn = np.sin(angs).astype(np.float32)[col % 32] * np.where((col % 64) < 32, -1.0, 1.0)
    c["c_srope"] = np.stack([cs, sn]).astype(np.float32)
    dl = np.zeros((16, 16, 128), np.float32)
    for b in range(16):
        dl[b, b, :] = 1.0
    c["c_delta"] = dl.reshape(16, 2048)
    es = np.zeros((128, 16, 16), np.float32)
    for b in range(16):
        es[:, b, b] = 1.0
    c["c_esel"] = es.reshape(128, 256)
    return c


def build(nseq, seq, ns, do_sample=True, debug=None, stop=None, past=PAST,
          sched_window={"pe": 64, "act": 32, "dve": 32, "pool": 24, "sp": 64}):
    nc = bass.Bass("TRN2", target_bir_lowering=False)
    P = Prog(nc)

    def stage(name):
        if stop is not None and name == stop:
            P.frozen = True
    NT = seq // TT
    WB = [min(w, past) for w in WINS]
    PW = [min(w, seq) for w in WINS]

    def din(name, shape):
        return nc.dram_tensor(name, list(shape), F32, kind="ExternalInput")

    def dout(name, shape):
        return nc.dram_tensor(name, list(shape), F32, kind="ExternalOutput")

    x_prompt = din("x_prompt", [nseq, seq, D])
    x_sample = din("x_sample", [ns, D])
    cache = [din("cache%d" % g, [ns, WB[g], 2, 512]) for g in range(3)]
    state_conv = din("state_conv", [ns, 3, 1536])
    state_ssm = din("state_ssm", [ns, 1024, 128])
    norm_mix_pre = din("norm_mix_pre", [1, D])
    norm_mix_post = din("norm_mix_post", [1, D])
    norm_ffn_pre = din("norm_ffn_pre", [1, D])
    norm_ffn_post = din("norm_ffn_post", [1, D])
    w_in = din("w_in", [D, MIX_IN])
    w_out = din("w_out", [1536, D])
    conv_w = din("conv_w", [4, 1536])
    conv_b = din("conv_b", [1, 1536])
    dt_bias = din("dt_bias", [1, 16])
    a_log = din("a_log", [1, 16])
    d_skip = din("d_skip", [1, 16])
    ssd_norm_w = din("ssd_norm_w", [1, D])
    w_ffn_in = din("w_ffn_in", [D, 2 * FFN_H])
    w_ffn_out = din("w_ffn_out", [FFN_H, D])
    c_ident = din("c_ident", [128, 128])
    c_perm = din("c_perm", [128, 128])
    c_masks = din("c_masks", [5, 128, 128])
    c_rope = din("c_rope", [3, 2, 128, seq])
    c_sel = din("c_sel", [65, 64])
    c_srope = din("c_srope", [2, 512])
    c_delta = din("c_delta", [16, 2048])
    c_esel = din("c_esel", [128, 256])

    y_prompt = dout("y_prompt", [nseq, seq, D])
    y_sample = dout("y_sample", [ns, D])
    p_kv = [dout("p_kv%d" % g, [nseq, PW[g], 2, 512]) for g in range(3)]
    p_conv = dout("p_conv", [nseq, 3, 1536])
    p_ssm = dout("p_ssm", [nseq, 1024, 128])
    s_kv = [dout("s_kv%d" % g, [ns, WB[g], 2, 512]) for g in range(3)]
    s_conv = dout("s_conv", [ns, 3, 1536])
    s_ssm = dout("s_ssm", [ns, 1024, 128])
    dbg = {}
    if debug:
        for nm, shp in debug.items():
            dbg[nm] = dout("dbg_" + nm, shp)

    win_b = P.dram("win_b", [D, MIX_IN], BF16)
    wout_b = P.dram("wout_b", [1536, D], BF16)
    wfi_b = P.dram("wfi_b", [D, 2 * FFN_H], BF16)
    wfo_b = P.dram("wfo_b", [FFN_H, D], BF16)
    kscr = [[P.dram("kscr%d_%d" % (g, T), [128, 4, 512], BF16) for T in range(NT)] for g in range(3)]
    vscr = [[P.dram("vscr%d_%d" % (g, T), [128, 4, 8, 65], BF16) for T in range(NT)] for g in range(3)]

    def fsz(ap):
        n = 1
        for d in ap.shape[1:]:
            n *= d
        return n

    def vcost(eng, ap):
        n = fsz(ap)
        if eng == "pool":
            return 100.0 + 2.1 * n
        if eng == "act":
            return 220.0 + 0.85 * n
        return 70.0 + 1.0 * n

    def ACT(out, in_, func, reads, writes, **kw):
        tbl = "silu" if func == AF.Silu else ("exp" if func in (AF.Exp, AF.Ln) else None)
        return P.op("act", lambda e: e.activation(out=out, in_=in_, func=func, **kw), reads, writes,
                    cost=vcost("act", in_), tbl=tbl)

    def TT_(eng, out, in0, in1, op, reads, writes):
        return P.op(eng, lambda e: e.tensor_tensor(out=out, in0=in0, in1=in1, op=op), reads, writes, cost=vcost(eng, out))

    def TS(eng, out, in0, s1, s2, op0, op1, reads, writes):
        if op1 is None:
            return P.op(eng, lambda e: e.tensor_scalar(out=out, in0=in0, scalar1=s1, scalar2=None, op0=op0), reads, writes,
                        cost=vcost(eng, out))
        return P.op(eng, lambda e: e.tensor_scalar(out=out, in0=in0, scalar1=s1, scalar2=s2, op0=op0, op1=op1), reads, writes,
                    cost=vcost(eng, out))

    def STT(eng, out, in0, scalar, in1, op0, op1, reads, writes):
        return P.op(eng, lambda e: e.scalar_tensor_tensor(out=out, in0=in0, scalar=scalar, in1=in1, op0=op0, op1=op1), reads, writes,
                    cost=vcost(eng, out))

    def CP(eng, out, in_, reads, writes):
        if eng == "act":
            return P.op("act", lambda e: e.copy(out=out, in_=in_), reads, writes, cost=vcost("act", out))
        return P.op(eng, lambda e: e.tensor_copy(out=out, in_=in_), reads, writes, cost=vcost(eng, out))

    def MM(out, lhsT, rhs, start, stop, reads, writes):
        c = max(64, fsz(rhs)) * 0.42 * (4.0 if lhsT.dtype == F32 else 1.0) + 8.0
        return P.op("pe", lambda e: e.matmul(out, lhsT=lhsT, rhs=rhs, start=start, stop=stop), reads, writes, cost=c)

    def TR(out, in_, ident, reads, writes):
        c = max(64, fsz(in_)) * 0.42 * (4.0 if in_.dtype == F32 else 1.0) + 8.0
        return P.op("pe", lambda e: e.transpose(out=out, in_=in_, identity=ident), reads, writes, cost=c)

    def MSET(eng, ap, val, writes):
        return P.op(eng, lambda e: e.memset(ap, val), (), writes, cost=vcost(eng, ap) * 0.5)

    DQ = "sp"
    WQ = "sp"

    banks = [P.psum("bank%d" % i, [128, 512], F32) for i in range(8)]
    held = set()
    lru = list(range(8))

    def palloc():
        for i in lru:
            if i not in held:
                held.add(i)
                lru.remove(i)
                lru.append(i)
                return banks[i]
        raise RuntimeError("out of PSUM banks")

    def pfree(b):
        held.discard(banks.index(b))

    cst = P.sbuf("cst_f", [128, 128], F32)
    ident_f = P.sbuf("ident_f", [128, 128], F32)
    ident_b = P.sbuf("ident_b", [128, 128], BF16)
    perm_b = P.sbuf("perm_b", [128, 128], BF16)
    masks_b = P.sbuf("masks_b", [128, 1, 128], BF16)
    U_f = P.sbuf("U_f", [128, 128], F32)
    ones_f = P.sbuf("ones_f", [128, 128], F32)
    ones_b = P.sbuf("ones_b", [128, 128], BF16)
    sel_f = P.sbuf("sel_f", [65, 64], F32)
    eps_t = P.sbuf("eps_t", [128, 1], F32)
    one_t = P.sbuf("one_t", [128, 1], F32)
    nw_mix = P.sbuf("nw_mix", [128, D], F32)
    nw_ffn = P.sbuf("nw_ffn", [128, D], F32)
    nwp = P.sbuf("nwp", [128, 3, 8], F32)
    cw_t = P.sbuf("cw_t", [128, 12, 4], F32)
    cb_t = P.sbuf("cb_t", [128, 12], F32)
    dtb_t = P.sbuf("dtb_t", [128, 16], F32)
    A_t = P.sbuf("A_t", [128, 16], F32)
    D_t = P.sbuf("D_t", [128, 8], F32)

    P.dma(DQ, ident_f[:], c_ident[:, :], ident_f, writes=[ident_f])
    CP("dve", ident_b[:], ident_f[:], [ident_f], [ident_b])
    P.dma(DQ, cst[:], c_perm[:, :], cst, writes=[cst])
    CP("dve", perm_b[:], cst[:], [cst], [perm_b])
    negm_b = P.sbuf("negm_b", [128, 5, 128], BF16)
    for i in range(5):
        P.dma(DQ, cst[:], c_masks[i, :, :], cst, writes=[cst])
        if i == 1:
            CP("dve", masks_b[:, 0, :], cst[:], [cst], [masks_b])
        TS("dve", negm_b[:, i, :], cst[:], -1.0, 30000.0, ALU.add, ALU.mult, [cst], [negm_b])
    P.dma(DQ, U_f[:], c_masks[1, :, :], U_f, writes=[U_f])
    MSET("dve", ones_f[:], 1.0, [ones_f])
    MSET("dve", ones_b[:], 1.0, [ones_b])
    MSET("dve", eps_t[:], EPS, [eps_t])
    MSET("dve", one_t[:], 1.0, [one_t])
    P.dma(DQ, sel_f[:], c_sel[:, :], sel_f, writes=[sel_f])
    P.dma(DQ, nw_mix[:], norm_mix_post[0:1, :].partition_broadcast(128), nw_mix, writes=[nw_mix])
    P.dma(DQ, nw_ffn[:], norm_ffn_post[0:1, :].partition_broadcast(128), nw_ffn, writes=[nw_ffn])
    for i, src in enumerate((norm_mix_pre, norm_ffn_pre, ssd_norm_w)):
        P.dma(DQ, nwp[:, i, :], src[0, :].rearrange("(k p) -> p k", p=128), nwp, writes=[nwp],
              allow_slow_non_contiguous=True)
    for j_ in range(4):
        P.dma(DQ, cw_t[:, :, j_], conv_w[j_, :].rearrange("(c p) -> p c", p=128), cw_t, writes=[cw_t],
              allow_slow_non_contiguous=True)
    P.dma(DQ, cb_t[:], conv_b[0, :].rearrange("(c p) -> p c", p=128), cb_t, writes=[cb_t],
          allow_slow_non_contiguous=True)
    P.dma(DQ, dtb_t[:], dt_bias[0:1, :].partition_broadcast(128), dtb_t, writes=[dtb_t])
    P.dma(DQ, A_t[:], a_log[0:1, :].partition_broadcast(128), A_t, writes=[A_t])
    ACT(A_t[:], A_t[:], AF.Exp, [A_t], [A_t])
    TS("dve", A_t[:], A_t[:], -1.0, None, ALU.mult, None, [A_t], [A_t])
    dsk2 = d_skip[0, :].rearrange("(c e) -> e c", e=2)
    for e_ in range(2):
        P.dma(DQ, D_t[64 * e_:64 * e_ + 64, :], dsk2[e_:e_ + 1, :].partition_broadcast(64), D_t, writes=[D_t],
              allow_slow_non_contiguous=True)

    stage("consts")
    NRING = 3
    wring = [P.sbuf("wring%d" % i, [128, 8, 512], BF16) for i in range(NRING)]
    wr_i = [0]

    def wnext():
        b = wring[wr_i[0] % NRING]
        wr_i[0] += 1
        return b

    xin = [P.sbuf("xin%d" % i, [128, D], F32) for i in range(2)]
    xnb = [P.sbuf("xnb%d" % i, [128, D], BF16) for i in range(2)]
    hb = [P.sbuf("hb%d" % j, [128, D], F32) for j in range(4)]
    xnT0 = P.sbuf("xnT0", [128, 8, 512], BF16)
    xnTp = P.sbuf("xnTp", [128, 8, 512], BF16)
    qraw = [P.sbuf("qraw%d" % i, [128, 512], BF16) for i in range(2)]
    vcur = [P.sbuf("vcur%d" % i, [128, 4, 8, 65], BF16) for i in range(2)]
    arK = Arena(P, "arK", 4096)
    kvst = [arK.take("kvst0", [128, 2, 512], F32)]
    arK.phase()
    junk = arK.take("junk", [128, D], BF16)
    kvst_i = [0]
    convtail = P.sbuf("convtail", [128, 12, 3], F32)
    yT = P.sbuf("yT", [128, 8, 512], BF16)
    stT = P.sbuf("stT", [128, 16, 64], F32)
    stz = P.sbuf("stz", [128, 16, 128], BF16)
    arA = Arena(P, "arA", 39936)
    acc = arA.take("acc", [128, 8, 512], F32)
    NPT = 6
    PTb = [arA.take("PT%d" % i, [128, 512], BF16) for i in range(NPT)]
    pt_i = [0]
    NSTR = 8
    kstr = [arA.take("kstr%d" % i, [128, 512], BF16) for i in range(NSTR)]
    vstr = [arA.take("vstr%d" % i, [128, 4, 2, 65], BF16) for i in range(NSTR)]
    arA.phase()
    stg = [arA.take("stg%d" % i, [128, 515], F32) for i in range(2)]
    cacc = [arA.take("cacc%d" % i, [128, 512], F32) for i in range(2)]
    xdtz = arA.take("xdtz", [128, 16, 128], BF16)
    xdd = arA.take("xdd", [128, 16, 64], BF16)
    Btok = arA.take("Btok", [128, 2, 128], BF16)
    Rb = arA.take("Rb", [128, 4, 128], F32)
    Eb = arA.take("Eb", [128, 4, 128], F32)
    ecs = arA.take("ecs", [128, 4, 128], F32)
    Wb = arA.take("Wb", [128, 16, 128], BF16)
    Cdec = arA.take("Cdec", [128, 16, 128], BF16)
    Gm = arA.take("Gm", [128, 2, 128], F32)
    sqb = [arA.take("sqb%d" % i, [128, 512], BF16) for i in range(2)]
    rstd_b = arA.take("rstd_b", [128, 512], F32)
    ytmp = [arA.take("ytmp%d" % i, [128, 128], F32) for i in range(2)]
    arA.phase()
    Rb = [Rb, arA.take("Rb1", [128, 4, 128], F32)]
    Eb = [Eb, arA.take("Eb1", [128, 4, 128], F32)]
    ecs = [ecs, arA.take("ecs1", [128, 4, 128], F32)]
    Gm = [Gm, arA.take("Gm1", [128, 2, 128], F32)]
    arA.phase()
    fst = [arA.take("fst%d" % i, [128, 4, 512], F32) for i in range(4)]
    arB = Arena(P, "arB", 8192)
    kcur = [arB.take("kcur%d" % i, [128, 4, 512], BF16) for i in range(2)]
    arB.phase()
    m0 = [arB.take("m0_%d" % j, [128, 512], F32) for j in range(4)]
    arC = Arena(P, "arC", 8192)
    qT = arC.take("qT", [128, 4, 512], BF16)
    ropet = arC.take("ropet", [128, 2, 512], F32)
    arC.phase()
    attnT = arC.take("attnT", [64, 8, 512], BF16)
    arD = Arena(P, "arD", 8192)
    rt1 = [arD.take("rt1_%d" % i, [128, 512], F32) for i in range(2)]
    rt2 = [arD.take("rt2_%d" % i, [128, 512], F32) for i in range(2)]
    arD.phase()
    sgt = [arD.take("sgt%d" % i, [128, 512], F32) for i in range(2)]
    arE = Arena(P, "arE", 22528)
    actT = arE.take("actT", [128, NFC, 512], BF16)
    arE.phase()
    sz = arE.take("sz", [128, 8, 512], BF16)
    xc = arE.take("xc", [128, 12, 512], BF16)
    small = {}

    def sm(name, cols=1):
        if name not in small:
            small[name] = P.sbuf("sm_" + name, [128, cols], F32)
        return small[name]

    bst = [P.view(wring[i // 2].t[:, 4 * (i % 2):4 * (i % 2) + 4, :], "bst%d" % i) for i in range(6)]
    for i in range(6):
        Prog.alias(wring[i // 2], [bst[i]])
    prep_i = [0]

    def prep_piece(src, dst, r0, c0, ncol, scale_ap):
        i = prep_i[0]
        prep_i[0] += 1
        f = fst[i % 4]
        b = bst[i % 6]
        fv = f.t.rearrange("p a b -> p (a b)")[:, 0:ncol]
        bv = b.t.rearrange("p a b -> p (a b)")[:, 0:ncol]
        P.dma(WQ, fv, src[r0:r0 + 128, c0:c0 + ncol], f, writes=[f])
        eng = ("dve", "act")[i % 2]
        if scale_ap is None:
            CP(eng, bv, fv, [f], [b])
        elif eng == "act":
            ACT(bv, fv, AF.Copy, [f, nwp], [b], scale=scale_ap)
        else:
            TS(eng, bv, fv, scale_ap, None, ALU.mult, None, [f, nwp], [b])
        P.dma(WQ, dst.t[r0:r0 + 128, c0:c0 + ncol], bv, b, reads=[b], writes=[dst])

    for kc in range(8):
        for c0 in range(0, MIX_IN, 2048):
            prep_piece(w_in, win_b, kc * 128, c0, min(2048, MIX_IN - c0), nwp[:, 0, kc:kc + 1])
    for rc in range(12):
        prep_piece(w_out, wout_b, rc * 128, 0, 1024, None if rc < 4 else nwp[:, 2, rc - 4:rc - 3])
    for kc in range(8):
        for c0 in range(0, 2 * FFN_H, 2048):
            prep_piece(w_ffn_in, wfi_b, kc * 128, c0, min(2048, 2 * FFN_H - c0), nwp[:, 1, kc:kc + 1])
    for rc in range(NFC):
        prep_piece(w_ffn_out, wfo_b, rc * 128, 0, 1024, None)

    stage("prep")
    def wblock_in(c0, ncol):
        b = wnext()
        P.dma(WQ, b.t[:, :, 0:ncol], win_b.t[:, c0:c0 + ncol].rearrange("(k p) n -> p k n", p=128), b,
              reads=[win_b], writes=[b])
        return b

    def wblock_fi(c0, ncol):
        b = wnext()
        P.dma(WQ, b.t[:, :, 0:ncol], wfi_b.t[:, c0:c0 + ncol].rearrange("(k p) n -> p k n", p=128), b,
              reads=[wfi_b], writes=[b])
        return b

    def wblock_outA(half):
        b = wnext()
        P.dma(WQ, b.t[0:64, :, :], wout_b.t[0:512, half * 512:half * 512 + 512].rearrange("(h p) n -> p h n", p=64), b,
              reads=[wout_b], writes=[b])
        return b

    def wblock_outS(half):
        b = wnext()
        P.dma(WQ, b.t[:, :, :], wout_b.t[512:1536, half * 512:half * 512 + 512].rearrange("(c p) n -> p c n", p=128), b,
              reads=[wout_b], writes=[b])
        return b

    def wblock_fo(c0, n, half):
        b = wnext()
        P.dma(WQ, b.t[:, 0:n, :], wfo_b.t[c0 * 128:(c0 + n) * 128, half * 512:half * 512 + 512].rearrange("(c p) n -> p c n", p=128), b,
              reads=[wfo_b], writes=[b])
        return b

    nrm_i = [0]

    def rstd_from(ss_ap, ss_buf, out_buf):
        ACT(out_buf[:], ss_ap, AF.Ln, [ss_buf, eps_t], [out_buf], scale=1.0 / D, bias=eps_t[:])
        ACT(out_buf[:], out_buf[:], AF.Exp, [out_buf], [out_buf], scale=-0.5)

    def norm_transpose(src_buf, j, dstT):
        i = nrm_i[0]
        nrm_i[0] += 1
        ss = sm("nss%d" % (i % 2))
        rs = sm("nrs%d" % (i % 2))
        xb = xnb[i % 2]
        ACT(junk[:], src_buf[:], AF.Square, [src_buf], [junk, ss], accum_out=ss[:])
        rstd_from(ss[:], ss, rs)
        ACT(xb[:], src_buf[:], AF.Copy, [src_buf, rs], [xb], scale=rs[:])
        bk = palloc()
        bv = bk.t[:].bitcast(BF16)
        for kc in range(8):
            TR(bv[:, kc * 128:(kc + 1) * 128], xb[:, kc * 128:(kc + 1) * 128], ident_b[:], [xb, ident_b], [bk])
        CP("act", dstT[:, :, j * 128:(j + 1) * 128], bv.rearrange("p (k t) -> p k t", k=8), [bk], [dstT])
        pfree(bk)

    def proj_fm(wb, coff, xT, bank, ncontract=8):
        for kc in range(ncontract):
            MM(bank[:], wb[:, kc, coff:coff + 128], xT[:, kc, :], kc == 0, kc == ncontract - 1, [wb, xT], [bank])

    def proj_tm(wb, ncol, xT, j, bank):
        for kc in range(8):
            MM(bank[:, 0:ncol], xT[:, kc, j * 128:(j + 1) * 128], wb[:, kc, 0:ncol], kc == 0, kc == 7, [wb, xT], [bank])

    rope_i = [0]

    def rope_evac(bank, dest_ap, dest_buf):
        i = rope_i[0]
        rope_i[0] += 1
        qr, t1, t2 = qraw[i % 2], rt1[i % 2], rt2[i % 2]
        stage("r0")
        CP("act", qr[:], bank[:], [bank], [qr])
        stage("r1")
        b2 = palloc()
        MM(b2[:], perm_b[:], qr[:], True, True, [perm_b, qr], [b2])
        stage("r2")
        TT_("dve", t1[:], bank[:], ropet[:, 0, :], ALU.mult, [bank, ropet], [t1])
        stage("r3")
        TT_("dve", t2[:], b2[:], ropet[:, 1, :], ALU.mult, [b2, ropet], [t2])
        pfree(b2)
        stage("r4")
        TT_("pool", dest_ap, t1[:], t2[:], ALU.add, [t1, t2], [dest_buf])

    def acc_view(g, h, rows):
        a = acc.t[rows, h, :]
        if g == 0:
            return a
        if g == 1:
            return a.rearrange("p (w u) -> p u w", u=4)
        return a.rearrange("p (i u r) -> p u r i", u=4, r=4)

    def bank_view(g, bank, rows):
        b = bank.t[rows, :]
        if g == 0:
            return b
        if g == 1:
            return b.rearrange("p (u w) -> p u w", u=4)
        return b.rearrange("p (u r i) -> p u r i", u=4, r=4)

    kv_i = [0]
    str_i = [0]

    quota = [0]

    def bd():
        if quota[0] > 0:
            quota[0] -= 1
            bulk_drain(1)

    def phase_A(s, T):
        t0 = T * TT
        for j in range(4):
            xi = xin[j % 2]
            P.dma(DQ, xi[:], x_prompt[s, t0 + j * 128:t0 + (j + 1) * 128, :], xi, writes=[xi])
            norm_transpose(xi, j, xnT0)

    def prompt_tile(s, T, nxt):
        t0 = T * TT
        for j in range(4):
            P.dma(WQ, hb[j][:], x_prompt[s, t0 + j * 128:t0 + (j + 1) * 128, :], hb[j], writes=[hb[j]])
        stage("A")
        for g in range(3):
            dil = DILS[g]
            if g == 0:
                xT = xnT0
            else:
                xT = xnTp
                for kc in range(8):
                    if g == 1:
                        src = xnT0.t[:, kc, :].rearrange("p (w u) -> p u w", u=4)
                        dst = xnTp.t[:, kc, :].rearrange("p (u w) -> p u w", u=4)
                    else:
                        src = xnT0.t[:, kc, :].rearrange("p (i u r) -> p u r i", u=4, r=4)
                        dst = xnTp.t[:, kc, :].rearrange("p (u r i) -> p u r i", u=4, r=4)
                    CP("act" if kc % 2 else "pool", dst, src, [xnT0], [xnTp])
            P.dma(DQ, ropet[:], c_rope[g, :, :, t0:t0 + TT].rearrange("c p n -> p c n"), ropet, writes=[ropet])
            kc_ = kcur[kv_i[0] % 2]
            vc_ = vcur[kv_i[0] % 2]
            kv_i[0] += 1
            cbase = g * 1536
            stage("b0_%d" % g)
            wq = wblock_in(cbase, 512)
            stage("b1_%d" % g)
            for fc in range(4):
                bk = palloc()
                proj_fm(wq, fc * 128, xT, bk)
                rope_evac(bk, qT[:, fc, :], qT)
                pfree(bk)
            stage("b2_%d" % g)
            wk = wblock_in(cbase + 512, 512)
            for fc in range(4):
                bk = palloc()
                proj_fm(wk, fc * 128, xT, bk)
                rope_evac(bk, kc_[:, fc, :], kc_)
                pfree(bk)
            stage("b3_%d" % g)
            wv = wblock_in(cbase + 1024, 512)
            for u in range(4):
                bk = palloc()
                proj_tm(wv, 512, xT, u, bk)
                CP("act", vc_[:, u, :, 0:64], bk.t[:, :].rearrange("p (h d) -> p h d", h=8), [bk], [vc_])
                pfree(bk)
            stage("proj%d" % g)
            nprev = {0: 1, 1: 1, 2: 4}[g]
            if T < NT - 1:
                P.dma(DQ, kscr[g][T].t[:, :, :], kc_[:, :, :], kc_, reads=[kc_], writes=[kscr[g][T]])
                P.dma(DQ, vscr[g][T].t[:, :, :, :], vc_[:, :, :, :], vc_, reads=[vc_], writes=[vscr[g][T]])
            first_out = seq - PW[g]
            units_out = []
            if g == 0:
                if T == NT - 1:
                    units_out = [3]
            elif (T + 1) * TT > first_out:
                units_out = [0, 1, 2, 3]
            for u in units_out:
                st = kvst[0]
                kvst_i[0] += 1
                bk = palloc()
                bv = bk.t[:].bitcast(BF16)
                for hp in range(4):
                    TR(bv[:, hp * 128:(hp + 1) * 128], kc_[:, hp, u * 128:(u + 1) * 128], ident_b[:], [kc_, ident_b], [bk])
                CP("act", st[:, 0, :], bv[:, 0:512], [bk], [st])
                pfree(bk)
                CP("pool", st[:, 1, :].rearrange("p (h d) -> p h d", h=8), vc_[:, u, :, 0:64], [vc_], [st])
                rbase = T * TT - first_out
                if g == 0:
                    P.dma(DQ, p_kv[g][s, 0:128, :, :], st[:, :, :], st, reads=[st])
                elif g == 1:
                    dv = p_kv[g][s, rbase:rbase + TT, :, :].rearrange("(w u) c f -> u w c f", u=4)
                    P.dma(DQ, dv[u], st[:, :, :], st, reads=[st])
                else:
                    dv = p_kv[g][s, rbase:rbase + TT, :, :].rearrange("(i u r) c f -> u r i c f", u=4, r=4)
                    for r in range(4):
                        P.dma(DQ, dv[u, r], st[r * 32:(r + 1) * 32, :, :], st, reads=[st])

            stage("pkv%d" % g)
            def cur_k(hp, u, kb=kc_):
                return kb, kb[:, hp, u * 128:(u + 1) * 128]

            def cur_v(h, u, vb=vc_):
                return vb, vb[:, u, h, 0:65]

            for hp in range(4):
                srcs = []
                deltas = []
                if g == 2:
                    deltas = [d_ for d_ in (4, 3, 2, 1) if T - d_ >= 0]
                elif T >= 1:
                    deltas = [1]
                for d_ in deltas:
                    ks = kstr[str_i[0] % NSTR]
                    vs = vstr[str_i[0] % NSTR]
                    str_i[0] += 1
                    Tp = T - d_
                    if g == 0:
                        P.dma(DQ, ks[:, 384:512], kscr[g][Tp].t[:, hp, 384:512], ks, reads=[kscr[g][Tp]], writes=[ks])
                        P.dma(DQ, vs[:, 3, :, :], vscr[g][Tp].t[:, 3, 2 * hp:2 * hp + 2, :], vs, reads=[vscr[g][Tp]], writes=[vs])
                    else:
                        P.dma(DQ, ks[:, :], kscr[g][Tp].t[:, hp, :], ks, reads=[kscr[g][Tp]], writes=[ks])
                        P.dma(DQ, vs[:, :, :, :], vscr[g][Tp].t[:, :, 2 * hp:2 * hp + 2, :], vs, reads=[vscr[g][Tp]], writes=[vs])

                    def sk(hp_, u, ks=ks):
                        return ks, ks[:, u * 128:(u + 1) * 128]

                    def sv(h, u, vs=vs):
                        return vs, vs[:, u, h % 2, 0:65]
                    if g == 0:
                        pass
                    elif g == 1:
                        srcs.append(dict(units=[0, 1, 2, 3], k=sk, v=sv, mask=0))
                    else:
                        srcs.append(dict(units=[0, 1, 2, 3], k=sk, v=sv, mask={4: 3, 3: 2, 2: 2, 1: 2}[d_]))
                if g == 0:
                    if T >= 1:
                        def ak(hp_, u, ks=ks):
                            if u == 0:
                                return ks, ks[:, 384:512]
                            return cur_k(hp_, u - 1)

                        def av(h, u, vs=vs):
                            if u == 0:
                                return vs, vs[:, 3, h % 2, 0:65]
                            return cur_v(h, u - 1)
                        srcs.append(dict(units=[0, 1, 2, 3], k=ak, v=av, mask=0))
                    else:
                        srcs.append(dict(units=[1, 2, 3], k=lambda hp_, u: cur_k(hp_, u - 1),
                                         v=lambda h, u: cur_v(h, u - 1), mask=0))
                    srcs.append(dict(units=[0, 1, 2, 3], k=cur_k, v=cur_v, mask=1))
                elif g == 1:
                    srcs.append(dict(units=[0, 1, 2, 3], k=cur_k, v=cur_v, mask=1))
                else:
                    srcs.append(dict(units=[0, 1, 2, 3], k=cur_k, v=cur_v, mask=4))

                for hh in range(2):
                    h = 2 * hp + hh
                    pr = slice(64 * hh, 64 * hh + 64)
                    pts = []
                    for src in srcs:
                        sb_ = palloc()
                        u0 = src["units"][0]
                        nu = 4 - u0
                        MM(sb_[:, u0 * 128:512].rearrange("p (u w) -> p u w", u=nu), ident_b[:, :],
                           negm_b[:, src["mask"], :].unsqueeze(1).to_broadcast([128, nu, 128]), True, False,
                           [ident_b, negm_b], [sb_])
                        for u in src["units"]:
                            kb, kap = src["k"](hp, u)
                            MM(sb_[:, u * 128:(u + 1) * 128], kap[pr, :], qT[pr, hp, u * 128:(u + 1) * 128], False, u == 3,
                               [kb, qT], [sb_])
                        pt = PTb[pt_i[0] % NPT]
                        pt_i[0] += 1
                        ACT(pt[:, u0 * 128:512], sb_[:, u0 * 128:512], AF.Exp, [sb_], [pt], scale=0.125)
                        pfree(sb_)
                        pts.append(pt)
                    ob = palloc()
                    for u in range(4):
                        contrib = [(src, pt) for src, pt in zip(srcs, pts) if u in src["units"]]
                        for ci, (src, pt) in enumerate(contrib):
                            vb, vap = src["v"](h, u)
                            MM(ob[0:65, u * 128:(u + 1) * 128], vap, pt[:, u * 128:(u + 1) * 128], ci == 0,
                               ci == len(contrib) - 1, [vb, pt], [ob])
                    if g == 0:
                        CP("act", acc[0:65, h, :], ob[0:65, :], [ob], [acc])
                    else:
                        av_ = acc_view(g, h, slice(0, 65))
                        TT_("dve", av_, av_, bank_view(g, ob, slice(0, 65)), ALU.add, [acc, ob], [acc])
                    pfree(ob)
            bd()

        stage("attn")
        ACT(acc[64:65, :, :], acc[64:65, :, :], AF.Ln, [acc], [acc])
        ACT(acc[64:65, :, :], acc[64:65, :, :], AF.Exp, [acc], [acc], scale=-1.0)
        for h in range(8):
            bk = palloc()
            MM(bk[0:64, :], sel_f[:, :], acc[0:65, h, :], True, True, [sel_f, acc], [bk])
            TT_("dve", attnT[:, h, :], acc[0:64, h, :], bk[0:64, :], ALU.mult, [acc, bk], [attnT])
            pfree(bk)

        stage("merge")
        if T == 0:
            MSET("pool", stT[:], 0.0, [stT])
            MSET("pool", stz[:], 0.0, [stz])
            MSET("pool", convtail[:], 0.0, [convtail])
        MSET("pool", xdtz[:], 0.0, [xdtz])
        for blk in range(2):
            wz = wblock_in(O_Z + blk * 512, 512)
            for fc in range(4):
                bk = palloc()
                proj_fm(wz, fc * 128, xnT0, bk)
                ACT(sz[:, blk * 4 + fc, :], bk[:], AF.Silu, [bk], [sz])
                pfree(bk)
        for blk in range(3):
            wx = wblock_in(O_XBC + blk * 512, 512)
            for fc in range(4):
                c = blk * 4 + fc
                sg_ = stg[c % 2]
                ca = cacc[c % 2]
                bk = palloc()
                proj_fm(wx, fc * 128, xnT0, bk)
                CP("pool", sg_[:, 0:3], convtail[:, c, :], [convtail], [sg_])
                CP("act", sg_[:, 3:515], bk[:], [bk], [sg_])
                pfree(bk)
                CP("pool", convtail[:, c, :], sg_[:, 512:515], [sg_], [convtail])
                TS("dve", ca[:], sg_[:, 0:512], cw_t[:, c, 0:1], None, ALU.mult, None, [sg_, cw_t], [ca])
                for jj in range(1, 4):
                    STT("dve", ca[:], sg_[:, jj:jj + 512], cw_t[:, c, jj:jj + 1], ca[:], ALU.mult, ALU.add,
                        [sg_, cw_t, ca], [ca])
                ACT(xc[:, c, :], ca[:], AF.Silu, [ca, cb_t], [xc], bias=cb_t[:, c:c + 1])
        stage("conv")
        bd()
        wdt = wblock_in(O_DT, 16)
        for j in range(4):
            bd()
            jb = slice(j * 128, (j + 1) * 128)
            pj = str(j % 2)
            dt_ = sm("dt" + pj, 16)
            a_ = sm("a" + pj, 16)
            cs_sb = sm("cs" + pj, 16)
            ncs = sm("ncs" + pj, 16)
            lastcs = sm("lastcs" + pj, 16)
            dend = sm("dend" + pj, 16)
            cdec = sm("cdec" + pj, 16)
            dtd = sm("dtd" + pj, 16)
            Gm_ = Gm[j % 2]
            bk = palloc()
            proj_tm(wdt, 16, xnT0, j, bk)
            TT_("dve", dt_[:], bk[:, 0:16], dtb_t[:], ALU.add, [bk, dtb_t], [dt_])
            pfree(bk)
            ACT(dt_[:], dt_[:], AF.Exp, [dt_], [dt_])
            ACT(dt_[:], dt_[:], AF.Ln, [dt_, one_t], [dt_], bias=one_t[:])
            TT_("dve", a_[:], dt_[:], A_t[:], ALU.mult, [dt_, A_t], [a_])
            bk = palloc()
            MM(bk[:, 0:16], U_f[:], a_[:], True, True, [U_f, a_], [bk])
            CP("dve", cs_sb[:], bk[:, 0:16], [bk], [cs_sb])
            pfree(bk)
            TS("dve", ncs[:], cs_sb[:], -1.0, None, ALU.mult, None, [cs_sb], [ncs])
            bk = palloc()
            for gg in range(2):
                MM(bk[:, gg * 128:(gg + 1) * 128], xc[:, 8 + gg, jb], xc[:, 10 + gg, jb], True, True, [xc], [bk])
            TT_("dve", Gm_[:], bk.t[:, 0:256].rearrange("p (g t) -> p g t", g=2),
                masks_b[:, 0, :].unsqueeze(1).to_broadcast([128, 2, 128]), ALU.mult, [bk, masks_b], [Gm_])
            pfree(bk)
            for qd in range(4):
                hs = slice(qd * 4, qd * 4 + 4)
                gg = qd // 2
                Rb_, Eb_, ecs_ = Rb[qd % 2], Eb[qd % 2], ecs[qd % 2]
                TT_("pool", Rb_[:], a_[:, hs].unsqueeze(2).to_broadcast([128, 4, 128]),
                    U_f[:].unsqueeze(1).to_broadcast([128, 4, 128]), ALU.mult, [a_, U_f], [Rb_])
                bk = palloc()
                MM(bk[:], ones_f[:], Rb_[:].rearrange("p h t -> p (h t)"), True, True, [ones_f, Rb_], [bk])
                bk3 = bk.t[:, :].rearrange("p (h t) -> p h t", h=4)
                for hh in range(4):
                    ACT(Eb_[:, hh, :], bk3[:, hh, :], AF.Exp, [bk, ncs], [Eb_], bias=ncs[:, qd * 4 + hh:qd * 4 + hh + 1])
                ACT(ecs_[:], bk3, AF.Exp, [bk], [ecs_])
                CP("act", lastcs[:, hs], bk3[:, :, 127], [bk], [lastcs])
                pfree(bk)
                STT("dve", Wb[:, hs, :], Eb_[:], 1e30, Gm_[:, gg, :].unsqueeze(1).to_broadcast([128, 4, 128]),
                    ALU.min, ALU.mult, [Eb_, Gm_], [Wb])
                TT_("pool", Cdec[:, hs, :], ecs_[:], xc[:, 10 + gg, jb].unsqueeze(1).to_broadcast([128, 4, 128]), ALU.mult,
                    [ecs_, xc], [Cdec])
            TT_("dve", dend[:], lastcs[:], cs_sb[:], ALU.subtract, [lastcs, cs_sb], [dend])
            ACT(dend[:], dend[:], AF.Exp, [dend], [dend])
            ACT(cdec[:], lastcs[:], AF.Exp, [lastcs], [cdec])
            TT_("dve", dtd[:], dt_[:], dend[:], ALU.mult, [dt_, dend], [dtd])
            bk = palloc()
            bv = bk.t[:].bitcast(BF16)
            for c in range(8):
                TR(bv[:, c * 128:(c + 1) * 128], xc[:, c, jb], ident_b[:], [xc, ident_b], [bk])
            xv = bv.rearrange("p (c e d) -> p c e d", c=8, e=2)
            xz = xdtz[:].rearrange("p (c e) f -> p c e f", e=2)
            dtv = dt_[:].rearrange("p (c e) -> p c e", e=2)
            for e_ in range(2):
                TT_("dve", xz[:, :, e_, 64 * e_:64 * e_ + 64], xv[:, :, e_, :],
                    dtv[:, :, e_].unsqueeze(2).to_broadcast([128, 8, 64]), ALU.mult, [bk, dt_], [xdtz])
            TT_("dve", xdd[:], bv[:, 0:1024].rearrange("p (h d) -> p h d", h=16),
                dtd[:].unsqueeze(2).to_broadcast([128, 16, 64]), ALU.mult, [bk, dtd], [xdd])
            pfree(bk)
            bk = palloc()
            bv = bk.t[:].bitcast(BF16)
            for gg in range(2):
                TR(bv[:, gg * 128:(gg + 1) * 128], xc[:, 8 + gg, jb], ident_b[:], [xc, ident_b], [bk])
            CP("act", Btok[:].rearrange("p g n -> p (g n)"), bv[:, 0:256], [bk], [Btok])
            pfree(bk)
            for k2 in range(2):
                bk = palloc()
                for cc in range(4):
                    c = k2 * 4 + cc
                    reg = bk[:, cc * 128:(cc + 1) * 128]
                    MM(reg, xdtz[:, 2 * c, :], Wb[:, 2 * c, :], True, False, [xdtz, Wb], [bk])
                    MM(reg, xdtz[:, 2 * c + 1, :], Wb[:, 2 * c + 1, :], False, False, [xdtz, Wb], [bk])
                    MM(reg, stz[:, 2 * c, :], Cdec[:, 2 * c, :], False, False, [stz, Cdec], [bk])
                    MM(reg, stz[:, 2 * c + 1, :], Cdec[:, 2 * c + 1, :], False, True, [stz, Cdec], [bk])
                for cc in range(4):
                    c = k2 * 4 + cc
                    yt = ytmp[cc % 2]
                    STT("dve", yt[:], xc[:, c, jb], D_t[:, c:c + 1], bk[:, cc * 128:(cc + 1) * 128], ALU.mult, ALU.add,
                        [xc, D_t, bk], [yt])
                    TT_("pool", yT[:, c, jb], yt[:], sz[:, c, jb], ALU.mult, [yt, sz], [yT])
                pfree(bk)
            for gg in range(2):
                bk = palloc()
                MM(bk[:], Btok[:, gg, :], xdd[:, 8 * gg:8 * gg + 8, :].rearrange("p h d -> p (h d)"), True, True,
                   [Btok, xdd], [bk])
                sv_ = stT[:, 8 * gg:8 * gg + 8, :]
                TT_("dve", sv_, sv_, cdec[:, 8 * gg:8 * gg + 8].unsqueeze(2).to_broadcast([128, 8, 64]), ALU.mult,
                    [stT, cdec], [stT])
                TT_("dve", sv_, sv_, bk.t[:, :].rearrange("p (h d) -> p h d", h=8), ALU.add, [stT, bk], [stT])
                pfree(bk)
            sz_ = stz[:].rearrange("p (c e) f -> p c e f", e=2)
            st_ = stT[:].rearrange("p (c e) d -> p c e d", e=2)
            for e_ in range(2):
                CP("pool", sz_[:, :, e_, 64 * e_:64 * e_ + 64], st_[:, :, e_, :], [stT], [stz])
        stage("ssd")
        bk = palloc()
        for c in range(8):
            sq = sqb[c % 2]
            ACT(sq[:], yT[:, c, :], AF.Square, [yT], [sq])
            MM(bk[:], ones_b[:], sq[:], c == 0, c == 7, [ones_b, sq], [bk])
        ACT(rstd_b[:], bk[:], AF.Ln, [bk, eps_t], [rstd_b], scale=1.0 / D, bias=eps_t[:])
        pfree(bk)
        ACT(rstd_b[:], rstd_b[:], AF.Exp, [rstd_b], [rstd_b], scale=-0.5)
        for c in range(8):
            TT_("dve" if c % 2 else "pool", yT[:, c, :], yT[:, c, :], rstd_b[:], ALU.mult, [yT, rstd_b], [yT])
        if T == NT - 1:
            for j_ in range(3):
                P.dma(DQ, p_conv[s, j_, :].rearrange("(c p) -> p c", p=128), convtail[:, :, j_], convtail,
                      reads=[convtail], allow_slow_non_contiguous=True)
            stf = stT[:].rearrange("p h d -> p (h d)")
            for half in range(2):
                bk = palloc()
                for cc in range(4):
                    c = half * 4 + cc
                    TR(bk[:, cc * 128:(cc + 1) * 128], stf[:, c * 128:(c + 1) * 128], ident_f[:], [stT, ident_f], [bk])
                st = kvst[0]
                kvst_i[0] += 1
                stv = st[:].rearrange("p a b -> p (a b)")[:, 0:512]
                CP("act", stv, bk[:], [bk], [st])
                pfree(bk)
                P.dma(DQ, p_ssm[s, half * 512:(half + 1) * 512, :].rearrange("(c q) n -> q c n", q=128),
                      stv.rearrange("p (c n) -> p c n", c=4), st, reads=[st])

        stage("ssdout")
        out_epilogue_phase(lambda half: (wblock_outA(half), wblock_outS(half)), "mix", nw_mix)

        stage("E")
        bd()
        for j in range(4):
            norm_transpose(hb[j], j, xnTp)
        stage("F")
        if nxt is not None:
            phase_A(*nxt)
        bd()
        for blk in range(6):
            ncol = 512 if blk < 5 else 256
            wg = wblock_fi(blk * 512, ncol)
            wu = wblock_fi(FFN_H + blk * 512, ncol)
            for fc in range(ncol // 128):
                c = blk * 4 + fc
                gb = palloc()
                proj_fm(wg, fc * 128, xnTp, gb)
                ub = palloc()
                proj_fm(wu, fc * 128, xnTp, ub)
                sg_ = sgt[c % 2]
                ACT(sg_[:], gb[:], AF.Silu, [gb], [sg_])
                pfree(gb)
                TT_("dve", actT[:, c, :], sg_[:], ub[:], ALU.mult, [sg_, ub], [actT])
                pfree(ub)
        stage("G")
        out_epilogue_phase(None, "ffn", nw_ffn)
        for j in range(4):
            P.dma(DQ, y_prompt[s, t0 + j * 128:t0 + (j + 1) * 128, :], hb[j][:], hb[j], reads=[hb[j]])

    def out_epilogue_phase(wfn, kind, nw):
        ssA = [sm("ssA%d" % j) for j in range(4)]
        ssB = [sm("ssB%d" % j) for j in range(4)]
        for half in range(2):
            bks = [palloc() for _ in range(4)]
            if kind == "mix":
                wA, wS = wfn(half)
                for j in range(4):
                    jb = slice(j * 128, (j + 1) * 128)
                    for h in range(8):
                        MM(bks[j][:], attnT[0:64, h, jb], wA[0:64, h, :], h == 0, False, [attnT, wA], [bks[j]])
                    for c in range(8):
                        MM(bks[j][:], yT[:, c, jb], wS[:, c, :], False, c == 7, [yT, wS], [bks[j]])
            else:
                for (c0, n) in ((0, 8), (8, 8), (16, 6)):
                    w = wblock_fo(c0, n, half)
                    for j in range(4):
                        jb = slice(j * 128, (j + 1) * 128)
                        for ci in range(n):
                            c = c0 + ci
                            MM(bks[j][:], actT[:, c, jb], w[:, ci, :], c == 0, c == NFC - 1, [actT, w], [bks[j]])
            for j in range(4):
                bk = bks[j]
                if half == 0:
                    CP("act", m0[j][:], bk[:], [bk], [m0[j]])
                    ACT(junk[:, 0:512], bk[:], AF.Square, [bk], [junk, ssA[j]], accum_out=ssA[j][:])
                    pfree(bk)
                else:
                    ACT(junk[:, 0:512], bk[:], AF.Square, [bk], [junk, ssB[j]], accum_out=ssB[j][:])
                    tot = sm("tot%d" % (j % 2))
                    rs = sm("ers%d" % (j % 2))
                    TT_("dve", tot[:], ssA[j][:], ssB[j][:], ALU.add, [ssA[j], ssB[j]], [tot])
                    rstd_from(tot[:], tot, rs)
                    t1 = rt1[j % 2]
                    t2 = rt2[j % 2]
                    STT("dve", t1[:], m0[j][:], rs[:], nw[:, 0:512], ALU.mult, ALU.mult, [m0[j], rs, nw], [t1])
                    STT("dve", t2[:], bk[:], rs[:], nw[:, 512:1024], ALU.mult, ALU.mult, [bk, rs, nw], [t2])
                    pfree(bk)
                    TT_("pool", hb[j][:, 0:512], hb[j][:, 0:512], t1[:], ALU.add, [hb[j], t1], [hb[j]])
                    TT_("pool", hb[j][:, 512:1024], hb[j][:, 512:1024], t2[:], ALU.add, [hb[j], t2], [hb[j]])

    bulk = []

    def bulk_drain(k):
        for _ in range(min(k, len(bulk))):
            o, i_, carrier = bulk.pop(0)
            P.dma("act", o, i_, carrier)

    def sample_phase():
        NS = ns
        R = slice(0, NS)
        qscr = P.dram("qscr", [NS, 3, 512], F32)
        dmy = [Buf(None, "dmy%d" % i) for i in range(4)]

        def sub(parent, ap, name):
            v = Buf(ap, name)
            Prog.alias(parent, [v])
            return v

        def f32view(parent, ncols, name, rows=NS):
            t = parent.t
            ap = t[0:rows] if len(t.shape) == 2 else t[0:rows].rearrange("p a b -> p (a b)")
            if ap.dtype != F32:
                ap = ap.bitcast(F32)
            return sub(parent, ap[:, 0:ncols], name)

        for g in range(3):
            wb = WB[g]
            assert wb == 128 * DILS[g], "sample path assumes a full window in the cache"
            step = 256
            for b in range(NS):
                for r0 in range(0, wb - 1, step):
                    n = min(step, wb - 1 - r0)
                    bulk.append((s_kv[g][b, r0:r0 + n, :, :], cache[g][b, r0 + 1:r0 + 1 + n, :, :], dmy[g]))
        for b0 in range(0, NS, 4):
            P.dma("act", s_conv[b0:b0 + 4, 0:2, :], state_conv[b0:b0 + 4, 1:3, :], dmy[3])

        arA.phase()
        proj_s = arA.take("proj_s", [NS, MIX_IN], F32)
        srope = arA.take("srope", [NS, 2, 512], F32)
        kv_t = [arA.take("kv_t0", [128, 2, 512], F32)]
        prod = arA.take("prod", [128, 512], F32)
        arE.phase()
        kv_t.append(arE.take("kv_t1", [128, 2, 512], F32))
        pvx = [arE.take("pvx%d" % i, [128, 520], F32) for i in range(2)]
        cwrow = [arE.take("cwrow%d" % i, [NS, 1536], F32) for i in range(2)]
        arE.phase()
        ffn_s = arE.take("ffn_s", [NS, 2 * FFN_H], F32)
        arB.phase()
        cacc_s = arB.take("cacc_s", [NS, 1536], F32)
        arC.phase()
        nn = arC.take("nn", [NS, 512], F32)
        attn_f = arC.take("attn_f", [NS, 512], F32)
        arC.phase()
        Bd = arC.take("Bd", [NS, 16, 128], F32)
        arD.phase()
        qb_t = [arD.take("qb_t%d" % i, [128, 512], F32) for i in range(2)]
        tq = arD.take("tq", [NS, 512], F32)
        rq = arD.take("rq", [NS, 512], F32)
        arD.phase()
        Cd = arD.take("Cd", [NS, 16, 128], F32)
        sc_rows = [f32view(xnT0, 1536, "sc0"), f32view(xnTp, 1536, "sc1"), f32view(yT, 1536, "sc2")]
        xs_t = sub(hb[0], hb[0].t[R, :], "xs_t")
        h_s = sub(hb[1], hb[1].t[R, :], "h_s")
        dtx = sub(hb[2], hb[2].t[R, :], "dtx")
        dAx = sub(hb[3], hb[3].t[R, :], "dAx")
        st_s = [sub(xin[i], xin[i].t[:, 0:512].rearrange("p (b n) -> p b n", b=4), "st_s%d" % i) for i in range(2)]
        t1s = f32view(junk, 512, "t1s", rows=128)
        t2s = f32view(xnb[0], 512, "t2s", rows=128)
        hnew = [f32view(xnb[1], 512, "hnew0", rows=128)]
        arS = Arena(P, "arS", 4096)
        hnew.append(arS.take("hnew1", [128, 512], F32))
        t3s = arS.take("t3s", [128, 512], F32)
        xsT = P.sbuf("xsT", [128, 8, NS], BF16)
        catT = P.sbuf("catT", [128, 12, NS], BF16)
        actsT = P.sbuf("actsT", [128, NFC, NS], BF16)
        dtxT = P.sbuf("dtxT", [128, 8, NS], F32)
        dAT = P.sbuf("dAT", [128, 8, NS], F32)
        ySST = P.sbuf("ySST", [128, 8, NS], F32)
        esel_t = P.sbuf("esel_t", [128, 16, 16], F32)
        s8 = P.sbuf("s8", [128, 8], F32)
        p8 = P.sbuf("p8", [128, 8], F32)
        snew = P.sbuf("snew", [NS, 8], F32)
        pnew = P.sbuf("pnew", [NS, 8], F32)
        dnew = P.sbuf("dnew", [NS, 8], F32)
        dts = P.sbuf("dts", [NS, 16], F32)
        dAs = P.sbuf("dAs", [NS, 16], F32)
        drow = P.sbuf("drow", [NS, 16], F32)
        ss16 = P.sbuf("ss16", [NS, 1], F32)
        rs16 = P.sbuf("rs16", [NS, 1], F32)
        catb0 = sub(qraw[0], qraw[0].t[R, :], "catb0")
        catb1 = sub(xnb[1], xnb[1].t[R, :], "catb1")
        nb16 = sub(xnb[0], xnb[0].t[R, :], "nb16")
        Prog.alias(catb1, [hnew[0]])
        Prog.alias(nb16, [t2s])

        P.dma(DQ, esel_t[:].rearrange("p a b -> p (a b)"), c_esel[:, :], esel_t, writes=[esel_t])
        P.dma(DQ, srope[:], c_srope[:, :].rearrange("c (o n) -> o c n", o=1).partition_broadcast(NS), srope, writes=[srope])
        P.dma(DQ, drow[:], d_skip[0:1, :].partition_broadcast(NS), drow, writes=[drow])

        def rms16(src_ap, src_buf, dst_ap, dst_buf, n=D):
            ACT(junk[R, 0:n], src_ap, AF.Square, [src_buf], [junk, ss16], accum_out=ss16[:])
            ACT(rs16[:], ss16[:], AF.Ln, [ss16, eps_t], [rs16], scale=1.0 / n, bias=eps_t[R, :])
            ACT(rs16[:], rs16[:], AF.Exp, [rs16], [rs16], scale=-0.5)
            TS("dve", dst_ap, src_ap, rs16[:], None, ALU.mult, None, [src_buf, rs16], [dst_buf])

        def transpose16(src_ap, src_buf, nchunk, dstT):
            bk = palloc()
            bv = bk.t[:].bitcast(BF16)
            for c in range(nchunk):
                TR(bv[:, c * NS:(c + 1) * NS], src_ap[:, c * 128:(c + 1) * 128], ident_b[R, R], [src_buf, ident_b], [bk])
            CP("act", dstT[:, 0:nchunk, :], bv[:, 0:nchunk * NS].rearrange("p (c t) -> p c t", c=nchunk), [bk], [dstT])
            pfree(bk)

        P.dma(DQ, xs_t[:], x_sample[:, :], xs_t, writes=[xs_t])
        rms16(xs_t[:], xs_t, nb16[:], nb16)
        transpose16(nb16, nb16, 8, xsT)
        for c0 in range(0, MIX_IN, 512):
            ncol = min(512, MIX_IN - c0)
            w = wblock_in(c0, ncol)
            bk = palloc()
            for kc in range(8):
                MM(bk[R, 0:ncol], xsT[:, kc, :], w[:, kc, 0:ncol], kc == 0, kc == 7, [xsT, w], [bk])
            CP("act", proj_s[:, c0:c0 + ncol], bk[R, 0:ncol], [bk], [proj_s])
            pfree(bk)
        for g in range(3):
            for which in range(2):
                c0 = g * 1536 + which * 512
                qv = proj_s[:, c0:c0 + 512]
                q3 = qv.rearrange("p (h e d) -> p h e d", h=8, e=2)
                r3 = rq[:].rearrange("p (h e d) -> p h e d", h=8, e=2)
                CP("pool", r3[:, :, 0, :], q3[:, :, 1, :], [proj_s], [rq])
                CP("pool", r3[:, :, 1, :], q3[:, :, 0, :], [proj_s], [rq])
                TT_("dve", tq[:], qv, srope[:, 0, :], ALU.mult, [proj_s, srope], [tq])
                TT_("dve", rq[:], rq[:], srope[:, 1, :], ALU.mult, [rq, srope], [rq])
                TT_("dve", qv, tq[:], rq[:], ALU.add, [tq, rq], [proj_s])
            P.dma(DQ, qscr.t[:, g, :], proj_s[:, g * 1536:g * 1536 + 512], proj_s, reads=[proj_s], writes=[qscr])
            P.dma(DQ, s_kv[g][:, WB[g] - 1, 0, :], proj_s[:, g * 1536 + 512:g * 1536 + 1024], proj_s, reads=[proj_s])
            P.dma(DQ, s_kv[g][:, WB[g] - 1, 1, :], proj_s[:, g * 1536 + 1024:g * 1536 + 1536], proj_s, reads=[proj_s])
        nbank = palloc()
        dbank = palloc()
        it = 0
        total = 3 * NS
        for g in range(3):
            dil = DILS[g]
            for b in range(NS):
                kv = kv_t[it % 2]
                qb = qb_t[it % 2]
                px = pvx[it % 2]
                P.dma(DQ, kv[:], cache[g][b, :, :, :].rearrange("(i d) c f -> i d c f", d=dil)[:, 0, :, :], kv, writes=[kv])
                P.dma(DQ, qb[:], qscr.t[b:b + 1, g, :].partition_broadcast(128), qb, reads=[qscr], writes=[qb])
                TT_("dve", prod[:], kv[:, 0, :], qb[:], ALU.mult, [kv, qb], [prod])
                P.op("dve", lambda e: e.tensor_reduce(out=s8[:], in_=prod[:].rearrange("p (h d) -> p h d", h=8),
                                                      axis=mybir.AxisListType.X, op=ALU.add), [prod], [s8], cost=600.0)
                ACT(px[:, 512:520], s8[:], AF.Exp, [s8], [px], scale=0.125)
                TT_("pool", px[:, 0:512].rearrange("p (h d) -> p h d", h=8), kv[:, 1, :].rearrange("p (h d) -> p h d", h=8),
                    px[:, 512:520].unsqueeze(2).to_broadcast([128, 8, 64]), ALU.mult, [kv, px], [px])
                MM(nbank[R, :], esel_t[:, b, :], px[:, 0:512], it == 0, it == total - 1, [esel_t, px], [nbank])
                MM(dbank[R, 0:8], esel_t[:, b, :], px[:, 512:520], it == 0, it == total - 1, [esel_t, px], [dbank])
                it += 1
        for g in range(3):
            c0 = g * 1536
            TT_("dve", tq[:], proj_s[:, c0:c0 + 512], proj_s[:, c0 + 512:c0 + 1024], ALU.mult, [proj_s], [tq])
            P.op("dve", lambda e: e.tensor_reduce(out=snew[:], in_=tq[:].rearrange("p (h d) -> p h d", h=8),
                                                  axis=mybir.AxisListType.X, op=ALU.add), [tq], [snew])
            ACT(pnew[:], snew[:], AF.Exp, [snew], [pnew], scale=0.125)
            tgt = nn if g == 0 else tq
            TT_("dve", tgt[:].rearrange("p (h d) -> p h d", h=8),
                proj_s[:, c0 + 1024:c0 + 1536].rearrange("p (h d) -> p h d", h=8),
                pnew[:].unsqueeze(2).to_broadcast([NS, 8, 64]), ALU.mult, [proj_s, pnew], [tgt])
            if g == 0:
                CP("dve", dnew[:], pnew[:], [pnew], [dnew])
            else:
                TT_("dve", nn[:], nn[:], tq[:], ALU.add, [nn, tq], [nn])
                TT_("dve", dnew[:], dnew[:], pnew[:], ALU.add, [dnew, pnew], [dnew])
        TT_("dve", nn[:], nn[:], nbank[R, :], ALU.add, [nn, nbank], [nn])
        TT_("dve", dnew[:], dnew[:], dbank[R, 0:8], ALU.add, [dnew, dbank], [dnew])
        pfree(nbank)
        pfree(dbank)
        P.op("dve", lambda e: e.reciprocal(out=dnew[:], in_=dnew[:]), [dnew], [dnew])
        TT_("dve", attn_f[:].rearrange("p (h d) -> p h d", h=8), nn[:].rearrange("p (h d) -> p h d", h=8),
            dnew[:].unsqueeze(2).to_broadcast([NS, 8, 64]), ALU.mult, [nn, dnew], [attn_f])
        CP("dve", catb0[:], attn_f[:], [attn_f], [catb0])

        for j_ in range(3):
            P.dma(DQ, sc_rows[j_][:], state_conv[:, j_, :], sc_rows[j_], writes=[sc_rows[j_]])
        xbc = proj_s[:, O_XBC:O_XBC + 1536]
        P.dma(DQ, s_conv[:, 2, :], xbc, proj_s, reads=[proj_s])
        for j_ in range(4):
            cw = cwrow[j_ % 2]
            P.dma(DQ, cw[:], conv_w[j_:j_ + 1, :].partition_broadcast(NS), cw, writes=[cw])
            src_ap, src_b = (sc_rows[j_][:], sc_rows[j_]) if j_ < 3 else (xbc, proj_s)
            if j_ == 0:
                TT_("dve", cacc_s[:], src_ap, cw[:], ALU.mult, [src_b, cw], [cacc_s])
            else:
                TT_("dve", cw[:], src_ap, cw[:], ALU.mult, [src_b, cw], [cw])
                TT_("dve", cacc_s[:], cacc_s[:], cw[:], ALU.add, [cacc_s, cw], [cacc_s])
        cw = cwrow[0]
        P.dma(DQ, cw[:], conv_b[0:1, :].partition_broadcast(NS), cw, writes=[cw])
        TT_("dve", cacc_s[:], cacc_s[:], cw[:], ALU.add, [cacc_s, cw], [cacc_s])
        ACT(cacc_s[:], cacc_s[:], AF.Silu, [cacc_s], [cacc_s])
        TT_("dve", dts[:], proj_s[:, O_DT:O_DT + 16], dtb_t[R, :], ALU.add, [proj_s, dtb_t], [dts])
        ACT(dts[:], dts[:], AF.Exp, [dts], [dts])
        ACT(dts[:], dts[:], AF.Ln, [dts, one_t], [dts], bias=one_t[R, :])
        TT_("dve", dAs[:], dts[:], A_t[R, :], ALU.mult, [dts, A_t], [dAs])
        ACT(dAs[:], dAs[:], AF.Exp, [dAs], [dAs])
        xs3 = cacc_s[:, 0:1024].rearrange("p (h d) -> p h d", h=16)
        TT_("dve", dtx[:].rearrange("p (h d) -> p h d", h=16), xs3, dts[:].unsqueeze(2).to_broadcast([NS, 16, 64]),
            ALU.mult, [cacc_s, dts], [dtx])
        CP("dve", dAx[:].rearrange("p (h d) -> p h d", h=16), dAs[:].unsqueeze(2).to_broadcast([NS, 16, 64]), [dAs], [dAx])
        for (srcb, dstT) in ((dtx, dtxT), (dAx, dAT)):
            bk = palloc()
            for c in range(8):
                TR(bk[:, c * NS:(c + 1) * NS], srcb[:, c * 128:(c + 1) * 128], ident_f[R, R], [srcb, ident_f], [bk])
            CP("act", dstT[:], bk[:, 0:8 * NS].rearrange("p (c t) -> p c t", c=8), [bk], [dstT])
            pfree(bk)
        idb = ident_f[R, 0:NS].unsqueeze(2).to_broadcast([NS, NS, 128])
        it = 0
        for gg in range(2):
            TT_("dve", Bd[:], cacc_s[:, 1024 + gg * 128:1024 + (gg + 1) * 128].unsqueeze(1).to_broadcast([NS, NS, 128]),
                idb, ALU.mult, [cacc_s, ident_f], [Bd])
            TT_("dve", Cd[:], cacc_s[:, 1280 + gg * 128:1280 + (gg + 1) * 128].unsqueeze(1).to_broadcast([NS, NS, 128]),
                idb, ALU.mult, [cacc_s, ident_f], [Cd])
            for qb_ in range(NS // 4):
                b0 = qb_ * 4
                Bbc = palloc()
                Cbc = palloc()
                MM(Bbc[:], ones_f[R, :], Bd[:, b0:b0 + 4, :].rearrange("p b n -> p (b n)"), True, True, [ones_f, Bd], [Bbc])
                MM(Cbc[:], ones_f[R, :], Cd[:, b0:b0 + 4, :].rearrange("p b n -> p (b n)"), True, True, [ones_f, Cd], [Cbc])
                for cc in range(4):
                    c = gg * 4 + cc
                    st = st_s[it % 2]
                    hn = hnew[it % 2]
                    it += 1
                    P.dma(DQ, st[:], state_ssm[b0:b0 + 4, c * 128:(c + 1) * 128, :].rearrange("b q n -> q b n"), st, writes=[st])
                    TT_("pool", t1s[:].rearrange("p (b n) -> p b n", b=4), st[:],
                        dAT[:, c, b0:b0 + 4].unsqueeze(2).to_broadcast([128, 4, 128]), ALU.mult, [st, dAT], [t1s])
                    TT_("dve", t2s[:].rearrange("p (b n) -> p b n", b=4), Bbc.t[:, :].rearrange("p (b n) -> p b n", b=4),
                        dtxT[:, c, b0:b0 + 4].unsqueeze(2).to_broadcast([128, 4, 128]), ALU.mult, [Bbc, dtxT], [t2s])
                    TT_("pool", hn[:], t1s[:], t2s[:], ALU.add, [t1s, t2s], [hn])
                    P.dma(DQ, s_ssm[b0:b0 + 4, c * 128:(c + 1) * 128, :].rearrange("b q n -> q b n"),
                          hn[:].rearrange("p (b n) -> p b n", b=4), hn, reads=[hn])
                    TT_("dve", t3s[:], hn[:], Cbc[:], ALU.mult, [hn, Cbc], [t3s])
                    P.op("dve", lambda e, c=c, b0=b0: e.tensor_reduce(
                        out=ySST[:, c, b0:b0 + 4], in_=t3s[:].rearrange("p (b n) -> p b n", b=4),
                        axis=mybir.AxisListType.X, op=ALU.add), [t3s], [ySST], cost=600.0)
                pfree(Bbc)
                pfree(Cbc)
        ytok = dtx
        for half in range(2):
            bk = palloc()
            for cc in range(4):
                c = half * 4 + cc
                TR(bk[R, cc * 128:(cc + 1) * 128], ySST[:, c, :], ident_f[:, :], [ySST, ident_f], [bk])
            CP("act", ytok[:, half * 512:(half + 1) * 512], bk[R, :], [bk], [ytok])
            pfree(bk)
        TT_("dve", dAx[:].rearrange("p (h d) -> p h d", h=16), xs3, drow[:].unsqueeze(2).to_broadcast([NS, 16, 64]),
            ALU.mult, [cacc_s, drow], [dAx])
        TT_("dve", ytok[:], ytok[:], dAx[:], ALU.add, [ytok, dAx], [ytok])
        ACT(dAx[:], proj_s[:, O_Z:O_Z + 1024], AF.Silu, [proj_s], [dAx])
        TT_("dve", ytok[:], ytok[:], dAx[:], ALU.mult, [ytok, dAx], [ytok])
        rms16(ytok[:], ytok, catb1[:], catb1)
        transpose16(catb0, catb0, 4, catT)
        bk = palloc()
        bv = bk.t[:].bitcast(BF16)
        for c in range(8):
            TR(bv[:, c * NS:(c + 1) * NS], catb1[:, c * 128:(c + 1) * 128], ident_b[R, R], [catb1, ident_b], [bk])
        CP("act", catT[:, 4:12, :], bv[:, 0:8 * NS].rearrange("p (c t) -> p c t", c=8), [bk], [catT])
        pfree(bk)

        def post(bks, nw, res_in, res_out):
            ACT(junk[R, 0:512], bks[0][R, :], AF.Square, [bks[0]], [junk, ss16], accum_out=ss16[:])
            ACT(junk[R, 512:1024], bks[1][R, :], AF.Square, [bks[1]], [junk, rs16], accum_out=rs16[:])
            TT_("dve", ss16[:], ss16[:], rs16[:], ALU.add, [ss16, rs16], [ss16])
            ACT(rs16[:], ss16[:], AF.Ln, [ss16, eps_t], [rs16], scale=1.0 / D, bias=eps_t[R, :])
            ACT(rs16[:], rs16[:], AF.Exp, [rs16], [rs16], scale=-0.5)
            for half in range(2):
                hs = slice(half * 512, (half + 1) * 512)
                STT("dve", tq[:], bks[half][R, :], rs16[:], nw[R, hs], ALU.mult, ALU.mult, [bks[half], rs16, nw], [tq])
                TT_("dve", res_out[:, hs], res_in[:, hs], tq[:], ALU.add, [res_in, tq], [res_out])
                pfree(bks[half])

        bks = [palloc(), palloc()]
        for half in range(2):
            wA = wnext()
            P.dma(WQ, wA.t[:, 0:4, :], wout_b.t[0:512, half * 512:half * 512 + 512].rearrange("(c p) n -> p c n", p=128), wA,
                  reads=[wout_b], writes=[wA])
            wS = wblock_outS(half)
            for c in range(12):
                w_ap = wA[:, c, :] if c < 4 else wS[:, c - 4, :]
                MM(bks[half][R, :], catT[:, c, :], w_ap, c == 0, c == 11, [catT, wA, wS], [bks[half]])
        post(bks, nw_mix, xs_t, h_s)
        rms16(h_s[:], h_s, nb16[:], nb16)
        transpose16(nb16, nb16, 8, xsT)
        for c0 in range(0, 2 * FFN_H, 512):
            w = wblock_fi(c0, 512)
            bk = palloc()
            for kc in range(8):
                MM(bk[R, :], xsT[:, kc, :], w[:, kc, :], kc == 0, kc == 7, [xsT, w], [bk])
            CP("act", ffn_s[:, c0:c0 + 512], bk[R, :], [bk], [ffn_s])
            pfree(bk)
        ACT(ffn_s[:, 0:FFN_H], ffn_s[:, 0:FFN_H], AF.Silu, [ffn_s], [ffn_s])
        TT_("dve", ffn_s[:, 0:FFN_H], ffn_s[:, 0:FFN_H], ffn_s[:, FFN_H:2 * FFN_H], ALU.mult, [ffn_s], [ffn_s])
        actb = sub(hb[2], hb[2].t[R, :].bitcast(BF16), "actb")
        Prog.alias(actb, [dtx])
        actb2 = sub(hb[3], hb[3].t[R, :].bitcast(BF16), "actb2")
        Prog.alias(actb2, [dAx])
        CP("dve", actb[:, 0:2048], ffn_s[:, 0:2048], [ffn_s], [actb])
        CP("dve", actb2[:, 0:768], ffn_s[:, 2048:FFN_H], [ffn_s], [actb2])
        transpose16(actb, actb, 16, actsT)
        bk = palloc()
        bv = bk.t[:].bitcast(BF16)
        for c in range(6):
            TR(bv[:, c * NS:(c + 1) * NS], actb2[:, c * 128:(c + 1) * 128], ident_b[R, R], [actb2, ident_b], [bk])
        CP("act", actsT[:, 16:22, :], bv[:, 0:6 * NS].rearrange("p (c t) -> p c t", c=6), [bk], [actsT])
        pfree(bk)
        bks = [palloc(), palloc()]
        for half in range(2):
            for (c0, n) in ((0, 8), (8, 8), (16, 6)):
                w = wblock_fo(c0, n, half)
                for ci in range(n):
                    c = c0 + ci
                    MM(bks[half][R, :], actsT[:, c, :], w[:, ci, :], c == 0, c == NFC - 1, [actsT, w], [bks[half]])
        post(bks, nw_ffn, h_s, xs_t)
        P.dma(DQ, y_sample[:, :], xs_t[:], xs_t, reads=[xs_t])

    if do_sample:
        sample_phase()

    for v_ in vcur:
        MSET("pool", v_[:, :, :, 64:65], 1.0, [v_])

    ntiles = nseq * NT
    per_tile = (len(bulk) + ntiles - 1) // ntiles
    tiles = [(s, T) for s in range(nseq) for T in range(NT)]
    phase_A(*tiles[0])
    for i, (s, T) in enumerate(tiles):
        quota[0] = per_tile
        prompt_tile(s, T, tiles[i + 1] if i + 1 < len(tiles) else None)
        bulk_drain(quota[0])
    bulk_drain(len(bulk))

    P.finalize(window=sched_window)
    P.close()
    return nc


_CACHE = {}
OUT_NAMES = ["y_prompt", "y_sample", "p_kv0", "p_kv1", "p_kv2", "p_conv", "p_ssm",
             "s_kv0", "s_kv1", "s_kv2", "s_conv", "s_ssm"]


def make_in_maps(inp, ncores, nseq, ns, seq, past_override=None):
    f = lambda a: np.ascontiguousarray(np.asarray(a, dtype=np.float32))
    consts = host_consts(seq)
    shared = {
        "norm_mix_pre": f(inp["norm_mix_pre"]), "norm_mix_post": f(inp["norm_mix_post"]),
        "norm_ffn_pre": f(inp["norm_ffn_pre"]), "norm_ffn_post": f(inp["norm_ffn_post"]),
        "w_in": f(inp["w_in"][0]), "w_out": f(inp["w_out"][0]),
        "conv_w": f(inp["conv_w"][0]), "conv_b": f(inp["conv_b"]),
        "dt_bias": f(inp["dt_bias"]), "a_log": f(inp["a_log"]), "d_skip": f(inp["d_skip"]),
        "ssd_norm_w": f(inp["ssd_norm_w"]),
        "w_ffn_in": f(inp["w_ffn_in"][0]), "w_ffn_out": f(inp["w_ffn_out"][0]),
    }
    shared.update(consts)
    xs = np.asarray(inp["x_sample"], dtype=np.float32)
    caches = [np.asarray(inp[k], dtype=np.float32) for k in ("cache_kv_w128", "cache_kv_w512", "cache_kv_w2048")]
    maps = []
    for c in range(ncores):
        m = dict(shared)
        m["x_prompt"] = f(inp["x_prompt"][c * nseq:(c + 1) * nseq])
        m["x_sample"] = f(xs[c * ns:(c + 1) * ns, 0, :])
        for g in range(3):
            cg = caches[g][0, c * ns:(c + 1) * ns]
            if past_override is not None:
                cg = cg[:, :min(WINS[g], past_override)]
            m["cache%d" % g] = f(cg.reshape(ns, cg.shape[1], 2, 512))
        m["state_conv"] = f(np.asarray(inp["state_conv"])[0, c * ns:(c + 1) * ns])
        m["state_ssm"] = f(np.asarray(inp["state_ssm"])[0, c * ns:(c + 1) * ns].reshape(ns, 1024, 128))
        maps.append(m)
    return maps


def assemble(results, ncores, nseq, ns, seq, past=PAST):
    cat = lambda k: np.concatenate([np.asarray(r[k]) for r in results], axis=0)
    pw = [min(w, seq) for w in WINS]
    wb = [min(w, past) for w in WINS]
    B = ncores * nseq
    S = ncores * ns
    outs = [
        cat("y_prompt"),
        cat("y_sample").reshape(S, 1, D),
        cat("p_kv0").reshape(1, B, pw[0], 2, NH, HD),
        cat("p_kv1").reshape(1, B, pw[1], 2, NH, HD),
        cat("p_kv2").reshape(1, B, pw[2], 2, NH, HD),
        cat("p_conv").reshape(1, B, 3, 1536),
        cat("p_ssm").reshape(1, B, 16, 64, 128),
        cat("s_kv0").reshape(1, S, wb[0], 2, NH, HD),
        cat("s_kv1").reshape(1, S, wb[1], 2, NH, HD),
        cat("s_kv2").reshape(1, S, wb[2], 2, NH, HD),
        cat("s_conv").reshape(1, S, 3, 1536),
        cat("s_ssm").reshape(1, S, 16, 64, 128),
    ]
    return tuple(np.ascontiguousarray(o, dtype=np.float32) for o in outs)


def kernel(**inp):
    ncores = 8
    B, seq = inp["x_prompt"].shape[0], inp["x_prompt"].shape[1]
    S = inp["x_sample"].shape[0]
    nseq, ns = B // ncores, S // ncores
    key = (nseq, seq, ns)
    if key not in _CACHE:
        _CACHE[key] = build(nseq, seq, ns)
    nc = _CACHE[key]
    maps = make_in_maps(inp, ncores, nseq, ns, seq)
    res = run_bass_kernel_spmd(nc, maps, core_ids=list(range(ncores)))
    return assemble(res.results, ncores, nseq, ns, seq)
```
